# Optimizing a Trainium2 kernel written in Bass

```python
import math
import jax, jax.numpy as jnp
from jax import lax
import numpy as np

D_MODEL = 1024
BATCH = 4
SEQ = 8192
DEPTH = 1

CHUNK = 64
Q_BLOCK = 2 * CHUNK
N_META = 16
HEAD_DIM = 64
SB_HEADS = 8
SB_WIDTH = SB_HEADS * HEAD_DIM
LRU_WIDTH = D_MODEL - SB_WIDTH
LRU_BLOCKS = 8
LRU_BLOCK_DIM = LRU_WIDTH // LRU_BLOCKS
LRU_C = 8.0
LRU_CONV = 4
D_FF = 2816
FFN_CONV = 3
IN_COLS = 3 * SB_WIDTH + 2 * LRU_WIDTH
EPS = 1e-6

kernel_name = "hymba_stickbreak_rglru_convffn"


def rmsnorm(x, g):
    xf = x.astype(jnp.float32)
    y = xf * lax.rsqrt(jnp.mean(xf * xf, axis=-1, keepdims=True) + EPS)
    return (y * g.astype(jnp.float32)).astype(x.dtype)


def causal_dwconv(x, w, b):
    k_w = w.shape[0]
    c = x.shape[-1]
    out = lax.conv_general_dilated(
        x, w[:, None, :].astype(x.dtype), window_strides=(1,), padding=[(k_w - 1, 0)],
        dimension_numbers=("NWC", "WIO", "NWC"), feature_group_count=c)
    return out + b.astype(x.dtype)


def stick_breaking(q, k, v):
    t_len = q.shape[2]
    scale = HEAD_DIM ** -0.5
    outs = []
    for blk in range(t_len // Q_BLOCK):
        q0 = blk * Q_BLOCK
        k_end = q0 + Q_BLOCK
        qb = q[:, :, q0:k_end]
        kb = k[:, :, :k_end]
        vb = v[:, :, :k_end]
        z = jnp.einsum("bhqd,bhkd->bhqk", qb, kb) * scale
        t_pos = q0 + jnp.arange(Q_BLOCK)[:, None]
        s_pos = jnp.arange(k_end)[None, :]
        mask = s_pos < t_pos
        log_keep = jnp.where(mask, jax.nn.log_sigmoid(-z), 0.0)
        after = lax.cumsum(log_keep, axis=3, reverse=True) - log_keep
        w = jnp.where(mask, jnp.exp(jax.nn.log_sigmoid(z) + after), 0.0)
        outs.append(jnp.einsum("bhqk,bhkd->bhqd", w, vb))
    return jnp.concatenate(outs, axis=2)


def rg_lru(xr, r, i, lam):
    log_a = -LRU_C * r * jax.nn.softplus(-lam.astype(jnp.float32))
    a = jnp.exp(log_a)
    b = jnp.sqrt(-jnp.expm1(2.0 * log_a)) * (i * xr)

    def combine(left, right):
        a1, b1 = left
        a2, b2 = right
        return a1 * a2, a2 * b1 + b2

    _, h = lax.associative_scan(combine, (a, b), axis=1)
    return h


def setup_inputs(seed: int = 0) -> dict:
    key = jax.random.key(seed)
    ks = jax.random.split(key, 24)
    f32 = jnp.float32

    def nrm(k, shape, scale):
        return jax.random.normal(k, shape, f32) * scale

    a8 = jax.random.uniform(ks[12], (DEPTH, LRU_WIDTH), f32, 0.9, 0.999)
    a_base = a8 ** (1.0 / LRU_C)
    lru_lambda = jnp.log(a_base) - jnp.log1p(-a_base)

    return {
        "x": nrm(ks[0], (BATCH, SEQ, D_MODEL), 1.0),
        "meta_tokens": nrm(ks[1], (N_META, D_MODEL), 1.0),
        "norm1_g": 1.0 + nrm(ks[2], (DEPTH, D_MODEL), 0.01),
        "w_in": nrm(ks[3], (DEPTH, D_MODEL, IN_COLS), D_MODEL ** -0.5),
        "q_norm_g": 1.0 + nrm(ks[4], (DEPTH, HEAD_DIM), 0.01),
        "k_norm_g": 1.0 + nrm(ks[5], (DEPTH, HEAD_DIM), 0.01),
        "conv_w": nrm(ks[6], (DEPTH, LRU_CONV, LRU_WIDTH), LRU_CONV ** -0.5),
        "conv_b": nrm(ks[7], (DEPTH, LRU_WIDTH), 0.01),
        "w_rg_a": nrm(ks[8], (DEPTH, LRU_BLOCKS, LRU_BLOCK_DIM, LRU_BLOCK_DIM), LRU_BLOCK_DIM ** -0.5),
        "b_rg_a": nrm(ks[9], (DEPTH, LRU_WIDTH), 0.01),
        "w_rg_i": nrm(ks[10], (DEPTH, LRU_BLOCKS, LRU_BLOCK_DIM, LRU_BLOCK_DIM), LRU_BLOCK_DIM ** -0.5),
        "b_rg_i": nrm(ks[11], (DEPTH, LRU_WIDTH), 0.01),
        "lru_lambda": lru_lambda,
        "w_out": nrm(ks[13], (DEPTH, D_MODEL, D_MODEL), D_MODEL ** -0.5),
        "norm2_g": 1.0 + nrm(ks[14], (DEPTH, D_MODEL), 0.01),
        "w_ffn_in": nrm(ks[15], (DEPTH, D_MODEL, 2 * D_FF), D_MODEL ** -0.5),
        "ffn_conv_w": nrm(ks[16], (DEPTH, FFN_CONV, 2 * D_FF), FFN_CONV ** -0.5),
        "ffn_conv_b": nrm(ks[17], (DEPTH, 2 * D_FF), 0.01),
        "w_ffn_out": nrm(ks[18], (DEPTH, D_FF, D_MODEL), D_FF ** -0.5),
    }


def reference(x, meta_tokens, norm1_g, w_in, q_norm_g, k_norm_g, conv_w, conv_b,
              w_rg_a, b_rg_a, w_rg_i, b_rg_i, lru_lambda, w_out, norm2_g,
              w_ffn_in, ffn_conv_w, ffn_conv_b, w_ffn_out):
    bsz, seq, d = x.shape
    total = seq + N_META
    t_pad = -(-total // Q_BLOCK) * Q_BLOCK
    meta = jnp.broadcast_to(meta_tokens[None].astype(x.dtype), (bsz, N_META, d))
    pad = jnp.zeros((bsz, t_pad - total, d), x.dtype)
    h = jnp.concatenate([meta, x, pad], axis=1)
    t_len = t_pad

    def to_heads(t):
        return t.reshape(bsz, t_len, SB_HEADS, HEAD_DIM).transpose(0, 2, 1, 3)

    for layer in range(DEPTH):
        hn = rmsnorm(h, norm1_g[layer])
        proj = hn @ w_in[layer]
        q, k, v, xr, yg = jnp.split(
            proj, [SB_WIDTH, 2 * SB_WIDTH, 3 * SB_WIDTH, 3 * SB_WIDTH + LRU_WIDTH], axis=-1)

        qh = rmsnorm(to_heads(q), q_norm_g[layer]).astype(jnp.float32)
        kh = rmsnorm(to_heads(k), k_norm_g[layer]).astype(jnp.float32)
        vh = to_heads(v).astype(jnp.float32)
        o_sb = stick_breaking(qh, kh, vh).transpose(0, 2, 1, 3).reshape(bsz, t_len, SB_WIDTH)

        xc = causal_dwconv(xr, conv_w[layer], conv_b[layer]).astype(jnp.float32)
        xblk = xc.reshape(bsz, t_len, LRU_BLOCKS, LRU_BLOCK_DIM)
        r_gate = jax.nn.sigmoid(
            jnp.einsum("btnc,ncd->btnd", xblk, w_rg_a[layer].astype(jnp.float32)).reshape(bsz, t_len, LRU_WIDTH)
            + b_rg_a[layer].astype(jnp.float32))
        i_gate = jax.nn.sigmoid(
            jnp.einsum("btnc,ncd->btnd", xblk, w_rg_i[layer].astype(jnp.float32)).reshape(bsz, t_len, LRU_WIDTH)
            + b_rg_i[layer].astype(jnp.float32))
        h_lru = rg_lru(xc, r_gate, i_gate, lru_lambda[layer])
        o_lru = jax.nn.gelu(yg.astype(jnp.float32), approximate=True) * h_lru

        mixed = jnp.concatenate([o_sb, o_lru], axis=-1).astype(h.dtype)
        h = h + mixed @ w_out[layer]

        hn2 = rmsnorm(h, norm2_g[layer])
        ug = causal_dwconv(hn2 @ w_ffn_in[layer], ffn_conv_w[layer], ffn_conv_b[layer])
        u, g = jnp.split(ug, [D_FF], axis=-1)
        h = h + (jax.nn.silu(g) * u) @ w_ffn_out[layer]

    return h[:, N_META:N_META + seq]
```

```python
import numpy as np
import concourse.bass as bass
import concourse.mybir as mybir
from concourse.bass_utils import run_bass_kernel_spmd

F32 = mybir.dt.float32
BF16 = mybir.dt.bfloat16
AF = mybir.ActivationFunctionType
ALU = mybir.AluOpType

D = 1024
DC = 8
DFF = 2816
NSLAB = 44
NPAIR = 22
EPS = 1e-6
NMETA = 16
PAD = 112
GELU_C = 0.7978845608028654

V_G1, V_G2, V_QG, V_KG, V_CW, V_CB, V_BA, V_BI, V_LAM, V_FCW, V_FCB = 0, 8, 16, 17, 18, 26, 28, 30, 32, 34, 166
NV = 210
C_ID, C_NL, C_NU, C_BO, C_MK = 0, 128, 256, 384, 512
NCST = 512 + 4 * 512


class Sched:
    ENG = ("pe", "act", "dve", "pool", "sp")

    def __init__(self):
        self.ops = []
        self.last_w = {}
        self.readers = {}
        self.eng_count = {e: 0 for e in self.ENG}
        self.chan_count = {}
        self.chan_inc = {}
        self.barrier_nodes = []

    def add(self, eng, fn, reads=(), writes=(), chan=None, ndma=1, inc=16):
        deps = set(self.barrier_nodes)
        for k in reads:
            w = self.last_w.get(k)
            if w is not None:
                deps.add(w)
        for k in writes:
            w = self.last_w.get(k)
            if w is not None:
                deps.add(w)
            for r in self.readers.get(k, ()):
                deps.add(r)
        if chan is None:
            self.eng_count[eng] += 1
            node = ("E", eng, self.eng_count[eng])
        else:
            self.chan_inc[chan] = inc
            self.chan_count[chan] = self.chan_count.get(chan, 0) + ndma * inc
            node = ("C", chan, self.chan_count[chan])
        self.ops.append(dict(eng=eng, fn=fn, deps=deps, node=node, chan=chan))
        for k in reads:
            self.readers.setdefault(k, []).append(node)
        for k in writes:
            self.last_w[k] = node
            self.readers[k] = []
        return node

    def barrier(self):
        nodes = []
        for e, c in self.eng_count.items():
            if c:
                nodes.append(("E", e, c))
        for ch, c in self.chan_count.items():
            nodes.append(("C", ch, c))
        self.barrier_nodes = nodes

    def finalize(self):
        known = {e: {} for e in self.ENG}
        signal = {e: set() for e in self.ENG}
        for op in self.ops:
            need = {}
            for kind, tgt, val in op["deps"]:
                if kind == "E" and tgt == "pe" and op["eng"] == "pe" and op["chan"] is None:
                    continue
                key = (kind, tgt)
                if val > need.get(key, 0):
                    need[key] = val
            waits = []
            kn = known[op["eng"]]
            for key, val in need.items():
                if kn.get(key, 0) >= val:
                    continue
                kn[key] = val
                waits.append((key, val))
                if key[0] == "E":
                    signal[key[1]].add(val)
            op["waits"] = waits
        self.rank = {}
        for e in self.ENG:
            self.rank[e] = {v: i + 1 for i, v in enumerate(sorted(signal[e]))}

    def emit(self, nc, block, sems, chan_sems):
        streams = {e: [op for op in self.ops if op["eng"] == e] for e in self.ENG}

        def run(eng_name, eng):
            for op in streams[eng_name]:
                for (kind, tgt), val in op["waits"]:
                    if kind == "E":
                        eng.wait_ge(sems[tgt], self.rank[tgt][val])
                    else:
                        eng.wait_ge(chan_sems[tgt], val)
                if op["chan"] is not None:
                    op["fn"](eng, chan_sems[op["chan"]])
                else:
                    ins = op["fn"](eng)
                    idx = op["node"][2]
                    if idx in self.rank[eng_name]:
                        assert ins is not None
                        ins.then_inc(sems[eng_name], 1)

        @block.tensor
        def _(e):
            run("pe", e)

        @block.scalar
        def _(e):
            run("act", e)

        @block.vector
        def _(e):
            run("dve", e)

        @block.gpsimd
        def _(e):
            run("pool", e)

        @block.sync
        def _(e):
            run("sp", e)


def build_nc(SEQ):
    assert SEQ % 1024 == 0
    HALF = SEQ // 2
    TP = SEQ + 128
    NB = TP // 128
    NQT = 1 + SEQ // 512
    NST = HALF // 256
    W3 = 256

    nc = bass.Bass("TRN2", target_bir_lowering=False)
    xpad = nc.dram_tensor("xpad", [TP, D], F32, kind="ExternalInput").ap()
    w_in = nc.dram_tensor("w_in", [D, 1280], F32, kind="ExternalInput").ap()
    vecs = nc.dram_tensor("vecs", [128, NV], F32, kind="ExternalInput").ap()
    w_rg = nc.dram_tensor("w_rg", [128, 4 * 128], F32, kind="ExternalInput").ap()
    w_out = nc.dram_tensor("w_out", [D, D], F32, kind="ExternalInput").ap()
    w_fi = nc.dram_tensor("w_ffn_in", [D, 2 * DFF], F32, kind="ExternalInput").ap()
    w_fo = nc.dram_tensor("w_ffn_out", [DFF, D], F32, kind="ExternalInput").ap()
    consts = nc.dram_tensor("consts", [128, NCST], F32, kind="ExternalInput").ap()
    out = nc.dram_tensor("out", [HALF, D], F32, kind="ExternalOutput").ap()
    GW = SEQ // 4
    mo_g = [nc.dram_tensor("mixed_own_%d" % g, [512, GW], BF16) for g in range(4)]
    mo_halo_t = nc.dram_tensor("mixed_own_halo", [512, 4], BF16)
    ma_big_t = nc.dram_tensor("mixed_all_big", [4 * 1024, GW], BF16)
    ma_halo_t = nc.dram_tensor("mixed_all_halo", [1024, 4], BF16)
    xhalf = nc.dram_tensor("xhalf", [HALF + 2, D], F32, kind="ExternalInput").ap()
    mh_t = nc.dram_tensor("mixed_half", [1024, HALF + 2], BF16)
    mixed_half = mh_t.ap()
    ma_big = ma_big_t.ap()
    ma_halo = ma_halo_t.ap()
    mo_halo = mo_halo_t.ap()

    def store_fn(src, row0, ti):
        pieces = []
        if ti == 0:
            pieces.append((mo_halo[row0:row0 + 128, 0:2], 126, 2))
        else:
            i0 = 512 * (ti - 1)
            c = 0
            while c < 512:
                g = (i0 + c) // GW
                gc = (i0 + c) % GW
                n = min(512 - c, GW - gc)
                pieces.append((mo_g[g].ap()[row0:row0 + 128, gc:gc + n], c, n))
                c += n
            if i0 <= HALF - 2 < i0 + 512:
                pieces.append((mo_halo[row0:row0 + 128, 2:4], HALF - 2 - i0, 2))

        def fn(e, s):
            ins = None
            for dst, c0, n in pieces:
                ins = e.dma_start(out=dst, in_=src[:, c0:c0 + n]).then_inc(s, 16)
            return ins
        return fn, len(pieces)

    S = Sched()
    ARENA_F = 53000

    ctx = nc.sbuf_tensor("arena", [128, ARENA_F], F32)
    arena = ctx.__enter__()
    pctx = nc.psum_tensor("ps", [128, 8 * 512], F32)
    ps = pctx.__enter__()

    class Arena:
        def __init__(self):
            self.off = 0

        def f32(self, n):
            o = self.off
            self.off += n
            assert self.off <= ARENA_F, self.off
            return arena[:, o:o + n]

        def bf(self, n):
            nf = (n + 1) // 2
            o = self.off
            self.off += nf
            assert self.off <= ARENA_F, self.off
            return arena[:, o:o + nf].bitcast(BF16)

    A = Arena()

    def bank(b, n=512):
        return ps[:, b * 512:b * 512 + n]

    def bank_bf(b):
        return ps[:, b * 512:(b + 1) * 512].bitcast(BF16)

    cst = A.bf(NCST)
    vec = A.f32(NV)
    ext = A.f32(16)
    ident = cst[:, C_ID:C_ID + 128]
    negL = cst[:, C_NL:C_NL + 128]
    negU = cst[:, C_NU:C_NU + 128]
    bones = cst[:, C_BO:C_BO + 128]

    def maskv(j, W):
        return cst[:, C_MK + j * 512:C_MK + j * 512 + W]

    mark_stage = A.off

    qT = A.bf(2 * TP)
    kT = A.bf(2 * TP)
    v_sb = A.bf(NB * 256)
    mark12 = A.off
    win_bf = A.bf(8 * 1280)
    wrg_bf = A.bf(4 * 128)
    stg = [A.f32(1280), A.f32(1280)]
    xs = [A.f32(1024), A.f32(1024)]
    sqj = A.bf(1024)
    hn_bf = [A.bf(1024), A.bf(1024)]
    ssb = A.f32(8)
    hnT = A.bf(8 * 512)
    sqb = A.bf(512)
    rqb = A.f32(512)
    xr_sb = [A.f32(3 + 512), A.f32(3 + 512)]
    xc_sb = A.f32(512)
    xc_bf = A.bf(512)
    tr_sb = A.f32(512)
    ti_sb = A.f32(512)
    la_sb = A.f32(512)
    a_sb = A.f32(512)
    th_sb = A.f32(512)
    m2_sb = A.f32(512)
    bt_sb = A.f32(512)
    hl_sb = [A.f32(512), A.f32(512)]
    hst = A.f32(2)
    y_sb = A.f32(512)
    y2_sb = A.f32(512)
    tg_sb = A.f32(512)
    ol_bf = [A.bf(512), A.bf(512)]

    def col(i, n=1):
        return vec[:, i:i + n]

    for hh in range(2):
        S.add("sp", lambda e, s, hh=hh: e.dma_start(out=stg[hh][:, 0:1280], in_=consts[:, hh * 1280:(hh + 1) * 1280]).then_inc(s, 16),
              writes=[("stg", hh)], chan=("stg", hh))
        S.add("dve", lambda e, hh=hh: e.tensor_copy(out=cst[:, hh * 1280:(hh + 1) * 1280], in_=stg[hh][:, 0:1280]),
              reads=[("stg", hh)], writes=["cst"])
    S.add("sp", lambda e, s: e.dma_start(out=vec[:, :], in_=vecs[:, :]).then_inc(s, 16), writes=["vec"], chan="vec")
    S.add("act", lambda e: e.activation(out=ext[:, 7:9], in_=col(V_LAM, 2), func=AF.Exp, scale=-1.0),
          reads=["vec"], writes=["ext_t"])
    S.add("act", lambda e: e.activation(out=ext[:, 7:9], in_=ext[:, 7:9], func=AF.Ln, bias=1.0),
          reads=["ext_t"], writes=["ext_t"])
    S.add("dve", lambda e: e.tensor_scalar(out=ext[:, 0:2], in0=ext[:, 7:9], scalar1=-8.0, scalar2=None, op0=ALU.mult),
          reads=["ext_t"], writes=["ext"])
    S.add("dve", lambda e: e.tensor_scalar(out=ext[:, 2:6], in0=col(V_BA, 4), scalar1=-1.0, scalar2=None, op0=ALU.mult),
          reads=["vec", "ext"], writes=["ext"])
    S.add("dve", lambda e: e.tensor_scalar(out=ext[:, 6:7], in0=col(V_KG), scalar1=0.125, scalar2=None, op0=ALU.mult),
          reads=["vec", "ext"], writes=["ext"])
    S.add("sp", lambda e, s: e.dma_start(out=stg[0][:, 0:512], in_=w_rg[:, :]).then_inc(s, 16),
          writes=[("stg", 0)], chan=("stg", 0))
    S.add("dve", lambda e: e.tensor_copy(out=wrg_bf[:, :], in_=stg[0][:, 0:512]), reads=[("stg", 0)], writes=["wrg"])
    for c in range(DC):
        hh = c % 2
        S.add("sp", lambda e, s, c=c, hh=hh: e.dma_start(out=stg[hh][:, :], in_=w_in[c * 128:(c + 1) * 128, :]).then_inc(s, 16),
              writes=[("stg", hh)], chan=("stg", hh))
        eng = "dve" if c % 2 == 0 else "pool"
        S.add(eng, lambda e, c=c, hh=hh: e.tensor_scalar(out=win_bf[:, c * 1280:(c + 1) * 1280], in0=stg[hh][:, :],
                                                          scalar1=col(V_G1 + c), scalar2=None, op0=ALU.mult),
              reads=[("stg", hh), "vec"], writes=[("win", c)])
    S.add("pool", lambda e: e.memset(xr_sb[0][:, 0:3], 0.0), writes=[("xr", 0)])
    S.add("pool", lambda e: e.memset(xr_sb[1][:, 0:3], 0.0), writes=[("xr", 1)])
    S.add("pool", lambda e: e.memset(hst[:, :], 0.0), writes=["hst"])

    win_reads = [("win", c) for c in range(DC)]

    B_TRP, B_V, B_PJ0, B_PJ1, B_PS2, B_GR, B_GI = 0, 1, 2, 3, 4, 5, 6

    def tile_info(ti):
        if ti == 0:
            return 0, 128
        return 128 + 512 * (ti - 1), 512

    pj_ctr = [0]

    def proj_slab(col0, W):
        b = B_PJ0 + (pj_ctr[0] % 2)
        pj_ctr[0] += 1

        def fn(e, b=b, col0=col0, W=W):
            ins = None
            for c in range(DC):
                ins = e.matmul(bank(b, W), lhsT=win_bf[:, c * 1280 + col0:c * 1280 + col0 + 128],
                               rhs=hnT[:, c * 512:c * 512 + W], start=(c == 0), stop=(c == DC - 1))
            return ins
        S.add("pe", fn, reads=win_reads + ["hnT"], writes=[("ps", b)])
        return b

    for ti in range(NQT):
        pos0, W = tile_info(ti)
        nsub = W // 128
        for sub in range(nsub):
            blk = pos0 // 128 + sub
            sl = blk % 2
            S.add("sp", lambda e, s, blk=blk, sl=sl: e.dma_start(out=xs[sl][:, :], in_=xpad[blk * 128:(blk + 1) * 128, :]).then_inc(s, 16),
                  writes=[("xs", sl)], chan=("xs", sl))
            S.add("pool", lambda e, sl=sl: e.memset(ssb[:, sl:sl + 1], 0.0), writes=[("ss", sl)])
            S.add("act", lambda e, sl=sl: e.activation(out=sqj[:, :], in_=xs[sl][:, :], func=AF.Square, accum_out=ssb[:, sl:sl + 1]),
                  reads=[("xs", sl)], writes=[("ss", sl), "sqj"])
            S.add("dve", lambda e, sl=sl: e.tensor_scalar(out=ssb[:, 2 + sl:3 + sl], in0=ssb[:, sl:sl + 1], scalar1=1.0 / D, scalar2=EPS,
                                                           op0=ALU.mult, op1=ALU.add),
                  reads=[("ss", sl)], writes=[("rstd", sl)])
            S.add("act", lambda e, sl=sl: e.activation(out=ssb[:, 2 + sl:3 + sl], in_=ssb[:, 2 + sl:3 + sl], func=AF.Ln),
                  reads=[("rstd", sl)], writes=[("rstd", sl)])
            S.add("act", lambda e, sl=sl: e.activation(out=ssb[:, 2 + sl:3 + sl], in_=ssb[:, 2 + sl:3 + sl], func=AF.Exp, scale=-0.5),
                  reads=[("rstd", sl)], writes=[("rstd", sl)])
            S.add("pool", lambda e, sl=sl: e.tensor_scalar(out=hn_bf[sl][:, :], in0=xs[sl][:, :], scalar1=ssb[:, 2 + sl:3 + sl], scalar2=None,
                                                            op0=ALU.mult),
                  reads=[("xs", sl), ("rstd", sl)], writes=[("hn", sl)])

            def tr_fn(e, sl=sl):
                ins = None
                for c in range(DC):
                    ins = e.transpose(bank_bf(B_TRP)[:, c * 128:(c + 1) * 128], hn_bf[sl][:, c * 128:(c + 1) * 128], ident)
                return ins
            S.add("pe", tr_fn, reads=[("hn", sl), "cst"], writes=[("ps", B_TRP)])
            S.add("dve", lambda e, sub=sub: e.tensor_copy(
                out=hnT.rearrange("p (c w) -> p c w", c=8)[:, :, sub * 128:(sub + 1) * 128],
                in_=bank_bf(B_TRP).rearrange("p (c w) -> p c w", c=8)),
                reads=[("ps", B_TRP)], writes=["hnT"])

            def v_fn(e, sub=sub):
                ins = None
                for c in range(DC):
                    ins = e.matmul(bank(B_V, 256), lhsT=hnT[:, c * 512 + sub * 128:c * 512 + (sub + 1) * 128],
                                   rhs=win_bf[:, c * 1280 + 512:c * 1280 + 768], start=(c == 0), stop=(c == DC - 1))
                return ins
            S.add("pe", v_fn, reads=win_reads + ["hnT"], writes=[("ps", B_V)])
            S.add("act", lambda e, blk=blk: e.copy(out=v_sb[:, blk * 256:(blk + 1) * 256], in_=bank(B_V, 256)),
                  reads=[("ps", B_V)], writes=[("v", blk)])

        for which in range(2):
            for j in range(2):
                b = proj_slab(which * 256 + j * 128, W)
                S.add("act", lambda e, b=b, W=W: e.activation(out=sqb[:, 0:W], in_=bank(b, W), func=AF.Square),
                      reads=[("ps", b)], writes=["sqb"])
                S.add("pe", lambda e, W=W: e.matmul(bank(B_PS2, W), lhsT=bones, rhs=sqb[:, 0:W], start=True, stop=True),
                      reads=["sqb", "cst"], writes=[("ps", B_PS2)])
                S.add("dve", lambda e, W=W: e.tensor_scalar(out=rqb[:, 0:W], in0=bank(B_PS2, W), scalar1=1.0 / 64, scalar2=EPS,
                                                             op0=ALU.mult, op1=ALU.add),
                      reads=[("ps", B_PS2)], writes=["rqb"])
                S.add("act", lambda e, W=W: e.activation(out=rqb[:, 0:W], in_=rqb[:, 0:W], func=AF.Ln), reads=["rqb"], writes=["rqb"])
                S.add("act", lambda e, W=W: e.activation(out=rqb[:, 0:W], in_=rqb[:, 0:W], func=AF.Exp, scale=-0.5),
                      reads=["rqb"], writes=["rqb"])
                dst = qT if which == 0 else kT
                gsc = col(V_QG) if which == 0 else ext[:, 6:7]
                S.add("dve", lambda e, b=b, W=W, dst=dst, gsc=gsc, j=j, pos0=pos0: e.scalar_tensor_tensor(
                    out=dst[:, j * TP + pos0:j * TP + pos0 + W], in0=bank(b, W), scalar=gsc, in1=rqb[:, 0:W],
                    op0=ALU.mult, op1=ALU.mult),
                    reads=[("ps", b), "rqb", "vec", "ext"], writes=[("qk", which, j, ti)])
        for c in range(2):
            b = proj_slab(768 + c * 128, W)
            S.add("act", lambda e, b=b, W=W, c=c: e.copy(out=xr_sb[c][:, 3:3 + W], in_=bank(b, W)),
                  reads=[("ps", b)], writes=[("xr", c)])
            S.add("dve", lambda e, c=c, W=W: e.tensor_scalar(out=xc_sb[:, 0:W], in0=xr_sb[c][:, 3:3 + W], scalar1=col(V_CW + c * 4 + 3),
                                                               scalar2=col(V_CB + c), op0=ALU.mult, op1=ALU.add),
                  reads=[("xr", c), "vec"], writes=["xc"])
            for k in range(3):
                S.add("dve", lambda e, c=c, W=W, k=k: e.scalar_tensor_tensor(out=xc_sb[:, 0:W], in0=xr_sb[c][:, k:k + W],
                                                                               scalar=col(V_CW + c * 4 + k), in1=xc_sb[:, 0:W],
                                                                               op0=ALU.mult, op1=ALU.add),
                      reads=[("xr", c), "vec", "xc"], writes=["xc"])
            S.add("pool", lambda e, c=c, W=W: e.tensor_copy(out=xr_sb[c][:, 0:3], in_=xr_sb[c][:, W:W + 3]),
                  reads=[("xr", c)], writes=[("xr", c)])
            S.add("pool", lambda e, W=W: e.tensor_copy(out=xc_bf[:, 0:W], in_=xc_sb[:, 0:W]), reads=["xc"], writes=["xcb"])
            S.add("pe", lambda e, c=c, W=W: e.matmul(bank(B_GR, W), lhsT=wrg_bf[:, (0 * 2 + c) * 128:(0 * 2 + c + 1) * 128],
                                                      rhs=xc_bf[:, 0:W], start=True, stop=True),
                  reads=["xcb", "wrg"], writes=[("ps", B_GR)])
            S.add("pe", lambda e, c=c, W=W: e.matmul(bank(B_GI, W), lhsT=wrg_bf[:, (1 * 2 + c) * 128:(1 * 2 + c + 1) * 128],
                                                      rhs=xc_bf[:, 0:W], start=True, stop=True),
                  reads=["xcb", "wrg"], writes=[("ps", B_GI)])
            S.add("act", lambda e, c=c, W=W: e.activation(out=tr_sb[:, 0:W], in_=bank(B_GR, W), func=AF.Exp,
                                                           bias=ext[:, 2 + c:3 + c], scale=-1.0),
                  reads=[("ps", B_GR), "ext"], writes=["tr"])
            S.add("act", lambda e, c=c, W=W: e.activation(out=ti_sb[:, 0:W], in_=bank(B_GI, W), func=AF.Exp,
                                                           bias=ext[:, 4 + c:5 + c], scale=-1.0),
                  reads=[("ps", B_GI), "ext"], writes=["tig"])
            S.add("dve", lambda e, W=W: e.tensor_scalar(out=tr_sb[:, 0:W], in0=tr_sb[:, 0:W], scalar1=1.0, scalar2=None, op0=ALU.add),
                  reads=["tr"], writes=["tr"])
            S.add("dve", lambda e, W=W: e.reciprocal(out=tr_sb[:, 0:W], in_=tr_sb[:, 0:W]), reads=["tr"], writes=["tr"])
            S.add("dve", lambda e, W=W: e.tensor_scalar(out=ti_sb[:, 0:W], in0=ti_sb[:, 0:W], scalar1=1.0, scalar2=None, op0=ALU.add),
                  reads=["tig"], writes=["tig"])
            S.add("dve", lambda e, W=W: e.reciprocal(out=ti_sb[:, 0:W], in_=ti_sb[:, 0:W]), reads=["tig"], writes=["tig"])
            S.add("act", lambda e, c=c, W=W: e.activation(out=a_sb[:, 0:W], in_=tr_sb[:, 0:W], func=AF.Exp, scale=ext[:, c:c + 1]),
                  reads=["tr", "ext"], writes=["a"])
            S.add("dve", lambda e, W=W: e.tensor_tensor(out=m2_sb[:, 0:W], in0=a_sb[:, 0:W], in1=a_sb[:, 0:W], op=ALU.mult),
                  reads=["a"], writes=["m2"])
            S.add("dve", lambda e, W=W: e.tensor_scalar(out=m2_sb[:, 0:W], in0=m2_sb[:, 0:W], scalar1=-1.0, scalar2=1.0,
                                                         op0=ALU.mult, op1=ALU.add),
                  reads=["m2"], writes=["m2"])
            S.add("act", lambda e, W=W: e.activation(out=m2_sb[:, 0:W], in_=m2_sb[:, 0:W], func=AF.Ln), reads=["m2"], writes=["m2"])
            S.add("act", lambda e, W=W: e.activation(out=m2_sb[:, 0:W], in_=m2_sb[:, 0:W], func=AF.Exp, scale=0.5), reads=["m2"], writes=["m2"])
            S.add("dve", lambda e, W=W: e.tensor_tensor(out=bt_sb[:, 0:W], in0=ti_sb[:, 0:W], in1=xc_sb[:, 0:W], op=ALU.mult),
                  reads=["tig", "xc"], writes=["bt"])
            S.add("dve", lambda e, W=W: e.tensor_tensor(out=bt_sb[:, 0:W], in0=bt_sb[:, 0:W], in1=m2_sb[:, 0:W], op=ALU.mult),
                  reads=["bt", "m2"], writes=["bt"])
            if ti == 0:
                S.add("dve", lambda e: e.memset(bt_sb[:, 0:PAD], 0.0), reads=["bt"], writes=["bt"])
            S.add("dve", lambda e, c=c, W=W: e.tensor_tensor_scan(out=hl_sb[c][:, 0:W], data0=a_sb[:, 0:W], data1=bt_sb[:, 0:W],
                                                                   initial=hst[:, c:c + 1], op0=ALU.mult, op1=ALU.add),
                  reads=["a", "bt", "hst"], writes=[("hl", c)])
            S.add("dve", lambda e, c=c, W=W: e.tensor_copy(out=hst[:, c:c + 1], in_=hl_sb[c][:, W - 1:W]),
                  reads=[("hl", c)], writes=["hst"])
        for c in range(2):
            b = proj_slab(1024 + c * 128, W)
            S.add("act", lambda e, b=b, W=W: e.copy(out=y_sb[:, 0:W], in_=bank(b, W)), reads=[("ps", b)], writes=["y"])
            S.add("act", lambda e, b=b, W=W: e.activation(out=y2_sb[:, 0:W], in_=bank(b, W), func=AF.Square),
                  reads=[("ps", b)], writes=["y2"])
            S.add("pool", lambda e, W=W: e.tensor_scalar(out=y2_sb[:, 0:W], in0=y2_sb[:, 0:W], scalar1=0.044715, scalar2=1.0,
                                                          op0=ALU.mult, op1=ALU.add),
                  reads=["y2"], writes=["y2"])
            S.add("pool", lambda e, W=W: e.tensor_tensor(out=y2_sb[:, 0:W], in0=y2_sb[:, 0:W], in1=y_sb[:, 0:W], op=ALU.mult),
                  reads=["y2", "y"], writes=["y2"])
            S.add("act", lambda e, W=W: e.activation(out=tg_sb[:, 0:W], in_=y2_sb[:, 0:W], func=AF.Exp, scale=-2.0 * GELU_C),
                  reads=["y2"], writes=["tg"])
            S.add("dve", lambda e, W=W: e.tensor_scalar(out=tg_sb[:, 0:W], in0=tg_sb[:, 0:W], scalar1=1.0, scalar2=None, op0=ALU.add),
                  reads=["tg"], writes=["tg"])
            S.add("dve", lambda e, W=W: e.reciprocal(out=tg_sb[:, 0:W], in_=tg_sb[:, 0:W]), reads=["tg"], writes=["tg"])
            S.add("dve", lambda e, W=W: e.tensor_tensor(out=tg_sb[:, 0:W], in0=tg_sb[:, 0:W], in1=y_sb[:, 0:W], op=ALU.mult),
                  reads=["tg", "y"], writes=["tg"])
            S.add("dve", lambda e, c=c, W=W: e.tensor_tensor(out=ol_bf[c][:, 0:W], in0=tg_sb[:, 0:W], in1=hl_sb[c][:, 0:W], op=ALU.mult),
                  reads=["tg", ("hl", c)], writes=[("ol", c)])
            sfn, nd = store_fn(ol_bf[c], 256 + c * 128, ti)
            S.add("pool", sfn, reads=[("ol", c)], writes=[("mo", "l", c, ti)], chan=("ol", c), ndma=nd)

    S.barrier()
    A.off = mark12
    e_sb = [A.f32(512), A.f32(512)]
    sp_sb = [A.bf(512) for _ in range(4)]
    w_sb = [A.bf(512) for _ in range(3)]
    qneg = [A.bf(2 * 512), A.bf(2 * 512)]
    osb = [A.bf(2 * 512), A.bf(2 * 512)]
    B_Z = [0, 1]
    B_P = [2, 3, 4, 5]
    B_O = [6, 7]
    items = []
    for ti in range(NQT):
        pos0, W = tile_info(ti)
        b0 = pos0 // 128
        nsub = W // 128
        kbs = list(range(b0 + nsub - 1, -1, -1))
        for n, kb in enumerate(kbs):
            for h in range(4):
                items.append(dict(ti=ti, pos0=pos0, W=W, b0=b0, kb=kb, h=h, first=(n == 0), last=(n == len(kbs) - 1)))
    NI = len(items)

    def qk_reads(it):
        return [("qk", 0, it["h"] // 2, it["ti"])] + [("qk", 1, it["h"] // 2, t) for t in range(NQT)][:0] + ["kall"]

    S.add("dve", lambda e: e.memset(ext[:, 9:10], 0.0),
          reads=[("qk", w, j, t) for w in range(2) for j in range(2) for t in range(NQT)] + [("v", bl) for bl in range(NB)],
          writes=["kall"])

    def add_qneg(ti):
        pos0, W = tile_info(ti)
        par = ti % 2
        for j in range(2):
            S.add("pool", lambda e, j=j, W=W, pos0=pos0, par=par: e.tensor_scalar(
                out=qneg[par][:, j * 512:j * 512 + W], in0=qT[:, j * TP + pos0:j * TP + pos0 + W], scalar1=-1.0, scalar2=None,
                op0=ALU.mult), reads=["kall"], writes=[("qneg", par, j)])

    def PE1(i):
        it = items[i]
        h, kb, W, pos0 = it["h"], it["kb"], it["W"], it["pos0"]
        j, r = h // 2, (h % 2) * 64
        zb = B_Z[i % 2]
        S.add("pe", lambda e: e.matmul(bank(zb, W), lhsT=kT[r:r + 64, j * TP + kb * 128:j * TP + (kb + 1) * 128],
                                       rhs=qT[r:r + 64, j * TP + pos0:j * TP + pos0 + W], start=True, stop=True),
              reads=["kall"], writes=[("ps", zb)])

    def ACT12(i):
        it = items[i]
        W = it["W"]
        zb = B_Z[i % 2]
        eb = e_sb[i % 2]
        sb = sp_sb[i % 4]
        S.add("act", lambda e: e.activation(out=eb[:, 0:W], in_=bank(zb, W), func=AF.Exp),
              reads=[("ps", zb)], writes=[("e", i % 2)])
        S.add("act", lambda e: e.activation(out=sb[:, 0:W], in_=eb[:, 0:W], func=AF.Ln, bias=1.0),
              reads=[("e", i % 2)], writes=[("sp", i % 4)])
        if it["kb"] >= it["b0"]:
            jj = it["kb"] - it["b0"]
            S.add("dve", lambda e: e.tensor_tensor(out=sb[:, 0:W], in0=sb[:, 0:W], in1=maskv(jj, W), op=ALU.mult),
                  reads=[("sp", i % 4), "cst"], writes=[("sp", i % 4)])

    def PE2(i):
        it = items[i]
        h, kb, W, pos0 = it["h"], it["kb"], it["W"], it["pos0"]
        j, r = h // 2, (h % 2) * 64
        pb = B_P[h]
        sb = sp_sb[i % 4]

        def fn(e):
            e.matmul(bank(pb, W), lhsT=kT[r:r + 64, j * TP + kb * 128:j * TP + (kb + 1) * 128],
                     rhs=qT[r:r + 64, j * TP + pos0:j * TP + pos0 + W], start=it["first"], stop=False, skip_group_check=True)
            return e.matmul(bank(pb, W), lhsT=negL, rhs=sb[:, 0:W], start=False, stop=False, skip_group_check=True)
        S.add("pe", fn, reads=["kall", ("sp", i % 4), "cst"], writes=[("ps", pb)])

    def ACT3(i):
        it = items[i]
        W = it["W"]
        pb = B_P[it["h"]]
        wb = w_sb[i % 3]
        S.add("act", lambda e: e.activation(out=wb[:, 0:W], in_=bank(pb, W), func=AF.Exp),
              reads=[("ps", pb)], writes=[("w", i % 3)])
        if it["kb"] >= it["b0"]:
            jj = it["kb"] - it["b0"]
            S.add("dve", lambda e: e.tensor_tensor(out=wb[:, 0:W], in0=wb[:, 0:W], in1=maskv(jj, W), op=ALU.mult),
                  reads=[("w", i % 3), "cst"], writes=[("w", i % 3)])

    def PE34(i):
        it = items[i]
        h, kb, W, pos0, ti = it["h"], it["kb"], it["W"], it["pos0"], it["ti"]
        j, r = h // 2, (h % 2) * 64
        pb = B_P[h]
        ob = B_O[j]
        wb = w_sb[i % 3]
        sb = sp_sb[i % 4]
        par = ti % 2
        S.add("pe", lambda e: e.matmul(ps[r:r + 64, ob * 512:ob * 512 + W], lhsT=v_sb[:, kb * 256 + h * 64:kb * 256 + (h + 1) * 64],
                                       rhs=wb[:, 0:W], start=it["first"], stop=it["last"], skip_group_check=True),
              reads=[("w", i % 3), "kall"], writes=[("ps", ob)])
        if not it["last"]:
            def fn(e):
                e.matmul(bank(pb, W), lhsT=kT[r:r + 64, j * TP + kb * 128:j * TP + (kb + 1) * 128],
                         rhs=qneg[par][r:r + 64, j * 512:j * 512 + W], start=False, stop=False, skip_group_check=True)
                return e.matmul(bank(pb, W), lhsT=negU, rhs=sb[:, 0:W], start=False, stop=False, skip_group_check=True)
            S.add("pe", fn, reads=["kall", ("sp", i % 4), ("qneg", par, j), "cst"], writes=[("ps", pb)])
        else:
            if h % 2 == 1:
                S.add("dve", lambda e: e.tensor_copy(out=osb[par][:, j * 512:j * 512 + W], in_=bank(ob, W)),
                      reads=[("ps", ob)], writes=[("osb", par, j)])
                sfn, nd = store_fn(osb[par][:, j * 512:(j + 1) * 512], j * 128, ti)
                S.add("pool", sfn, reads=[("osb", par, j)], writes=[("mo", "a", j, ti)], chan=("osb", par, j), ndma=nd)
                if j == 1:
                    maybe_gather(ti)

    RG = [[0, 1], [2, 3], [4, 5], [6, 7]]

    def tile_keys(ti):
        return [("mo", "l", c, ti) for c in range(2)] + [("mo", "a", j, ti) for j in range(2)]

    gathered = set()

    def maybe_gather(ti):
        if ti == 0:
            return
        done_tok = 512 * ti
        for g in range(4):
            if g in gathered or (g + 1) * GW > done_tok:
                continue
            gathered.add(g)
            t_lo = 1 + (g * GW) // 512
            t_hi = 1 + ((g + 1) * GW - 1) // 512
            keys = []
            for t in range(t_lo, t_hi + 1):
                keys += tile_keys(t)
            S.add("pool", lambda e, s, g=g: e.collective_compute(
                "AllGather", ALU.bypass, replica_groups=RG, ins=[mo_g[g].ap().opt()],
                outs=[ma_big[g * 1024:(g + 1) * 1024, :].opt()]).then_inc(s),
                reads=keys, writes=[("mall", g)], chan=("cc", g), inc=1)

    qneg_done = set()
    for s in range(-2, NI):
        if s + 2 < NI:
            t2 = items[s + 2]["ti"]
            if t2 not in qneg_done:
                qneg_done.add(t2)
                add_qneg(t2)
            PE1(s + 2)
        if 0 <= s + 1 < NI:
            ACT12(s + 1)
            PE2(s + 1)
        if s >= 0:
            ACT3(s)
            PE34(s)

    S.add("pool", lambda e, s: e.collective_compute("AllGather", ALU.bypass, replica_groups=RG,
                                                    ins=[mo_halo_t.ap().opt()], outs=[ma_halo_t.ap().opt()]).then_inc(s),
          reads=tile_keys(0) + tile_keys(1 + (HALF - 2) // 512), writes=[("mall", "h")], chan="cch", inc=1)

    def mh_fn(e, s):
        half = e.partition_id() % 2
        ins = None
        for j in range(2):
            for r in range(2):
                ins = e.dma_start(out=mixed_half[r * 512:(r + 1) * 512, 2 + j * GW:2 + (j + 1) * GW],
                                  in_=ma_big[bass.ds(half * 2048 + j * 1024 + r * 512, 512), :]).then_inc(s, 16)
        ins = e.dma_start(out=mixed_half[:, 0:2], in_=ma_halo[:, bass.ds(half * 2, 2)]).then_inc(s, 16)
        return ins
    S.add("sp", mh_fn, reads=[("mall", g) for g in range(4)] + [("mall", "h")], writes=["mhalf"], chan="mh", ndma=5)

    S.barrier()
    A.off = mark_stage
    wout_bf = A.bf(8 * 1024)
    wfi_bf = A.bf(8 * 2 * DFF)
    wfo_bf = A.bf(NPAIR * 1024)
    mark3 = A.off
    stg3 = [A.f32(2816), A.f32(2816)]

    stg_ctr = [0]

    def load_cast(dst_ap, src_ap, ncols, key, scale_col=None):
        sl = stg_ctr[0] % 2
        stg_ctr[0] += 1
        S.add("sp", lambda e, s: e.dma_start(out=stg3[sl][:, 0:ncols], in_=src_ap).then_inc(s, 16),
              writes=[("stg3", sl)], chan=("stg3", sl))
        eng = ("dve", "pool", "act")[stg_ctr[0] % 3]
        if scale_col is None:
            if eng == "act":
                S.add(eng, lambda e: e.copy(out=dst_ap, in_=stg3[sl][:, 0:ncols]), reads=[("stg3", sl)], writes=[key])
            else:
                S.add(eng, lambda e: e.tensor_copy(out=dst_ap, in_=stg3[sl][:, 0:ncols]), reads=[("stg3", sl)], writes=[key])
        else:
            if eng == "act":
                S.add(eng, lambda e: e.activation(out=dst_ap, in_=stg3[sl][:, 0:ncols], func=AF.Copy, scale=scale_col),
                      reads=[("stg3", sl), "vec"], writes=[key])
            else:
                S.add(eng, lambda e: e.tensor_scalar(out=dst_ap, in0=stg3[sl][:, 0:ncols], scalar1=scale_col, scalar2=None, op0=ALU.mult),
                      reads=[("stg3", sl), "vec"], writes=[key])

    w3_keys = []
    for c in range(DC):
        load_cast(wout_bf[:, c * 1024:(c + 1) * 1024], w_out[c * 128:(c + 1) * 128, :], 1024, ("wout", c))
        w3_keys.append(("wout", c))
    for c in range(DC):
        for hh in range(2):
            load_cast(wfi_bf[:, c * 2 * DFF + hh * DFF:c * 2 * DFF + (hh + 1) * DFF], w_fi[c * 128:(c + 1) * 128, hh * DFF:(hh + 1) * DFF],
                      DFF, ("wfi", c, hh), scale_col=col(V_G2 + c))
            w3_keys.append(("wfi", c, hh))
    for j in range(0, NPAIR, 2):
        for jj in range(2):
            load_cast(wfo_bf[:, (j + jj) * 1024:(j + jj + 1) * 1024], w_fo[(j + jj) * 128:(j + jj + 1) * 128, :], 1024, ("wfo", j + jj))
            w3_keys.append(("wfo", j + jj))
    S.add("dve", lambda e: e.memset(ext[:, 10:11], 0.0), reads=w3_keys, writes=["w3all"])
    S.barrier()
    A.off = mark3
    h2 = [A.f32(1024) for _ in range(4)]
    hn2_bf = [A.bf(1024), A.bf(1024)]
    ss3 = A.f32(8)
    sqj3 = A.bf(1024)
    hn2T = A.bf(8 * W3)
    mt = A.bf(8 * W3)
    ust = A.f32(NSLAB * 2)
    ubuf = [A.f32(2 + W3), A.f32(2 + W3)]
    gbuf = [A.f32(2 + W3), A.f32(2 + W3)]
    uc = [A.f32(W3), A.f32(W3)]
    gc = [A.f32(W3), A.f32(W3)]
    sg = [A.f32(W3), A.f32(W3)]
    act_bf = A.bf(NPAIR * W3)
    ctmp = A.f32(W3)

    B_HP = [0, 1]
    B_T3 = 2
    B_U = [3, 4]
    B_G = [5, 6]
    mixed_half_r = mixed_half.rearrange("(c p) w -> p c w", p=128)
    xflat = xpad.rearrange("t d -> (t d)")
    h2_ctr = [0]
    pair_ctr = [0]
    half_cache = {}

    def get_half(e):
        if 'h' not in half_cache:
            half_cache['h'] = e.partition_id() % 2
        return half_cache['h']

    def stage3_tile(rel0, Wt, final):
        nsub = max(1, Wt // 128)
        n = min(Wt, 128)

        def mt_fn(e, s):
            return e.dma_start(out=mt.rearrange("p (c w) -> p c w", c=8)[:, :, 0:Wt],
                               in_=mixed_half_r[:, :, rel0:rel0 + Wt]).then_inc(s, 16)
        S.add("sp", mt_fn, reads=["mhalf"], writes=["mt"], chan="mt")
        slots = []
        for sub in range(nsub):
            sl = h2_ctr[0] % 4
            h2_ctr[0] += 1
            slots.append(sl)

            def x_fn(e, s, sl=sl, sub=sub):
                r = rel0 + sub * 128
                return e.dma_start(out=h2[sl][0:n, :], in_=xhalf[r:r + n, :]).then_inc(s, 16)
            S.add("sp", x_fn, writes=[("h2", sl)], chan=("h2", sl))

            def wo_fn(e, sub=sub):
                ins = None
                for hf in range(2):
                    for c in range(DC):
                        ins = e.matmul(ps[0:n, B_HP[hf] * 512:(B_HP[hf] + 1) * 512], lhsT=mt[:, c * W3 + sub * 128:c * W3 + sub * 128 + n],
                                       rhs=wout_bf[:, c * 1024 + hf * 512:c * 1024 + (hf + 1) * 512], start=(c == 0), stop=(c == DC - 1))
                return ins
            S.add("pe", wo_fn, reads=["mt", "w3all"], writes=[("ps", 0), ("ps", 1)])
            S.add("dve", lambda e, sl=sl: e.tensor_tensor(out=h2[sl][0:n, :], in0=ps[0:n, 0:1024], in1=h2[sl][0:n, :], op=ALU.add),
                  reads=[("ps", 0), ("ps", 1), ("h2", sl)], writes=[("h2", sl)])
            q = sl % 2
            S.add("pool", lambda e, q=q: e.memset(ss3[:, q:q + 1], 0.0), writes=[("ss3", q)])
            S.add("act", lambda e, sl=sl, q=q: e.activation(out=sqj3[0:n, :], in_=h2[sl][0:n, :], func=AF.Square, accum_out=ss3[0:n, q:q + 1]),
                  reads=[("h2", sl)], writes=[("ss3", q), "sqj3"])
            S.add("dve", lambda e, q=q: e.tensor_scalar(out=ss3[0:n, 2 + q:3 + q], in0=ss3[0:n, q:q + 1], scalar1=1.0 / D, scalar2=EPS,
                                                         op0=ALU.mult, op1=ALU.add), reads=[("ss3", q)], writes=[("rs3", q)])
            S.add("act", lambda e, q=q: e.activation(out=ss3[0:n, 2 + q:3 + q], in_=ss3[0:n, 2 + q:3 + q], func=AF.Ln),
                  reads=[("rs3", q)], writes=[("rs3", q)])
            S.add("act", lambda e, q=q: e.activation(out=ss3[0:n, 2 + q:3 + q], in_=ss3[0:n, 2 + q:3 + q], func=AF.Exp, scale=-0.5),
                  reads=[("rs3", q)], writes=[("rs3", q)])
            S.add("pool", lambda e, sl=sl, q=q: e.tensor_scalar(out=hn2_bf[q][0:n, :], in0=h2[sl][0:n, :], scalar1=ss3[0:n, 2 + q:3 + q],
                                                                 scalar2=None, op0=ALU.mult),
                  reads=[("h2", sl), ("rs3", q)], writes=[("hn2", q)])

            def tr_fn(e, q=q):
                ins = None
                for c in range(DC):
                    ins = e.transpose(bank_bf(B_T3)[:, c * 128:c * 128 + n], hn2_bf[q][0:n, c * 128:(c + 1) * 128], ident[0:n, 0:n])
                return ins
            S.add("pe", tr_fn, reads=[("hn2", q), "cst"], writes=[("ps", B_T3)])
            S.add("dve", lambda e, sub=sub: e.tensor_copy(
                out=hn2T.rearrange("p (c w) -> p c w", c=8)[:, :, sub * 128:sub * 128 + n],
                in_=bank_bf(B_T3).rearrange("p (c w) -> p c w", c=8)[:, :, 0:n]),
                reads=[("ps", B_T3)], writes=["hn2T"])
        for j in range(NPAIR):
            pc = pair_ctr[0] % 2
            pair_ctr[0] += 1
            for which, (bb, buf, cbuf) in enumerate(((B_U[pc], ubuf[pc], uc[pc]), (B_G[pc], gbuf[pc], gc[pc]))):
                slab = j + which * NPAIR

                def fi_fn(e, bb=bb, slab=slab):
                    ins = None
                    for c in range(DC):
                        ins = e.matmul(bank(bb, Wt), lhsT=wfi_bf[:, c * 2 * DFF + slab * 128:c * 2 * DFF + (slab + 1) * 128],
                                       rhs=hn2T[:, c * W3:c * W3 + Wt], start=(c == 0), stop=(c == DC - 1))
                    return ins
                S.add("pe", fi_fn, reads=["hn2T", "w3all"], writes=[("ps", bb)])
                if not final:
                    S.add("act", lambda e, bb=bb, slab=slab: e.copy(out=ust[:, slab * 2:slab * 2 + 2], in_=bank(bb, 2)),
                          reads=[("ps", bb)], writes=[("ust", slab)])
                    continue
                bk = ("ub", which, pc)
                S.add("act", lambda e, bb=bb, buf=buf: e.copy(out=buf[:, 2:2 + Wt], in_=bank(bb, Wt)), reads=[("ps", bb)], writes=[bk])
                S.add("pool", lambda e, buf=buf, slab=slab: e.tensor_copy(out=buf[:, 0:2], in_=ust[:, slab * 2:slab * 2 + 2]),
                      reads=[("ust", slab), bk], writes=[bk])
                ck = ("cb", which, pc)
                if which == 0:
                    S.add("dve", lambda e, buf=buf, cbuf=cbuf, slab=slab: e.tensor_scalar(
                        out=cbuf[:, 0:Wt], in0=buf[:, 2:2 + Wt], scalar1=col(V_FCW + slab * 3 + 2), scalar2=col(V_FCB + slab),
                        op0=ALU.mult, op1=ALU.add), reads=[bk, "vec"], writes=[ck])
                    for k in range(2):
                        S.add("dve", lambda e, buf=buf, cbuf=cbuf, slab=slab, k=k: e.scalar_tensor_tensor(
                            out=cbuf[:, 0:Wt], in0=buf[:, k:k + Wt], scalar=col(V_FCW + slab * 3 + k), in1=cbuf[:, 0:Wt],
                            op0=ALU.mult, op1=ALU.add), reads=[bk, "vec", ck], writes=[ck])
                else:
                    S.add("act", lambda e, buf=buf, cbuf=cbuf, slab=slab: e.activation(
                        out=cbuf[:, 0:Wt], in_=buf[:, 2:2 + Wt], func=AF.Identity, scale=col(V_FCW + slab * 3 + 2), bias=col(V_FCB + slab)),
                        reads=[bk, "vec"], writes=[ck])
                    for k in range(2):
                        S.add("pool", lambda e, buf=buf, slab=slab, k=k: e.tensor_scalar(
                            out=ctmp[:, 0:Wt], in0=buf[:, k:k + Wt], scalar1=col(V_FCW + slab * 3 + k), scalar2=None, op0=ALU.mult),
                            reads=[bk, "vec"], writes=["ctmp"])
                        S.add("pool", lambda e, cbuf=cbuf: e.tensor_tensor(out=cbuf[:, 0:Wt], in0=cbuf[:, 0:Wt], in1=ctmp[:, 0:Wt], op=ALU.add),
                              reads=["ctmp", ck], writes=[ck])
                S.add("pool", lambda e, buf=buf, slab=slab: e.tensor_copy(out=ust[:, slab * 2:slab * 2 + 2], in_=buf[:, Wt:Wt + 2]),
                      reads=[bk], writes=[("ust", slab)])
            if final:
                S.add("act", lambda e, pc=pc: e.activation(out=sg[pc][:, 0:Wt], in_=gc[pc][:, 0:Wt], func=AF.Silu),
                      reads=[("cb", 1, pc)], writes=[("sg", pc)])
                S.add("dve", lambda e, pc=pc, j=j: e.tensor_tensor(out=act_bf[:, j * W3:j * W3 + Wt], in0=sg[pc][:, 0:Wt], in1=uc[pc][:, 0:Wt],
                                                                   op=ALU.mult),
                      reads=[("sg", pc), ("cb", 0, pc)], writes=[("actT", j)])
        if not final:
            return
        for sub in range(nsub):
            sl = slots[sub]

            def fo_fn(e, sub=sub):
                ins = None
                for hf in range(2):
                    for j in range(NPAIR):
                        ins = e.matmul(bank(B_HP[hf], 512), lhsT=act_bf[:, j * W3 + sub * 128:j * W3 + (sub + 1) * 128],
                                       rhs=wfo_bf[:, j * 1024 + hf * 512:j * 1024 + (hf + 1) * 512], start=(j == 0), stop=(j == NPAIR - 1))
                return ins
            S.add("pe", fo_fn, reads=[("actT", j) for j in range(NPAIR)] + ["w3all"], writes=[("ps", 0), ("ps", 1)])
            S.add("dve", lambda e, sl=sl: e.tensor_tensor(out=h2[sl][:, :], in0=ps[:, 0:1024], in1=h2[sl][:, :], op=ALU.add),
                  reads=[("ps", 0), ("ps", 1), ("h2", sl)], writes=[("h2", sl)])
            r0 = rel0 - 2 + sub * 128
            S.add("pool", lambda e, s, sl=sl, r0=r0: e.dma_start(out=out[r0:r0 + 128, :], in_=h2[sl][:, :]).then_inc(s, 16),
                  reads=[("h2", sl)], writes=[("h2", sl), ("out", r0)], chan=("h2", sl))

    stage3_tile(0, 2, False)
    for st in range(NST):
        stage3_tile(2 + st * W3, W3, True)

    S.add("sp", lambda e: None, reads=[("out", r0) for r0 in range(0, HALF, 128)], writes=["done"])

    S.finalize()
    chans = list(S.chan_count.keys())
    sem_ctxs = []
    sems = {}
    for e in Sched.ENG:
        c = nc.semaphore("s_" + e)
        sems[e] = c.__enter__()
        sem_ctxs.append(c)
    chan_sems = {}
    for i, ch in enumerate(chans):
        c = nc.semaphore("c_%d" % i)
        chan_sems[ch] = c.__enter__()
        sem_ctxs.append(c)
    with nc.Block() as block:
        S.emit(nc, block, sems, chan_sems)
    for c in reversed(sem_ctxs):
        c.__exit__(None, None, None)
    pctx.__exit__(None, None, None)
    ctx.__exit__(None, None, None)
    return nc


def _consts():
    c = np.zeros((128, NCST), np.float32)
    j = np.arange(128)[:, None]
    s = np.arange(128)[None, :]
    c[:, C_ID:C_ID + 128] = (j == s)
    c[:, C_NL:C_NL + 128] = -(j >= s).astype(np.float32)
    c[:, C_NU:C_NU + 128] = -(j < s).astype(np.float32)
    c[:, C_BO:C_BO + 128] = ((j // 64) == (s // 64))
    t = np.arange(512)[None, :]
    for k in range(4):
        c[:, C_MK + k * 512:C_MK + (k + 1) * 512] = ((128 * k + j) < t)
    return c


def _prep_inputs(inputs, SEQ):
    f = lambda a: np.ascontiguousarray(np.asarray(a), dtype=np.float32)
    x = f(inputs["x"])
    meta = f(inputs["meta_tokens"])
    w_in = f(inputs["w_in"])[0]
    w_out = f(inputs["w_out"])[0]
    w_fi = f(inputs["w_ffn_in"])[0]
    w_fo = f(inputs["w_ffn_out"])[0]
    g1 = f(inputs["norm1_g"])[0]
    g2 = f(inputs["norm2_g"])[0]
    qg = f(inputs["q_norm_g"])[0]
    kg = f(inputs["k_norm_g"])[0]
    cw = f(inputs["conv_w"])[0]
    cb = f(inputs["conv_b"])[0]
    wa = f(inputs["w_rg_a"])[0]
    wi = f(inputs["w_rg_i"])[0]
    ba = f(inputs["b_rg_a"])[0]
    bi = f(inputs["b_rg_i"])[0]
    lam = f(inputs["lru_lambda"])[0]
    fcw = f(inputs["ffn_conv_w"])[0]
    fcb = f(inputs["ffn_conv_b"])[0]
    consts = _consts()
    TP = SEQ + 128
    maps = []
    for core in range(8):
        b, p = core // 2, core % 2
        xpad = np.zeros((TP, D), np.float32)
        xpad[PAD:128] = meta
        xpad[128:] = x[b]
        cs = slice(256 * p, 256 * p + 256)
        wic = np.concatenate([w_in[:, 0:512][:, cs], w_in[:, 512:1024][:, cs], w_in[:, 1024:1536][:, cs],
                              w_in[:, 1536:2048][:, cs], w_in[:, 2048:2560][:, cs]], axis=1)
        vec = np.zeros((128, NV), np.float32)
        vec[:, V_G1:V_G1 + 8] = g1.reshape(8, 128).T
        vec[:, V_G2:V_G2 + 8] = g2.reshape(8, 128).T
        vec[:, V_QG] = np.tile(qg, 2)
        vec[:, V_KG] = np.tile(kg, 2)
        for c in range(2):
            ch = slice(256 * p + 128 * c, 256 * p + 128 * c + 128)
            vec[:, V_CW + c * 4:V_CW + c * 4 + 4] = cw[:, ch].T
            vec[:, V_CB + c] = cb[ch]
            vec[:, V_BA + c] = ba[ch]
            vec[:, V_BI + c] = bi[ch]
            vec[:, V_LAM + c] = lam[ch]
        for s in range(NSLAB):
            vec[:, V_FCW + s * 3:V_FCW + s * 3 + 3] = fcw[:, s * 128:(s + 1) * 128].T
            vec[:, V_FCB + s] = fcb[s * 128:(s + 1) * 128]
        wrg = np.zeros((128, 4 * 128), np.float32)
        for gi, wsrc in enumerate((wa, wi)):
            for c in range(2):
                for k in range(2):
                    blk = 4 * p + 2 * c + k
                    o = (gi * 2 + c) * 128
                    wrg[64 * k:64 * k + 64, o + 64 * k:o + 64 * k + 64] = wsrc[blk]
        perm = []
        for r in range(2):
            perm += list(range(256 * r, 256 * r + 256))
            perm += list(range(512 + 256 * r, 512 + 256 * r + 256))
        maps.append({
            "xpad": xpad, "xhalf": np.ascontiguousarray(xpad[126 + (SEQ // 2) * p:126 + (SEQ // 2) * p + SEQ // 2 + 2]), "w_in": np.ascontiguousarray(wic), "vecs": vec, "w_rg": wrg,
            "w_out": np.ascontiguousarray(w_out[perm]), "w_ffn_in": w_fi, "w_ffn_out": w_fo, "consts": consts,
        })
    return maps


_NC_CACHE = {}


def kernel(**inputs):
    x = np.asarray(inputs["x"])
    B, SEQ, _ = x.shape
    assert B == 4
    if SEQ not in _NC_CACHE:
        _NC_CACHE[SEQ] = build_nc(SEQ)
    nc = _NC_CACHE[SEQ]
    maps = _prep_inputs(inputs, SEQ)
    res = run_bass_kernel_spmd(nc, maps, core_ids=list(range(8)))
    outp = np.empty((B, SEQ, D), np.float32)
    HALF = SEQ // 2
    for core in range(8):
        b, p = core // 2, core % 2
        outp[b, p * HALF:(p + 1) * HALF] = res.results[core]["out"]
    return outp
```

```python
import numpy as np
import concourse.bass as bass
import concourse.mybir as mybir
from concourse.bass_utils import run_bass_kernel_spmd

F32 = mybir.dt.float32
BF16 = mybir.dt.bfloat16
AF = mybir.ActivationFunctionType
ALU = mybir.AluOpType

D = 1024
DC = 8
DFF = 2816
NSLAB = 44
NPAIR = 22
EPS = 1e-6
NMETA = 16
PAD = 112
GELU_C = 0.7978845608028654

V_G1, V_G2, V_QG, V_KG, V_CW, V_CB, V_BA, V_BI, V_LAM, V_FCW, V_FCB = 0, 8, 16, 17, 18, 26, 28, 30, 32, 34, 166
NV = 210
C_ID, C_NL, C_NU, C_BO, C_MK = 0, 128, 256, 384, 512
NCST = 512 + 4 * 512


class Sched:
    ENG = ("pe", "act", "dve", "pool", "sp")

    def __init__(self):
        self.ops = []
        self.last_w = {}
        self.readers = {}
        self.eng_count = {e: 0 for e in self.ENG}
        self.chan_count = {}
        self.chan_inc = {}
        self.barrier_nodes = []

    def add(self, eng, fn, reads=(), writes=(), chan=None, ndma=1, inc=16):
        deps = set(self.barrier_nodes)
        for k in reads:
            w = self.last_w.get(k)
            if w is not None:
                deps.add(w)
        for k in writes:
            w = self.last_w.get(k)
            if w is not None:
                deps.add(w)
            for r in self.readers.get(k, ()):
                deps.add(r)
        if chan is None:
            self.eng_count[eng] += 1
            node = ("E", eng, self.eng_count[eng])
        else:
            self.chan_inc[chan] = inc
            self.chan_count[chan] = self.chan_count.get(chan, 0) + ndma * inc
            node = ("C", chan, self.chan_count[chan])
        self.ops.append(dict(eng=eng, fn=fn, deps=deps, node=node, chan=chan))
        for k in reads:
            self.readers.setdefault(k, []).append(node)
        for k in writes:
            self.last_w[k] = node
            self.readers[k] = []
        return node

    def barrier(self):
        nodes = []
        for e, c in self.eng_count.items():
            if c:
                nodes.append(("E", e, c))
        for ch, c in self.chan_count.items():
            nodes.append(("C", ch, c))
        self.barrier_nodes = nodes

    def finalize(self):
        known = {e: {} for e in self.ENG}
        signal = {e: set() for e in self.ENG}
        for op in self.ops:
            need = {}
            for kind, tgt, val in op["deps"]:
                if kind == "E" and tgt == "pe" and op["eng"] == "pe" and op["chan"] is None:
                    continue
                key = (kind, tgt)
                if val > need.get(key, 0):
                    need[key] = val
            waits = []
            kn = known[op["eng"]]
            for key, val in need.items():
                if kn.get(key, 0) >= val:
                    continue
                kn[key] = val
                waits.append((key, val))
                if key[0] == "E":
                    signal[key[1]].add(val)
            op["waits"] = waits
        self.rank = {}
        for e in self.ENG:
            self.rank[e] = {v: i + 1 for i, v in enumerate(sorted(signal[e]))}

    def emit(self, nc, block, sems, chan_sems):
        streams = {e: [op for op in self.ops if op["eng"] == e] for e in self.ENG}

        def run(eng_name, eng):
            for op in streams[eng_name]:
                for (kind, tgt), val in op["waits"]:
                    if kind == "E":
                        eng.wait_ge(sems[tgt], self.rank[tgt][val])
                    else:
                        eng.wait_ge(chan_sems[tgt], val)
                if op["chan"] is not None:
                    op["fn"](eng, chan_sems[op["chan"]])
                else:
                    ins = op["fn"](eng)
                    idx = op["node"][2]
                    if idx in self.rank[eng_name]:
                        assert ins is not None
                        ins.then_inc(sems[eng_name], 1)

        @block.tensor
        def _(e):
            run("pe", e)

        @block.scalar
        def _(e):
            run("act", e)

        @block.vector
        def _(e):
            run("dve", e)

        @block.gpsimd
        def _(e):
            run("pool", e)

        @block.sync
        def _(e):
            run("sp", e)


def build_nc(SEQ):
    assert SEQ % 1024 == 0
    HALF = SEQ // 2
    TP = SEQ + 128
    NB = TP // 128
    NQT = 1 + SEQ // 512
    NST = HALF // 256
    W3 = 256

    nc = bass.Bass("TRN2", target_bir_lowering=False)
    xpad = nc.dram_tensor("xpad", [TP, D], F32, kind="ExternalInput").ap()
    w_in = nc.dram_tensor("w_in", [D, 1280], F32, kind="ExternalInput").ap()
    vecs = nc.dram_tensor("vecs", [128, NV], F32, kind="ExternalInput").ap()
    w_rg = nc.dram_tensor("w_rg", [128, 4 * 128], F32, kind="ExternalInput").ap()
    w_out = nc.dram_tensor("w_out", [D, D], F32, kind="ExternalInput").ap()
    w_fi = nc.dram_tensor("w_ffn_in", [D, 2 * DFF], F32, kind="ExternalInput").ap()
    w_fo = nc.dram_tensor("w_ffn_out", [DFF, D], F32, kind="ExternalInput").ap()
    consts = nc.dram_tensor("consts", [128, NCST], F32, kind="ExternalInput").ap()
    out = nc.dram_tensor("out", [HALF, D], F32, kind="ExternalOutput").ap()
    GW = SEQ // 4
    mo_g = [nc.dram_tensor("mixed_own_%d" % g, [512, GW], BF16) for g in range(4)]
    mo_halo_t = nc.dram_tensor("mixed_own_halo", [512, 4], BF16)
    ma_big_t = nc.dram_tensor("mixed_all_big", [4 * 1024, GW], BF16)
    ma_halo_t = nc.dram_tensor("mixed_all_halo", [1024, 4], BF16)
    xhalf = nc.dram_tensor("xhalf", [HALF + 2, D], F32, kind="ExternalInput").ap()
    mh_t = nc.dram_tensor("mixed_half", [1024, HALF + 2], BF16)
    mixed_half = mh_t.ap()
    ma_big = ma_big_t.ap()
    ma_halo = ma_halo_t.ap()
    mo_halo = mo_halo_t.ap()

    def store_fn(src, row0, ti):
        pieces = []
        if ti == 0:
            pieces.append((mo_halo[row0:row0 + 128, 0:2], 126, 2))
        else:
            i0 = 512 * (ti - 1)
            c = 0
            while c < 512:
                g = (i0 + c) // GW
                gc = (i0 + c) % GW
                n = min(512 - c, GW - gc)
                pieces.append((mo_g[g].ap()[row0:row0 + 128, gc:gc + n], c, n))
                c += n
            if i0 <= HALF - 2 < i0 + 512:
                pieces.append((mo_halo[row0:row0 + 128, 2:4], HALF - 2 - i0, 2))

        def fn(e, s):
            ins = None
            for dst, c0, n in pieces:
                ins = e.dma_start(out=dst, in_=src[:, c0:c0 + n]).then_inc(s, 16)
            return ins
        return fn, len(pieces)

    S = Sched()
    ARENA_F = 53000

    ctx = nc.sbuf_tensor("arena", [128, ARENA_F], F32)
    arena = ctx.__enter__()
    pctx = nc.psum_tensor("ps", [128, 8 * 512], F32)
    ps = pctx.__enter__()

    class Arena:
        def __init__(self):
            self.off = 0

        def f32(self, n):
            o = self.off
            self.off += n
            assert self.off <= ARENA_F, self.off
            return arena[:, o:o + n]

        def bf(self, n):
            nf = (n + 1) // 2
            o = self.off
            self.off += nf
            assert self.off <= ARENA_F, self.off
            return arena[:, o:o + nf].bitcast(BF16)

    A = Arena()

    def bank(b, n=512):
        return ps[:, b * 512:b * 512 + n]

    def bank_bf(b):
        return ps[:, b * 512:(b + 1) * 512].bitcast(BF16)

    cst = A.bf(NCST)
    vec = A.f32(NV)
    ext = A.f32(16)
    ident = cst[:, C_ID:C_ID + 128]
    negL = cst[:, C_NL:C_NL + 128]
    negU = cst[:, C_NU:C_NU + 128]
    bones = cst[:, C_BO:C_BO + 128]

    def maskv(j, W):
        return cst[:, C_MK + j * 512:C_MK + j * 512 + W]

    mark_stage = A.off

    qT = A.bf(2 * TP)
    kT = A.bf(2 * TP)
    v_sb = A.bf(NB * 256)
    mark12 = A.off
    win_bf = A.bf(8 * 1280)
    wrg_bf = A.bf(4 * 128)
    stg = [A.f32(1280), A.f32(1280)]
    xs = [A.f32(1024), A.f32(1024)]
    sqj = A.bf(1024)
    hn_bf = [A.bf(1024), A.bf(1024)]
    ssb = A.f32(8)
    hnT = A.bf(8 * 512)
    sqb = A.bf(512)
    rqb = A.f32(512)
    xr_sb = [A.f32(3 + 512), A.f32(3 + 512)]
    xc_sb = A.f32(512)
    xc_bf = A.bf(512)
    tr_sb = A.f32(512)
    ti_sb = A.f32(512)
    la_sb = A.f32(512)
    a_sb = A.f32(512)
    th_sb = A.f32(512)
    m2_sb = A.f32(512)
    bt_sb = A.f32(512)
    hl_sb = [A.f32(512), A.f32(512)]
    hst = A.f32(2)
    y_sb = A.f32(512)
    y2_sb = A.f32(512)
    tg_sb = A.f32(512)
    ol_bf = [A.bf(512), A.bf(512)]

    def col(i, n=1):
        return vec[:, i:i + n]

    for hh in range(2):
        S.add("sp", lambda e, s, hh=hh: e.dma_start(out=stg[hh][:, 0:1280], in_=consts[:, hh * 1280:(hh + 1) * 1280]).then_inc(s, 16),
              writes=[("stg", hh)], chan=("stg", hh))
        S.add("dve", lambda e, hh=hh: e.tensor_copy(out=cst[:, hh * 1280:(hh + 1) * 1280], in_=stg[hh][:, 0:1280]),
              reads=[("stg", hh)], writes=["cst"])
    S.add("sp", lambda e, s: e.dma_start(out=vec[:, :], in_=vecs[:, :]).then_inc(s, 16), writes=["vec"], chan="vec")
    S.add("act", lambda e: e.activation(out=ext[:, 7:9], in_=col(V_LAM, 2), func=AF.Exp, scale=-1.0),
          reads=["vec"], writes=["ext_t"])
    S.add("act", lambda e: e.activation(out=ext[:, 7:9], in_=ext[:, 7:9], func=AF.Ln, bias=1.0),
          reads=["ext_t"], writes=["ext_t"])
    S.add("dve", lambda e: e.tensor_scalar(out=ext[:, 0:2], in0=ext[:, 7:9], scalar1=-8.0, scalar2=None, op0=ALU.mult),
          reads=["ext_t"], writes=["ext"])
    S.add("dve", lambda e: e.tensor_scalar(out=ext[:, 2:6], in0=col(V_BA, 4), scalar1=-1.0, scalar2=None, op0=ALU.mult),
          reads=["vec", "ext"], writes=["ext"])
    S.add("dve", lambda e: e.tensor_scalar(out=ext[:, 6:7], in0=col(V_KG), scalar1=0.125, scalar2=None, op0=ALU.mult),
          reads=["vec", "ext"], writes=["ext"])
    S.add("sp", lambda e, s: e.dma_start(out=stg[0][:, 0:512], in_=w_rg[:, :]).then_inc(s, 16),
          writes=[("stg", 0)], chan=("stg", 0))
    S.add("dve", lambda e: e.tensor_copy(out=wrg_bf[:, :], in_=stg[0][:, 0:512]), reads=[("stg", 0)], writes=["wrg"])
    for c in range(DC):
        hh = c % 2
        S.add("sp", lambda e, s, c=c, hh=hh: e.dma_start(out=stg[hh][:, :], in_=w_in[c * 128:(c + 1) * 128, :]).then_inc(s, 16),
              writes=[("stg", hh)], chan=("stg", hh))
        if c % 2 == 0:
            S.add("dve", lambda e, c=c, hh=hh: e.tensor_scalar(out=win_bf[:, c * 1280:(c + 1) * 1280], in0=stg[hh][:, :],
                                                                scalar1=col(V_G1 + c), scalar2=None, op0=ALU.mult),
                  reads=[("stg", hh), "vec"], writes=[("win", c)])
        else:
            S.add("act", lambda e, c=c, hh=hh: e.activation(out=win_bf[:, c * 1280:(c + 1) * 1280], in_=stg[hh][:, :],
                                                             func=AF.Copy, scale=col(V_G1 + c)),
                  reads=[("stg", hh), "vec"], writes=[("win", c)])
    S.add("pool", lambda e: e.memset(xr_sb[0][:, 0:3], 0.0), writes=[("xr", 0)])
    S.add("pool", lambda e: e.memset(xr_sb[1][:, 0:3], 0.0), writes=[("xr", 1)])
    S.add("pool", lambda e: e.memset(hst[:, :], 0.0), writes=["hst"])

    win_reads = [("win", c) for c in range(DC)]

    B_TRP, B_V, B_PJ0, B_PJ1, B_PS2, B_GR, B_GI = 0, 1, 2, 3, 4, 5, 6

    def tile_info(ti):
        if ti == 0:
            return 0, 128
        return 128 + 512 * (ti - 1), 512

    pj_ctr = [0]

    def proj_slab(col0, W):
        b = B_PJ0 + (pj_ctr[0] % 2)
        pj_ctr[0] += 1

        def fn(e, b=b, col0=col0, W=W):
            ins = None
            for c in range(DC):
                ins = e.matmul(bank(b, W), lhsT=win_bf[:, c * 1280 + col0:c * 1280 + col0 + 128],
                               rhs=hnT[:, c * 512:c * 512 + W], start=(c == 0), stop=(c == DC - 1))
            return ins
        S.add("pe", fn, reads=win_reads + ["hnT"], writes=[("ps", b)])
        return b

    for ti in range(NQT):
        pos0, W = tile_info(ti)
        nsub = W // 128
        for sub in range(nsub):
            blk = pos0 // 128 + sub
            sl = blk % 2
            S.add("sp", lambda e, s, blk=blk, sl=sl: e.dma_start(out=xs[sl][:, :], in_=xpad[blk * 128:(blk + 1) * 128, :]).then_inc(s, 16),
                  writes=[("xs", sl)], chan=("xs", sl))
            S.add("pool", lambda e, sl=sl: e.memset(ssb[:, sl:sl + 1], 0.0), writes=[("ss", sl)])
            S.add("act", lambda e, sl=sl: e.activation(out=sqj[:, :], in_=xs[sl][:, :], func=AF.Square, accum_out=ssb[:, sl:sl + 1]),
                  reads=[("xs", sl)], writes=[("ss", sl), "sqj"])
            S.add("dve", lambda e, sl=sl: e.tensor_scalar(out=ssb[:, 2 + sl:3 + sl], in0=ssb[:, sl:sl + 1], scalar1=1.0 / D, scalar2=EPS,
                                                           op0=ALU.mult, op1=ALU.add),
                  reads=[("ss", sl)], writes=[("rstd", sl)])
            S.add("act", lambda e, sl=sl: e.activation(out=ssb[:, 2 + sl:3 + sl], in_=ssb[:, 2 + sl:3 + sl], func=AF.Ln),
                  reads=[("rstd", sl)], writes=[("rstd", sl)])
            S.add("act", lambda e, sl=sl: e.activation(out=ssb[:, 2 + sl:3 + sl], in_=ssb[:, 2 + sl:3 + sl], func=AF.Exp, scale=-0.5),
                  reads=[("rstd", sl)], writes=[("rstd", sl)])
            S.add("act", lambda e, sl=sl: e.activation(out=hn_bf[sl][:, :], in_=xs[sl][:, :], func=AF.Copy, scale=ssb[:, 2 + sl:3 + sl]),
                  reads=[("xs", sl), ("rstd", sl)], writes=[("hn", sl)])

            def tr_fn(e, sl=sl):
                ins = None
                for c in range(DC):
                    ins = e.transpose(bank_bf(B_TRP)[:, c * 128:(c + 1) * 128], hn_bf[sl][:, c * 128:(c + 1) * 128], ident)
                return ins
            S.add("pe", tr_fn, reads=[("hn", sl), "cst"], writes=[("ps", B_TRP)])
            S.add("dve", lambda e, sub=sub: e.tensor_copy(
                out=hnT.rearrange("p (c w) -> p c w", c=8)[:, :, sub * 128:(sub + 1) * 128],
                in_=bank_bf(B_TRP).rearrange("p (c w) -> p c w", c=8)),
                reads=[("ps", B_TRP)], writes=["hnT"])

            def v_fn(e, sub=sub):
                ins = None
                for c in range(DC):
                    ins = e.matmul(bank(B_V, 256), lhsT=hnT[:, c * 512 + sub * 128:c * 512 + (sub + 1) * 128],
                                   rhs=win_bf[:, c * 1280 + 512:c * 1280 + 768], start=(c == 0), stop=(c == DC - 1))
                return ins
            S.add("pe", v_fn, reads=win_reads + ["hnT"], writes=[("ps", B_V)])
            S.add("act", lambda e, blk=blk: e.copy(out=v_sb[:, blk * 256:(blk + 1) * 256], in_=bank(B_V, 256)),
                  reads=[("ps", B_V)], writes=[("v", blk)])

        for which in range(2):
            for j in range(2):
                b = proj_slab(which * 256 + j * 128, W)
                S.add("act", lambda e, b=b, W=W: e.activation(out=sqb[:, 0:W], in_=bank(b, W), func=AF.Square),
                      reads=[("ps", b)], writes=["sqb"])
                S.add("pe", lambda e, W=W: e.matmul(bank(B_PS2, W), lhsT=bones, rhs=sqb[:, 0:W], start=True, stop=True),
                      reads=["sqb", "cst"], writes=[("ps", B_PS2)])
                S.add("dve", lambda e, W=W: e.tensor_scalar(out=rqb[:, 0:W], in0=bank(B_PS2, W), scalar1=1.0 / 64, scalar2=EPS,
                                                             op0=ALU.mult, op1=ALU.add),
                      reads=[("ps", B_PS2)], writes=["rqb"])
                S.add("act", lambda e, W=W: e.activation(out=rqb[:, 0:W], in_=rqb[:, 0:W], func=AF.Ln), reads=["rqb"], writes=["rqb"])
                S.add("act", lambda e, W=W: e.activation(out=rqb[:, 0:W], in_=rqb[:, 0:W], func=AF.Exp, scale=-0.5),
                      reads=["rqb"], writes=["rqb"])
                dst = qT if which == 0 else kT
                gsc = col(V_QG) if which == 0 else ext[:, 6:7]
                S.add("dve", lambda e, b=b, W=W, dst=dst, gsc=gsc, j=j, pos0=pos0: e.scalar_tensor_tensor(
                    out=dst[:, j * TP + pos0:j * TP + pos0 + W], in0=bank(b, W), scalar=gsc, in1=rqb[:, 0:W],
                    op0=ALU.mult, op1=ALU.mult),
                    reads=[("ps", b), "rqb", "vec", "ext"], writes=[("qk", which, j, ti)])
        for c in range(2):
            b = proj_slab(768 + c * 128, W)
            S.add("act", lambda e, b=b, W=W, c=c: e.copy(out=xr_sb[c][:, 3:3 + W], in_=bank(b, W)),
                  reads=[("ps", b)], writes=[("xr", c)])
            S.add("dve", lambda e, c=c, W=W: e.tensor_scalar(out=xc_sb[:, 0:W], in0=xr_sb[c][:, 3:3 + W], scalar1=col(V_CW + c * 4 + 3),
                                                               scalar2=col(V_CB + c), op0=ALU.mult, op1=ALU.add),
                  reads=[("xr", c), "vec"], writes=["xc"])
            for k in range(3):
                S.add("dve", lambda e, c=c, W=W, k=k: e.scalar_tensor_tensor(out=xc_sb[:, 0:W], in0=xr_sb[c][:, k:k + W],
                                                                               scalar=col(V_CW + c * 4 + k), in1=xc_sb[:, 0:W],
                                                                               op0=ALU.mult, op1=ALU.add),
                      reads=[("xr", c), "vec", "xc"], writes=["xc"])
            S.add("pool", lambda e, c=c, W=W: e.tensor_copy(out=xr_sb[c][:, 0:3], in_=xr_sb[c][:, W:W + 3]),
                  reads=[("xr", c)], writes=[("xr", c)])
            S.add("act", lambda e, W=W: e.copy(out=xc_bf[:, 0:W], in_=xc_sb[:, 0:W]), reads=["xc"], writes=["xcb"])
            S.add("pe", lambda e, c=c, W=W: e.matmul(bank(B_GR, W), lhsT=wrg_bf[:, (0 * 2 + c) * 128:(0 * 2 + c + 1) * 128],
                                                      rhs=xc_bf[:, 0:W], start=True, stop=True),
                  reads=["xcb", "wrg"], writes=[("ps", B_GR)])
            S.add("pe", lambda e, c=c, W=W: e.matmul(bank(B_GI, W), lhsT=wrg_bf[:, (1 * 2 + c) * 128:(1 * 2 + c + 1) * 128],
                                                      rhs=xc_bf[:, 0:W], start=True, stop=True),
                  reads=["xcb", "wrg"], writes=[("ps", B_GI)])
            S.add("act", lambda e, c=c, W=W: e.activation(out=tr_sb[:, 0:W], in_=bank(B_GR, W), func=AF.Exp,
                                                           bias=ext[:, 2 + c:3 + c], scale=-1.0),
                  reads=[("ps", B_GR), "ext"], writes=["tr"])
            S.add("act", lambda e, c=c, W=W: e.activation(out=ti_sb[:, 0:W], in_=bank(B_GI, W), func=AF.Exp,
                                                           bias=ext[:, 4 + c:5 + c], scale=-1.0),
                  reads=[("ps", B_GI), "ext"], writes=["tig"])
            S.add("act", lambda e, W=W: e.activation(out=tr_sb[:, 0:W], in_=tr_sb[:, 0:W], func=AF.Ln, bias=1.0), reads=["tr"], writes=["tr"])
            S.add("act", lambda e, W=W: e.activation(out=tr_sb[:, 0:W], in_=tr_sb[:, 0:W], func=AF.Exp, scale=-1.0), reads=["tr"], writes=["tr"])
            S.add("act", lambda e, W=W: e.activation(out=ti_sb[:, 0:W], in_=ti_sb[:, 0:W], func=AF.Ln, bias=1.0), reads=["tig"], writes=["tig"])
            S.add("act", lambda e, W=W: e.activation(out=ti_sb[:, 0:W], in_=ti_sb[:, 0:W], func=AF.Exp, scale=-1.0), reads=["tig"], writes=["tig"])
            S.add("act", lambda e, c=c, W=W: e.activation(out=a_sb[:, 0:W], in_=tr_sb[:, 0:W], func=AF.Exp, scale=ext[:, c:c + 1]),
                  reads=["tr", "ext"], writes=["a"])
            S.add("dve", lambda e, W=W: e.tensor_tensor(out=m2_sb[:, 0:W], in0=a_sb[:, 0:W], in1=a_sb[:, 0:W], op=ALU.mult),
                  reads=["a"], writes=["m2"])
            S.add("dve", lambda e, W=W: e.tensor_scalar(out=m2_sb[:, 0:W], in0=m2_sb[:, 0:W], scalar1=-1.0, scalar2=1.0,
                                                         op0=ALU.mult, op1=ALU.add),
                  reads=["m2"], writes=["m2"])
            S.add("act", lambda e, W=W: e.activation(out=m2_sb[:, 0:W], in_=m2_sb[:, 0:W], func=AF.Ln), reads=["m2"], writes=["m2"])
            S.add("act", lambda e, W=W: e.activation(out=m2_sb[:, 0:W], in_=m2_sb[:, 0:W], func=AF.Exp, scale=0.5), reads=["m2"], writes=["m2"])
            S.add("dve", lambda e, W=W: e.tensor_tensor(out=bt_sb[:, 0:W], in0=ti_sb[:, 0:W], in1=xc_sb[:, 0:W], op=ALU.mult),
                  reads=["tig", "xc"], writes=["bt"])
            S.add("dve", lambda e, W=W: e.tensor_tensor(out=bt_sb[:, 0:W], in0=bt_sb[:, 0:W], in1=m2_sb[:, 0:W], op=ALU.mult),
                  reads=["bt", "m2"], writes=["bt"])
            if ti == 0:
                S.add("dve", lambda e: e.memset(bt_sb[:, 0:PAD], 0.0), reads=["bt"], writes=["bt"])
            S.add("dve", lambda e, c=c, W=W: e.tensor_tensor_scan(out=hl_sb[c][:, 0:W], data0=a_sb[:, 0:W], data1=bt_sb[:, 0:W],
                                                                   initial=hst[:, c:c + 1], op0=ALU.mult, op1=ALU.add),
                  reads=["a", "bt", "hst"], writes=[("hl", c)])
            S.add("dve", lambda e, c=c, W=W: e.tensor_copy(out=hst[:, c:c + 1], in_=hl_sb[c][:, W - 1:W]),
                  reads=[("hl", c)], writes=["hst"])
        for c in range(2):
            b = proj_slab(1024 + c * 128, W)
            S.add("act", lambda e, b=b, W=W: e.copy(out=y_sb[:, 0:W], in_=bank(b, W)), reads=[("ps", b)], writes=["y"])
            S.add("act", lambda e, b=b, W=W: e.activation(out=y2_sb[:, 0:W], in_=bank(b, W), func=AF.Square),
                  reads=[("ps", b)], writes=["y2"])
            S.add("dve", lambda e, W=W: e.tensor_scalar(out=y2_sb[:, 0:W], in0=y2_sb[:, 0:W], scalar1=0.044715, scalar2=1.0,
                                                          op0=ALU.mult, op1=ALU.add),
                  reads=["y2"], writes=["y2"])
            S.add("dve", lambda e, W=W: e.tensor_tensor(out=y2_sb[:, 0:W], in0=y2_sb[:, 0:W], in1=y_sb[:, 0:W], op=ALU.mult),
                  reads=["y2", "y"], writes=["y2"])
            S.add("act", lambda e, W=W: e.activation(out=tg_sb[:, 0:W], in_=y2_sb[:, 0:W], func=AF.Exp, scale=-2.0 * GELU_C),
                  reads=["y2"], writes=["tg"])
            S.add("act", lambda e, W=W: e.activation(out=tg_sb[:, 0:W], in_=tg_sb[:, 0:W], func=AF.Ln, bias=1.0), reads=["tg"], writes=["tg"])
            S.add("act", lambda e, W=W: e.activation(out=tg_sb[:, 0:W], in_=tg_sb[:, 0:W], func=AF.Exp, scale=-1.0), reads=["tg"], writes=["tg"])
            S.add("dve", lambda e, W=W: e.tensor_tensor(out=tg_sb[:, 0:W], in0=tg_sb[:, 0:W], in1=y_sb[:, 0:W], op=ALU.mult),
                  reads=["tg", "y"], writes=["tg"])
            S.add("dve", lambda e, c=c, W=W: e.tensor_tensor(out=ol_bf[c][:, 0:W], in0=tg_sb[:, 0:W], in1=hl_sb[c][:, 0:W], op=ALU.mult),
                  reads=["tg", ("hl", c)], writes=[("ol", c)])
            sfn, nd = store_fn(ol_bf[c], 256 + c * 128, ti)
            S.add("pool", sfn, reads=[("ol", c)], writes=[("mo", "l", c, ti)], chan=("ol", c), ndma=nd)

    S.barrier()
    A.off = mark12
    e_sb = [A.f32(512), A.f32(512)]
    sp_sb = [A.bf(512) for _ in range(4)]
    w_sb = [A.bf(512) for _ in range(3)]
    qneg = [A.bf(2 * 512), A.bf(2 * 512)]
    osb = [A.bf(2 * 512), A.bf(2 * 512)]
    B_Z = [0, 1]
    B_P = [2, 3, 4, 5]
    B_O = [6, 7]
    items = []
    for ti in range(NQT):
        pos0, W = tile_info(ti)
        b0 = pos0 // 128
        nsub = W // 128
        kbs = list(range(b0 + nsub - 1, -1, -1))
        for n, kb in enumerate(kbs):
            for h in range(4):
                items.append(dict(ti=ti, pos0=pos0, W=W, b0=b0, kb=kb, h=h, first=(n == 0), last=(n == len(kbs) - 1)))
    NI = len(items)

    def qk_reads(it):
        return [("qk", 0, it["h"] // 2, it["ti"])] + [("qk", 1, it["h"] // 2, t) for t in range(NQT)][:0] + ["kall"]

    S.add("dve", lambda e: e.memset(ext[:, 9:10], 0.0),
          reads=[("qk", w, j, t) for w in range(2) for j in range(2) for t in range(NQT)] + [("v", bl) for bl in range(NB)],
          writes=["kall"])

    def add_qneg(ti):
        pos0, W = tile_info(ti)
        par = ti % 2
        for j in range(2):
            S.add("dve", lambda e, j=j, W=W, pos0=pos0, par=par: e.tensor_scalar(
                out=qneg[par][:, j * 512:j * 512 + W], in0=qT[:, j * TP + pos0:j * TP + pos0 + W], scalar1=-1.0, scalar2=None,
                op0=ALU.mult), reads=["kall"], writes=[("qneg", par, j)])

    def PE1(i):
        it = items[i]
        h, kb, W, pos0 = it["h"], it["kb"], it["W"], it["pos0"]
        j, r = h // 2, (h % 2) * 64
        zb = B_Z[i % 2]
        S.add("pe", lambda e: e.matmul(bank(zb, W), lhsT=kT[r:r + 64, j * TP + kb * 128:j * TP + (kb + 1) * 128],
                                       rhs=qT[r:r + 64, j * TP + pos0:j * TP + pos0 + W], start=True, stop=True),
              reads=["kall"], writes=[("ps", zb)])

    def ACT12(i):
        it = items[i]
        W = it["W"]
        zb = B_Z[i % 2]
        eb = e_sb[i % 2]
        sb = sp_sb[i % 4]
        S.add("act", lambda e: e.activation(out=eb[:, 0:W], in_=bank(zb, W), func=AF.Exp),
              reads=[("ps", zb)], writes=[("e", i % 2)])
        S.add("act", lambda e: e.activation(out=sb[:, 0:W], in_=eb[:, 0:W], func=AF.Ln, bias=1.0),
              reads=[("e", i % 2)], writes=[("sp", i % 4)])
        if it["kb"] >= it["b0"]:
            jj = it["kb"] - it["b0"]
            S.add("dve", lambda e: e.tensor_tensor(out=sb[:, 0:W], in0=sb[:, 0:W], in1=maskv(jj, W), op=ALU.mult),
                  reads=[("sp", i % 4), "cst"], writes=[("sp", i % 4)])

    def PE2(i):
        it = items[i]
        h, kb, W, pos0 = it["h"], it["kb"], it["W"], it["pos0"]
        j, r = h // 2, (h % 2) * 64
        pb = B_P[h]
        sb = sp_sb[i % 4]

        def fn(e):
            e.matmul(bank(pb, W), lhsT=kT[r:r + 64, j * TP + kb * 128:j * TP + (kb + 1) * 128],
                     rhs=qT[r:r + 64, j * TP + pos0:j * TP + pos0 + W], start=it["first"], stop=False, skip_group_check=True)
            return e.matmul(bank(pb, W), lhsT=negL, rhs=sb[:, 0:W], start=False, stop=False, skip_group_check=True)
        S.add("pe", fn, reads=["kall", ("sp", i % 4), "cst"], writes=[("ps", pb)])

    def ACT3(i):
        it = items[i]
        W = it["W"]
        pb = B_P[it["h"]]
        wb = w_sb[i % 3]
        S.add("act", lambda e: e.activation(out=wb[:, 0:W], in_=bank(pb, W), func=AF.Exp),
              reads=[("ps", pb)], writes=[("w", i % 3)])
        if it["kb"] >= it["b0"]:
            jj = it["kb"] - it["b0"]
            S.add("dve", lambda e: e.tensor_tensor(out=wb[:, 0:W], in0=wb[:, 0:W], in1=maskv(jj, W), op=ALU.mult),
                  reads=[("w", i % 3), "cst"], writes=[("w", i % 3)])

    def PE34(i):
        it = items[i]
        h, kb, W, pos0, ti = it["h"], it["kb"], it["W"], it["pos0"], it["ti"]
        j, r = h // 2, (h % 2) * 64
        pb = B_P[h]
        ob = B_O[j]
        wb = w_sb[i % 3]
        sb = sp_sb[i % 4]
        par = ti % 2
        S.add("pe", lambda e: e.matmul(ps[r:r + 64, ob * 512:ob * 512 + W], lhsT=v_sb[:, kb * 256 + h * 64:kb * 256 + (h + 1) * 64],
                                       rhs=wb[:, 0:W], start=it["first"], stop=it["last"], skip_group_check=True),
              reads=[("w", i % 3), "kall"], writes=[("ps", ob)])
        if not it["last"]:
            def fn(e):
                e.matmul(bank(pb, W), lhsT=kT[r:r + 64, j * TP + kb * 128:j * TP + (kb + 1) * 128],
                         rhs=qneg[par][r:r + 64, j * 512:j * 512 + W], start=False, stop=False, skip_group_check=True)
                return e.matmul(bank(pb, W), lhsT=negU, rhs=sb[:, 0:W], start=False, stop=False, skip_group_check=True)
            S.add("pe", fn, reads=["kall", ("sp", i % 4), ("qneg", par, j), "cst"], writes=[("ps", pb)])
        else:
            if h % 2 == 1:
                S.add("dve", lambda e: e.tensor_copy(out=osb[par][:, j * 512:j * 512 + W], in_=bank(ob, W)),
                      reads=[("ps", ob)], writes=[("osb", par, j)])
                sfn, nd = store_fn(osb[par][:, j * 512:(j + 1) * 512], j * 128, ti)
                S.add("pool", sfn, reads=[("osb", par, j)], writes=[("mo", "a", j, ti)], chan=("osb", par, j), ndma=nd)
                if j == 1:
                    maybe_gather(ti)

    RG = [[0, 1], [2, 3], [4, 5], [6, 7]]

    def tile_keys(ti):
        return [("mo", "l", c, ti) for c in range(2)] + [("mo", "a", j, ti) for j in range(2)]

    gathered = set()

    def maybe_gather(ti):
        if ti == 0:
            return
        done_tok = 512 * ti
        for g in range(4):
            if g in gathered or (g + 1) * GW > done_tok:
                continue
            gathered.add(g)
            t_lo = 1 + (g * GW) // 512
            t_hi = 1 + ((g + 1) * GW - 1) // 512
            keys = []
            for t in range(t_lo, t_hi + 1):
                keys += tile_keys(t)
            S.add("pool", lambda e, s, g=g: e.collective_compute(
                "AllGather", ALU.bypass, replica_groups=RG, ins=[mo_g[g].ap().opt()],
                outs=[ma_big[g * 1024:(g + 1) * 1024, :].opt()]).then_inc(s),
                reads=keys, writes=[("mall", g)], chan=("cc", g), inc=1)

    qneg_done = set()
    for s in range(-2, NI):
        if s + 2 < NI:
            t2 = items[s + 2]["ti"]
            if t2 not in qneg_done:
                qneg_done.add(t2)
                add_qneg(t2)
            PE1(s + 2)
        if 0 <= s + 1 < NI:
            ACT12(s + 1)
            PE2(s + 1)
        if s >= 0:
            ACT3(s)
            PE34(s)

    S.add("pool", lambda e, s: e.collective_compute("AllGather", ALU.bypass, replica_groups=RG,
                                                    ins=[mo_halo_t.ap().opt()], outs=[ma_halo_t.ap().opt()]).then_inc(s),
          reads=tile_keys(0) + tile_keys(1 + (HALF - 2) // 512), writes=[("mall", "h")], chan="cch", inc=1)

    def mh_fn(e, s):
        half = e.partition_id() % 2
        ins = None
        for j in range(2):
            for r in range(2):
                ins = e.dma_start(out=mixed_half[r * 512:(r + 1) * 512, 2 + j * GW:2 + (j + 1) * GW],
                                  in_=ma_big[bass.ds(half * 2048 + j * 1024 + r * 512, 512), :]).then_inc(s, 16)
        ins = e.dma_start(out=mixed_half[:, 0:2], in_=ma_halo[:, bass.ds(half * 2, 2)]).then_inc(s, 16)
        return ins
    S.add("sp", mh_fn, reads=[("mall", g) for g in range(4)] + [("mall", "h")], writes=["mhalf"], chan="mh", ndma=5)

    S.barrier()
    A.off = mark_stage
    wout_bf = A.bf(8 * 1024)
    wfi_bf = A.bf(8 * 2 * DFF)
    wfo_bf = A.bf(NPAIR * 1024)
    mark3 = A.off
    stg3 = [A.f32(2816), A.f32(2816)]

    stg_ctr = [0]

    def load_cast(dst_ap, src_ap, ncols, key, scale_col=None):
        sl = stg_ctr[0] % 2
        stg_ctr[0] += 1
        S.add("sp", lambda e, s: e.dma_start(out=stg3[sl][:, 0:ncols], in_=src_ap).then_inc(s, 16),
              writes=[("stg3", sl)], chan=("stg3", sl))
        eng = ("dve", "act")[stg_ctr[0] % 2]
        if scale_col is None:
            if eng == "act":
                S.add(eng, lambda e: e.copy(out=dst_ap, in_=stg3[sl][:, 0:ncols]), reads=[("stg3", sl)], writes=[key])
            else:
                S.add(eng, lambda e: e.tensor_copy(out=dst_ap, in_=stg3[sl][:, 0:ncols]), reads=[("stg3", sl)], writes=[key])
        else:
            if eng == "act":
                S.add(eng, lambda e: e.activation(out=dst_ap, in_=stg3[sl][:, 0:ncols], func=AF.Copy, scale=scale_col),
                      reads=[("stg3", sl), "vec"], writes=[key])
            else:
                S.add(eng, lambda e: e.tensor_scalar(out=dst_ap, in0=stg3[sl][:, 0:ncols], scalar1=scale_col, scalar2=None, op0=ALU.mult),
                      reads=[("stg3", sl), "vec"], writes=[key])

    w3_keys = []
    for c in range(DC):
        load_cast(wout_bf[:, c * 1024:(c + 1) * 1024], w_out[c * 128:(c + 1) * 128, :], 1024, ("wout", c))
        w3_keys.append(("wout", c))
    for c in range(DC):
        for hh in range(2):
            load_cast(wfi_bf[:, c * 2 * DFF + hh * DFF:c * 2 * DFF + (hh + 1) * DFF], w_fi[c * 128:(c + 1) * 128, hh * DFF:(hh + 1) * DFF],
                      DFF, ("wfi", c, hh), scale_col=col(V_G2 + c))
            w3_keys.append(("wfi", c, hh))
    for j in range(0, NPAIR, 2):
        for jj in range(2):
            load_cast(wfo_bf[:, (j + jj) * 1024:(j + jj + 1) * 1024], w_fo[(j + jj) * 128:(j + jj + 1) * 128, :], 1024, ("wfo", j + jj))
            w3_keys.append(("wfo", j + jj))
    S.add("dve", lambda e: e.memset(ext[:, 10:11], 0.0), reads=w3_keys, writes=["w3all"])
    S.barrier()
    A.off = mark3
    h2 = [A.f32(1024) for _ in range(4)]
    hn2_bf = [A.bf(1024), A.bf(1024)]
    ss3 = A.f32(8)
    sqj3 = A.bf(1024)
    hn2T = A.bf(8 * W3)
    mt = A.bf(8 * W3)
    ust = A.f32(NSLAB * 2)
    ubuf = [A.f32(2 + W3), A.f32(2 + W3)]
    gbuf = [A.f32(2 + W3), A.f32(2 + W3)]
    uc = [A.f32(W3), A.f32(W3)]
    gc = [A.f32(W3), A.f32(W3)]
    sg = [A.f32(W3), A.f32(W3)]
    act_bf = A.bf(NPAIR * W3)
    ctmp = A.f32(W3)

    B_HP = [0, 1]
    B_T3 = 2
    B_U = [3, 4]
    B_G = [5, 6]
    mixed_half_r = mixed_half.rearrange("(c p) w -> p c w", p=128)
    xflat = xpad.rearrange("t d -> (t d)")
    h2_ctr = [0]
    pair_ctr = [0]
    half_cache = {}

    def get_half(e):
        if 'h' not in half_cache:
            half_cache['h'] = e.partition_id() % 2
        return half_cache['h']

    def stage3_tile(rel0, Wt, final):
        nsub = max(1, Wt // 128)
        n = min(Wt, 128)

        def mt_fn(e, s):
            return e.dma_start(out=mt.rearrange("p (c w) -> p c w", c=8)[:, :, 0:Wt],
                               in_=mixed_half_r[:, :, rel0:rel0 + Wt]).then_inc(s, 16)
        S.add("sp", mt_fn, reads=["mhalf"], writes=["mt"], chan="mt")
        slots = []
        for sub in range(nsub):
            sl = h2_ctr[0] % 4
            h2_ctr[0] += 1
            slots.append(sl)

            def x_fn(e, s, sl=sl, sub=sub):
                r = rel0 + sub * 128
                return e.dma_start(out=h2[sl][0:n, :], in_=xhalf[r:r + n, :]).then_inc(s, 16)
            S.add("sp", x_fn, writes=[("h2", sl)], chan=("h2", sl))

            def wo_fn(e, sub=sub):
                ins = None
                for hf in range(2):
                    for c in range(DC):
                        ins = e.matmul(ps[0:n, B_HP[hf] * 512:(B_HP[hf] + 1) * 512], lhsT=mt[:, c * W3 + sub * 128:c * W3 + sub * 128 + n],
                                       rhs=wout_bf[:, c * 1024 + hf * 512:c * 1024 + (hf + 1) * 512], start=(c == 0), stop=(c == DC - 1))
                return ins
            S.add("pe", wo_fn, reads=["mt", "w3all"], writes=[("ps", 0), ("ps", 1)])
            S.add("dve", lambda e, sl=sl: e.tensor_tensor(out=h2[sl][0:n, :], in0=ps[0:n, 0:1024], in1=h2[sl][0:n, :], op=ALU.add),
                  reads=[("ps", 0), ("ps", 1), ("h2", sl)], writes=[("h2", sl)])
            q = sl % 2
            S.add("pool", lambda e, q=q: e.memset(ss3[:, q:q + 1], 0.0), writes=[("ss3", q)])
            S.add("act", lambda e, sl=sl, q=q: e.activation(out=sqj3[0:n, :], in_=h2[sl][0:n, :], func=AF.Square, accum_out=ss3[0:n, q:q + 1]),
                  reads=[("h2", sl)], writes=[("ss3", q), "sqj3"])
            S.add("dve", lambda e, q=q: e.tensor_scalar(out=ss3[0:n, 2 + q:3 + q], in0=ss3[0:n, q:q + 1], scalar1=1.0 / D, scalar2=EPS,
                                                         op0=ALU.mult, op1=ALU.add), reads=[("ss3", q)], writes=[("rs3", q)])
            S.add("act", lambda e, q=q: e.activation(out=ss3[0:n, 2 + q:3 + q], in_=ss3[0:n, 2 + q:3 + q], func=AF.Ln),
                  reads=[("rs3", q)], writes=[("rs3", q)])
            S.add("act", lambda e, q=q: e.activation(out=ss3[0:n, 2 + q:3 + q], in_=ss3[0:n, 2 + q:3 + q], func=AF.Exp, scale=-0.5),
                  reads=[("rs3", q)], writes=[("rs3", q)])
            S.add("act", lambda e, sl=sl, q=q: e.activation(out=hn2_bf[q][0:n, :], in_=h2[sl][0:n, :], func=AF.Copy, scale=ss3[0:n, 2 + q:3 + q]),
                  reads=[("h2", sl), ("rs3", q)], writes=[("hn2", q)])

            def tr_fn(e, q=q):
                ins = None
                for c in range(DC):
                    ins = e.transpose(bank_bf(B_T3)[:, c * 128:c * 128 + n], hn2_bf[q][0:n, c * 128:(c + 1) * 128], ident[0:n, 0:n])
                return ins
            S.add("pe", tr_fn, reads=[("hn2", q), "cst"], writes=[("ps", B_T3)])
            S.add("dve", lambda e, sub=sub: e.tensor_copy(
                out=hn2T.rearrange("p (c w) -> p c w", c=8)[:, :, sub * 128:sub * 128 + n],
                in_=bank_bf(B_T3).rearrange("p (c w) -> p c w", c=8)[:, :, 0:n]),
                reads=[("ps", B_T3)], writes=["hn2T"])
        for j in range(NPAIR):
            pc = pair_ctr[0] % 2
            pair_ctr[0] += 1
            for which, (bb, buf, cbuf) in enumerate(((B_U[pc], ubuf[pc], uc[pc]), (B_G[pc], gbuf[pc], gc[pc]))):
                slab = j + which * NPAIR

                def fi_fn(e, bb=bb, slab=slab):
                    ins = None
                    for c in range(DC):
                        ins = e.matmul(bank(bb, Wt), lhsT=wfi_bf[:, c * 2 * DFF + slab * 128:c * 2 * DFF + (slab + 1) * 128],
                                       rhs=hn2T[:, c * W3:c * W3 + Wt], start=(c == 0), stop=(c == DC - 1))
                    return ins
                S.add("pe", fi_fn, reads=["hn2T", "w3all"], writes=[("ps", bb)])
                if not final:
                    S.add("act", lambda e, bb=bb, slab=slab: e.copy(out=ust[:, slab * 2:slab * 2 + 2], in_=bank(bb, 2)),
                          reads=[("ps", bb)], writes=[("ust", slab)])
                    continue
                bk = ("ub", which, pc)
                S.add("act", lambda e, bb=bb, buf=buf: e.copy(out=buf[:, 2:2 + Wt], in_=bank(bb, Wt)), reads=[("ps", bb)], writes=[bk])
                S.add("pool", lambda e, buf=buf, slab=slab: e.tensor_copy(out=buf[:, 0:2], in_=ust[:, slab * 2:slab * 2 + 2]),
                      reads=[("ust", slab), bk], writes=[bk])
                ck = ("cb", which, pc)
                if which == 0:
                    S.add("act", lambda e, buf=buf, cbuf=cbuf, slab=slab: e.activation(
                        out=cbuf[:, 0:Wt], in_=buf[:, 2:2 + Wt], func=AF.Identity, scale=col(V_FCW + slab * 3 + 2), bias=col(V_FCB + slab)),
                        reads=[bk, "vec"], writes=[ck])
                    for k in range(2):
                        S.add("dve", lambda e, buf=buf, cbuf=cbuf, slab=slab, k=k: e.scalar_tensor_tensor(
                            out=cbuf[:, 0:Wt], in0=buf[:, k:k + Wt], scalar=col(V_FCW + slab * 3 + k), in1=cbuf[:, 0:Wt],
                            op0=ALU.mult, op1=ALU.add), reads=[bk, "vec", ck], writes=[ck])
                else:
                    S.add("act", lambda e, buf=buf, cbuf=cbuf, slab=slab: e.activation(
                        out=cbuf[:, 0:Wt], in_=buf[:, 2:2 + Wt], func=AF.Identity, scale=col(V_FCW + slab * 3 + 2), bias=col(V_FCB + slab)),
                        reads=[bk, "vec"], writes=[ck])
                    for k in range(2):
                        S.add("dve", lambda e, buf=buf, cbuf=cbuf, slab=slab, k=k: e.scalar_tensor_tensor(
                            out=cbuf[:, 0:Wt], in0=buf[:, k:k + Wt], scalar=col(V_FCW + slab * 3 + k), in1=cbuf[:, 0:Wt],
                            op0=ALU.mult, op1=ALU.add), reads=[bk, "vec", ck], writes=[ck])
                S.add("pool", lambda e, buf=buf, slab=slab: e.tensor_copy(out=ust[:, slab * 2:slab * 2 + 2], in_=buf[:, Wt:Wt + 2]),
                      reads=[bk], writes=[("ust", slab)])
            if final:
                S.add("act", lambda e, pc=pc: e.activation(out=sg[pc][:, 0:Wt], in_=gc[pc][:, 0:Wt], func=AF.Silu),
                      reads=[("cb", 1, pc)], writes=[("sg", pc)])
                S.add("dve", lambda e, pc=pc, j=j: e.tensor_tensor(out=act_bf[:, j * W3:j * W3 + Wt], in0=sg[pc][:, 0:Wt], in1=uc[pc][:, 0:Wt],
                                                                   op=ALU.mult),
                      reads=[("sg", pc), ("cb", 0, pc)], writes=[("actT", j)])
        if not final:
            return
        for sub in range(nsub):
            sl = slots[sub]

            def fo_fn(e, sub=sub):
                ins = None
                for hf in range(2):
                    for j in range(NPAIR):
                        ins = e.matmul(bank(B_HP[hf], 512), lhsT=act_bf[:, j * W3 + sub * 128:j * W3 + (sub + 1) * 128],
                                       rhs=wfo_bf[:, j * 1024 + hf * 512:j * 1024 + (hf + 1) * 512], start=(j == 0), stop=(j == NPAIR - 1))
                return ins
            S.add("pe", fo_fn, reads=[("actT", j) for j in range(NPAIR)] + ["w3all"], writes=[("ps", 0), ("ps", 1)])
            S.add("dve", lambda e, sl=sl: e.tensor_tensor(out=h2[sl][:, :], in0=ps[:, 0:1024], in1=h2[sl][:, :], op=ALU.add),
                  reads=[("ps", 0), ("ps", 1), ("h2", sl)], writes=[("h2", sl)])
            r0 = rel0 - 2 + sub * 128
            S.add("pool", lambda e, s, sl=sl, r0=r0: e.dma_start(out=out[r0:r0 + 128, :], in_=h2[sl][:, :]).then_inc(s, 16),
                  reads=[("h2", sl)], writes=[("h2", sl), ("out", r0)], chan=("h2", sl))

    stage3_tile(0, 2, False)
    for st in range(NST):
        stage3_tile(2 + st * W3, W3, True)

    S.add("sp", lambda e: None, reads=[("out", r0) for r0 in range(0, HALF, 128)], writes=["done"])

    S.finalize()
    chans = list(S.chan_count.keys())
    sem_ctxs = []
    sems = {}
    for e in Sched.ENG:
        c = nc.semaphore("s_" + e)
        sems[e] = c.__enter__()
        sem_ctxs.append(c)
    chan_sems = {}
    for i, ch in enumerate(chans):
        c = nc.semaphore("c_%d" % i)
        chan_sems[ch] = c.__enter__()
        sem_ctxs.append(c)
    with nc.Block() as block:
        S.emit(nc, block, sems, chan_sems)
    for c in reversed(sem_ctxs):
        c.__exit__(None, None, None)
    pctx.__exit__(None, None, None)
    ctx.__exit__(None, None, None)
    return nc


def _consts():
    c = np.zeros((128, NCST), np.float32)
    j = np.arange(128)[:, None]
    s = np.arange(128)[None, :]
    c[:, C_ID:C_ID + 128] = (j == s)
    c[:, C_NL:C_NL + 128] = -(j >= s).astype(np.float32)
    c[:, C_NU:C_NU + 128] = -(j < s).astype(np.float32)
    c[:, C_BO:C_BO + 128] = ((j // 64) == (s // 64))
    t = np.arange(512)[None, :]
    for k in range(4):
        c[:, C_MK + k * 512:C_MK + (k + 1) * 512] = ((128 * k + j) < t)
    return c


def _prep_inputs(inputs, SEQ):
    f = lambda a: np.ascontiguousarray(np.asarray(a), dtype=np.float32)
    x = f(inputs["x"])
    meta = f(inputs["meta_tokens"])
    w_in = f(inputs["w_in"])[0]
    w_out = f(inputs["w_out"])[0]
    w_fi = f(inputs["w_ffn_in"])[0]
    w_fo = f(inputs["w_ffn_out"])[0]
    g1 = f(inputs["norm1_g"])[0]
    g2 = f(inputs["norm2_g"])[0]
    qg = f(inputs["q_norm_g"])[0]
    kg = f(inputs["k_norm_g"])[0]
    cw = f(inputs["conv_w"])[0]
    cb = f(inputs["conv_b"])[0]
    wa = f(inputs["w_rg_a"])[0]
    wi = f(inputs["w_rg_i"])[0]
    ba = f(inputs["b_rg_a"])[0]
    bi = f(inputs["b_rg_i"])[0]
    lam = f(inputs["lru_lambda"])[0]
    fcw = f(inputs["ffn_conv_w"])[0]
    fcb = f(inputs["ffn_conv_b"])[0]
    consts = _consts()
    TP = SEQ + 128
    maps = []
    for core in range(8):
        b, p = core // 2, core % 2
        xpad = np.zeros((TP, D), np.float32)
        xpad[PAD:128] = meta
        xpad[128:] = x[b]
        cs = slice(256 * p, 256 * p + 256)
        wic = np.concatenate([w_in[:, 0:512][:, cs], w_in[:, 512:1024][:, cs], w_in[:, 1024:1536][:, cs],
                              w_in[:, 1536:2048][:, cs], w_in[:, 2048:2560][:, cs]], axis=1)
        vec = np.zeros((128, NV), np.float32)
        vec[:, V_G1:V_G1 + 8] = g1.reshape(8, 128).T
        vec[:, V_G2:V_G2 + 8] = g2.reshape(8, 128).T
        vec[:, V_QG] = np.tile(qg, 2)
        vec[:, V_KG] = np.tile(kg, 2)
        for c in range(2):
            ch = slice(256 * p + 128 * c, 256 * p + 128 * c + 128)
            vec[:, V_CW + c * 4:V_CW + c * 4 + 4] = cw[:, ch].T
            vec[:, V_CB + c] = cb[ch]
            vec[:, V_BA + c] = ba[ch]
            vec[:, V_BI + c] = bi[ch]
            vec[:, V_LAM + c] = lam[ch]
        for s in range(NSLAB):
            vec[:, V_FCW + s * 3:V_FCW + s * 3 + 3] = fcw[:, s * 128:(s + 1) * 128].T
            vec[:, V_FCB + s] = fcb[s * 128:(s + 1) * 128]
        wrg = np.zeros((128, 4 * 128), np.float32)
        for gi, wsrc in enumerate((wa, wi)):
            for c in range(2):
                for k in range(2):
                    blk = 4 * p + 2 * c + k
                    o = (gi * 2 + c) * 128
                    wrg[64 * k:64 * k + 64, o + 64 * k:o + 64 * k + 64] = wsrc[blk]
        perm = []
        for r in range(2):
            perm += list(range(256 * r, 256 * r + 256))
            perm += list(range(512 + 256 * r, 512 + 256 * r + 256))
        maps.append({
            "xpad": xpad, "xhalf": np.ascontiguousarray(xpad[126 + (SEQ // 2) * p:126 + (SEQ // 2) * p + SEQ // 2 + 2]), "w_in": np.ascontiguousarray(wic), "vecs": vec, "w_rg": wrg,
            "w_out": np.ascontiguousarray(w_out[perm]), "w_ffn_in": w_fi, "w_ffn_out": w_fo, "consts": consts,
        })
    return maps


_NC_CACHE = {}


def kernel(**inputs):
    x = np.asarray(inputs["x"])
    B, SEQ, _ = x.shape
    assert B == 4
    if SEQ not in _NC_CACHE:
        _NC_CACHE[SEQ] = build_nc(SEQ)
    nc = _NC_CACHE[SEQ]
    maps = _prep_inputs(inputs, SEQ)
    res = run_bass_kernel_spmd(nc, maps, core_ids=list(range(8)))
    outp = np.empty((B, SEQ, D), np.float32)
    HALF = SEQ // 2
    for core in range(8):
        b, p = core // 2, core % 2
        outp[b, p * HALF:(p + 1) * HALF] = res.results[core]["out"]
    return outp
```

```python
import numpy as np
import concourse.bass as bass
import concourse.mybir as mybir
from concourse.bass_utils import run_bass_kernel_spmd

F32 = mybir.dt.float32
BF16 = mybir.dt.bfloat16
AF = mybir.ActivationFunctionType
ALU = mybir.AluOpType

D = 1024
DC = 8
DFF = 2816
NSLAB = 44
NPAIR = 22
EPS = 1e-6
NMETA = 16
PAD = 112
GELU_C = 0.7978845608028654

V_G1, V_G2, V_QG, V_KG, V_CW, V_CB, V_BA, V_BI, V_LAM, V_FCW, V_FCB = 0, 8, 16, 17, 18, 26, 28, 30, 32, 34, 166
NV = 210
C_ID, C_NL, C_NU, C_BO, C_MK = 0, 128, 256, 384, 512
NCST = 512 + 4 * 512


class Sched:
    ENG = ("pe", "act", "dve", "pool", "sp")

    def __init__(self):
        self.ops = []
        self.last_w = {}
        self.readers = {}
        self.eng_count = {e: 0 for e in self.ENG}
        self.chan_count = {}
        self.chan_inc = {}
        self.barrier_nodes = []

    def add(self, eng, fn, reads=(), writes=(), chan=None, ndma=1, inc=16):
        deps = set(self.barrier_nodes)
        for k in reads:
            w = self.last_w.get(k)
            if w is not None:
                deps.add(w)
        for k in writes:
            w = self.last_w.get(k)
            if w is not None:
                deps.add(w)
            for r in self.readers.get(k, ()):
                deps.add(r)
        if chan is None:
            self.eng_count[eng] += 1
            node = ("E", eng, self.eng_count[eng])
        else:
            self.chan_inc[chan] = inc
            self.chan_count[chan] = self.chan_count.get(chan, 0) + ndma * inc
            node = ("C", chan, self.chan_count[chan])
        self.ops.append(dict(eng=eng, fn=fn, deps=deps, node=node, chan=chan))
        for k in reads:
            self.readers.setdefault(k, []).append(node)
        for k in writes:
            self.last_w[k] = node
            self.readers[k] = []
        return node

    def barrier(self):
        nodes = []
        for e, c in self.eng_count.items():
            if c:
                nodes.append(("E", e, c))
        for ch, c in self.chan_count.items():
            nodes.append(("C", ch, c))
        self.barrier_nodes = nodes

    def finalize(self):
        known = {e: {} for e in self.ENG}
        signal = {e: set() for e in self.ENG}
        for op in self.ops:
            need = {}
            for kind, tgt, val in op["deps"]:
                if kind == "E" and tgt == "pe" and op["eng"] == "pe" and op["chan"] is None:
                    continue
                key = (kind, tgt)
                if val > need.get(key, 0):
                    need[key] = val
            waits = []
            kn = known[op["eng"]]
            for key, val in need.items():
                if kn.get(key, 0) >= val:
                    continue
                kn[key] = val
                waits.append((key, val))
                if key[0] == "E":
                    signal[key[1]].add(val)
            op["waits"] = waits
        self.rank = {}
        for e in self.ENG:
            self.rank[e] = {v: i + 1 for i, v in enumerate(sorted(signal[e]))}

    def emit(self, nc, block, sems, chan_sems):
        streams = {e: [op for op in self.ops if op["eng"] == e] for e in self.ENG}

        def run(eng_name, eng):
            for op in streams[eng_name]:
                for (kind, tgt), val in op["waits"]:
                    if kind == "E":
                        eng.wait_ge(sems[tgt], self.rank[tgt][val])
                    else:
                        eng.wait_ge(chan_sems[tgt], val)
                if op["chan"] is not None:
                    op["fn"](eng, chan_sems[op["chan"]])
                else:
                    ins = op["fn"](eng)
                    idx = op["node"][2]
                    if idx in self.rank[eng_name]:
                        assert ins is not None
                        ins.then_inc(sems[eng_name], 1)

        @block.tensor
        def _(e):
            run("pe", e)

        @block.scalar
        def _(e):
            run("act", e)

        @block.vector
        def _(e):
            run("dve", e)

        @block.gpsimd
        def _(e):
            run("pool", e)

        @block.sync
        def _(e):
            run("sp", e)


def build_nc(SEQ):
    assert SEQ % 1024 == 0
    HALF = SEQ // 2
    TP = SEQ + 128
    NB = TP // 128
    NQT = 1 + SEQ // 512
    NST = HALF // 256
    W3 = 256

    nc = bass.Bass("TRN2", target_bir_lowering=False)
    xpad = nc.dram_tensor("xpad", [TP, D], F32, kind="ExternalInput").ap()
    w_in = nc.dram_tensor("w_in", [D, 1280], F32, kind="ExternalInput").ap()
    vecs = nc.dram_tensor("vecs", [128, NV], F32, kind="ExternalInput").ap()
    w_rg = nc.dram_tensor("w_rg", [128, 4 * 128], F32, kind="ExternalInput").ap()
    w_out = nc.dram_tensor("w_out", [D, D], F32, kind="ExternalInput").ap()
    w_fi = nc.dram_tensor("w_ffn_in", [D, 2 * DFF], F32, kind="ExternalInput").ap()
    w_fo = nc.dram_tensor("w_ffn_out", [DFF, D], F32, kind="ExternalInput").ap()
    consts = nc.dram_tensor("consts", [128, NCST], F32, kind="ExternalInput").ap()
    out = nc.dram_tensor("out", [HALF, D], F32, kind="ExternalOutput").ap()
    GW = SEQ // 4
    mo_g = [nc.dram_tensor("mixed_own_%d" % g, [512, GW], BF16) for g in range(4)]
    mo_halo_t = nc.dram_tensor("mixed_own_halo", [512, 4], BF16)
    ma_big_t = nc.dram_tensor("mixed_all_big", [4 * 1024, GW], BF16)
    ma_halo_t = nc.dram_tensor("mixed_all_halo", [1024, 4], BF16)
    xhalf = nc.dram_tensor("xhalf", [HALF + 2, D], F32, kind="ExternalInput").ap()
    mh_t = nc.dram_tensor("mixed_half", [1024, HALF + 2], BF16)
    mixed_half = mh_t.ap()
    ma_big = ma_big_t.ap()
    ma_halo = ma_halo_t.ap()
    mo_halo = mo_halo_t.ap()

    def store_fn(src, row0, ti):
        pieces = []
        if ti == 0:
            pieces.append((mo_halo[row0:row0 + 128, 0:2], 126, 2))
        else:
            i0 = 512 * (ti - 1)
            c = 0
            while c < 512:
                g = (i0 + c) // GW
                gc = (i0 + c) % GW
                n = min(512 - c, GW - gc)
                pieces.append((mo_g[g].ap()[row0:row0 + 128, gc:gc + n], c, n))
                c += n
            if i0 <= HALF - 2 < i0 + 512:
                pieces.append((mo_halo[row0:row0 + 128, 2:4], HALF - 2 - i0, 2))

        def fn(e, s):
            ins = None
            for dst, c0, n in pieces:
                ins = e.dma_start(out=dst, in_=src[:, c0:c0 + n]).then_inc(s, 16)
            return ins
        return fn, len(pieces)

    S = Sched()
    ARENA_F = 53000

    ctx = nc.sbuf_tensor("arena", [128, ARENA_F], F32)
    arena = ctx.__enter__()
    pctx = nc.psum_tensor("ps", [128, 8 * 512], F32)
    ps = pctx.__enter__()

    class Arena:
        def __init__(self):
            self.off = 0

        def f32(self, n):
            o = self.off
            self.off += n
            assert self.off <= ARENA_F, self.off
            return arena[:, o:o + n]

        def bf(self, n):
            nf = (n + 1) // 2
            o = self.off
            self.off += nf
            assert self.off <= ARENA_F, self.off
            return arena[:, o:o + nf].bitcast(BF16)

    A = Arena()

    def bank(b, n=512):
        return ps[:, b * 512:b * 512 + n]

    def bank_bf(b):
        return ps[:, b * 512:(b + 1) * 512].bitcast(BF16)

    cst = A.bf(NCST)
    vec = A.f32(NV)
    ext = A.f32(16)
    ident = cst[:, C_ID:C_ID + 128]
    negL = cst[:, C_NL:C_NL + 128]
    negU = cst[:, C_NU:C_NU + 128]
    bones = cst[:, C_BO:C_BO + 128]

    def maskv(j, W):
        return cst[:, C_MK + j * 512:C_MK + j * 512 + W]

    mark_stage = A.off

    qT = A.bf(2 * TP)
    kT = A.bf(2 * TP)
    v_sb = A.bf(NB * 256)
    mark12 = A.off
    win_bf = A.bf(8 * 1280)
    wrg_bf = A.bf(4 * 128)
    stg = [A.f32(1280), A.f32(1280)]
    xs = [A.f32(1024), A.f32(1024)]
    sqj = A.bf(1024)
    hn_bf = [A.bf(1024), A.bf(1024)]
    ssb = A.f32(8)
    hnT = A.bf(8 * 512)
    sqb = A.bf(512)
    rqb = A.f32(512)
    xr_sb = [A.f32(3 + 512), A.f32(3 + 512)]
    xc_sb = A.f32(512)
    xc_bf = A.bf(512)
    tr_sb = A.f32(512)
    ti_sb = A.f32(512)
    la_sb = A.f32(512)
    a_sb = A.f32(512)
    th_sb = A.f32(512)
    m2_sb = A.f32(512)
    bt_sb = A.f32(512)
    hl_sb = [A.f32(512), A.f32(512)]
    hst = A.f32(2)
    y_sb = A.f32(512)
    y2_sb = A.f32(512)
    tg_sb = A.f32(512)
    ol_bf = [A.bf(512), A.bf(512)]

    def col(i, n=1):
        return vec[:, i:i + n]

    for hh in range(2):
        S.add("sp", lambda e, s, hh=hh: e.dma_start(out=stg[hh][:, 0:1280], in_=consts[:, hh * 1280:(hh + 1) * 1280]).then_inc(s, 16),
              writes=[("stg", hh)], chan=("stg", hh))
        S.add("dve", lambda e, hh=hh: e.tensor_copy(out=cst[:, hh * 1280:(hh + 1) * 1280], in_=stg[hh][:, 0:1280]),
              reads=[("stg", hh)], writes=["cst"])
    S.add("sp", lambda e, s: e.dma_start(out=vec[:, :], in_=vecs[:, :]).then_inc(s, 16), writes=["vec"], chan="vec")
    S.add("act", lambda e: e.activation(out=ext[:, 7:9], in_=col(V_LAM, 2), func=AF.Exp, scale=-1.0),
          reads=["vec"], writes=["ext_t"])
    S.add("act", lambda e: e.activation(out=ext[:, 7:9], in_=ext[:, 7:9], func=AF.Ln, bias=1.0),
          reads=["ext_t"], writes=["ext_t"])
    S.add("dve", lambda e: e.tensor_scalar(out=ext[:, 0:2], in0=ext[:, 7:9], scalar1=-8.0, scalar2=None, op0=ALU.mult),
          reads=["ext_t"], writes=["ext"])
    S.add("dve", lambda e: e.tensor_scalar(out=ext[:, 2:6], in0=col(V_BA, 4), scalar1=-1.0, scalar2=None, op0=ALU.mult),
          reads=["vec", "ext"], writes=["ext"])
    S.add("dve", lambda e: e.tensor_scalar(out=ext[:, 6:7], in0=col(V_KG), scalar1=0.125, scalar2=None, op0=ALU.mult),
          reads=["vec", "ext"], writes=["ext"])
    S.add("sp", lambda e, s: e.dma_start(out=stg[0][:, 0:512], in_=w_rg[:, :]).then_inc(s, 16),
          writes=[("stg", 0)], chan=("stg", 0))
    S.add("dve", lambda e: e.tensor_copy(out=wrg_bf[:, :], in_=stg[0][:, 0:512]), reads=[("stg", 0)], writes=["wrg"])
    for c in range(DC):
        hh = c % 2
        S.add("sp", lambda e, s, c=c, hh=hh: e.dma_start(out=stg[hh][:, :], in_=w_in[c * 128:(c + 1) * 128, :]).then_inc(s, 16),
              writes=[("stg", hh)], chan=("stg", hh))
        if c % 2 == 0:
            S.add("dve", lambda e, c=c, hh=hh: e.tensor_scalar(out=win_bf[:, c * 1280:(c + 1) * 1280], in0=stg[hh][:, :],
                                                                scalar1=col(V_G1 + c), scalar2=None, op0=ALU.mult),
                  reads=[("stg", hh), "vec"], writes=[("win", c)])
        else:
            S.add("act", lambda e, c=c, hh=hh: e.activation(out=win_bf[:, c * 1280:(c + 1) * 1280], in_=stg[hh][:, :],
                                                             func=AF.Copy, scale=col(V_G1 + c)),
                  reads=[("stg", hh), "vec"], writes=[("win", c)])
    S.add("pool", lambda e: e.memset(xr_sb[0][:, 0:3], 0.0), writes=[("xr", 0)])
    S.add("pool", lambda e: e.memset(xr_sb[1][:, 0:3], 0.0), writes=[("xr", 1)])
    S.add("pool", lambda e: e.memset(hst[:, :], 0.0), writes=["hst"])

    win_reads = [("win", c) for c in range(DC)]

    B_TRP, B_V, B_PJ0, B_PJ1, B_PS2, B_GR, B_GI = 0, 1, 2, 3, 4, 5, 6

    def tile_info(ti):
        if ti == 0:
            return 0, 128
        return 128 + 512 * (ti - 1), 512

    pj_ctr = [0]

    def proj_slab(col0, W):
        b = B_PJ0 + (pj_ctr[0] % 2)
        pj_ctr[0] += 1

        def fn(e, b=b, col0=col0, W=W):
            ins = None
            for c in range(DC):
                ins = e.matmul(bank(b, W), lhsT=win_bf[:, c * 1280 + col0:c * 1280 + col0 + 128],
                               rhs=hnT[:, c * 512:c * 512 + W], start=(c == 0), stop=(c == DC - 1))
            return ins
        S.add("pe", fn, reads=win_reads + ["hnT"], writes=[("ps", b)])
        return b

    for ti in range(NQT):
        pos0, W = tile_info(ti)
        nsub = W // 128
        for sub in range(nsub):
            blk = pos0 // 128 + sub
            sl = blk % 2
            S.add("sp", lambda e, s, blk=blk, sl=sl: e.dma_start(out=xs[sl][:, :], in_=xpad[blk * 128:(blk + 1) * 128, :]).then_inc(s, 16),
                  writes=[("xs", sl)], chan=("xs", sl))
            S.add("pool", lambda e, sl=sl: e.memset(ssb[:, sl:sl + 1], 0.0), writes=[("ss", sl)])
            S.add("act", lambda e, sl=sl: e.activation(out=sqj[:, :], in_=xs[sl][:, :], func=AF.Square, accum_out=ssb[:, sl:sl + 1]),
                  reads=[("xs", sl)], writes=[("ss", sl), "sqj"])
            S.add("dve", lambda e, sl=sl: e.tensor_scalar(out=ssb[:, 2 + sl:3 + sl], in0=ssb[:, sl:sl + 1], scalar1=1.0 / D, scalar2=EPS,
                                                           op0=ALU.mult, op1=ALU.add),
                  reads=[("ss", sl)], writes=[("rstd", sl)])
            S.add("act", lambda e, sl=sl: e.activation(out=ssb[:, 2 + sl:3 + sl], in_=ssb[:, 2 + sl:3 + sl], func=AF.Ln),
                  reads=[("rstd", sl)], writes=[("rstd", sl)])
            S.add("act", lambda e, sl=sl: e.activation(out=ssb[:, 2 + sl:3 + sl], in_=ssb[:, 2 + sl:3 + sl], func=AF.Exp, scale=-0.5),
                  reads=[("rstd", sl)], writes=[("rstd", sl)])
            S.add("act", lambda e, sl=sl: e.activation(out=hn_bf[sl][:, :], in_=xs[sl][:, :], func=AF.Copy, scale=ssb[:, 2 + sl:3 + sl]),
                  reads=[("xs", sl), ("rstd", sl)], writes=[("hn", sl)])

            def tr_fn(e, sl=sl):
                ins = None
                for c in range(DC):
                    ins = e.transpose(bank_bf(B_TRP)[:, c * 128:(c + 1) * 128], hn_bf[sl][:, c * 128:(c + 1) * 128], ident)
                return ins
            S.add("pe", tr_fn, reads=[("hn", sl), "cst"], writes=[("ps", B_TRP)])
            S.add("dve", lambda e, sub=sub: e.tensor_copy(
                out=hnT.rearrange("p (c w) -> p c w", c=8)[:, :, sub * 128:(sub + 1) * 128],
                in_=bank_bf(B_TRP).rearrange("p (c w) -> p c w", c=8)),
                reads=[("ps", B_TRP)], writes=["hnT"])

            def v_fn(e, sub=sub):
                ins = None
                for c in range(DC):
                    ins = e.matmul(bank(B_V, 256), lhsT=hnT[:, c * 512 + sub * 128:c * 512 + (sub + 1) * 128],
                                   rhs=win_bf[:, c * 1280 + 512:c * 1280 + 768], start=(c == 0), stop=(c == DC - 1))
                return ins
            S.add("pe", v_fn, reads=win_reads + ["hnT"], writes=[("ps", B_V)])
            S.add("act", lambda e, blk=blk: e.copy(out=v_sb[:, blk * 256:(blk + 1) * 256], in_=bank(B_V, 256)),
                  reads=[("ps", B_V)], writes=[("v", blk)])

        for which in range(2):
            for j in range(2):
                b = proj_slab(which * 256 + j * 128, W)
                S.add("act", lambda e, b=b, W=W: e.activation(out=sqb[:, 0:W], in_=bank(b, W), func=AF.Square),
                      reads=[("ps", b)], writes=["sqb"])
                S.add("pe", lambda e, W=W: e.matmul(bank(B_PS2, W), lhsT=bones, rhs=sqb[:, 0:W], start=True, stop=True),
                      reads=["sqb", "cst"], writes=[("ps", B_PS2)])
                S.add("dve", lambda e, W=W: e.tensor_scalar(out=rqb[:, 0:W], in0=bank(B_PS2, W), scalar1=1.0 / 64, scalar2=EPS,
                                                             op0=ALU.mult, op1=ALU.add),
                      reads=[("ps", B_PS2)], writes=["rqb"])
                S.add("act", lambda e, W=W: e.activation(out=rqb[:, 0:W], in_=rqb[:, 0:W], func=AF.Ln), reads=["rqb"], writes=["rqb"])
                S.add("act", lambda e, W=W: e.activation(out=rqb[:, 0:W], in_=rqb[:, 0:W], func=AF.Exp, scale=-0.5),
                      reads=["rqb"], writes=["rqb"])
                dst = qT if which == 0 else kT
                gsc = col(V_QG) if which == 0 else ext[:, 6:7]
                S.add("dve", lambda e, b=b, W=W, dst=dst, gsc=gsc, j=j, pos0=pos0: e.scalar_tensor_tensor(
                    out=dst[:, j * TP + pos0:j * TP + pos0 + W], in0=bank(b, W), scalar=gsc, in1=rqb[:, 0:W],
                    op0=ALU.mult, op1=ALU.mult),
                    reads=[("ps", b), "rqb", "vec", "ext"], writes=[("qk", which, j, ti)])
        for c in range(2):
            b = proj_slab(768 + c * 128, W)
            S.add("act", lambda e, b=b, W=W, c=c: e.copy(out=xr_sb[c][:, 3:3 + W], in_=bank(b, W)),
                  reads=[("ps", b)], writes=[("xr", c)])
            S.add("dve", lambda e, c=c, W=W: e.tensor_scalar(out=xc_sb[:, 0:W], in0=xr_sb[c][:, 3:3 + W], scalar1=col(V_CW + c * 4 + 3),
                                                               scalar2=col(V_CB + c), op0=ALU.mult, op1=ALU.add),
                  reads=[("xr", c), "vec"], writes=["xc"])
            for k in range(3):
                S.add("dve", lambda e, c=c, W=W, k=k: e.scalar_tensor_tensor(out=xc_sb[:, 0:W], in0=xr_sb[c][:, k:k + W],
                                                                               scalar=col(V_CW + c * 4 + k), in1=xc_sb[:, 0:W],
                                                                               op0=ALU.mult, op1=ALU.add),
                      reads=[("xr", c), "vec", "xc"], writes=["xc"])
            S.add("pool", lambda e, c=c, W=W: e.tensor_copy(out=xr_sb[c][:, 0:3], in_=xr_sb[c][:, W:W + 3]),
                  reads=[("xr", c)], writes=[("xr", c)])
            S.add("act", lambda e, W=W: e.copy(out=xc_bf[:, 0:W], in_=xc_sb[:, 0:W]), reads=["xc"], writes=["xcb"])
            S.add("pe", lambda e, c=c, W=W: e.matmul(bank(B_GR, W), lhsT=wrg_bf[:, (0 * 2 + c) * 128:(0 * 2 + c + 1) * 128],
                                                      rhs=xc_bf[:, 0:W], start=True, stop=True),
                  reads=["xcb", "wrg"], writes=[("ps", B_GR)])
            S.add("pe", lambda e, c=c, W=W: e.matmul(bank(B_GI, W), lhsT=wrg_bf[:, (1 * 2 + c) * 128:(1 * 2 + c + 1) * 128],
                                                      rhs=xc_bf[:, 0:W], start=True, stop=True),
                  reads=["xcb", "wrg"], writes=[("ps", B_GI)])
            S.add("act", lambda e, c=c, W=W: e.activation(out=tr_sb[:, 0:W], in_=bank(B_GR, W), func=AF.Exp,
                                                           bias=ext[:, 2 + c:3 + c], scale=-1.0),
                  reads=[("ps", B_GR), "ext"], writes=["tr"])
            S.add("act", lambda e, c=c, W=W: e.activation(out=ti_sb[:, 0:W], in_=bank(B_GI, W), func=AF.Exp,
                                                           bias=ext[:, 4 + c:5 + c], scale=-1.0),
                  reads=[("ps", B_GI), "ext"], writes=["tig"])
            S.add("act", lambda e, W=W: e.activation(out=tr_sb[:, 0:W], in_=tr_sb[:, 0:W], func=AF.Ln, bias=1.0), reads=["tr"], writes=["tr"])
            S.add("act", lambda e, W=W: e.activation(out=tr_sb[:, 0:W], in_=tr_sb[:, 0:W], func=AF.Exp, scale=-1.0), reads=["tr"], writes=["tr"])
            S.add("act", lambda e, W=W: e.activation(out=ti_sb[:, 0:W], in_=ti_sb[:, 0:W], func=AF.Ln, bias=1.0), reads=["tig"], writes=["tig"])
            S.add("act", lambda e, W=W: e.activation(out=ti_sb[:, 0:W], in_=ti_sb[:, 0:W], func=AF.Exp, scale=-1.0), reads=["tig"], writes=["tig"])
            S.add("act", lambda e, c=c, W=W: e.activation(out=a_sb[:, 0:W], in_=tr_sb[:, 0:W], func=AF.Exp, scale=ext[:, c:c + 1]),
                  reads=["tr", "ext"], writes=["a"])
            S.add("dve", lambda e, W=W: e.tensor_tensor(out=m2_sb[:, 0:W], in0=a_sb[:, 0:W], in1=a_sb[:, 0:W], op=ALU.mult),
                  reads=["a"], writes=["m2"])
            S.add("dve", lambda e, W=W: e.tensor_scalar(out=m2_sb[:, 0:W], in0=m2_sb[:, 0:W], scalar1=-1.0, scalar2=1.0,
                                                         op0=ALU.mult, op1=ALU.add),
                  reads=["m2"], writes=["m2"])
            S.add("act", lambda e, W=W: e.activation(out=m2_sb[:, 0:W], in_=m2_sb[:, 0:W], func=AF.Ln), reads=["m2"], writes=["m2"])
            S.add("act", lambda e, W=W: e.activation(out=m2_sb[:, 0:W], in_=m2_sb[:, 0:W], func=AF.Exp, scale=0.5), reads=["m2"], writes=["m2"])
            S.add("dve", lambda e, W=W: e.tensor_tensor(out=bt_sb[:, 0:W], in0=ti_sb[:, 0:W], in1=xc_sb[:, 0:W], op=ALU.mult),
                  reads=["tig", "xc"], writes=["bt"])
            S.add("dve", lambda e, W=W: e.tensor_tensor(out=bt_sb[:, 0:W], in0=bt_sb[:, 0:W], in1=m2_sb[:, 0:W], op=ALU.mult),
                  reads=["bt", "m2"], writes=["bt"])
            if ti == 0:
                S.add("dve", lambda e: e.memset(bt_sb[:, 0:PAD], 0.0), reads=["bt"], writes=["bt"])
            S.add("dve", lambda e, c=c, W=W: e.tensor_tensor_scan(out=hl_sb[c][:, 0:W], data0=a_sb[:, 0:W], data1=bt_sb[:, 0:W],
                                                                   initial=hst[:, c:c + 1], op0=ALU.mult, op1=ALU.add),
                  reads=["a", "bt", "hst"], writes=[("hl", c)])
            S.add("dve", lambda e, c=c, W=W: e.tensor_copy(out=hst[:, c:c + 1], in_=hl_sb[c][:, W - 1:W]),
                  reads=[("hl", c)], writes=["hst"])
        for c in range(2):
            b = proj_slab(1024 + c * 128, W)
            S.add("act", lambda e, b=b, W=W: e.copy(out=y_sb[:, 0:W], in_=bank(b, W)), reads=[("ps", b)], writes=["y"])
            S.add("act", lambda e, b=b, W=W: e.activation(out=y2_sb[:, 0:W], in_=bank(b, W), func=AF.Square),
                  reads=[("ps", b)], writes=["y2"])
            S.add("dve", lambda e, W=W: e.tensor_scalar(out=y2_sb[:, 0:W], in0=y2_sb[:, 0:W], scalar1=0.044715, scalar2=1.0,
                                                          op0=ALU.mult, op1=ALU.add),
                  reads=["y2"], writes=["y2"])
            S.add("dve", lambda e, W=W: e.tensor_tensor(out=y2_sb[:, 0:W], in0=y2_sb[:, 0:W], in1=y_sb[:, 0:W], op=ALU.mult),
                  reads=["y2", "y"], writes=["y2"])
            S.add("act", lambda e, W=W: e.activation(out=tg_sb[:, 0:W], in_=y2_sb[:, 0:W], func=AF.Exp, scale=-2.0 * GELU_C),
                  reads=["y2"], writes=["tg"])
            S.add("act", lambda e, W=W: e.activation(out=tg_sb[:, 0:W], in_=tg_sb[:, 0:W], func=AF.Ln, bias=1.0), reads=["tg"], writes=["tg"])
            S.add("act", lambda e, W=W: e.activation(out=tg_sb[:, 0:W], in_=tg_sb[:, 0:W], func=AF.Exp, scale=-1.0), reads=["tg"], writes=["tg"])
            S.add("dve", lambda e, W=W: e.tensor_tensor(out=tg_sb[:, 0:W], in0=tg_sb[:, 0:W], in1=y_sb[:, 0:W], op=ALU.mult),
                  reads=["tg", "y"], writes=["tg"])
            S.add("dve", lambda e, c=c, W=W: e.tensor_tensor(out=ol_bf[c][:, 0:W], in0=tg_sb[:, 0:W], in1=hl_sb[c][:, 0:W], op=ALU.mult),
                  reads=["tg", ("hl", c)], writes=[("ol", c)])
            sfn, nd = store_fn(ol_bf[c], 256 + c * 128, ti)
            S.add("pool", sfn, reads=[("ol", c)], writes=[("mo", "l", c, ti)], chan=("ol", c), ndma=nd)

    S.barrier()
    A.off = mark12
    e_sb = [A.f32(1024) for _ in range(3)]
    sp_sb = [A.bf(1024) for _ in range(3)]
    g_sb = [A.bf(1024) for _ in range(2)]
    w_sb = [A.bf(1024) for _ in range(3)]
    osb = [A.bf(2 * 512), A.bf(2 * 512)]
    items = []
    for ti in range(NQT):
        pos0, W = tile_info(ti)
        b0 = pos0 // 128
        nsub = W // 128
        kbs = list(range(b0 + nsub - 1, -1, -1))
        for n, kb in enumerate(kbs):
            for j in range(2):
                items.append(dict(ti=ti, pos0=pos0, W=W, b0=b0, kb=kb, j=j, first=(n == 0), last=(n == len(kbs) - 1)))
    NI = len(items)

    S.add("dve", lambda e: e.memset(ext[:, 9:10], 0.0),
          reads=[("qk", w, j, t) for w in range(2) for j in range(2) for t in range(NQT)] + [("v", bl) for bl in range(NB)],
          writes=["kall"])

    def v3(buf, W):
        return buf.rearrange("p (h w) -> p h w", h=2)[:, :, 0:W]

    def zview(W):
        return ps[:, 0:1024].rearrange("p (h w) -> p h w", h=2)[:, :, 0:W]

    def pview(j, W):
        return ps[:, (2 + 2 * j) * 512:(4 + 2 * j) * 512].rearrange("p (h w) -> p h w", h=2)[:, :, 0:W]

    RG = [[0, 1], [2, 3], [4, 5], [6, 7]]

    def tile_keys(ti):
        return [("mo", "l", c, ti) for c in range(2)] + [("mo", "a", j, ti) for j in range(2)]

    gathered = set()

    def maybe_gather(ti):
        if ti == 0:
            return
        done_tok = 512 * ti
        for g in range(4):
            if g in gathered or (g + 1) * GW > done_tok:
                continue
            gathered.add(g)
            t_lo = 1 + (g * GW) // 512
            t_hi = 1 + ((g + 1) * GW - 1) // 512
            keys = []
            for t in range(t_lo, t_hi + 1):
                keys += tile_keys(t)
            S.add("pool", lambda e, s, g=g: e.collective_compute(
                "AllGather", ALU.bypass, replica_groups=RG, ins=[mo_g[g].ap().opt()],
                outs=[ma_big[g * 1024:(g + 1) * 1024, :].opt()]).then_inc(s),
                reads=keys, writes=[("mall", g)], chan=("cc", g), inc=1)

    def PE1(i):
        it = items[i]
        kb, W, pos0, j = it["kb"], it["W"], it["pos0"], it["j"]

        def fn(e):
            ins = None
            for hh in range(2):
                r = hh * 64
                ins = e.matmul(bank(hh, W), lhsT=kT[r:r + 64, j * TP + kb * 128:j * TP + (kb + 1) * 128],
                               rhs=qT[r:r + 64, j * TP + pos0:j * TP + pos0 + W], start=True, stop=True)
            return ins
        S.add("pe", fn, reads=["kall"], writes=[("ps", 0), ("ps", 1)])

    def ACT12(i):
        it = items[i]
        W = it["W"]
        eb = e_sb[i % 3]
        sb = sp_sb[i % 3]
        S.add("act", lambda e: e.activation(out=v3(eb, W), in_=zview(W), func=AF.Exp),
              reads=[("ps", 0), ("ps", 1)], writes=[("e", i % 3)])
        S.add("act", lambda e: e.activation(out=v3(sb, W), in_=v3(eb, W), func=AF.Ln, bias=1.0),
              reads=[("e", i % 3)], writes=[("sp", i % 3)])
        if it["kb"] >= it["b0"]:
            jj = it["kb"] - it["b0"]
            for hh in range(2):
                S.add("dve", lambda e, hh=hh: e.tensor_tensor(out=sb[:, hh * 512:hh * 512 + W], in0=sb[:, hh * 512:hh * 512 + W],
                                                               in1=maskv(jj, W), op=ALU.mult),
                      reads=[("sp", i % 3), "cst"], writes=[("sp", i % 3)])

    def PE2(i):
        it = items[i]
        W, j = it["W"], it["j"]
        sb = sp_sb[i % 3]

        def fn(e):
            ins = None
            for hh in range(2):
                ins = e.matmul(bank(2 + 2 * j + hh, W), lhsT=negL, rhs=sb[:, hh * 512:hh * 512 + W],
                               start=it["first"], stop=False, skip_group_check=True)
            return ins
        S.add("pe", fn, reads=[("sp", i % 3), "cst"], writes=[("ps", 2 + 2 * j), ("ps", 3 + 2 * j)])

    def ACT3(i):
        it = items[i]
        W, j = it["W"], it["j"]
        gb = g_sb[i % 2]
        eb = e_sb[i % 3]
        wb = w_sb[i % 3]
        S.add("act", lambda e: e.activation(out=v3(gb, W), in_=pview(j, W), func=AF.Exp),
              reads=[("ps", 2 + 2 * j), ("ps", 3 + 2 * j)], writes=[("g", i % 2)])
        S.add("dve", lambda e: e.tensor_tensor(out=v3(wb, W), in0=v3(eb, W), in1=v3(gb, W), op=ALU.mult),
              reads=[("e", i % 3), ("g", i % 2)], writes=[("w", i % 3)])
        if it["kb"] >= it["b0"]:
            jj = it["kb"] - it["b0"]
            for hh in range(2):
                S.add("dve", lambda e, hh=hh: e.tensor_tensor(out=wb[:, hh * 512:hh * 512 + W], in0=wb[:, hh * 512:hh * 512 + W],
                                                               in1=maskv(jj, W), op=ALU.mult),
                      reads=[("w", i % 3), "cst"], writes=[("w", i % 3)])

    def PE4(i):
        it = items[i]
        if it["last"]:
            return
        W, j = it["W"], it["j"]
        sb = sp_sb[i % 3]

        def fn(e):
            ins = None
            for hh in range(2):
                ins = e.matmul(bank(2 + 2 * j + hh, W), lhsT=negU, rhs=sb[:, hh * 512:hh * 512 + W],
                               start=False, stop=False, skip_group_check=True)
            return ins
        S.add("pe", fn, reads=[("sp", i % 3), "cst"], writes=[("ps", 2 + 2 * j), ("ps", 3 + 2 * j)])

    def PE3(i):
        it = items[i]
        kb, W, pos0, ti, j = it["kb"], it["W"], it["pos0"], it["ti"], it["j"]
        ob = 6 + j
        wb = w_sb[i % 3]
        par = ti % 2

        def fn(e):
            ins = None
            for hh in range(2):
                h = 2 * j + hh
                ins = e.matmul(ps[hh * 64:(hh + 1) * 64, ob * 512:ob * 512 + W], lhsT=v_sb[:, kb * 256 + h * 64:kb * 256 + (h + 1) * 64],
                               rhs=wb[:, hh * 512:hh * 512 + W], start=it["first"], stop=it["last"], skip_group_check=True)
            return ins
        S.add("pe", fn, reads=[("w", i % 3), "kall"], writes=[("ps", ob)])
        if it["last"]:
            S.add("dve", lambda e: e.tensor_copy(out=osb[par][:, j * 512:j * 512 + W], in_=bank(ob, W)),
                  reads=[("ps", ob)], writes=[("osb", par, j)])
            sfn, nd = store_fn(osb[par][:, j * 512:(j + 1) * 512], j * 128, ti)
            S.add("pool", sfn, reads=[("osb", par, j)], writes=[("mo", "a", j, ti)], chan=("osb", par, j), ndma=nd)
            if j == 1:
                maybe_gather(ti)

    PE1(0)
    for s in range(-1, NI + 1):
        if 0 <= s + 1 < NI:
            ACT12(s + 1)
        if s + 2 < NI:
            PE1(s + 2)
        if 0 <= s + 1 < NI:
            PE2(s + 1)
        if 0 <= s < NI:
            ACT3(s)
            PE4(s)
        if 0 <= s - 1 < NI:
            PE3(s - 1)

    S.add("pool", lambda e, s: e.collective_compute("AllGather", ALU.bypass, replica_groups=RG,
                                                    ins=[mo_halo_t.ap().opt()], outs=[ma_halo_t.ap().opt()]).then_inc(s),
          reads=tile_keys(0) + tile_keys(1 + (HALF - 2) // 512), writes=[("mall", "h")], chan="cch", inc=1)

    def mh_fn(e, s):
        half = e.partition_id() % 2
        ins = None
        for j in range(2):
            for r in range(2):
                ins = e.dma_start(out=mixed_half[r * 512:(r + 1) * 512, 2 + j * GW:2 + (j + 1) * GW],
                                  in_=ma_big[bass.ds(half * 2048 + j * 1024 + r * 512, 512), :]).then_inc(s, 16)
        ins = e.dma_start(out=mixed_half[:, 0:2], in_=ma_halo[:, bass.ds(half * 2, 2)]).then_inc(s, 16)
        return ins
    S.add("sp", mh_fn, reads=[("mall", g) for g in range(4)] + [("mall", "h")], writes=["mhalf"], chan="mh", ndma=5)

    S.barrier()
    A.off = mark_stage
    wout_bf = A.bf(8 * 1024)
    wfi_bf = A.bf(8 * 2 * DFF)
    wfo_bf = A.bf(NPAIR * 1024)
    mark3 = A.off
    stg3 = [A.f32(2816), A.f32(2816)]

    stg_ctr = [0]

    def load_cast(dst_ap, src_ap, ncols, key, scale_col=None):
        sl = stg_ctr[0] % 2
        stg_ctr[0] += 1
        S.add("sp", lambda e, s: e.dma_start(out=stg3[sl][:, 0:ncols], in_=src_ap).then_inc(s, 16),
              writes=[("stg3", sl)], chan=("stg3", sl))
        eng = ("dve", "act")[stg_ctr[0] % 2]
        if scale_col is None:
            if eng == "act":
                S.add(eng, lambda e: e.copy(out=dst_ap, in_=stg3[sl][:, 0:ncols]), reads=[("stg3", sl)], writes=[key])
            else:
                S.add(eng, lambda e: e.tensor_copy(out=dst_ap, in_=stg3[sl][:, 0:ncols]), reads=[("stg3", sl)], writes=[key])
        else:
            if eng == "act":
                S.add(eng, lambda e: e.activation(out=dst_ap, in_=stg3[sl][:, 0:ncols], func=AF.Copy, scale=scale_col),
                      reads=[("stg3", sl), "vec"], writes=[key])
            else:
                S.add(eng, lambda e: e.tensor_scalar(out=dst_ap, in0=stg3[sl][:, 0:ncols], scalar1=scale_col, scalar2=None, op0=ALU.mult),
                      reads=[("stg3", sl), "vec"], writes=[key])

    w3_keys = []
    for c in range(DC):
        load_cast(wout_bf[:, c * 1024:(c + 1) * 1024], w_out[c * 128:(c + 1) * 128, :], 1024, ("wout", c))
        w3_keys.append(("wout", c))
    for c in range(DC):
        for hh in range(2):
            load_cast(wfi_bf[:, c * 2 * DFF + hh * DFF:c * 2 * DFF + (hh + 1) * DFF], w_fi[c * 128:(c + 1) * 128, hh * DFF:(hh + 1) * DFF],
                      DFF, ("wfi", c, hh), scale_col=col(V_G2 + c))
            w3_keys.append(("wfi", c, hh))
    for j in range(0, NPAIR, 2):
        for jj in range(2):
            load_cast(wfo_bf[:, (j + jj) * 1024:(j + jj + 1) * 1024], w_fo[(j + jj) * 128:(j + jj + 1) * 128, :], 1024, ("wfo", j + jj))
            w3_keys.append(("wfo", j + jj))
    S.add("dve", lambda e: e.memset(ext[:, 10:11], 0.0), reads=w3_keys, writes=["w3all"])
    S.barrier()
    A.off = mark3
    h2 = [A.f32(1024) for _ in range(4)]
    hn2_bf = [A.bf(1024), A.bf(1024)]
    ss3 = A.f32(8)
    sqj3 = A.bf(1024)
    hn2T = A.bf(8 * W3)
    mt = A.bf(8 * W3)
    ust = A.f32(NSLAB * 2)
    ubuf = [A.f32(2 + W3), A.f32(2 + W3)]
    gbuf = [A.f32(2 + W3), A.f32(2 + W3)]
    uc = [A.f32(W3), A.f32(W3)]
    gc = [A.f32(W3), A.f32(W3)]
    sg = [A.f32(W3), A.f32(W3)]
    act_bf = A.bf(NPAIR * W3)
    ctmp = A.f32(W3)

    B_HP = [0, 1]
    B_T3 = 2
    B_U = [3, 4]
    B_G = [5, 6]
    mixed_half_r = mixed_half.rearrange("(c p) w -> p c w", p=128)
    xflat = xpad.rearrange("t d -> (t d)")
    h2_ctr = [0]
    pair_ctr = [0]
    half_cache = {}

    def get_half(e):
        if 'h' not in half_cache:
            half_cache['h'] = e.partition_id() % 2
        return half_cache['h']

    def stage3_tile(rel0, Wt, final):
        nsub = max(1, Wt // 128)
        n = min(Wt, 128)

        def mt_fn(e, s):
            return e.dma_start(out=mt.rearrange("p (c w) -> p c w", c=8)[:, :, 0:Wt],
                               in_=mixed_half_r[:, :, rel0:rel0 + Wt]).then_inc(s, 16)
        S.add("sp", mt_fn, reads=["mhalf"], writes=["mt"], chan="mt")
        slots = []
        for sub in range(nsub):
            sl = h2_ctr[0] % 4
            h2_ctr[0] += 1
            slots.append(sl)

            def x_fn(e, s, sl=sl, sub=sub):
                r = rel0 + sub * 128
                return e.dma_start(out=h2[sl][0:n, :], in_=xhalf[r:r + n, :]).then_inc(s, 16)
            S.add("sp", x_fn, writes=[("h2", sl)], chan=("h2", sl))

            def wo_fn(e, sub=sub):
                ins = None
                for hf in range(2):
                    for c in range(DC):
                        ins = e.matmul(ps[0:n, B_HP[hf] * 512:(B_HP[hf] + 1) * 512], lhsT=mt[:, c * W3 + sub * 128:c * W3 + sub * 128 + n],
                                       rhs=wout_bf[:, c * 1024 + hf * 512:c * 1024 + (hf + 1) * 512], start=(c == 0), stop=(c == DC - 1))
                return ins
            S.add("pe", wo_fn, reads=["mt", "w3all"], writes=[("ps", 0), ("ps", 1)])
            S.add("dve", lambda e, sl=sl: e.tensor_tensor(out=h2[sl][0:n, :], in0=ps[0:n, 0:1024], in1=h2[sl][0:n, :], op=ALU.add),
                  reads=[("ps", 0), ("ps", 1), ("h2", sl)], writes=[("h2", sl)])
            q = sl % 2
            S.add("pool", lambda e, q=q: e.memset(ss3[:, q:q + 1], 0.0), writes=[("ss3", q)])
            S.add("act", lambda e, sl=sl, q=q: e.activation(out=sqj3[0:n, :], in_=h2[sl][0:n, :], func=AF.Square, accum_out=ss3[0:n, q:q + 1]),
                  reads=[("h2", sl)], writes=[("ss3", q), "sqj3"])
            S.add("dve", lambda e, q=q: e.tensor_scalar(out=ss3[0:n, 2 + q:3 + q], in0=ss3[0:n, q:q + 1], scalar1=1.0 / D, scalar2=EPS,
                                                         op0=ALU.mult, op1=ALU.add), reads=[("ss3", q)], writes=[("rs3", q)])
            S.add("act", lambda e, q=q: e.activation(out=ss3[0:n, 2 + q:3 + q], in_=ss3[0:n, 2 + q:3 + q], func=AF.Ln),
                  reads=[("rs3", q)], writes=[("rs3", q)])
            S.add("act", lambda e, q=q: e.activation(out=ss3[0:n, 2 + q:3 + q], in_=ss3[0:n, 2 + q:3 + q], func=AF.Exp, scale=-0.5),
                  reads=[("rs3", q)], writes=[("rs3", q)])
            S.add("act", lambda e, sl=sl, q=q: e.activation(out=hn2_bf[q][0:n, :], in_=h2[sl][0:n, :], func=AF.Copy, scale=ss3[0:n, 2 + q:3 + q]),
                  reads=[("h2", sl), ("rs3", q)], writes=[("hn2", q)])

            def tr_fn(e, q=q):
                ins = None
                for c in range(DC):
                    ins = e.transpose(bank_bf(B_T3)[:, c * 128:c * 128 + n], hn2_bf[q][0:n, c * 128:(c + 1) * 128], ident[0:n, 0:n])
                return ins
            S.add("pe", tr_fn, reads=[("hn2", q), "cst"], writes=[("ps", B_T3)])
            S.add("dve", lambda e, sub=sub: e.tensor_copy(
                out=hn2T.rearrange("p (c w) -> p c w", c=8)[:, :, sub * 128:sub * 128 + n],
                in_=bank_bf(B_T3).rearrange("p (c w) -> p c w", c=8)[:, :, 0:n]),
                reads=[("ps", B_T3)], writes=["hn2T"])
        for j in range(NPAIR):
            pc = pair_ctr[0] % 2
            pair_ctr[0] += 1
            for which, (bb, buf, cbuf) in enumerate(((B_U[pc], ubuf[pc], uc[pc]), (B_G[pc], gbuf[pc], gc[pc]))):
                slab = j + which * NPAIR

                def fi_fn(e, bb=bb, slab=slab):
                    ins = None
                    for c in range(DC):
                        ins = e.matmul(bank(bb, Wt), lhsT=wfi_bf[:, c * 2 * DFF + slab * 128:c * 2 * DFF + (slab + 1) * 128],
                                       rhs=hn2T[:, c * W3:c * W3 + Wt], start=(c == 0), stop=(c == DC - 1))
                    return ins
                S.add("pe", fi_fn, reads=["hn2T", "w3all"], writes=[("ps", bb)])
                if not final:
                    S.add("act", lambda e, bb=bb, slab=slab: e.copy(out=ust[:, slab * 2:slab * 2 + 2], in_=bank(bb, 2)),
                          reads=[("ps", bb)], writes=[("ust", slab)])
                    continue
                bk = ("ub", which, pc)
                S.add("act", lambda e, bb=bb, buf=buf: e.copy(out=buf[:, 2:2 + Wt], in_=bank(bb, Wt)), reads=[("ps", bb)], writes=[bk])
                S.add("pool", lambda e, buf=buf, slab=slab: e.tensor_copy(out=buf[:, 0:2], in_=ust[:, slab * 2:slab * 2 + 2]),
                      reads=[("ust", slab), bk], writes=[bk])
                ck = ("cb", which, pc)
                if which == 0:
                    S.add("act", lambda e, buf=buf, cbuf=cbuf, slab=slab: e.activation(
                        out=cbuf[:, 0:Wt], in_=buf[:, 2:2 + Wt], func=AF.Identity, scale=col(V_FCW + slab * 3 + 2), bias=col(V_FCB + slab)),
                        reads=[bk, "vec"], writes=[ck])
                    for k in range(2):
                        S.add("dve", lambda e, buf=buf, cbuf=cbuf, slab=slab, k=k: e.scalar_tensor_tensor(
                            out=cbuf[:, 0:Wt], in0=buf[:, k:k + Wt], scalar=col(V_FCW + slab * 3 + k), in1=cbuf[:, 0:Wt],
                            op0=ALU.mult, op1=ALU.add), reads=[bk, "vec", ck], writes=[ck])
                else:
                    S.add("act", lambda e, buf=buf, cbuf=cbuf, slab=slab: e.activation(
                        out=cbuf[:, 0:Wt], in_=buf[:, 2:2 + Wt], func=AF.Identity, scale=col(V_FCW + slab * 3 + 2), bias=col(V_FCB + slab)),
                        reads=[bk, "vec"], writes=[ck])
                    for k in range(2):
                        S.add("dve", lambda e, buf=buf, cbuf=cbuf, slab=slab, k=k: e.scalar_tensor_tensor(
                            out=cbuf[:, 0:Wt], in0=buf[:, k:k + Wt], scalar=col(V_FCW + slab * 3 + k), in1=cbuf[:, 0:Wt],
                            op0=ALU.mult, op1=ALU.add), reads=[bk, "vec", ck], writes=[ck])
                S.add("pool", lambda e, buf=buf, slab=slab: e.tensor_copy(out=ust[:, slab * 2:slab * 2 + 2], in_=buf[:, Wt:Wt + 2]),
                      reads=[bk], writes=[("ust", slab)])
            if final:
                S.add("act", lambda e, pc=pc: e.activation(out=sg[pc][:, 0:Wt], in_=gc[pc][:, 0:Wt], func=AF.Silu),
                      reads=[("cb", 1, pc)], writes=[("sg", pc)])
                S.add("dve", lambda e, pc=pc, j=j: e.tensor_tensor(out=act_bf[:, j * W3:j * W3 + Wt], in0=sg[pc][:, 0:Wt], in1=uc[pc][:, 0:Wt],
                                                                   op=ALU.mult),
                      reads=[("sg", pc), ("cb", 0, pc)], writes=[("actT", j)])
        if not final:
            return
        for sub in range(nsub):
            sl = slots[sub]

            def fo_fn(e, sub=sub):
                ins = None
                for hf in range(2):
                    for j in range(NPAIR):
                        ins = e.matmul(bank(B_HP[hf], 512), lhsT=act_bf[:, j * W3 + sub * 128:j * W3 + (sub + 1) * 128],
                                       rhs=wfo_bf[:, j * 1024 + hf * 512:j * 1024 + (hf + 1) * 512], start=(j == 0), stop=(j == NPAIR - 1))
                return ins
            S.add("pe", fo_fn, reads=[("actT", j) for j in range(NPAIR)] + ["w3all"], writes=[("ps", 0), ("ps", 1)])
            S.add("dve", lambda e, sl=sl: e.tensor_tensor(out=h2[sl][:, :], in0=ps[:, 0:1024], in1=h2[sl][:, :], op=ALU.add),
                  reads=[("ps", 0), ("ps", 1), ("h2", sl)], writes=[("h2", sl)])
            r0 = rel0 - 2 + sub * 128
            S.add("pool", lambda e, s, sl=sl, r0=r0: e.dma_start(out=out[r0:r0 + 128, :], in_=h2[sl][:, :]).then_inc(s, 16),
                  reads=[("h2", sl)], writes=[("h2", sl), ("out", r0)], chan=("h2", sl))

    stage3_tile(0, 2, False)
    for st in range(NST):
        stage3_tile(2 + st * W3, W3, True)

    S.add("sp", lambda e: None, reads=[("out", r0) for r0 in range(0, HALF, 128)], writes=["done"])

    S.finalize()
    chans = list(S.chan_count.keys())
    sem_ctxs = []
    sems = {}
    for e in Sched.ENG:
        c = nc.semaphore("s_" + e)
        sems[e] = c.__enter__()
        sem_ctxs.append(c)
    chan_sems = {}
    for i, ch in enumerate(chans):
        c = nc.semaphore("c_%d" % i)
        chan_sems[ch] = c.__enter__()
        sem_ctxs.append(c)
    with nc.Block() as block:
        S.emit(nc, block, sems, chan_sems)
    for c in reversed(sem_ctxs):
        c.__exit__(None, None, None)
    pctx.__exit__(None, None, None)
    ctx.__exit__(None, None, None)
    return nc


def _consts():
    c = np.zeros((128, NCST), np.float32)
    j = np.arange(128)[:, None]
    s = np.arange(128)[None, :]
    c[:, C_ID:C_ID + 128] = (j == s)
    c[:, C_NL:C_NL + 128] = -(j >= s).astype(np.float32)
    c[:, C_NU:C_NU + 128] = -(j < s).astype(np.float32)
    c[:, C_BO:C_BO + 128] = ((j // 64) == (s // 64))
    t = np.arange(512)[None, :]
    for k in range(4):
        c[:, C_MK + k * 512:C_MK + (k + 1) * 512] = ((128 * k + j) < t)
    return c


def _prep_inputs(inputs, SEQ):
    f = lambda a: np.ascontiguousarray(np.asarray(a), dtype=np.float32)
    x = f(inputs["x"])
    meta = f(inputs["meta_tokens"])
    w_in = f(inputs["w_in"])[0]
    w_out = f(inputs["w_out"])[0]
    w_fi = f(inputs["w_ffn_in"])[0]
    w_fo = f(inputs["w_ffn_out"])[0]
    g1 = f(inputs["norm1_g"])[0]
    g2 = f(inputs["norm2_g"])[0]
    qg = f(inputs["q_norm_g"])[0]
    kg = f(inputs["k_norm_g"])[0]
    cw = f(inputs["conv_w"])[0]
    cb = f(inputs["conv_b"])[0]
    wa = f(inputs["w_rg_a"])[0]
    wi = f(inputs["w_rg_i"])[0]
    ba = f(inputs["b_rg_a"])[0]
    bi = f(inputs["b_rg_i"])[0]
    lam = f(inputs["lru_lambda"])[0]
    fcw = f(inputs["ffn_conv_w"])[0]
    fcb = f(inputs["ffn_conv_b"])[0]
    consts = _consts()
    TP = SEQ + 128
    maps = []
    for core in range(8):
        b, p = core // 2, core % 2
        xpad = np.zeros((TP, D), np.float32)
        xpad[PAD:128] = meta
        xpad[128:] = x[b]
        cs = slice(256 * p, 256 * p + 256)
        wic = np.concatenate([w_in[:, 0:512][:, cs], w_in[:, 512:1024][:, cs], w_in[:, 1024:1536][:, cs],
                              w_in[:, 1536:2048][:, cs], w_in[:, 2048:2560][:, cs]], axis=1)
        vec = np.zeros((128, NV), np.float32)
        vec[:, V_G1:V_G1 + 8] = g1.reshape(8, 128).T
        vec[:, V_G2:V_G2 + 8] = g2.reshape(8, 128).T
        vec[:, V_QG] = np.tile(qg, 2)
        vec[:, V_KG] = np.tile(kg, 2)
        for c in range(2):
            ch = slice(256 * p + 128 * c, 256 * p + 128 * c + 128)
            vec[:, V_CW + c * 4:V_CW + c * 4 + 4] = cw[:, ch].T
            vec[:, V_CB + c] = cb[ch]
            vec[:, V_BA + c] = ba[ch]
            vec[:, V_BI + c] = bi[ch]
            vec[:, V_LAM + c] = lam[ch]
        for s in range(NSLAB):
            vec[:, V_FCW + s * 3:V_FCW + s * 3 + 3] = fcw[:, s * 128:(s + 1) * 128].T
            vec[:, V_FCB + s] = fcb[s * 128:(s + 1) * 128]
        wrg = np.zeros((128, 4 * 128), np.float32)
        for gi, wsrc in enumerate((wa, wi)):
            for c in range(2):
                for k in range(2):
                    blk = 4 * p + 2 * c + k
                    o = (gi * 2 + c) * 128
                    wrg[64 * k:64 * k + 64, o + 64 * k:o + 64 * k + 64] = wsrc[blk]
        perm = []
        for r in range(2):
            perm += list(range(256 * r, 256 * r + 256))
            perm += list(range(512 + 256 * r, 512 + 256 * r + 256))
        maps.append({
            "xpad": xpad, "xhalf": np.ascontiguousarray(xpad[126 + (SEQ // 2) * p:126 + (SEQ // 2) * p + SEQ // 2 + 2]), "w_in": np.ascontiguousarray(wic), "vecs": vec, "w_rg": wrg,
            "w_out": np.ascontiguousarray(w_out[perm]), "w_ffn_in": w_fi, "w_ffn_out": w_fo, "consts": consts,
        })
    return maps


_NC_CACHE = {}


def kernel(**inputs):
    x = np.asarray(inputs["x"])
    B, SEQ, _ = x.shape
    assert B == 4
    if SEQ not in _NC_CACHE:
        _NC_CACHE[SEQ] = build_nc(SEQ)
    nc = _NC_CACHE[SEQ]
    maps = _prep_inputs(inputs, SEQ)
    res = run_bass_kernel_spmd(nc, maps, core_ids=list(range(8)))
    outp = np.empty((B, SEQ, D), np.float32)
    HALF = SEQ // 2
    for core in range(8):
        b, p = core // 2, core % 2
        outp[b, p * HALF:(p + 1) * HALF] = res.results[core]["out"]
    return outp
```

```python
import numpy as np
import concourse.bass as bass
import concourse.mybir as mybir
from concourse.bass_utils import run_bass_kernel_spmd

F32 = mybir.dt.float32
BF16 = mybir.dt.bfloat16
AF = mybir.ActivationFunctionType
ALU = mybir.AluOpType

D = 1024
DC = 8
DFF = 2816
NSLAB = 44
NPAIR = 22
EPS = 1e-6
NMETA = 16
PAD = 112
GELU_C = 0.7978845608028654

V_G1, V_G2, V_QG, V_KG, V_CW, V_CB, V_BA, V_BI, V_LAM, V_FCW, V_FCB = 0, 8, 16, 17, 18, 26, 28, 30, 32, 34, 166
NV = 210
C_ID, C_NL, C_NU, C_BO, C_MK = 0, 128, 256, 384, 512
NCST = 512 + 4 * 512


class Sched:
    ENG = ("pe", "act", "dve", "pool", "sp")

    def __init__(self):
        self.ops = []
        self.last_w = {}
        self.readers = {}
        self.eng_count = {e: 0 for e in self.ENG}
        self.chan_count = {}
        self.chan_inc = {}
        self.barrier_nodes = []

    def add(self, eng, fn, reads=(), writes=(), chan=None, ndma=1, inc=16):
        deps = set(self.barrier_nodes)
        for k in reads:
            w = self.last_w.get(k)
            if w is not None:
                deps.add(w)
        for k in writes:
            w = self.last_w.get(k)
            if w is not None:
                deps.add(w)
            for r in self.readers.get(k, ()):
                deps.add(r)
        if chan is None:
            self.eng_count[eng] += 1
            node = ("E", eng, self.eng_count[eng])
        else:
            self.chan_inc[chan] = inc
            self.chan_count[chan] = self.chan_count.get(chan, 0) + ndma * inc
            node = ("C", chan, self.chan_count[chan])
        self.ops.append(dict(eng=eng, fn=fn, deps=deps, node=node, chan=chan))
        for k in reads:
            self.readers.setdefault(k, []).append(node)
        for k in writes:
            self.last_w[k] = node
            self.readers[k] = []
        return node

    def barrier(self):
        nodes = []
        for e, c in self.eng_count.items():
            if c:
                nodes.append(("E", e, c))
        for ch, c in self.chan_count.items():
            nodes.append(("C", ch, c))
        self.barrier_nodes = nodes

    def finalize(self):
        known = {e: {} for e in self.ENG}
        signal = {e: set() for e in self.ENG}
        for op in self.ops:
            need = {}
            for kind, tgt, val in op["deps"]:
                if kind == "E" and tgt == "pe" and op["eng"] == "pe" and op["chan"] is None:
                    continue
                key = (kind, tgt)
                if val > need.get(key, 0):
                    need[key] = val
            waits = []
            kn = known[op["eng"]]
            for key, val in need.items():
                if kn.get(key, 0) >= val:
                    continue
                kn[key] = val
                waits.append((key, val))
                if key[0] == "E":
                    signal[key[1]].add(val)
            op["waits"] = waits
        self.rank = {}
        for e in self.ENG:
            self.rank[e] = {v: i + 1 for i, v in enumerate(sorted(signal[e]))}

    def emit(self, nc, block, sems, chan_sems):
        streams = {e: [op for op in self.ops if op["eng"] == e] for e in self.ENG}

        def run(eng_name, eng):
            for op in streams[eng_name]:
                for (kind, tgt), val in op["waits"]:
                    if kind == "E":
                        eng.wait_ge(sems[tgt], self.rank[tgt][val])
                    else:
                        eng.wait_ge(chan_sems[tgt], val)
                if op["chan"] is not None:
                    op["fn"](eng, chan_sems[op["chan"]])
                else:
                    ins = op["fn"](eng)
                    idx = op["node"][2]
                    if idx in self.rank[eng_name]:
                        assert ins is not None
                        ins.then_inc(sems[eng_name], 1)

        @block.tensor
        def _(e):
            run("pe", e)

        @block.scalar
        def _(e):
            run("act", e)

        @block.vector
        def _(e):
            run("dve", e)

        @block.gpsimd
        def _(e):
            run("pool", e)

        @block.sync
        def _(e):
            run("sp", e)


def build_nc(SEQ):
    assert SEQ % 1024 == 0
    HALF = SEQ // 2
    TP = SEQ + 128
    NB = TP // 128
    NQT = 1 + SEQ // 512
    NST = HALF // 256
    W3 = 256

    nc = bass.Bass("TRN2", target_bir_lowering=False)
    xpad = nc.dram_tensor("xpad", [TP, D], F32, kind="ExternalInput").ap()
    w_in = nc.dram_tensor("w_in", [D, 1280], F32, kind="ExternalInput").ap()
    vecs = nc.dram_tensor("vecs", [128, NV], F32, kind="ExternalInput").ap()
    w_rg = nc.dram_tensor("w_rg", [128, 4 * 128], F32, kind="ExternalInput").ap()
    w_out = nc.dram_tensor("w_out", [D, D], F32, kind="ExternalInput").ap()
    w_fi = nc.dram_tensor("w_ffn_in", [D, 2 * DFF], F32, kind="ExternalInput").ap()
    w_fo = nc.dram_tensor("w_ffn_out", [DFF, D], F32, kind="ExternalInput").ap()
    consts = nc.dram_tensor("consts", [128, NCST], F32, kind="ExternalInput").ap()
    out = nc.dram_tensor("out", [HALF, D], F32, kind="ExternalOutput").ap()
    GW = SEQ // 4
    mo_g = [nc.dram_tensor("mixed_own_%d" % g, [512, GW], BF16) for g in range(4)]
    mo_halo_t = nc.dram_tensor("mixed_own_halo", [512, 4], BF16)
    ma_big_t = nc.dram_tensor("mixed_all_big", [4 * 1024, GW], BF16)
    ma_halo_t = nc.dram_tensor("mixed_all_halo", [1024, 4], BF16)
    xhalf = nc.dram_tensor("xhalf", [HALF + 2, D], F32, kind="ExternalInput").ap()
    mh_t = nc.dram_tensor("mixed_half", [1024, HALF + 2], BF16)
    mixed_half = mh_t.ap()
    ma_big = ma_big_t.ap()
    ma_halo = ma_halo_t.ap()
    mo_halo = mo_halo_t.ap()

    def store_fn(src, row0, ti):
        pieces = []
        if ti == 0:
            pieces.append((mo_halo[row0:row0 + 128, 0:2], 126, 2))
        else:
            i0 = 512 * (ti - 1)
            c = 0
            while c < 512:
                g = (i0 + c) // GW
                gc = (i0 + c) % GW
                n = min(512 - c, GW - gc)
                pieces.append((mo_g[g].ap()[row0:row0 + 128, gc:gc + n], c, n))
                c += n
            if i0 <= HALF - 2 < i0 + 512:
                pieces.append((mo_halo[row0:row0 + 128, 2:4], HALF - 2 - i0, 2))

        def fn(e, s):
            ins = None
            for dst, c0, n in pieces:
                ins = e.dma_start(out=dst, in_=src[:, c0:c0 + n]).then_inc(s, 16)
            return ins
        return fn, len(pieces)

    S = Sched()
    ARENA_F = 53200

    ctx = nc.sbuf_tensor("arena", [128, ARENA_F], F32)
    arena = ctx.__enter__()
    pctx = nc.psum_tensor("ps", [128, 8 * 512], F32)
    ps = pctx.__enter__()

    class Arena:
        def __init__(self):
            self.off = 0

        def f32(self, n):
            o = self.off
            self.off += n
            assert self.off <= ARENA_F, self.off
            return arena[:, o:o + n]

        def bf(self, n):
            nf = (n + 1) // 2
            o = self.off
            self.off += nf
            assert self.off <= ARENA_F, self.off
            return arena[:, o:o + nf].bitcast(BF16)

    A = Arena()

    def bank(b, n=512):
        return ps[:, b * 512:b * 512 + n]

    def bank_bf(b):
        return ps[:, b * 512:(b + 1) * 512].bitcast(BF16)

    cst = A.bf(NCST)
    vec = A.f32(NV)
    ext = A.f32(16)
    ident = cst[:, C_ID:C_ID + 128]
    negL = cst[:, C_NL:C_NL + 128]
    negU = cst[:, C_NU:C_NU + 128]
    bones = cst[:, C_BO:C_BO + 128]

    def maskv(j, W):
        return cst[:, C_MK + j * 512:C_MK + j * 512 + W]

    mark_stage = A.off

    qT = A.bf(2 * TP)
    kT = A.bf(2 * TP)
    v_sb = A.bf(NB * 256)
    mark12 = A.off
    win_bf = A.bf(8 * 1280)
    wrg_bf = A.bf(4 * 128)
    stg = [A.f32(1280), A.f32(1280)]
    xs = [A.f32(1024), A.f32(1024)]
    sqj = A.bf(1024)
    hn_bf = [A.bf(1024), A.bf(1024)]
    ssb = A.f32(8)
    hnT = A.bf(8 * 512)
    sqb = A.bf(512)
    rqb = A.f32(512)
    xr_sb = [A.f32(3 + 512), A.f32(3 + 512)]
    xc_sb = A.f32(512)
    xc_bf = A.bf(512)
    tr_sb = A.f32(512)
    ti_sb = A.f32(512)
    la_sb = A.f32(512)
    a_sb = A.f32(512)
    th_sb = A.f32(512)
    m2_sb = A.f32(512)
    bt_sb = A.f32(512)
    hl_sb = [A.f32(512), A.f32(512)]
    hst = A.f32(2)
    y_sb = A.f32(512)
    y2_sb = A.f32(512)
    tg_sb = A.f32(512)
    ol_bf = [A.bf(512), A.bf(512)]

    def col(i, n=1):
        return vec[:, i:i + n]

    for hh in range(2):
        S.add("sp", lambda e, s, hh=hh: e.dma_start(out=stg[hh][:, 0:1280], in_=consts[:, hh * 1280:(hh + 1) * 1280]).then_inc(s, 16),
              writes=[("stg", hh)], chan=("stg", hh))
        S.add("dve", lambda e, hh=hh: e.tensor_copy(out=cst[:, hh * 1280:(hh + 1) * 1280], in_=stg[hh][:, 0:1280]),
              reads=[("stg", hh)], writes=["cst"])
    S.add("sp", lambda e, s: e.dma_start(out=vec[:, :], in_=vecs[:, :]).then_inc(s, 16), writes=["vec"], chan="vec")
    S.add("act", lambda e: e.activation(out=ext[:, 7:9], in_=col(V_LAM, 2), func=AF.Exp, scale=-1.0),
          reads=["vec"], writes=["ext_t"])
    S.add("act", lambda e: e.activation(out=ext[:, 7:9], in_=ext[:, 7:9], func=AF.Ln, bias=1.0),
          reads=["ext_t"], writes=["ext_t"])
    S.add("dve", lambda e: e.tensor_scalar(out=ext[:, 0:2], in0=ext[:, 7:9], scalar1=-8.0, scalar2=None, op0=ALU.mult),
          reads=["ext_t"], writes=["ext"])
    S.add("dve", lambda e: e.tensor_scalar(out=ext[:, 2:6], in0=col(V_BA, 4), scalar1=-1.0, scalar2=None, op0=ALU.mult),
          reads=["vec", "ext"], writes=["ext"])
    S.add("dve", lambda e: e.tensor_scalar(out=ext[:, 6:7], in0=col(V_KG), scalar1=0.125, scalar2=None, op0=ALU.mult),
          reads=["vec", "ext"], writes=["ext"])
    S.add("sp", lambda e, s: e.dma_start(out=stg[0][:, 0:512], in_=w_rg[:, :]).then_inc(s, 16),
          writes=[("stg", 0)], chan=("stg", 0))
    S.add("dve", lambda e: e.tensor_copy(out=wrg_bf[:, :], in_=stg[0][:, 0:512]), reads=[("stg", 0)], writes=["wrg"])
    for c in range(DC):
        hh = c % 2
        S.add("sp", lambda e, s, c=c, hh=hh: e.dma_start(out=stg[hh][:, :], in_=w_in[c * 128:(c + 1) * 128, :]).then_inc(s, 16),
              writes=[("stg", hh)], chan=("stg", hh))
        if c % 2 == 0:
            S.add("dve", lambda e, c=c, hh=hh: e.tensor_scalar(out=win_bf[:, c * 1280:(c + 1) * 1280], in0=stg[hh][:, :],
                                                                scalar1=col(V_G1 + c), scalar2=None, op0=ALU.mult),
                  reads=[("stg", hh), "vec"], writes=[("win", c)])
        else:
            S.add("act", lambda e, c=c, hh=hh: e.activation(out=win_bf[:, c * 1280:(c + 1) * 1280], in_=stg[hh][:, :],
                                                             func=AF.Copy, scale=col(V_G1 + c)),
                  reads=[("stg", hh), "vec"], writes=[("win", c)])
    S.add("pool", lambda e: e.memset(xr_sb[0][:, 0:3], 0.0), writes=[("xr", 0)])
    S.add("pool", lambda e: e.memset(xr_sb[1][:, 0:3], 0.0), writes=[("xr", 1)])
    S.add("pool", lambda e: e.memset(hst[:, :], 0.0), writes=["hst"])

    win_reads = [("win", c) for c in range(DC)]

    B_TRP, B_V, B_PJ0, B_PJ1, B_PS2, B_GR, B_GI = 0, 1, 2, 3, 4, 5, 6

    def tile_info(ti):
        if ti == 0:
            return 0, 128
        return 128 + 512 * (ti - 1), 512

    pj_ctr = [0]

    def proj_slab(col0, W):
        b = B_PJ0 + (pj_ctr[0] % 2)
        pj_ctr[0] += 1

        def fn(e, b=b, col0=col0, W=W):
            ins = None
            for c in range(DC):
                ins = e.matmul(bank(b, W), lhsT=win_bf[:, c * 1280 + col0:c * 1280 + col0 + 128],
                               rhs=hnT[:, c * 512:c * 512 + W], start=(c == 0), stop=(c == DC - 1))
            return ins
        S.add("pe", fn, reads=win_reads + ["hnT"], writes=[("ps", b)])
        return b

    for ti in range(NQT):
        pos0, W = tile_info(ti)
        nsub = W // 128
        for sub in range(nsub):
            blk = pos0 // 128 + sub
            sl = blk % 2
            S.add("sp", lambda e, s, blk=blk, sl=sl: e.dma_start(out=xs[sl][:, :], in_=xpad[blk * 128:(blk + 1) * 128, :]).then_inc(s, 16),
                  writes=[("xs", sl)], chan=("xs", sl))
            S.add("pool", lambda e, sl=sl: e.memset(ssb[:, sl:sl + 1], 0.0), writes=[("ss", sl)])
            S.add("act", lambda e, sl=sl: e.activation(out=sqj[:, :], in_=xs[sl][:, :], func=AF.Square, accum_out=ssb[:, sl:sl + 1]),
                  reads=[("xs", sl)], writes=[("ss", sl), "sqj"])
            S.add("dve", lambda e, sl=sl: e.tensor_scalar(out=ssb[:, 2 + sl:3 + sl], in0=ssb[:, sl:sl + 1], scalar1=1.0 / D, scalar2=EPS,
                                                           op0=ALU.mult, op1=ALU.add),
                  reads=[("ss", sl)], writes=[("rstd", sl)])
            S.add("act", lambda e, sl=sl: e.activation(out=ssb[:, 2 + sl:3 + sl], in_=ssb[:, 2 + sl:3 + sl], func=AF.Ln),
                  reads=[("rstd", sl)], writes=[("rstd", sl)])
            S.add("act", lambda e, sl=sl: e.activation(out=ssb[:, 2 + sl:3 + sl], in_=ssb[:, 2 + sl:3 + sl], func=AF.Exp, scale=-0.5),
                  reads=[("rstd", sl)], writes=[("rstd", sl)])
            S.add("act", lambda e, sl=sl: e.activation(out=hn_bf[sl][:, :], in_=xs[sl][:, :], func=AF.Copy, scale=ssb[:, 2 + sl:3 + sl]),
                  reads=[("xs", sl), ("rstd", sl)], writes=[("hn", sl)])

            def tr_fn(e, sl=sl):
                ins = None
                for c in range(DC):
                    ins = e.transpose(bank_bf(B_TRP)[:, c * 128:(c + 1) * 128], hn_bf[sl][:, c * 128:(c + 1) * 128], ident)
                return ins
            S.add("pe", tr_fn, reads=[("hn", sl), "cst"], writes=[("ps", B_TRP)])
            S.add("dve", lambda e, sub=sub: e.tensor_copy(
                out=hnT.rearrange("p (c w) -> p c w", c=8)[:, :, sub * 128:(sub + 1) * 128],
                in_=bank_bf(B_TRP).rearrange("p (c w) -> p c w", c=8)),
                reads=[("ps", B_TRP)], writes=["hnT"])

            def v_fn(e, sub=sub):
                ins = None
                for c in range(DC):
                    ins = e.matmul(bank(B_V, 256), lhsT=hnT[:, c * 512 + sub * 128:c * 512 + (sub + 1) * 128],
                                   rhs=win_bf[:, c * 1280 + 512:c * 1280 + 768], start=(c == 0), stop=(c == DC - 1))
                return ins
            S.add("pe", v_fn, reads=win_reads + ["hnT"], writes=[("ps", B_V)])
            S.add("act", lambda e, blk=blk: e.copy(out=v_sb[:, blk * 256:(blk + 1) * 256], in_=bank(B_V, 256)),
                  reads=[("ps", B_V)], writes=[("v", blk)])

        for which in range(2):
            for j in range(2):
                b = proj_slab(which * 256 + j * 128, W)
                S.add("act", lambda e, b=b, W=W: e.activation(out=sqb[:, 0:W], in_=bank(b, W), func=AF.Square),
                      reads=[("ps", b)], writes=["sqb"])
                S.add("pe", lambda e, W=W: e.matmul(bank(B_PS2, W), lhsT=bones, rhs=sqb[:, 0:W], start=True, stop=True),
                      reads=["sqb", "cst"], writes=[("ps", B_PS2)])
                S.add("dve", lambda e, W=W: e.tensor_scalar(out=rqb[:, 0:W], in0=bank(B_PS2, W), scalar1=1.0 / 64, scalar2=EPS,
                                                             op0=ALU.mult, op1=ALU.add),
                      reads=[("ps", B_PS2)], writes=["rqb"])
                S.add("act", lambda e, W=W: e.activation(out=rqb[:, 0:W], in_=rqb[:, 0:W], func=AF.Ln), reads=["rqb"], writes=["rqb"])
                S.add("act", lambda e, W=W: e.activation(out=rqb[:, 0:W], in_=rqb[:, 0:W], func=AF.Exp, scale=-0.5),
                      reads=["rqb"], writes=["rqb"])
                dst = qT if which == 0 else kT
                gsc = col(V_QG) if which == 0 else ext[:, 6:7]
                S.add("dve", lambda e, b=b, W=W, dst=dst, gsc=gsc, j=j, pos0=pos0: e.scalar_tensor_tensor(
                    out=dst[:, j * TP + pos0:j * TP + pos0 + W], in0=bank(b, W), scalar=gsc, in1=rqb[:, 0:W],
                    op0=ALU.mult, op1=ALU.mult),
                    reads=[("ps", b), "rqb", "vec", "ext"], writes=[("qk", which, j, ti)])
        for c in range(2):
            b = proj_slab(768 + c * 128, W)
            S.add("act", lambda e, b=b, W=W, c=c: e.copy(out=xr_sb[c][:, 3:3 + W], in_=bank(b, W)),
                  reads=[("ps", b)], writes=[("xr", c)])
            S.add("dve", lambda e, c=c, W=W: e.tensor_scalar(out=xc_sb[:, 0:W], in0=xr_sb[c][:, 3:3 + W], scalar1=col(V_CW + c * 4 + 3),
                                                               scalar2=col(V_CB + c), op0=ALU.mult, op1=ALU.add),
                  reads=[("xr", c), "vec"], writes=["xc"])
            for k in range(3):
                S.add("dve", lambda e, c=c, W=W, k=k: e.scalar_tensor_tensor(out=xc_sb[:, 0:W], in0=xr_sb[c][:, k:k + W],
                                                                               scalar=col(V_CW + c * 4 + k), in1=xc_sb[:, 0:W],
                                                                               op0=ALU.mult, op1=ALU.add),
                      reads=[("xr", c), "vec", "xc"], writes=["xc"])
            S.add("pool", lambda e, c=c, W=W: e.tensor_copy(out=xr_sb[c][:, 0:3], in_=xr_sb[c][:, W:W + 3]),
                  reads=[("xr", c)], writes=[("xr", c)])
            S.add("act", lambda e, W=W: e.copy(out=xc_bf[:, 0:W], in_=xc_sb[:, 0:W]), reads=["xc"], writes=["xcb"])
            S.add("pe", lambda e, c=c, W=W: e.matmul(bank(B_GR, W), lhsT=wrg_bf[:, (0 * 2 + c) * 128:(0 * 2 + c + 1) * 128],
                                                      rhs=xc_bf[:, 0:W], start=True, stop=True),
                  reads=["xcb", "wrg"], writes=[("ps", B_GR)])
            S.add("pe", lambda e, c=c, W=W: e.matmul(bank(B_GI, W), lhsT=wrg_bf[:, (1 * 2 + c) * 128:(1 * 2 + c + 1) * 128],
                                                      rhs=xc_bf[:, 0:W], start=True, stop=True),
                  reads=["xcb", "wrg"], writes=[("ps", B_GI)])
            S.add("act", lambda e, c=c, W=W: e.activation(out=tr_sb[:, 0:W], in_=bank(B_GR, W), func=AF.Exp,
                                                           bias=ext[:, 2 + c:3 + c], scale=-1.0),
                  reads=[("ps", B_GR), "ext"], writes=["tr"])
            S.add("act", lambda e, c=c, W=W: e.activation(out=ti_sb[:, 0:W], in_=bank(B_GI, W), func=AF.Exp,
                                                           bias=ext[:, 4 + c:5 + c], scale=-1.0),
                  reads=[("ps", B_GI), "ext"], writes=["tig"])
            S.add("act", lambda e, W=W: e.activation(out=tr_sb[:, 0:W], in_=tr_sb[:, 0:W], func=AF.Ln, bias=1.0), reads=["tr"], writes=["tr"])
            S.add("act", lambda e, W=W: e.activation(out=tr_sb[:, 0:W], in_=tr_sb[:, 0:W], func=AF.Exp, scale=-1.0), reads=["tr"], writes=["tr"])
            S.add("act", lambda e, W=W: e.activation(out=ti_sb[:, 0:W], in_=ti_sb[:, 0:W], func=AF.Ln, bias=1.0), reads=["tig"], writes=["tig"])
            S.add("act", lambda e, W=W: e.activation(out=ti_sb[:, 0:W], in_=ti_sb[:, 0:W], func=AF.Exp, scale=-1.0), reads=["tig"], writes=["tig"])
            S.add("act", lambda e, c=c, W=W: e.activation(out=a_sb[:, 0:W], in_=tr_sb[:, 0:W], func=AF.Exp, scale=ext[:, c:c + 1]),
                  reads=["tr", "ext"], writes=["a"])
            S.add("dve", lambda e, W=W: e.tensor_tensor(out=m2_sb[:, 0:W], in0=a_sb[:, 0:W], in1=a_sb[:, 0:W], op=ALU.mult),
                  reads=["a"], writes=["m2"])
            S.add("dve", lambda e, W=W: e.tensor_scalar(out=m2_sb[:, 0:W], in0=m2_sb[:, 0:W], scalar1=-1.0, scalar2=1.0,
                                                         op0=ALU.mult, op1=ALU.add),
                  reads=["m2"], writes=["m2"])
            S.add("act", lambda e, W=W: e.activation(out=m2_sb[:, 0:W], in_=m2_sb[:, 0:W], func=AF.Ln), reads=["m2"], writes=["m2"])
            S.add("act", lambda e, W=W: e.activation(out=m2_sb[:, 0:W], in_=m2_sb[:, 0:W], func=AF.Exp, scale=0.5), reads=["m2"], writes=["m2"])
            S.add("dve", lambda e, W=W: e.tensor_tensor(out=bt_sb[:, 0:W], in0=ti_sb[:, 0:W], in1=xc_sb[:, 0:W], op=ALU.mult),
                  reads=["tig", "xc"], writes=["bt"])
            S.add("dve", lambda e, W=W: e.tensor_tensor(out=bt_sb[:, 0:W], in0=bt_sb[:, 0:W], in1=m2_sb[:, 0:W], op=ALU.mult),
                  reads=["bt", "m2"], writes=["bt"])
            if ti == 0:
                S.add("dve", lambda e: e.memset(bt_sb[:, 0:PAD], 0.0), reads=["bt"], writes=["bt"])
            S.add("dve", lambda e, c=c, W=W: e.tensor_tensor_scan(out=hl_sb[c][:, 0:W], data0=a_sb[:, 0:W], data1=bt_sb[:, 0:W],
                                                                   initial=hst[:, c:c + 1], op0=ALU.mult, op1=ALU.add),
                  reads=["a", "bt", "hst"], writes=[("hl", c)])
            S.add("dve", lambda e, c=c, W=W: e.tensor_copy(out=hst[:, c:c + 1], in_=hl_sb[c][:, W - 1:W]),
                  reads=[("hl", c)], writes=["hst"])
        for c in range(2):
            b = proj_slab(1024 + c * 128, W)
            S.add("act", lambda e, b=b, W=W: e.copy(out=y_sb[:, 0:W], in_=bank(b, W)), reads=[("ps", b)], writes=["y"])
            S.add("act", lambda e, b=b, W=W: e.activation(out=y2_sb[:, 0:W], in_=bank(b, W), func=AF.Square),
                  reads=[("ps", b)], writes=["y2"])
            S.add("dve", lambda e, W=W: e.tensor_scalar(out=y2_sb[:, 0:W], in0=y2_sb[:, 0:W], scalar1=0.044715, scalar2=1.0,
                                                          op0=ALU.mult, op1=ALU.add),
                  reads=["y2"], writes=["y2"])
            S.add("dve", lambda e, W=W: e.tensor_tensor(out=y2_sb[:, 0:W], in0=y2_sb[:, 0:W], in1=y_sb[:, 0:W], op=ALU.mult),
                  reads=["y2", "y"], writes=["y2"])
            S.add("act", lambda e, W=W: e.activation(out=tg_sb[:, 0:W], in_=y2_sb[:, 0:W], func=AF.Exp, scale=-2.0 * GELU_C),
                  reads=["y2"], writes=["tg"])
            S.add("act", lambda e, W=W: e.activation(out=tg_sb[:, 0:W], in_=tg_sb[:, 0:W], func=AF.Ln, bias=1.0), reads=["tg"], writes=["tg"])
            S.add("act", lambda e, W=W: e.activation(out=tg_sb[:, 0:W], in_=tg_sb[:, 0:W], func=AF.Exp, scale=-1.0), reads=["tg"], writes=["tg"])
            S.add("dve", lambda e, W=W: e.tensor_tensor(out=tg_sb[:, 0:W], in0=tg_sb[:, 0:W], in1=y_sb[:, 0:W], op=ALU.mult),
                  reads=["tg", "y"], writes=["tg"])
            S.add("dve", lambda e, c=c, W=W: e.tensor_tensor(out=ol_bf[c][:, 0:W], in0=tg_sb[:, 0:W], in1=hl_sb[c][:, 0:W], op=ALU.mult),
                  reads=["tg", ("hl", c)], writes=[("ol", c)])
            sfn, nd = store_fn(ol_bf[c], 256 + c * 128, ti)
            S.add("pool", sfn, reads=[("ol", c)], writes=[("mo", "l", c, ti)], chan=("ol", c), ndma=nd)

    S.barrier()
    A.off = mark12
    e_sb = [A.f32(1024) for _ in range(3)]
    sp_sb = [A.bf(1024) for _ in range(3)]
    g_sb = [A.bf(1024) for _ in range(2)]
    w_sb = [A.bf(1024) for _ in range(3)]
    osb = [A.bf(2 * 512), A.bf(2 * 512)]
    items = []
    for ti in range(NQT):
        pos0, W = tile_info(ti)
        b0 = pos0 // 128
        nsub = W // 128
        kbs = list(range(b0 + nsub - 1, -1, -1))
        for n, kb in enumerate(kbs):
            for j in range(2):
                items.append(dict(ti=ti, pos0=pos0, W=W, b0=b0, kb=kb, j=j, first=(n == 0), last=(n == len(kbs) - 1)))
    NI = len(items)

    S.add("dve", lambda e: e.memset(ext[:, 9:10], 0.0),
          reads=[("qk", w, j, t) for w in range(2) for j in range(2) for t in range(NQT)] + [("v", bl) for bl in range(NB)],
          writes=["kall"])

    def v3(buf, c0, W):
        return buf.rearrange("p (h w) -> p h w", h=2)[:, :, c0:W]

    def zview(c0, W):
        return ps[:, 0:1024].rearrange("p (h w) -> p h w", h=2)[:, :, c0:W]

    def pview(j, c0, W):
        return ps[:, (2 + 2 * j) * 512:(4 + 2 * j) * 512].rearrange("p (h w) -> p h w", h=2)[:, :, c0:W]

    RG = [[0, 1], [2, 3], [4, 5], [6, 7]]

    def tile_keys(ti):
        return [("mo", "l", c, ti) for c in range(2)] + [("mo", "a", j, ti) for j in range(2)]

    gathered = set()

    def maybe_gather(ti):
        if ti == 0:
            return
        done_tok = 512 * ti
        for g in range(4):
            if g in gathered or (g + 1) * GW > done_tok:
                continue
            gathered.add(g)
            t_lo = 1 + (g * GW) // 512
            t_hi = 1 + ((g + 1) * GW - 1) // 512
            keys = []
            for t in range(t_lo, t_hi + 1):
                keys += tile_keys(t)
            S.add("pool", lambda e, s, g=g: e.collective_compute(
                "AllGather", ALU.bypass, replica_groups=RG, ins=[mo_g[g].ap().opt()],
                outs=[ma_big[g * 1024:(g + 1) * 1024, :].opt()]).then_inc(s),
                reads=keys, writes=[("mall", g)], chan=("cc", g), inc=1)

    def c0_of(it):
        return 128 * (it["kb"] - it["b0"]) if it["kb"] >= it["b0"] else 0

    def PE1(i):
        it = items[i]
        kb, W, pos0, j = it["kb"], it["W"], it["pos0"], it["j"]
        c0 = c0_of(it)

        def fn(e):
            ins = None
            for hh in range(2):
                r = hh * 64
                ins = e.matmul(ps[:, hh * 512 + c0:hh * 512 + W], lhsT=kT[r:r + 64, j * TP + kb * 128:j * TP + (kb + 1) * 128],
                               rhs=qT[r:r + 64, j * TP + pos0 + c0:j * TP + pos0 + W], start=True, stop=True)
            return ins
        S.add("pe", fn, reads=["kall"], writes=[("ps", 0), ("ps", 1)])

    def ACT12(i):
        it = items[i]
        W = it["W"]
        c0 = c0_of(it)
        eb = e_sb[i % 3]
        sb = sp_sb[i % 3]
        S.add("act", lambda e: e.activation(out=v3(eb, c0, W), in_=zview(c0, W), func=AF.Exp),
              reads=[("ps", 0), ("ps", 1)], writes=[("e", i % 3)])
        S.add("act", lambda e: e.activation(out=v3(sb, c0, W), in_=v3(eb, c0, W), func=AF.Ln, bias=1.0),
              reads=[("e", i % 3)], writes=[("sp", i % 3)])
        if it["kb"] >= it["b0"]:
            for hh in range(2):
                S.add("dve", lambda e, hh=hh: e.tensor_tensor(out=sb[:, hh * 512 + c0:hh * 512 + c0 + 128],
                                                               in0=sb[:, hh * 512 + c0:hh * 512 + c0 + 128], in1=maskv(0, 128), op=ALU.mult),
                      reads=[("sp", i % 3), "cst"], writes=[("sp", i % 3)])

    def PE2(i):
        it = items[i]
        W, j = it["W"], it["j"]
        c0 = c0_of(it)
        sb = sp_sb[i % 3]

        def fn(e):
            ins = None
            for hh in range(2):
                b_ = 2 + 2 * j + hh
                ins = e.matmul(ps[:, b_ * 512 + c0:b_ * 512 + W], lhsT=negL, rhs=sb[:, hh * 512 + c0:hh * 512 + W],
                               start=it["first"], stop=False, skip_group_check=True)
            return ins
        S.add("pe", fn, reads=[("sp", i % 3), "cst"], writes=[("ps", 2 + 2 * j), ("ps", 3 + 2 * j)])

    def ACT3(i):
        it = items[i]
        W, j = it["W"], it["j"]
        c0 = c0_of(it)
        gb = g_sb[i % 2]
        eb = e_sb[i % 3]
        wb = w_sb[i % 3]
        S.add("act", lambda e: e.activation(out=v3(gb, c0, W), in_=pview(j, c0, W), func=AF.Exp),
              reads=[("ps", 2 + 2 * j), ("ps", 3 + 2 * j)], writes=[("g", i % 2)])
        S.add("dve", lambda e: e.tensor_tensor(out=v3(wb, c0, W), in0=v3(eb, c0, W), in1=v3(gb, c0, W), op=ALU.mult),
              reads=[("e", i % 3), ("g", i % 2)], writes=[("w", i % 3)])
        if it["kb"] >= it["b0"]:
            for hh in range(2):
                S.add("dve", lambda e, hh=hh: e.tensor_tensor(out=wb[:, hh * 512 + c0:hh * 512 + c0 + 128],
                                                               in0=wb[:, hh * 512 + c0:hh * 512 + c0 + 128], in1=maskv(0, 128), op=ALU.mult),
                      reads=[("w", i % 3), "cst"], writes=[("w", i % 3)])

    def PE4(i):
        it = items[i]
        if it["last"]:
            return
        W, j = it["W"], it["j"]
        c0 = c0_of(it)
        sb = sp_sb[i % 3]

        def fn(e):
            ins = None
            for hh in range(2):
                b_ = 2 + 2 * j + hh
                ins = e.matmul(ps[:, b_ * 512 + c0:b_ * 512 + W], lhsT=negU, rhs=sb[:, hh * 512 + c0:hh * 512 + W],
                               start=False, stop=False, skip_group_check=True)
            return ins
        S.add("pe", fn, reads=[("sp", i % 3), "cst"], writes=[("ps", 2 + 2 * j), ("ps", 3 + 2 * j)])

    def PE3(i):
        it = items[i]
        kb, W, pos0, ti, j = it["kb"], it["W"], it["pos0"], it["ti"], it["j"]
        c0 = c0_of(it)
        ob = 6 + j
        wb = w_sb[i % 3]
        par = ti % 2

        def fn(e):
            ins = None
            for hh in range(2):
                h = 2 * j + hh
                ins = e.matmul(ps[hh * 64:(hh + 1) * 64, ob * 512 + c0:ob * 512 + W], lhsT=v_sb[:, kb * 256 + h * 64:kb * 256 + (h + 1) * 64],
                               rhs=wb[:, hh * 512 + c0:hh * 512 + W], start=it["first"], stop=it["last"], skip_group_check=True)
            return ins
        S.add("pe", fn, reads=[("w", i % 3), "kall"], writes=[("ps", ob)])
        if it["last"]:
            S.add("dve", lambda e: e.tensor_copy(out=osb[par][:, j * 512:j * 512 + W], in_=bank(ob, W)),
                  reads=[("ps", ob)], writes=[("osb", par, j)])
            sfn, nd = store_fn(osb[par][:, j * 512:(j + 1) * 512], j * 128, ti)
            S.add("pool", sfn, reads=[("osb", par, j)], writes=[("mo", "a", j, ti)], chan=("osb", par, j), ndma=nd)
            if j == 1:
                maybe_gather(ti)

    PE1(0)
    for s in range(-1, NI + 1):
        if 0 <= s + 1 < NI:
            ACT12(s + 1)
        if s + 2 < NI:
            PE1(s + 2)
        if 0 <= s + 1 < NI:
            PE2(s + 1)
        if 0 <= s < NI:
            ACT3(s)
            PE4(s)
        if 0 <= s - 1 < NI:
            PE3(s - 1)

    S.add("pool", lambda e, s: e.collective_compute("AllGather", ALU.bypass, replica_groups=RG,
                                                    ins=[mo_halo_t.ap().opt()], outs=[ma_halo_t.ap().opt()]).then_inc(s),
          reads=tile_keys(0) + tile_keys(1 + (HALF - 2) // 512), writes=[("mall", "h")], chan="cch", inc=1)

    def mh_fn(e, s):
        half = e.partition_id() % 2
        ins = None
        for j in range(2):
            for r in range(2):
                ins = e.dma_start(out=mixed_half[r * 512:(r + 1) * 512, 2 + j * GW:2 + (j + 1) * GW],
                                  in_=ma_big[bass.ds(half * 2048 + j * 1024 + r * 512, 512), :]).then_inc(s, 16)
        ins = e.dma_start(out=mixed_half[:, 0:2], in_=ma_halo[:, bass.ds(half * 2, 2)]).then_inc(s, 16)
        return ins
    S.add("sp", mh_fn, reads=[("mall", g) for g in range(4)] + [("mall", "h")], writes=["mhalf"], chan="mh", ndma=5)

    S.barrier()
    A.off = mark_stage
    wout_bf = A.bf(8 * 1024)
    wfi_bf = A.bf(8 * 2 * DFF)
    wfo_bf = A.bf(NPAIR * 1024)
    mark3 = A.off
    stg3 = [A.f32(2816), A.f32(2816)]

    stg_ctr = [0]

    def load_cast(dst_ap, src_ap, ncols, key, scale_col=None):
        sl = stg_ctr[0] % 2
        stg_ctr[0] += 1
        S.add("sp", lambda e, s: e.dma_start(out=stg3[sl][:, 0:ncols], in_=src_ap).then_inc(s, 16),
              writes=[("stg3", sl)], chan=("stg3", sl))
        eng = ("dve", "act")[stg_ctr[0] % 2]
        if scale_col is None:
            if eng == "act":
                S.add(eng, lambda e: e.copy(out=dst_ap, in_=stg3[sl][:, 0:ncols]), reads=[("stg3", sl)], writes=[key])
            else:
                S.add(eng, lambda e: e.tensor_copy(out=dst_ap, in_=stg3[sl][:, 0:ncols]), reads=[("stg3", sl)], writes=[key])
        else:
            if eng == "act":
                S.add(eng, lambda e: e.activation(out=dst_ap, in_=stg3[sl][:, 0:ncols], func=AF.Copy, scale=scale_col),
                      reads=[("stg3", sl), "vec"], writes=[key])
            else:
                S.add(eng, lambda e: e.tensor_scalar(out=dst_ap, in0=stg3[sl][:, 0:ncols], scalar1=scale_col, scalar2=None, op0=ALU.mult),
                      reads=[("stg3", sl), "vec"], writes=[key])

    w3_keys = []
    for c in range(DC):
        load_cast(wout_bf[:, c * 1024:(c + 1) * 1024], w_out[c * 128:(c + 1) * 128, :], 1024, ("wout", c))
        w3_keys.append(("wout", c))
    for c in range(DC):
        for hh in range(2):
            load_cast(wfi_bf[:, c * 2 * DFF + hh * DFF:c * 2 * DFF + (hh + 1) * DFF], w_fi[c * 128:(c + 1) * 128, hh * DFF:(hh + 1) * DFF],
                      DFF, ("wfi", c, hh), scale_col=col(V_G2 + c))
            w3_keys.append(("wfi", c, hh))
    for j in range(0, NPAIR, 2):
        for jj in range(2):
            load_cast(wfo_bf[:, (j + jj) * 1024:(j + jj + 1) * 1024], w_fo[(j + jj) * 128:(j + jj + 1) * 128, :], 1024, ("wfo", j + jj))
            w3_keys.append(("wfo", j + jj))
    S.add("dve", lambda e: e.memset(ext[:, 10:11], 0.0), reads=w3_keys, writes=["w3all"])
    S.barrier()
    A.off = mark3
    h2 = [A.f32(1024) for _ in range(4)]
    hn2_bf = [A.bf(1024), A.bf(1024)]
    ss3 = A.f32(8)
    hn2T = [A.bf(8 * W3), A.bf(8 * W3)]
    mt = A.bf(8 * W3)
    ust = A.f32(NSLAB * 2)
    ubuf = [A.f32(2 + W3), A.f32(2 + W3)]
    gbuf = [A.f32(2 + W3), A.f32(2 + W3)]
    uc = [A.f32(W3), A.f32(W3)]
    gc = [A.f32(W3), A.f32(W3)]
    sg = [A.f32(W3), A.f32(W3)]
    act_bf = A.bf(NPAIR * W3)

    B_FO = [0, 1]
    B_T3 = 2
    B_U = [3, 4]
    B_G = [5, 6]
    B_WO = 7
    mixed_half_r = mixed_half.rearrange("(c p) w -> p c w", p=128)

    def tile_geom(t):
        if t < 0:
            return 0, 2, 2, 1, 1
        return 2 + t * W3, W3, 128, W3 // 128, t % 2

    def prep1(t):
        rel0, Wt, n, nsub, par = tile_geom(t)
        S.add("sp", lambda e, s: e.dma_start(out=mt.rearrange("p (c w) -> p c w", c=8)[:, :, 0:Wt],
                                              in_=mixed_half_r[:, :, rel0:rel0 + Wt]).then_inc(s, 16),
              reads=["mhalf"], writes=["mt"], chan="mt")
        for sub in range(nsub):
            sl = 2 * par + sub
            q = sub
            r = rel0 + sub * 128
            S.add("sp", lambda e, s, sl=sl, r=r: e.dma_start(out=h2[sl][0:n, :], in_=xhalf[r:r + n, :]).then_inc(s, 16),
                  writes=[("h2", sl)], chan=("h2", sl))
            for hf in range(2):
                def wo_fn(e, sub=sub, hf=hf):
                    ins = None
                    for c in range(DC):
                        ins = e.matmul(ps[0:n, B_WO * 512:(B_WO + 1) * 512], lhsT=mt[:, c * W3 + sub * 128:c * W3 + sub * 128 + n],
                                       rhs=wout_bf[:, c * 1024 + hf * 512:c * 1024 + (hf + 1) * 512], start=(c == 0), stop=(c == DC - 1))
                    return ins
                S.add("pe", wo_fn, reads=["mt", "w3all"], writes=[("ps", B_WO)])
                S.add("dve", lambda e, sl=sl, hf=hf: e.tensor_tensor(out=h2[sl][0:n, hf * 512:(hf + 1) * 512],
                                                                     in0=ps[0:n, B_WO * 512:(B_WO + 1) * 512],
                                                                     in1=h2[sl][0:n, hf * 512:(hf + 1) * 512], op=ALU.add),
                      reads=[("ps", B_WO), ("h2", sl)], writes=[("h2", sl)])
            S.add("pool", lambda e, q=q: e.memset(ss3[:, q:q + 1], 0.0), writes=[("ss3", q)])
            S.add("act", lambda e, sl=sl, q=q: e.activation(out=hn2_bf[q][0:n, :], in_=h2[sl][0:n, :], func=AF.Square,
                                                            accum_out=ss3[0:n, q:q + 1]),
                  reads=[("h2", sl)], writes=[("ss3", q), ("hn2", q)])
            S.add("dve", lambda e, q=q: e.tensor_scalar(out=ss3[0:n, 2 + q:3 + q], in0=ss3[0:n, q:q + 1], scalar1=1.0 / D, scalar2=EPS,
                                                         op0=ALU.mult, op1=ALU.add), reads=[("ss3", q)], writes=[("rs3", q)])
            S.add("act", lambda e, q=q: e.activation(out=ss3[0:n, 2 + q:3 + q], in_=ss3[0:n, 2 + q:3 + q], func=AF.Ln),
                  reads=[("rs3", q)], writes=[("rs3", q)])
            S.add("act", lambda e, q=q: e.activation(out=ss3[0:n, 2 + q:3 + q], in_=ss3[0:n, 2 + q:3 + q], func=AF.Exp, scale=-0.5),
                  reads=[("rs3", q)], writes=[("rs3", q)])
            S.add("act", lambda e, sl=sl, q=q: e.activation(out=hn2_bf[q][0:n, :], in_=h2[sl][0:n, :], func=AF.Copy, scale=ss3[0:n, 2 + q:3 + q]),
                  reads=[("h2", sl), ("rs3", q)], writes=[("hn2", q)])

    def prep2(t):
        rel0, Wt, n, nsub, par = tile_geom(t)
        for sub in range(nsub):
            q = sub

            def tr_fn(e, q=q):
                ins = None
                for c in range(DC):
                    ins = e.transpose(bank_bf(B_T3)[:, c * 128:c * 128 + n], hn2_bf[q][0:n, c * 128:(c + 1) * 128], ident[0:n, 0:n])
                return ins
            S.add("pe", tr_fn, reads=[("hn2", q), "cst"], writes=[("ps", B_T3)])
            S.add("dve", lambda e, sub=sub: e.tensor_copy(
                out=hn2T[par].rearrange("p (c w) -> p c w", c=8)[:, :, sub * 128:sub * 128 + n],
                in_=bank_bf(B_T3).rearrange("p (c w) -> p c w", c=8)[:, :, 0:n]),
                reads=[("ps", B_T3)], writes=[("hn2T", par)])

    pair_ctr = [0]

    def ffn_in(t):
        rel0, Wt, n, nsub, par = tile_geom(t)
        final = t >= 0
        pcs = {}

        def mm(j):
            pc = pair_ctr[0] % 2
            pair_ctr[0] += 1
            pcs[j] = pc
            for which, bb in enumerate((B_U[pc], B_G[pc])):
                slab = j + which * NPAIR

                def fi_fn(e, bb=bb, slab=slab):
                    ins = None
                    for c in range(DC):
                        ins = e.matmul(bank(bb, Wt), lhsT=wfi_bf[:, c * 2 * DFF + slab * 128:c * 2 * DFF + (slab + 1) * 128],
                                       rhs=hn2T[par][:, c * W3:c * W3 + Wt], start=(c == 0), stop=(c == DC - 1))
                    return ins
                S.add("pe", fi_fn, reads=[("hn2T", par), "w3all"], writes=[("ps", bb)])
                if not final:
                    S.add("act", lambda e, bb=bb, slab=slab: e.copy(out=ust[:, slab * 2:slab * 2 + 2], in_=bank(bb, 2)),
                          reads=[("ps", bb)], writes=[("ust", slab)])

        def post1(j):
            pc = pcs[j]
            for which, (bb, buf, cbuf) in enumerate(((B_U[pc], ubuf[pc], uc[pc]), (B_G[pc], gbuf[pc], gc[pc]))):
                slab = j + which * NPAIR
                bk = ("ub", which, pc)
                ck = ("cb", which, pc)
                S.add("act", lambda e, bb=bb, buf=buf: e.copy(out=buf[:, 2:2 + Wt], in_=bank(bb, Wt)), reads=[("ps", bb)], writes=[bk])
                S.add("pool", lambda e, buf=buf, slab=slab: e.tensor_copy(out=buf[:, 0:2], in_=ust[:, slab * 2:slab * 2 + 2]),
                      reads=[("ust", slab), bk], writes=[bk])
                S.add("act", lambda e, buf=buf, cbuf=cbuf, slab=slab: e.activation(
                    out=cbuf[:, 0:Wt], in_=buf[:, 2:2 + Wt], func=AF.Identity, scale=col(V_FCW + slab * 3 + 2), bias=col(V_FCB + slab)),
                    reads=[bk, "vec"], writes=[ck])
                for k in range(2):
                    S.add("dve", lambda e, buf=buf, cbuf=cbuf, slab=slab, k=k: e.scalar_tensor_tensor(
                        out=cbuf[:, 0:Wt], in0=buf[:, k:k + Wt], scalar=col(V_FCW + slab * 3 + k), in1=cbuf[:, 0:Wt],
                        op0=ALU.mult, op1=ALU.add), reads=[bk, "vec", ck], writes=[ck])
                S.add("pool", lambda e, buf=buf, slab=slab: e.tensor_copy(out=ust[:, slab * 2:slab * 2 + 2], in_=buf[:, Wt:Wt + 2]),
                      reads=[bk], writes=[("ust", slab)])

        def post2(j):
            pc = pcs[j]
            S.add("act", lambda e: e.activation(out=sg[pc][:, 0:Wt], in_=gc[pc][:, 0:Wt], func=AF.Silu),
                  reads=[("cb", 1, pc)], writes=[("sg", pc)])
            S.add("dve", lambda e: e.tensor_tensor(out=act_bf[:, j * W3:j * W3 + Wt], in0=sg[pc][:, 0:Wt], in1=uc[pc][:, 0:Wt], op=ALU.mult),
                  reads=[("sg", pc), ("cb", 0, pc)], writes=[("actT", j)])

        if not final:
            for j in range(NPAIR):
                mm(j)
            return
        for j in range(NPAIR + 2):
            if j < NPAIR:
                mm(j)
            if 0 <= j - 1 < NPAIR:
                post1(j - 1)
            if 0 <= j - 2 < NPAIR:
                post2(j - 2)

    def ffn_out(t, sub):
        rel0, Wt, n, nsub, par = tile_geom(t)
        sl = 2 * par + sub

        def fo_fn(e):
            ins = None
            for hf in range(2):
                for j in range(NPAIR):
                    ins = e.matmul(bank(B_FO[hf], 512), lhsT=act_bf[:, j * W3 + sub * 128:j * W3 + (sub + 1) * 128],
                                   rhs=wfo_bf[:, j * 1024 + hf * 512:j * 1024 + (hf + 1) * 512], start=(j == 0), stop=(j == NPAIR - 1))
            return ins
        S.add("pe", fo_fn, reads=[("actT", j) for j in range(NPAIR)] + ["w3all"], writes=[("ps", 0), ("ps", 1)])
        S.add("dve", lambda e: e.tensor_tensor(out=h2[sl][:, :], in0=ps[:, 0:1024], in1=h2[sl][:, :], op=ALU.add),
              reads=[("ps", 0), ("ps", 1), ("h2", sl)], writes=[("h2", sl)])
        r0 = rel0 - 2 + sub * 128
        S.add("pool", lambda e, s: e.dma_start(out=out[r0:r0 + 128, :], in_=h2[sl][:, :]).then_inc(s, 16),
              reads=[("h2", sl)], writes=[("h2", sl), ("out", r0)], chan=("h2", sl))

    prep1(-1)
    prep2(-1)
    ffn_in(-1)
    prep1(0)
    prep2(0)
    for t in range(NST):
        ffn_in(t)
        if t + 1 < NST:
            prep1(t + 1)
        ffn_out(t, 0)
        if t + 1 < NST:
            prep2(t + 1)
        ffn_out(t, 1)

    S.add("sp", lambda e: None, reads=[("out", r0) for r0 in range(0, HALF, 128)], writes=["done"])

    S.finalize()
    chans = list(S.chan_count.keys())
    sem_ctxs = []
    sems = {}
    for e in Sched.ENG:
        c = nc.semaphore("s_" + e)
        sems[e] = c.__enter__()
        sem_ctxs.append(c)
    chan_sems = {}
    for i, ch in enumerate(chans):
        c = nc.semaphore("c_%d" % i)
        chan_sems[ch] = c.__enter__()
        sem_ctxs.append(c)
    with nc.Block() as block:
        S.emit(nc, block, sems, chan_sems)
    for c in reversed(sem_ctxs):
        c.__exit__(None, None, None)
    pctx.__exit__(None, None, None)
    ctx.__exit__(None, None, None)
    return nc


def _consts():
    c = np.zeros((128, NCST), np.float32)
    j = np.arange(128)[:, None]
    s = np.arange(128)[None, :]
    c[:, C_ID:C_ID + 128] = (j == s)
    c[:, C_NL:C_NL + 128] = -(j >= s).astype(np.float32)
    c[:, C_NU:C_NU + 128] = -(j < s).astype(np.float32)
    c[:, C_BO:C_BO + 128] = ((j // 64) == (s // 64))
    t = np.arange(512)[None, :]
    for k in range(4):
        c[:, C_MK + k * 512:C_MK + (k + 1) * 512] = ((128 * k + j) < t)
    return c


def _prep_inputs(inputs, SEQ):
    f = lambda a: np.ascontiguousarray(np.asarray(a), dtype=np.float32)
    x = f(inputs["x"])
    meta = f(inputs["meta_tokens"])
    w_in = f(inputs["w_in"])[0]
    w_out = f(inputs["w_out"])[0]
    w_fi = f(inputs["w_ffn_in"])[0]
    w_fo = f(inputs["w_ffn_out"])[0]
    g1 = f(inputs["norm1_g"])[0]
    g2 = f(inputs["norm2_g"])[0]
    qg = f(inputs["q_norm_g"])[0]
    kg = f(inputs["k_norm_g"])[0]
    cw = f(inputs["conv_w"])[0]
    cb = f(inputs["conv_b"])[0]
    wa = f(inputs["w_rg_a"])[0]
    wi = f(inputs["w_rg_i"])[0]
    ba = f(inputs["b_rg_a"])[0]
    bi = f(inputs["b_rg_i"])[0]
    lam = f(inputs["lru_lambda"])[0]
    fcw = f(inputs["ffn_conv_w"])[0]
    fcb = f(inputs["ffn_conv_b"])[0]
    consts = _consts()
    TP = SEQ + 128
    maps = []
    for core in range(8):
        b, p = core // 2, core % 2
        xpad = np.zeros((TP, D), np.float32)
        xpad[PAD:128] = meta
        xpad[128:] = x[b]
        cs = slice(256 * p, 256 * p + 256)
        wic = np.concatenate([w_in[:, 0:512][:, cs], w_in[:, 512:1024][:, cs], w_in[:, 1024:1536][:, cs],
                              w_in[:, 1536:2048][:, cs], w_in[:, 2048:2560][:, cs]], axis=1)
        vec = np.zeros((128, NV), np.float32)
        vec[:, V_G1:V_G1 + 8] = g1.reshape(8, 128).T
        vec[:, V_G2:V_G2 + 8] = g2.reshape(8, 128).T
        vec[:, V_QG] = np.tile(qg, 2)
        vec[:, V_KG] = np.tile(kg, 2)
        for c in range(2):
            ch = slice(256 * p + 128 * c, 256 * p + 128 * c + 128)
            vec[:, V_CW + c * 4:V_CW + c * 4 + 4] = cw[:, ch].T
            vec[:, V_CB + c] = cb[ch]
            vec[:, V_BA + c] = ba[ch]
            vec[:, V_BI + c] = bi[ch]
            vec[:, V_LAM + c] = lam[ch]
        for s in range(NSLAB):
            vec[:, V_FCW + s * 3:V_FCW + s * 3 + 3] = fcw[:, s * 128:(s + 1) * 128].T
            vec[:, V_FCB + s] = fcb[s * 128:(s + 1) * 128]
        wrg = np.zeros((128, 4 * 128), np.float32)
        for gi, wsrc in enumerate((wa, wi)):
            for c in range(2):
                for k in range(2):
                    blk = 4 * p + 2 * c + k
                    o = (gi * 2 + c) * 128
                    wrg[64 * k:64 * k + 64, o + 64 * k:o + 64 * k + 64] = wsrc[blk]
        perm = []
        for r in range(2):
            perm += list(range(256 * r, 256 * r + 256))
            perm += list(range(512 + 256 * r, 512 + 256 * r + 256))
        maps.append({
            "xpad": xpad, "xhalf": np.ascontiguousarray(xpad[126 + (SEQ // 2) * p:126 + (SEQ // 2) * p + SEQ // 2 + 2]), "w_in": np.ascontiguousarray(wic), "vecs": vec, "w_rg": wrg,
            "w_out": np.ascontiguousarray(w_out[perm]), "w_ffn_in": w_fi, "w_ffn_out": w_fo, "consts": consts,
        })
    return maps


_NC_CACHE = {}


def kernel(**inputs):
    x = np.asarray(inputs["x"])
    B, SEQ, _ = x.shape
    assert B == 4
    if SEQ not in _NC_CACHE:
        _NC_CACHE[SEQ] = build_nc(SEQ)
    nc = _NC_CACHE[SEQ]
    maps = _prep_inputs(inputs, SEQ)
    res = run_bass_kernel_spmd(nc, maps, core_ids=list(range(8)))
    outp = np.empty((B, SEQ, D), np.float32)
    HALF = SEQ // 2
    for core in range(8):
        b, p = core // 2, core % 2
        outp[b, p * HALF:(p + 1) * HALF] = res.results[core]["out"]
    return outp
```

```python
import numpy as np
import concourse.bass as bass
import concourse.mybir as mybir
from concourse.bass_utils import run_bass_kernel_spmd

F32 = mybir.dt.float32
BF16 = mybir.dt.bfloat16
AF = mybir.ActivationFunctionType
ALU = mybir.AluOpType

D = 1024
DC = 8
DFF = 2816
NSLAB = 44
NPAIR = 22
EPS = 1e-6
NMETA = 16
PAD = 112
GELU_C = 0.7978845608028654

V_G1, V_G2, V_QG, V_KG, V_CW, V_CB, V_BA, V_BI, V_LAM, V_FCW, V_FCB = 0, 8, 16, 17, 18, 26, 28, 30, 32, 34, 166
NV = 210
C_ID, C_NL, C_NU, C_BO, C_MK = 0, 128, 256, 384, 512
NCST = 512 + 4 * 512


class Sched:
    ENG = ("pe", "act", "dve", "pool", "sp")

    def __init__(self):
        self.ops = []
        self.last_w = {}
        self.readers = {}
        self.eng_count = {e: 0 for e in self.ENG}
        self.chan_count = {}
        self.chan_inc = {}
        self.barrier_nodes = []

    def add(self, eng, fn, reads=(), writes=(), chan=None, ndma=1, inc=16):
        deps = set(self.barrier_nodes)
        for k in reads:
            w = self.last_w.get(k)
            if w is not None:
                deps.add(w)
        for k in writes:
            w = self.last_w.get(k)
            if w is not None:
                deps.add(w)
            for r in self.readers.get(k, ()):
                deps.add(r)
        if chan is None:
            self.eng_count[eng] += 1
            node = ("E", eng, self.eng_count[eng])
        else:
            self.chan_inc[chan] = inc
            self.chan_count[chan] = self.chan_count.get(chan, 0) + ndma * inc
            node = ("C", chan, self.chan_count[chan])
        self.ops.append(dict(eng=eng, fn=fn, deps=deps, node=node, chan=chan))
        for k in reads:
            self.readers.setdefault(k, []).append(node)
        for k in writes:
            self.last_w[k] = node
            self.readers[k] = []
        return node

    def barrier(self):
        nodes = []
        for e, c in self.eng_count.items():
            if c:
                nodes.append(("E", e, c))
        for ch, c in self.chan_count.items():
            nodes.append(("C", ch, c))
        self.barrier_nodes = nodes

    def finalize(self):
        known = {e: {} for e in self.ENG}
        signal = {e: set() for e in self.ENG}
        for op in self.ops:
            need = {}
            for kind, tgt, val in op["deps"]:
                if kind == "E" and tgt == "pe" and op["eng"] == "pe" and op["chan"] is None:
                    continue
                key = (kind, tgt)
                if val > need.get(key, 0):
                    need[key] = val
            waits = []
            kn = known[op["eng"]]
            for key, val in need.items():
                if kn.get(key, 0) >= val:
                    continue
                kn[key] = val
                waits.append((key, val))
                if key[0] == "E":
                    signal[key[1]].add(val)
            op["waits"] = waits
        self.rank = {}
        for e in self.ENG:
            self.rank[e] = {v: i + 1 for i, v in enumerate(sorted(signal[e]))}

    def emit(self, nc, block, sems, chan_sems):
        streams = {e: [op for op in self.ops if op["eng"] == e] for e in self.ENG}

        def run(eng_name, eng):
            for op in streams[eng_name]:
                for (kind, tgt), val in op["waits"]:
                    if kind == "E":
                        eng.wait_ge(sems[tgt], self.rank[tgt][val])
                    else:
                        eng.wait_ge(chan_sems[tgt], val)
                if op["chan"] is not None:
                    op["fn"](eng, chan_sems[op["chan"]])
                else:
                    ins = op["fn"](eng)
                    idx = op["node"][2]
                    if idx in self.rank[eng_name]:
                        assert ins is not None
                        ins.then_inc(sems[eng_name], 1)

        @block.tensor
        def _(e):
            run("pe", e)

        @block.scalar
        def _(e):
            run("act", e)

        @block.vector
        def _(e):
            run("dve", e)

        @block.gpsimd
        def _(e):
            run("pool", e)

        @block.sync
        def _(e):
            run("sp", e)


def build_nc(SEQ):
    assert SEQ % 1024 == 0
    HALF = SEQ // 2
    TP = SEQ + 128
    NB = TP // 128
    NQT = 1 + SEQ // 512
    NST = HALF // 256
    W3 = 256

    nc = bass.Bass("TRN2", target_bir_lowering=False)
    xpad = nc.dram_tensor("xpad", [TP, D], F32, kind="ExternalInput").ap()
    w_in = nc.dram_tensor("w_in", [D, 1280], F32, kind="ExternalInput").ap()
    vecs = nc.dram_tensor("vecs", [128, NV], F32, kind="ExternalInput").ap()
    w_rg = nc.dram_tensor("w_rg", [128, 4 * 128], F32, kind="ExternalInput").ap()
    w_out = nc.dram_tensor("w_out", [D, D], F32, kind="ExternalInput").ap()
    w_fi = nc.dram_tensor("w_ffn_in", [D, 2 * DFF], F32, kind="ExternalInput").ap()
    w_fo = nc.dram_tensor("w_ffn_out", [DFF, D], F32, kind="ExternalInput").ap()
    consts = nc.dram_tensor("consts", [128, NCST], F32, kind="ExternalInput").ap()
    out = nc.dram_tensor("out", [HALF, D], F32, kind="ExternalOutput").ap()
    GW = SEQ // 4
    mo_g = [nc.dram_tensor("mixed_own_%d" % g, [512, GW], BF16) for g in range(4)]
    mo_halo_t = nc.dram_tensor("mixed_own_halo", [512, 4], BF16)
    ma_big_t = nc.dram_tensor("mixed_all_big", [4 * 1024, GW], BF16)
    ma_halo_t = nc.dram_tensor("mixed_all_halo", [1024, 4], BF16)
    xhalf = nc.dram_tensor("xhalf", [HALF + 2, D], F32, kind="ExternalInput").ap()
    mh_t = nc.dram_tensor("mixed_half", [1024, HALF + 2], BF16)
    mixed_half = mh_t.ap()
    ma_big = ma_big_t.ap()
    ma_halo = ma_halo_t.ap()
    mo_halo = mo_halo_t.ap()

    def store_fn(src, row0, ti):
        pieces = []
        if ti == 0:
            pieces.append((mo_halo[row0:row0 + 128, 0:2], 126, 2))
        else:
            i0 = 512 * (ti - 1)
            c = 0
            while c < 512:
                g = (i0 + c) // GW
                gc = (i0 + c) % GW
                n = min(512 - c, GW - gc)
                pieces.append((mo_g[g].ap()[row0:row0 + 128, gc:gc + n], c, n))
                c += n
            if i0 <= HALF - 2 < i0 + 512:
                pieces.append((mo_halo[row0:row0 + 128, 2:4], HALF - 2 - i0, 2))

        def fn(e, s):
            ins = None
            for dst, c0, n in pieces:
                ins = e.dma_start(out=dst, in_=src[:, c0:c0 + n]).then_inc(s, 16)
            return ins
        return fn, len(pieces)

    S = Sched()
    ARENA_F = 53200

    ctx = nc.sbuf_tensor("arena", [128, ARENA_F], F32)
    arena = ctx.__enter__()
    pctx = nc.psum_tensor("ps", [128, 8 * 512], F32)
    ps = pctx.__enter__()

    class Arena:
        def __init__(self):
            self.off = 0

        def f32(self, n):
            o = self.off
            self.off += n
            assert self.off <= ARENA_F, self.off
            return arena[:, o:o + n]

        def bf(self, n):
            nf = (n + 1) // 2
            o = self.off
            self.off += nf
            assert self.off <= ARENA_F, self.off
            return arena[:, o:o + nf].bitcast(BF16)

    A = Arena()

    def bank(b, n=512):
        return ps[:, b * 512:b * 512 + n]

    def bank_bf(b):
        return ps[:, b * 512:(b + 1) * 512].bitcast(BF16)

    cst = A.bf(NCST)
    vec = A.f32(NV)
    ext = A.f32(16)
    ident = cst[:, C_ID:C_ID + 128]
    negL = cst[:, C_NL:C_NL + 128]
    negU = cst[:, C_NU:C_NU + 128]
    bones = cst[:, C_BO:C_BO + 128]

    def maskv(j, W):
        return cst[:, C_MK + j * 512:C_MK + j * 512 + W]

    mark_stage = A.off

    qT = A.bf(2 * TP)
    kT = A.bf(2 * TP)
    v_sb = A.bf(NB * 256)
    mark12 = A.off
    win_bf = A.bf(8 * 1280)
    wrg_bf = A.bf(4 * 128)
    stg = [A.f32(1280), A.f32(1280)]
    xs = [A.f32(1024), A.f32(1024)]
    sqj = A.bf(1024)
    hn_bf = [A.bf(1024), A.bf(1024)]
    ssb = A.f32(8)
    hnT = A.bf(8 * 512)
    sqb = A.bf(512)
    rqb = A.f32(512)
    xr_sb = [A.f32(3 + 512), A.f32(3 + 512)]
    xc_sb = A.f32(512)
    xc_bf = A.bf(512)
    tr_sb = A.f32(512)
    ti_sb = A.f32(512)
    la_sb = A.f32(512)
    a_sb = A.f32(512)
    th_sb = A.f32(512)
    m2_sb = A.f32(512)
    bt_sb = A.f32(512)
    hl_sb = [A.f32(512), A.f32(512)]
    hst = A.f32(2)
    y_sb = A.f32(512)
    y2_sb = A.f32(512)
    tg_sb = A.f32(512)
    ol_bf = [A.bf(512), A.bf(512)]

    def col(i, n=1):
        return vec[:, i:i + n]

    for hh in range(2):
        S.add("sp", lambda e, s, hh=hh: e.dma_start(out=stg[hh][:, 0:1280], in_=consts[:, hh * 1280:(hh + 1) * 1280]).then_inc(s, 16),
              writes=[("stg", hh)], chan=("stg", hh))
        S.add("dve", lambda e, hh=hh: e.tensor_copy(out=cst[:, hh * 1280:(hh + 1) * 1280], in_=stg[hh][:, 0:1280]),
              reads=[("stg", hh)], writes=["cst"])
    S.add("sp", lambda e, s: e.dma_start(out=vec[:, :], in_=vecs[:, :]).then_inc(s, 16), writes=["vec"], chan="vec")
    S.add("act", lambda e: e.activation(out=ext[:, 7:9], in_=col(V_LAM, 2), func=AF.Exp, scale=-1.0),
          reads=["vec"], writes=["ext_t"])
    S.add("act", lambda e: e.activation(out=ext[:, 7:9], in_=ext[:, 7:9], func=AF.Ln, bias=1.0),
          reads=["ext_t"], writes=["ext_t"])
    S.add("dve", lambda e: e.tensor_scalar(out=ext[:, 0:2], in0=ext[:, 7:9], scalar1=-8.0, scalar2=None, op0=ALU.mult),
          reads=["ext_t"], writes=["ext"])
    S.add("dve", lambda e: e.tensor_scalar(out=ext[:, 2:6], in0=col(V_BA, 4), scalar1=-1.0, scalar2=None, op0=ALU.mult),
          reads=["vec", "ext"], writes=["ext"])
    S.add("dve", lambda e: e.tensor_scalar(out=ext[:, 6:7], in0=col(V_KG), scalar1=0.125, scalar2=None, op0=ALU.mult),
          reads=["vec", "ext"], writes=["ext"])
    S.add("sp", lambda e, s: e.dma_start(out=stg[0][:, 0:512], in_=w_rg[:, :]).then_inc(s, 16),
          writes=[("stg", 0)], chan=("stg", 0))
    S.add("dve", lambda e: e.tensor_copy(out=wrg_bf[:, :], in_=stg[0][:, 0:512]), reads=[("stg", 0)], writes=["wrg"])
    for c in range(DC):
        hh = c % 2
        S.add("sp", lambda e, s, c=c, hh=hh: e.dma_start(out=stg[hh][:, :], in_=w_in[c * 128:(c + 1) * 128, :]).then_inc(s, 16),
              writes=[("stg", hh)], chan=("stg", hh))
        if c % 2 == 0:
            S.add("dve", lambda e, c=c, hh=hh: e.tensor_scalar(out=win_bf[:, c * 1280:(c + 1) * 1280], in0=stg[hh][:, :],
                                                                scalar1=col(V_G1 + c), scalar2=None, op0=ALU.mult),
                  reads=[("stg", hh), "vec"], writes=[("win", c)])
        else:
            S.add("act", lambda e, c=c, hh=hh: e.activation(out=win_bf[:, c * 1280:(c + 1) * 1280], in_=stg[hh][:, :],
                                                             func=AF.Copy, scale=col(V_G1 + c)),
                  reads=[("stg", hh), "vec"], writes=[("win", c)])
    S.add("pool", lambda e: e.memset(xr_sb[0][:, 0:3], 0.0), writes=[("xr", 0)])
    S.add("pool", lambda e: e.memset(xr_sb[1][:, 0:3], 0.0), writes=[("xr", 1)])
    S.add("pool", lambda e: e.memset(hst[:, :], 0.0), writes=["hst"])

    win_reads = [("win", c) for c in range(DC)]

    B_TRP, B_V, B_PJ0, B_PJ1, B_PS2, B_GR, B_GI = 0, 1, 2, 3, 4, 5, 6

    def tile_info(ti):
        if ti == 0:
            return 0, 128
        return 128 + 512 * (ti - 1), 512

    pj_ctr = [0]

    def proj_slab(col0, W):
        b = B_PJ0 + (pj_ctr[0] % 2)
        pj_ctr[0] += 1

        def fn(e, b=b, col0=col0, W=W):
            ins = None
            for c in range(DC):
                ins = e.matmul(bank(b, W), lhsT=win_bf[:, c * 1280 + col0:c * 1280 + col0 + 128],
                               rhs=hnT[:, c * 512:c * 512 + W], start=(c == 0), stop=(c == DC - 1))
            return ins
        S.add("pe", fn, reads=win_reads + ["hnT"], writes=[("ps", b)])
        return b

    for ti in range(NQT):
        pos0, W = tile_info(ti)
        nsub = W // 128
        for sub in range(nsub):
            blk = pos0 // 128 + sub
            sl = blk % 2
            S.add("sp", lambda e, s, blk=blk, sl=sl: e.dma_start(out=xs[sl][:, :], in_=xpad[blk * 128:(blk + 1) * 128, :]).then_inc(s, 16),
                  writes=[("xs", sl)], chan=("xs", sl))
            S.add("pool", lambda e, sl=sl: e.memset(ssb[:, sl:sl + 1], 0.0), writes=[("ss", sl)])
            S.add("act", lambda e, sl=sl: e.activation(out=sqj[:, :], in_=xs[sl][:, :], func=AF.Square, accum_out=ssb[:, sl:sl + 1]),
                  reads=[("xs", sl)], writes=[("ss", sl), "sqj"])
            S.add("dve", lambda e, sl=sl: e.tensor_scalar(out=ssb[:, 2 + sl:3 + sl], in0=ssb[:, sl:sl + 1], scalar1=1.0 / D, scalar2=EPS,
                                                           op0=ALU.mult, op1=ALU.add),
                  reads=[("ss", sl)], writes=[("rstd", sl)])
            S.add("act", lambda e, sl=sl: e.activation(out=ssb[:, 2 + sl:3 + sl], in_=ssb[:, 2 + sl:3 + sl], func=AF.Ln),
                  reads=[("rstd", sl)], writes=[("rstd", sl)])
            S.add("act", lambda e, sl=sl: e.activation(out=ssb[:, 2 + sl:3 + sl], in_=ssb[:, 2 + sl:3 + sl], func=AF.Exp, scale=-0.5),
                  reads=[("rstd", sl)], writes=[("rstd", sl)])
            S.add("act", lambda e, sl=sl: e.activation(out=hn_bf[sl][:, :], in_=xs[sl][:, :], func=AF.Copy, scale=ssb[:, 2 + sl:3 + sl]),
                  reads=[("xs", sl), ("rstd", sl)], writes=[("hn", sl)])

            def tr_fn(e, sl=sl):
                ins = None
                for c in range(DC):
                    ins = e.transpose(bank_bf(B_TRP)[:, c * 128:(c + 1) * 128], hn_bf[sl][:, c * 128:(c + 1) * 128], ident)
                return ins
            S.add("pe", tr_fn, reads=[("hn", sl), "cst"], writes=[("ps", B_TRP)])
            S.add("dve", lambda e, sub=sub: e.tensor_copy(
                out=hnT.rearrange("p (c w) -> p c w", c=8)[:, :, sub * 128:(sub + 1) * 128],
                in_=bank_bf(B_TRP).rearrange("p (c w) -> p c w", c=8)),
                reads=[("ps", B_TRP)], writes=["hnT"])

            def v_fn(e, sub=sub):
                ins = None
                for c in range(DC):
                    ins = e.matmul(bank(B_V, 256), lhsT=hnT[:, c * 512 + sub * 128:c * 512 + (sub + 1) * 128],
                                   rhs=win_bf[:, c * 1280 + 512:c * 1280 + 768], start=(c == 0), stop=(c == DC - 1))
                return ins
            S.add("pe", v_fn, reads=win_reads + ["hnT"], writes=[("ps", B_V)])
            S.add("act", lambda e, blk=blk: e.copy(out=v_sb[:, blk * 256:(blk + 1) * 256], in_=bank(B_V, 256)),
                  reads=[("ps", B_V)], writes=[("v", blk)])

        for which in range(2):
            for j in range(2):
                b = proj_slab(which * 256 + j * 128, W)
                S.add("act", lambda e, b=b, W=W: e.activation(out=sqb[:, 0:W], in_=bank(b, W), func=AF.Square),
                      reads=[("ps", b)], writes=["sqb"])
                S.add("pe", lambda e, W=W: e.matmul(bank(B_PS2, W), lhsT=bones, rhs=sqb[:, 0:W], start=True, stop=True),
                      reads=["sqb", "cst"], writes=[("ps", B_PS2)])
                S.add("dve", lambda e, W=W: e.tensor_scalar(out=rqb[:, 0:W], in0=bank(B_PS2, W), scalar1=1.0 / 64, scalar2=EPS,
                                                             op0=ALU.mult, op1=ALU.add),
                      reads=[("ps", B_PS2)], writes=["rqb"])
                S.add("act", lambda e, W=W: e.activation(out=rqb[:, 0:W], in_=rqb[:, 0:W], func=AF.Ln), reads=["rqb"], writes=["rqb"])
                S.add("act", lambda e, W=W: e.activation(out=rqb[:, 0:W], in_=rqb[:, 0:W], func=AF.Exp, scale=-0.5),
                      reads=["rqb"], writes=["rqb"])
                dst = qT if which == 0 else kT
                gsc = col(V_QG) if which == 0 else ext[:, 6:7]
                S.add("dve", lambda e, b=b, W=W, dst=dst, gsc=gsc, j=j, pos0=pos0: e.scalar_tensor_tensor(
                    out=dst[:, j * TP + pos0:j * TP + pos0 + W], in0=bank(b, W), scalar=gsc, in1=rqb[:, 0:W],
                    op0=ALU.mult, op1=ALU.mult),
                    reads=[("ps", b), "rqb", "vec", "ext"], writes=[("qk", which, j, ti)])
        for c in range(2):
            b = proj_slab(768 + c * 128, W)
            S.add("act", lambda e, b=b, W=W, c=c: e.copy(out=xr_sb[c][:, 3:3 + W], in_=bank(b, W)),
                  reads=[("ps", b)], writes=[("xr", c)])
            S.add("dve", lambda e, c=c, W=W: e.tensor_scalar(out=xc_sb[:, 0:W], in0=xr_sb[c][:, 3:3 + W], scalar1=col(V_CW + c * 4 + 3),
                                                               scalar2=col(V_CB + c), op0=ALU.mult, op1=ALU.add),
                  reads=[("xr", c), "vec"], writes=["xc"])
            for k in range(3):
                S.add("dve", lambda e, c=c, W=W, k=k: e.scalar_tensor_tensor(out=xc_sb[:, 0:W], in0=xr_sb[c][:, k:k + W],
                                                                               scalar=col(V_CW + c * 4 + k), in1=xc_sb[:, 0:W],
                                                                               op0=ALU.mult, op1=ALU.add),
                      reads=[("xr", c), "vec", "xc"], writes=["xc"])
            S.add("pool", lambda e, c=c, W=W: e.tensor_copy(out=xr_sb[c][:, 0:3], in_=xr_sb[c][:, W:W + 3]),
                  reads=[("xr", c)], writes=[("xr", c)])
            S.add("act", lambda e, W=W: e.copy(out=xc_bf[:, 0:W], in_=xc_sb[:, 0:W]), reads=["xc"], writes=["xcb"])
            S.add("pe", lambda e, c=c, W=W: e.matmul(bank(B_GR, W), lhsT=wrg_bf[:, (0 * 2 + c) * 128:(0 * 2 + c + 1) * 128],
                                                      rhs=xc_bf[:, 0:W], start=True, stop=True),
                  reads=["xcb", "wrg"], writes=[("ps", B_GR)])
            S.add("pe", lambda e, c=c, W=W: e.matmul(bank(B_GI, W), lhsT=wrg_bf[:, (1 * 2 + c) * 128:(1 * 2 + c + 1) * 128],
                                                      rhs=xc_bf[:, 0:W], start=True, stop=True),
                  reads=["xcb", "wrg"], writes=[("ps", B_GI)])
            S.add("act", lambda e, c=c, W=W: e.activation(out=tr_sb[:, 0:W], in_=bank(B_GR, W), func=AF.Exp,
                                                           bias=ext[:, 2 + c:3 + c], scale=-1.0),
                  reads=[("ps", B_GR), "ext"], writes=["tr"])
            S.add("act", lambda e, c=c, W=W: e.activation(out=ti_sb[:, 0:W], in_=bank(B_GI, W), func=AF.Exp,
                                                           bias=ext[:, 4 + c:5 + c], scale=-1.0),
                  reads=[("ps", B_GI), "ext"], writes=["tig"])
            S.add("act", lambda e, W=W: e.activation(out=tr_sb[:, 0:W], in_=tr_sb[:, 0:W], func=AF.Ln, bias=1.0), reads=["tr"], writes=["tr"])
            S.add("act", lambda e, W=W: e.activation(out=tr_sb[:, 0:W], in_=tr_sb[:, 0:W], func=AF.Exp, scale=-1.0), reads=["tr"], writes=["tr"])
            S.add("act", lambda e, W=W: e.activation(out=ti_sb[:, 0:W], in_=ti_sb[:, 0:W], func=AF.Ln, bias=1.0), reads=["tig"], writes=["tig"])
            S.add("act", lambda e, W=W: e.activation(out=ti_sb[:, 0:W], in_=ti_sb[:, 0:W], func=AF.Exp, scale=-1.0), reads=["tig"], writes=["tig"])
            S.add("act", lambda e, c=c, W=W: e.activation(out=a_sb[:, 0:W], in_=tr_sb[:, 0:W], func=AF.Exp, scale=ext[:, c:c + 1]),
                  reads=["tr", "ext"], writes=["a"])
            S.add("dve", lambda e, W=W: e.tensor_tensor(out=m2_sb[:, 0:W], in0=a_sb[:, 0:W], in1=a_sb[:, 0:W], op=ALU.mult),
                  reads=["a"], writes=["m2"])
            S.add("dve", lambda e, W=W: e.tensor_scalar(out=m2_sb[:, 0:W], in0=m2_sb[:, 0:W], scalar1=-1.0, scalar2=1.0,
                                                         op0=ALU.mult, op1=ALU.add),
                  reads=["m2"], writes=["m2"])
            S.add("act", lambda e, W=W: e.activation(out=m2_sb[:, 0:W], in_=m2_sb[:, 0:W], func=AF.Ln), reads=["m2"], writes=["m2"])
            S.add("act", lambda e, W=W: e.activation(out=m2_sb[:, 0:W], in_=m2_sb[:, 0:W], func=AF.Exp, scale=0.5), reads=["m2"], writes=["m2"])
            S.add("dve", lambda e, W=W: e.tensor_tensor(out=bt_sb[:, 0:W], in0=ti_sb[:, 0:W], in1=xc_sb[:, 0:W], op=ALU.mult),
                  reads=["tig", "xc"], writes=["bt"])
            S.add("dve", lambda e, W=W: e.tensor_tensor(out=bt_sb[:, 0:W], in0=bt_sb[:, 0:W], in1=m2_sb[:, 0:W], op=ALU.mult),
                  reads=["bt", "m2"], writes=["bt"])
            if ti == 0:
                S.add("dve", lambda e: e.memset(bt_sb[:, 0:PAD], 0.0), reads=["bt"], writes=["bt"])
            S.add("dve", lambda e, c=c, W=W: e.tensor_tensor_scan(out=hl_sb[c][:, 0:W], data0=a_sb[:, 0:W], data1=bt_sb[:, 0:W],
                                                                   initial=hst[:, c:c + 1], op0=ALU.mult, op1=ALU.add),
                  reads=["a", "bt", "hst"], writes=[("hl", c)])
            S.add("dve", lambda e, c=c, W=W: e.tensor_copy(out=hst[:, c:c + 1], in_=hl_sb[c][:, W - 1:W]),
                  reads=[("hl", c)], writes=["hst"])
        for c in range(2):
            b = proj_slab(1024 + c * 128, W)
            S.add("act", lambda e, b=b, W=W: e.copy(out=y_sb[:, 0:W], in_=bank(b, W)), reads=[("ps", b)], writes=["y"])
            S.add("act", lambda e, b=b, W=W: e.activation(out=y2_sb[:, 0:W], in_=bank(b, W), func=AF.Square),
                  reads=[("ps", b)], writes=["y2"])
            S.add("dve", lambda e, W=W: e.tensor_scalar(out=y2_sb[:, 0:W], in0=y2_sb[:, 0:W], scalar1=0.044715, scalar2=1.0,
                                                          op0=ALU.mult, op1=ALU.add),
                  reads=["y2"], writes=["y2"])
            S.add("dve", lambda e, W=W: e.tensor_tensor(out=y2_sb[:, 0:W], in0=y2_sb[:, 0:W], in1=y_sb[:, 0:W], op=ALU.mult),
                  reads=["y2", "y"], writes=["y2"])
            S.add("act", lambda e, W=W: e.activation(out=tg_sb[:, 0:W], in_=y2_sb[:, 0:W], func=AF.Exp, scale=-2.0 * GELU_C),
                  reads=["y2"], writes=["tg"])
            S.add("act", lambda e, W=W: e.activation(out=tg_sb[:, 0:W], in_=tg_sb[:, 0:W], func=AF.Ln, bias=1.0), reads=["tg"], writes=["tg"])
            S.add("act", lambda e, W=W: e.activation(out=tg_sb[:, 0:W], in_=tg_sb[:, 0:W], func=AF.Exp, scale=-1.0), reads=["tg"], writes=["tg"])
            S.add("dve", lambda e, W=W: e.tensor_tensor(out=tg_sb[:, 0:W], in0=tg_sb[:, 0:W], in1=y_sb[:, 0:W], op=ALU.mult),
                  reads=["tg", "y"], writes=["tg"])
            S.add("dve", lambda e, c=c, W=W: e.tensor_tensor(out=ol_bf[c][:, 0:W], in0=tg_sb[:, 0:W], in1=hl_sb[c][:, 0:W], op=ALU.mult),
                  reads=["tg", ("hl", c)], writes=[("ol", c)])
            sfn, nd = store_fn(ol_bf[c], 256 + c * 128, ti)
            S.add("pool", sfn, reads=[("ol", c)], writes=[("mo", "l", c, ti)], chan=("ol", c), ndma=nd)

    S.barrier()
    A.off = mark12
    e_sb = [A.f32(1024) for _ in range(3)]
    sp_sb = [A.bf(1024) for _ in range(3)]
    g_sb = [A.bf(1024) for _ in range(2)]
    w_sb = [A.bf(1024) for _ in range(3)]
    osb = [A.bf(2 * 512), A.bf(2 * 512)]
    items = []
    for ti in range(NQT):
        pos0, W = tile_info(ti)
        b0 = pos0 // 128
        nsub = W // 128
        kbs = list(range(b0 + nsub - 1, -1, -1))
        for n, kb in enumerate(kbs):
            for j in range(2):
                items.append(dict(ti=ti, pos0=pos0, W=W, b0=b0, kb=kb, j=j, first=(n == 0), last=(n == len(kbs) - 1)))
    NI = len(items)

    S.add("dve", lambda e: e.memset(ext[:, 9:10], 0.0),
          reads=[("qk", w, j, t) for w in range(2) for j in range(2) for t in range(NQT)] + [("v", bl) for bl in range(NB)],
          writes=["kall"])

    def v3(buf, c0, W):
        return buf.rearrange("p (h w) -> p h w", h=2)[:, :, c0:W]

    def zview(c0, W):
        return ps[:, 0:1024].rearrange("p (h w) -> p h w", h=2)[:, :, c0:W]

    def pview(j, c0, W):
        return ps[:, (2 + 2 * j) * 512:(4 + 2 * j) * 512].rearrange("p (h w) -> p h w", h=2)[:, :, c0:W]

    RG = [[0, 1], [2, 3], [4, 5], [6, 7]]

    def tile_keys(ti):
        return [("mo", "l", c, ti) for c in range(2)] + [("mo", "a", j, ti) for j in range(2)]

    gathered = set()

    def maybe_gather(ti):
        if ti == 0:
            return
        done_tok = 512 * ti
        for g in range(4):
            if g in gathered or (g + 1) * GW > done_tok:
                continue
            gathered.add(g)
            t_lo = 1 + (g * GW) // 512
            t_hi = 1 + ((g + 1) * GW - 1) // 512
            keys = []
            for t in range(t_lo, t_hi + 1):
                keys += tile_keys(t)
            S.add("pool", lambda e, s, g=g: e.collective_compute(
                "AllGather", ALU.bypass, replica_groups=RG, ins=[mo_g[g].ap().opt()],
                outs=[ma_big[g * 1024:(g + 1) * 1024, :].opt()]).then_inc(s),
                reads=keys, writes=[("mall", g)], chan=("cc", g), inc=1)

    def c0_of(it):
        return 128 * (it["kb"] - it["b0"]) if it["kb"] >= it["b0"] else 0

    def PE1(i):
        it = items[i]
        kb, W, pos0, j = it["kb"], it["W"], it["pos0"], it["j"]
        c0 = c0_of(it)

        def fn(e):
            ins = None
            for hh in range(2):
                r = hh * 64
                ins = e.matmul(ps[:, hh * 512 + c0:hh * 512 + W], lhsT=kT[r:r + 64, j * TP + kb * 128:j * TP + (kb + 1) * 128],
                               rhs=qT[r:r + 64, j * TP + pos0 + c0:j * TP + pos0 + W], start=True, stop=True)
            return ins
        S.add("pe", fn, reads=["kall"], writes=[("ps", 0), ("ps", 1)])

    def ACT12(i):
        it = items[i]
        W = it["W"]
        c0 = c0_of(it)
        eb = e_sb[i % 3]
        sb = sp_sb[i % 3]
        S.add("act", lambda e: e.activation(out=v3(eb, c0, W), in_=zview(c0, W), func=AF.Exp),
              reads=[("ps", 0), ("ps", 1)], writes=[("e", i % 3)])
        S.add("act", lambda e: e.activation(out=v3(sb, c0, W), in_=v3(eb, c0, W), func=AF.Ln, bias=1.0),
              reads=[("e", i % 3)], writes=[("sp", i % 3)])
        if it["kb"] >= it["b0"]:
            for hh in range(2):
                S.add("dve", lambda e, hh=hh: e.tensor_tensor(out=sb[:, hh * 512 + c0:hh * 512 + c0 + 128],
                                                               in0=sb[:, hh * 512 + c0:hh * 512 + c0 + 128], in1=maskv(0, 128), op=ALU.mult),
                      reads=[("sp", i % 3), "cst"], writes=[("sp", i % 3)])

    def PE2(i):
        it = items[i]
        W, j = it["W"], it["j"]
        c0 = c0_of(it)
        sb = sp_sb[i % 3]

        def fn(e):
            ins = None
            for hh in range(2):
                b_ = 2 + 2 * j + hh
                ins = e.matmul(ps[:, b_ * 512 + c0:b_ * 512 + W], lhsT=negL, rhs=sb[:, hh * 512 + c0:hh * 512 + W],
                               start=it["first"], stop=False, skip_group_check=True)
            return ins
        S.add("pe", fn, reads=[("sp", i % 3), "cst"], writes=[("ps", 2 + 2 * j), ("ps", 3 + 2 * j)])

    def ACT3(i):
        it = items[i]
        W, j = it["W"], it["j"]
        c0 = c0_of(it)
        gb = g_sb[i % 2]
        eb = e_sb[i % 3]
        wb = w_sb[i % 3]
        S.add("act", lambda e: e.activation(out=v3(gb, c0, W), in_=pview(j, c0, W), func=AF.Exp),
              reads=[("ps", 2 + 2 * j), ("ps", 3 + 2 * j)], writes=[("g", i % 2)])
        S.add("dve", lambda e: e.tensor_tensor(out=v3(wb, c0, W), in0=v3(eb, c0, W), in1=v3(gb, c0, W), op=ALU.mult),
              reads=[("e", i % 3), ("g", i % 2)], writes=[("w", i % 3)])
        if it["kb"] >= it["b0"]:
            for hh in range(2):
                S.add("dve", lambda e, hh=hh: e.tensor_tensor(out=wb[:, hh * 512 + c0:hh * 512 + c0 + 128],
                                                               in0=wb[:, hh * 512 + c0:hh * 512 + c0 + 128], in1=maskv(0, 128), op=ALU.mult),
                      reads=[("w", i % 3), "cst"], writes=[("w", i % 3)])

    def PE4(i):
        it = items[i]
        if it["last"]:
            return
        W, j = it["W"], it["j"]
        c0 = c0_of(it)
        sb = sp_sb[i % 3]

        def fn(e):
            ins = None
            for hh in range(2):
                b_ = 2 + 2 * j + hh
                ins = e.matmul(ps[:, b_ * 512 + c0:b_ * 512 + W], lhsT=negU, rhs=sb[:, hh * 512 + c0:hh * 512 + W],
                               start=False, stop=False, skip_group_check=True)
            return ins
        S.add("pe", fn, reads=[("sp", i % 3), "cst"], writes=[("ps", 2 + 2 * j), ("ps", 3 + 2 * j)])

    def PE3(i):
        it = items[i]
        kb, W, pos0, ti, j = it["kb"], it["W"], it["pos0"], it["ti"], it["j"]
        c0 = c0_of(it)
        ob = 6 + j
        wb = w_sb[i % 3]
        par = ti % 2

        def fn(e):
            ins = None
            for hh in range(2):
                h = 2 * j + hh
                ins = e.matmul(ps[hh * 64:(hh + 1) * 64, ob * 512 + c0:ob * 512 + W], lhsT=v_sb[:, kb * 256 + h * 64:kb * 256 + (h + 1) * 64],
                               rhs=wb[:, hh * 512 + c0:hh * 512 + W], start=it["first"], stop=it["last"], skip_group_check=True)
            return ins
        S.add("pe", fn, reads=[("w", i % 3), "kall"], writes=[("ps", ob)])
        if it["last"]:
            S.add("dve", lambda e: e.tensor_copy(out=osb[par][:, j * 512:j * 512 + W], in_=bank(ob, W)),
                  reads=[("ps", ob)], writes=[("osb", par, j)])
            sfn, nd = store_fn(osb[par][:, j * 512:(j + 1) * 512], j * 128, ti)
            S.add("pool", sfn, reads=[("osb", par, j)], writes=[("mo", "a", j, ti)], chan=("osb", par, j), ndma=nd)
            if j == 1:
                maybe_gather(ti)

    PE1(0)
    for s in range(-1, NI + 1):
        if 0 <= s + 1 < NI:
            ACT12(s + 1)
        if s + 2 < NI:
            PE1(s + 2)
        if 0 <= s + 1 < NI:
            PE2(s + 1)
        if 0 <= s < NI:
            ACT3(s)
            PE4(s)
        if 0 <= s - 1 < NI:
            PE3(s - 1)

    S.add("pool", lambda e, s: e.collective_compute("AllGather", ALU.bypass, replica_groups=RG,
                                                    ins=[mo_halo_t.ap().opt()], outs=[ma_halo_t.ap().opt()]).then_inc(s),
          reads=tile_keys(0) + tile_keys(1 + (HALF - 2) // 512), writes=[("mall", "h")], chan="cch", inc=1)

    def mh_fn(e, s):
        half = e.partition_id() % 2
        ins = None
        for j in range(2):
            for r in range(2):
                ins = e.dma_start(out=mixed_half[r * 512:(r + 1) * 512, 2 + j * GW:2 + (j + 1) * GW],
                                  in_=ma_big[bass.ds(half * 2048 + j * 1024 + r * 512, 512), :]).then_inc(s, 16)
        ins = e.dma_start(out=mixed_half[:, 0:2], in_=ma_halo[:, bass.ds(half * 2, 2)]).then_inc(s, 16)
        return ins
    S.add("sp", mh_fn, reads=[("mall", g) for g in range(4)] + [("mall", "h")], writes=["mhalf"], chan="mh", ndma=5)

    S.barrier()
    A.off = mark_stage
    wout_bf = A.bf(8 * 1024)
    wfi_bf = A.bf(8 * 2 * DFF)
    wfo_bf = A.bf(NPAIR * 1024)
    mark3 = A.off
    stg3 = [A.f32(2816), A.f32(2816)]

    stg_ctr = [0]

    def load_cast(dst_ap, src_ap, ncols, key, scale_col=None):
        sl = stg_ctr[0] % 2
        stg_ctr[0] += 1
        S.add("sp", lambda e, s: e.dma_start(out=stg3[sl][:, 0:ncols], in_=src_ap).then_inc(s, 16),
              writes=[("stg3", sl)], chan=("stg3", sl))
        eng = ("dve", "act")[stg_ctr[0] % 2]
        if scale_col is None:
            if eng == "act":
                S.add(eng, lambda e: e.copy(out=dst_ap, in_=stg3[sl][:, 0:ncols]), reads=[("stg3", sl)], writes=[key])
            else:
                S.add(eng, lambda e: e.tensor_copy(out=dst_ap, in_=stg3[sl][:, 0:ncols]), reads=[("stg3", sl)], writes=[key])
        else:
            if eng == "act":
                S.add(eng, lambda e: e.activation(out=dst_ap, in_=stg3[sl][:, 0:ncols], func=AF.Copy, scale=scale_col),
                      reads=[("stg3", sl), "vec"], writes=[key])
            else:
                S.add(eng, lambda e: e.tensor_scalar(out=dst_ap, in0=stg3[sl][:, 0:ncols], scalar1=scale_col, scalar2=None, op0=ALU.mult),
                      reads=[("stg3", sl), "vec"], writes=[key])

    w3_keys = []
    for c in range(DC):
        load_cast(wout_bf[:, c * 1024:(c + 1) * 1024], w_out[c * 128:(c + 1) * 128, :], 1024, ("wout", c))
        w3_keys.append(("wout", c))
    for c in range(DC):
        for hh in range(2):
            load_cast(wfi_bf[:, c * 2 * DFF + hh * DFF:c * 2 * DFF + (hh + 1) * DFF], w_fi[c * 128:(c + 1) * 128, hh * DFF:(hh + 1) * DFF],
                      DFF, ("wfi", c, hh), scale_col=col(V_G2 + c))
            w3_keys.append(("wfi", c, hh))
    for j in range(0, NPAIR, 2):
        for jj in range(2):
            load_cast(wfo_bf[:, (j + jj) * 1024:(j + jj + 1) * 1024], w_fo[(j + jj) * 128:(j + jj + 1) * 128, :], 1024, ("wfo", j + jj))
            w3_keys.append(("wfo", j + jj))
    S.add("dve", lambda e: e.memset(ext[:, 10:11], 0.0), reads=w3_keys, writes=["w3all"])
    S.barrier()
    A.off = mark3
    h2 = [A.f32(1024) for _ in range(4)]
    hn2_bf = [A.bf(1024), A.bf(1024)]
    ss3 = A.f32(8)
    HW = 2 + W3
    hn2T = [A.bf(8 * HW), A.bf(8 * HW)]
    mt = A.bf(8 * W3)
    ubuf = [A.f32(2 + W3), A.f32(2 + W3)]
    gbuf = [A.f32(2 + W3), A.f32(2 + W3)]
    uc = [A.f32(W3), A.f32(W3)]
    gc = [A.f32(W3), A.f32(W3)]
    sg = [A.f32(W3), A.f32(W3)]
    act_bf = A.bf(NPAIR * W3)

    B_FO = [0, 1]
    B_T3 = 2
    B_U = [3, 4, 0]
    B_G = [5, 6, 1]
    B_WO = 7
    mixed_half_r = mixed_half.rearrange("(c p) w -> p c w", p=128)

    def tile_geom(t):
        if t < 0:
            return 0, 2, 2, 1, 1
        return 2 + t * W3, W3, 128, W3 // 128, t % 2

    def prep1(t):
        rel0, Wt, n, nsub, par = tile_geom(t)
        S.add("sp", lambda e, s: e.dma_start(out=mt.rearrange("p (c w) -> p c w", c=8)[:, :, 0:Wt],
                                              in_=mixed_half_r[:, :, rel0:rel0 + Wt]).then_inc(s, 16),
              reads=["mhalf"], writes=["mt"], chan="mt")
        for sub in range(nsub):
            sl = 2 * par + sub
            q = sub
            r = rel0 + sub * 128
            S.add("sp", lambda e, s, sl=sl, r=r: e.dma_start(out=h2[sl][0:n, :], in_=xhalf[r:r + n, :]).then_inc(s, 16),
                  writes=[("h2", sl)], chan=("h2", sl))
            for hf in range(2):
                def wo_fn(e, sub=sub, hf=hf):
                    ins = None
                    for c in range(DC):
                        ins = e.matmul(ps[0:n, B_WO * 512:(B_WO + 1) * 512], lhsT=mt[:, c * W3 + sub * 128:c * W3 + sub * 128 + n],
                                       rhs=wout_bf[:, c * 1024 + hf * 512:c * 1024 + (hf + 1) * 512], start=(c == 0), stop=(c == DC - 1))
                    return ins
                S.add("pe", wo_fn, reads=["mt", "w3all"], writes=[("ps", B_WO)])
                S.add("dve", lambda e, sl=sl, hf=hf: e.tensor_tensor(out=h2[sl][0:n, hf * 512:(hf + 1) * 512],
                                                                     in0=ps[0:n, B_WO * 512:(B_WO + 1) * 512],
                                                                     in1=h2[sl][0:n, hf * 512:(hf + 1) * 512], op=ALU.add),
                      reads=[("ps", B_WO), ("h2", sl)], writes=[("h2", sl)])
            S.add("pool", lambda e, q=q: e.memset(ss3[:, q:q + 1], 0.0), writes=[("ss3", q)])
            S.add("act", lambda e, sl=sl, q=q: e.activation(out=hn2_bf[q][0:n, :], in_=h2[sl][0:n, :], func=AF.Square,
                                                            accum_out=ss3[0:n, q:q + 1]),
                  reads=[("h2", sl)], writes=[("ss3", q), ("hn2", q)])
            S.add("dve", lambda e, q=q: e.tensor_scalar(out=ss3[0:n, 2 + q:3 + q], in0=ss3[0:n, q:q + 1], scalar1=1.0 / D, scalar2=EPS,
                                                         op0=ALU.mult, op1=ALU.add), reads=[("ss3", q)], writes=[("rs3", q)])
            S.add("act", lambda e, q=q: e.activation(out=ss3[0:n, 2 + q:3 + q], in_=ss3[0:n, 2 + q:3 + q], func=AF.Ln),
                  reads=[("rs3", q)], writes=[("rs3", q)])
            S.add("act", lambda e, q=q: e.activation(out=ss3[0:n, 2 + q:3 + q], in_=ss3[0:n, 2 + q:3 + q], func=AF.Exp, scale=-0.5),
                  reads=[("rs3", q)], writes=[("rs3", q)])
            S.add("act", lambda e, sl=sl, q=q: e.activation(out=hn2_bf[q][0:n, :], in_=h2[sl][0:n, :], func=AF.Copy, scale=ss3[0:n, 2 + q:3 + q]),
                  reads=[("h2", sl), ("rs3", q)], writes=[("hn2", q)])

    def prep2(t):
        rel0, Wt, n, nsub, par = tile_geom(t)
        for sub in range(nsub):
            q = sub

            def tr_fn(e, q=q):
                ins = None
                for c in range(DC):
                    ins = e.transpose(bank_bf(B_T3)[:, c * 128:c * 128 + n], hn2_bf[q][0:n, c * 128:(c + 1) * 128], ident[0:n, 0:n])
                return ins
            S.add("pe", tr_fn, reads=[("hn2", q), "cst"], writes=[("ps", B_T3)])
            dcol = 0 if t < 0 else 2 + sub * 128
            dpar = 0 if t < 0 else par
            S.add("dve", lambda e, dcol=dcol, dpar=dpar: e.tensor_copy(
                out=hn2T[dpar].rearrange("p (c w) -> p c w", c=8)[:, :, dcol:dcol + n],
                in_=bank_bf(B_T3).rearrange("p (c w) -> p c w", c=8)[:, :, 0:n]),
                reads=[("ps", B_T3)], writes=[("hn2T", dpar)])
        if t >= 0 and t + 1 < NST:
            S.add("dve", lambda e: e.tensor_copy(
                out=hn2T[1 - par].rearrange("p (c w) -> p c w", c=8)[:, :, 0:2],
                in_=hn2T[par].rearrange("p (c w) -> p c w", c=8)[:, :, W3:W3 + 2]),
                reads=[("hn2T", par)], writes=[("hn2T", 1 - par)])

    pair_ctr = [0]

    def ffn_in(t):
        rel0, Wt, n, nsub, par = tile_geom(t)
        final = t >= 0
        pcs = {}

        def mm(j):
            pc = pair_ctr[0] % 3
            pair_ctr[0] += 1
            pcs[j] = pc
            for which, bb in enumerate((B_U[pc], B_G[pc])):
                slab = j + which * NPAIR

                def fi_fn(e, bb=bb, slab=slab):
                    ins = None
                    for c in range(DC):
                        ins = e.matmul(bank(bb, Wt + 2), lhsT=wfi_bf[:, c * 2 * DFF + slab * 128:c * 2 * DFF + (slab + 1) * 128],
                                       rhs=hn2T[par][:, c * HW:c * HW + Wt + 2], start=(c == 0), stop=(c == DC - 1))
                    return ins
                S.add("pe", fi_fn, reads=[("hn2T", par), "w3all"], writes=[("ps", bb)])

        def post1(j):
            pc = pcs[j]
            p2 = j % 2
            for which, (bb, buf, cbuf) in enumerate(((B_U[pc], ubuf[p2], uc[p2]), (B_G[pc], gbuf[p2], gc[p2]))):
                slab = j + which * NPAIR
                bk = ("ub", which, p2)
                ck = ("cb", which, p2)
                S.add("act", lambda e, bb=bb, buf=buf: e.copy(out=buf[:, 0:Wt + 2], in_=bank(bb, Wt + 2)), reads=[("ps", bb)], writes=[bk])
                S.add("act", lambda e, buf=buf, cbuf=cbuf, slab=slab: e.activation(
                    out=cbuf[:, 0:Wt], in_=buf[:, 2:2 + Wt], func=AF.Identity, scale=col(V_FCW + slab * 3 + 2), bias=col(V_FCB + slab)),
                    reads=[bk, "vec"], writes=[ck])
                for k in range(2):
                    S.add("dve", lambda e, buf=buf, cbuf=cbuf, slab=slab, k=k: e.scalar_tensor_tensor(
                        out=cbuf[:, 0:Wt], in0=buf[:, k:k + Wt], scalar=col(V_FCW + slab * 3 + k), in1=cbuf[:, 0:Wt],
                        op0=ALU.mult, op1=ALU.add), reads=[bk, "vec", ck], writes=[ck])

        def post2(j):
            pc = j % 2
            S.add("act", lambda e: e.activation(out=sg[pc][:, 0:Wt], in_=gc[pc][:, 0:Wt], func=AF.Silu),
                  reads=[("cb", 1, pc)], writes=[("sg", pc)])
            S.add("dve", lambda e: e.tensor_tensor(out=act_bf[:, j * W3:j * W3 + Wt], in0=sg[pc][:, 0:Wt], in1=uc[pc][:, 0:Wt], op=ALU.mult),
                  reads=[("sg", pc), ("cb", 0, pc)], writes=[("actT", j)])

        for j in range(NPAIR + 2):
            if j < NPAIR:
                mm(j)
            if 0 <= j - 1 < NPAIR:
                post1(j - 1)
            if 0 <= j - 2 < NPAIR:
                post2(j - 2)

    def ffn_out(t, sub):
        rel0, Wt, n, nsub, par = tile_geom(t)
        sl = 2 * par + sub

        def fo_fn(e):
            ins = None
            for hf in range(2):
                for j in range(NPAIR):
                    ins = e.matmul(bank(B_FO[hf], 512), lhsT=act_bf[:, j * W3 + sub * 128:j * W3 + (sub + 1) * 128],
                                   rhs=wfo_bf[:, j * 1024 + hf * 512:j * 1024 + (hf + 1) * 512], start=(j == 0), stop=(j == NPAIR - 1))
            return ins
        S.add("pe", fo_fn, reads=[("actT", j) for j in range(NPAIR)] + ["w3all"], writes=[("ps", 0), ("ps", 1)])
        S.add("dve", lambda e: e.tensor_tensor(out=h2[sl][:, :], in0=ps[:, 0:1024], in1=h2[sl][:, :], op=ALU.add),
              reads=[("ps", 0), ("ps", 1), ("h2", sl)], writes=[("h2", sl)])
        r0 = rel0 - 2 + sub * 128
        S.add("pool", lambda e, s: e.dma_start(out=out[r0:r0 + 128, :], in_=h2[sl][:, :]).then_inc(s, 16),
              reads=[("h2", sl)], writes=[("h2", sl), ("out", r0)], chan=("h2", sl))

    prep1(-1)
    prep2(-1)
    prep1(0)
    prep2(0)
    for t in range(NST):
        ffn_in(t)
        if t + 1 < NST:
            prep1(t + 1)
        ffn_out(t, 0)
        if t + 1 < NST:
            prep2(t + 1)
        ffn_out(t, 1)

    S.add("sp", lambda e: None, reads=[("out", r0) for r0 in range(0, HALF, 128)], writes=["done"])

    S.finalize()
    chans = list(S.chan_count.keys())
    sem_ctxs = []
    sems = {}
    for e in Sched.ENG:
        c = nc.semaphore("s_" + e)
        sems[e] = c.__enter__()
        sem_ctxs.append(c)
    chan_sems = {}
    for i, ch in enumerate(chans):
        c = nc.semaphore("c_%d" % i)
        chan_sems[ch] = c.__enter__()
        sem_ctxs.append(c)
    with nc.Block() as block:
        S.emit(nc, block, sems, chan_sems)
    for c in reversed(sem_ctxs):
        c.__exit__(None, None, None)
    pctx.__exit__(None, None, None)
    ctx.__exit__(None, None, None)
    return nc


def _consts():
    c = np.zeros((128, NCST), np.float32)
    j = np.arange(128)[:, None]
    s = np.arange(128)[None, :]
    c[:, C_ID:C_ID + 128] = (j == s)
    c[:, C_NL:C_NL + 128] = -(j >= s).astype(np.float32)
    c[:, C_NU:C_NU + 128] = -(j < s).astype(np.float32)
    c[:, C_BO:C_BO + 128] = ((j // 64) == (s // 64))
    t = np.arange(512)[None, :]
    for k in range(4):
        c[:, C_MK + k * 512:C_MK + (k + 1) * 512] = ((128 * k + j) < t)
    return c


def _prep_inputs(inputs, SEQ):
    f = lambda a: np.ascontiguousarray(np.asarray(a), dtype=np.float32)
    x = f(inputs["x"])
    meta = f(inputs["meta_tokens"])
    w_in = f(inputs["w_in"])[0]
    w_out = f(inputs["w_out"])[0]
    w_fi = f(inputs["w_ffn_in"])[0]
    w_fo = f(inputs["w_ffn_out"])[0]
    g1 = f(inputs["norm1_g"])[0]
    g2 = f(inputs["norm2_g"])[0]
    qg = f(inputs["q_norm_g"])[0]
    kg = f(inputs["k_norm_g"])[0]
    cw = f(inputs["conv_w"])[0]
    cb = f(inputs["conv_b"])[0]
    wa = f(inputs["w_rg_a"])[0]
    wi = f(inputs["w_rg_i"])[0]
    ba = f(inputs["b_rg_a"])[0]
    bi = f(inputs["b_rg_i"])[0]
    lam = f(inputs["lru_lambda"])[0]
    fcw = f(inputs["ffn_conv_w"])[0]
    fcb = f(inputs["ffn_conv_b"])[0]
    consts = _consts()
    TP = SEQ + 128
    maps = []
    for core in range(8):
        b, p = core // 2, core % 2
        xpad = np.zeros((TP, D), np.float32)
        xpad[PAD:128] = meta
        xpad[128:] = x[b]
        cs = slice(256 * p, 256 * p + 256)
        wic = np.concatenate([w_in[:, 0:512][:, cs], w_in[:, 512:1024][:, cs], w_in[:, 1024:1536][:, cs],
                              w_in[:, 1536:2048][:, cs], w_in[:, 2048:2560][:, cs]], axis=1)
        vec = np.zeros((128, NV), np.float32)
        vec[:, V_G1:V_G1 + 8] = g1.reshape(8, 128).T
        vec[:, V_G2:V_G2 + 8] = g2.reshape(8, 128).T
        vec[:, V_QG] = np.tile(qg, 2)
        vec[:, V_KG] = np.tile(kg, 2)
        for c in range(2):
            ch = slice(256 * p + 128 * c, 256 * p + 128 * c + 128)
            vec[:, V_CW + c * 4:V_CW + c * 4 + 4] = cw[:, ch].T
            vec[:, V_CB + c] = cb[ch]
            vec[:, V_BA + c] = ba[ch]
            vec[:, V_BI + c] = bi[ch]
            vec[:, V_LAM + c] = lam[ch]
        for s in range(NSLAB):
            vec[:, V_FCW + s * 3:V_FCW + s * 3 + 3] = fcw[:, s * 128:(s + 1) * 128].T
            vec[:, V_FCB + s] = fcb[s * 128:(s + 1) * 128]
        wrg = np.zeros((128, 4 * 128), np.float32)
        for gi, wsrc in enumerate((wa, wi)):
            for c in range(2):
                for k in range(2):
                    blk = 4 * p + 2 * c + k
                    o = (gi * 2 + c) * 128
                    wrg[64 * k:64 * k + 64, o + 64 * k:o + 64 * k + 64] = wsrc[blk]
        perm = []
        for r in range(2):
            perm += list(range(256 * r, 256 * r + 256))
            perm += list(range(512 + 256 * r, 512 + 256 * r + 256))
        maps.append({
            "xpad": xpad, "xhalf": np.ascontiguousarray(xpad[126 + (SEQ // 2) * p:126 + (SEQ // 2) * p + SEQ // 2 + 2]), "w_in": np.ascontiguousarray(wic), "vecs": vec, "w_rg": wrg,
            "w_out": np.ascontiguousarray(w_out[perm]), "w_ffn_in": w_fi, "w_ffn_out": w_fo, "consts": consts,
        })
    return maps


_NC_CACHE = {}


def kernel(**inputs):
    x = np.asarray(inputs["x"])
    B, SEQ, _ = x.shape
    assert B == 4
    if SEQ not in _NC_CACHE:
        _NC_CACHE[SEQ] = build_nc(SEQ)
    nc = _NC_CACHE[SEQ]
    maps = _prep_inputs(inputs, SEQ)
    res = run_bass_kernel_spmd(nc, maps, core_ids=list(range(8)))
    outp = np.empty((B, SEQ, D), np.float32)
    HALF = SEQ // 2
    for core in range(8):
        b, p = core // 2, core % 2
        outp[b, p * HALF:(p + 1) * HALF] = res.results[core]["out"]
    return outp
```

```python
import numpy as np
import concourse.bass as bass
import concourse.mybir as mybir
from concourse.bass_utils import run_bass_kernel_spmd

F32 = mybir.dt.float32
BF16 = mybir.dt.bfloat16
AF = mybir.ActivationFunctionType
ALU = mybir.AluOpType

D = 1024
DC = 8
DFF = 2816
NSLAB = 44
NPAIR = 22
EPS = 1e-6
NMETA = 16
PAD = 112
GELU_C = 0.7978845608028654

V_G1, V_G2, V_QG, V_KG, V_CW, V_CB, V_BA, V_BI, V_LAM, V_FCW, V_FCB = 0, 8, 16, 17, 18, 26, 28, 30, 32, 34, 166
NV = 210
C_ID, C_NL, C_NU, C_BO, C_MK = 0, 128, 256, 384, 512
NCST = 512 + 4 * 512


class Sched:
    ENG = ("pe", "act", "dve", "pool", "sp")

    def __init__(self):
        self.ops = []
        self.last_w = {}
        self.readers = {}
        self.eng_count = {e: 0 for e in self.ENG}
        self.chan_count = {}
        self.chan_inc = {}
        self.barrier_nodes = []

    def add(self, eng, fn, reads=(), writes=(), chan=None, ndma=1, inc=16):
        deps = set(self.barrier_nodes)
        for k in reads:
            w = self.last_w.get(k)
            if w is not None:
                deps.add(w)
        for k in writes:
            w = self.last_w.get(k)
            if w is not None:
                deps.add(w)
            for r in self.readers.get(k, ()):
                deps.add(r)
        if chan is None:
            self.eng_count[eng] += 1
            node = ("E", eng, self.eng_count[eng])
        else:
            self.chan_inc[chan] = inc
            self.chan_count[chan] = self.chan_count.get(chan, 0) + ndma * inc
            node = ("C", chan, self.chan_count[chan])
        self.ops.append(dict(eng=eng, fn=fn, deps=deps, node=node, chan=chan))
        for k in reads:
            self.readers.setdefault(k, []).append(node)
        for k in writes:
            self.last_w[k] = node
            self.readers[k] = []
        return node

    def barrier(self):
        nodes = []
        for e, c in self.eng_count.items():
            if c:
                nodes.append(("E", e, c))
        for ch, c in self.chan_count.items():
            nodes.append(("C", ch, c))
        self.barrier_nodes = nodes

    def finalize(self):
        known = {e: {} for e in self.ENG}
        signal = {e: set() for e in self.ENG}
        for op in self.ops:
            need = {}
            for kind, tgt, val in op["deps"]:
                if kind == "E" and tgt == "pe" and op["eng"] == "pe" and op["chan"] is None:
                    continue
                key = (kind, tgt)
                if val > need.get(key, 0):
                    need[key] = val
            waits = []
            kn = known[op["eng"]]
            for key, val in need.items():
                if kn.get(key, 0) >= val:
                    continue
                kn[key] = val
                waits.append((key, val))
                if key[0] == "E":
                    signal[key[1]].add(val)
            op["waits"] = waits
        self.rank = {}
        for e in self.ENG:
            self.rank[e] = {v: i + 1 for i, v in enumerate(sorted(signal[e]))}

    def emit(self, nc, block, sems, chan_sems):
        streams = {e: [op for op in self.ops if op["eng"] == e] for e in self.ENG}

        def run(eng_name, eng):
            for op in streams[eng_name]:
                for (kind, tgt), val in op["waits"]:
                    if kind == "E":
                        eng.wait_ge(sems[tgt], self.rank[tgt][val])
                    else:
                        eng.wait_ge(chan_sems[tgt], val)
                if op["chan"] is not None:
                    op["fn"](eng, chan_sems[op["chan"]])
                else:
                    ins = op["fn"](eng)
                    idx = op["node"][2]
                    if idx in self.rank[eng_name]:
                        assert ins is not None
                        ins.then_inc(sems[eng_name], 1)

        @block.tensor
        def _(e):
            run("pe", e)

        @block.scalar
        def _(e):
            run("act", e)

        @block.vector
        def _(e):
            run("dve", e)

        @block.gpsimd
        def _(e):
            run("pool", e)

        @block.sync
        def _(e):
            run("sp", e)


def build_nc(SEQ):
    assert SEQ % 1024 == 0
    HALF = SEQ // 2
    TP = SEQ + 128
    NB = TP // 128
    NQT = 1 + SEQ // 512
    NST = HALF // 256
    W3 = 256

    nc = bass.Bass("TRN2", target_bir_lowering=False)
    xpad = nc.dram_tensor("xpad", [TP, D], F32, kind="ExternalInput").ap()
    w_in = nc.dram_tensor("w_in", [D, 1280], F32, kind="ExternalInput").ap()
    vecs = nc.dram_tensor("vecs", [128, NV], F32, kind="ExternalInput").ap()
    w_rg = nc.dram_tensor("w_rg", [128, 4 * 128], F32, kind="ExternalInput").ap()
    w_out = nc.dram_tensor("w_out", [D, D], F32, kind="ExternalInput").ap()
    w_fi = nc.dram_tensor("w_ffn_in", [D, 2 * DFF], F32, kind="ExternalInput").ap()
    w_fo = nc.dram_tensor("w_ffn_out", [DFF, D], F32, kind="ExternalInput").ap()
    consts = nc.dram_tensor("consts", [128, NCST], F32, kind="ExternalInput").ap()
    out = nc.dram_tensor("out", [HALF, D], F32, kind="ExternalOutput").ap()
    GW = SEQ // 4
    mo_g = [nc.dram_tensor("mixed_own_%d" % g, [512, GW], BF16) for g in range(4)]
    mo_halo_t = nc.dram_tensor("mixed_own_halo", [512, 4], BF16)
    ma_big_t = nc.dram_tensor("mixed_all_big", [4 * 1024, GW], BF16)
    ma_halo_t = nc.dram_tensor("mixed_all_halo", [1024, 4], BF16)
    xhalf = nc.dram_tensor("xhalf", [HALF + 2, D], F32, kind="ExternalInput").ap()
    mh_t = nc.dram_tensor("mixed_half", [1024, HALF + 2], BF16)
    mixed_half = mh_t.ap()
    ma_big = ma_big_t.ap()
    ma_halo = ma_halo_t.ap()
    mo_halo = mo_halo_t.ap()

    def store_fn(src, row0, ti):
        pieces = []
        if ti == 0:
            pieces.append((mo_halo[row0:row0 + 128, 0:2], 126, 2))
        else:
            i0 = 512 * (ti - 1)
            c = 0
            while c < 512:
                g = (i0 + c) // GW
                gc = (i0 + c) % GW
                n = min(512 - c, GW - gc)
                pieces.append((mo_g[g].ap()[row0:row0 + 128, gc:gc + n], c, n))
                c += n
            if i0 <= HALF - 2 < i0 + 512:
                pieces.append((mo_halo[row0:row0 + 128, 2:4], HALF - 2 - i0, 2))

        def fn(e, s):
            ins = None
            for dst, c0, n in pieces:
                ins = e.dma_start(out=dst, in_=src[:, c0:c0 + n]).then_inc(s, 16)
            return ins
        return fn, len(pieces)

    S = Sched()
    ARENA_F = 53200

    ctx = nc.sbuf_tensor("arena", [128, ARENA_F], F32)
    arena = ctx.__enter__()
    pctx = nc.psum_tensor("ps", [128, 8 * 512], F32)
    ps = pctx.__enter__()

    class Arena:
        def __init__(self):
            self.off = 0

        def f32(self, n):
            o = self.off
            self.off += n
            assert self.off <= ARENA_F, self.off
            return arena[:, o:o + n]

        def bf(self, n):
            nf = (n + 1) // 2
            o = self.off
            self.off += nf
            assert self.off <= ARENA_F, self.off
            return arena[:, o:o + nf].bitcast(BF16)

    A = Arena()

    def bank(b, n=512):
        return ps[:, b * 512:b * 512 + n]

    def bank_bf(b):
        return ps[:, b * 512:(b + 1) * 512].bitcast(BF16)

    cst = A.bf(NCST)
    vec = A.f32(NV)
    ext = A.f32(16)
    ident = cst[:, C_ID:C_ID + 128]
    negL = cst[:, C_NL:C_NL + 128]
    negU = cst[:, C_NU:C_NU + 128]
    bones = cst[:, C_BO:C_BO + 128]

    def maskv(j, W):
        return cst[:, C_MK + j * 512:C_MK + j * 512 + W]

    mark_stage = A.off

    qT = A.bf(2 * TP)
    kT = A.bf(2 * TP)
    v_sb = A.bf(NB * 256)
    mark12 = A.off
    win_bf = A.bf(8 * 1280)
    wrg_bf = A.bf(4 * 128)
    stg = [A.f32(1280), A.f32(1280)]
    xs = [A.f32(1024), A.f32(1024)]
    hn_bf = [A.bf(1024), A.bf(1024)]
    ssb = A.f32(8)
    hnT = [A.bf(8 * 512), A.bf(8 * 512)]
    sqb = [A.bf(512), A.bf(512)]
    rqb = [A.f32(512), A.f32(512)]
    xr_sb = [A.f32(3 + 512), A.f32(3 + 512)]
    xc_sb = [A.f32(512), A.f32(512)]
    xc_bf = [A.bf(512), A.bf(512)]
    tr_sb = [A.f32(512), A.f32(512)]
    ti_sb = [A.f32(512), A.f32(512)]
    a_sb = [A.f32(512), A.f32(512)]
    m2_sb = [A.f32(512), A.f32(512)]
    bt_sb = [A.f32(512), A.f32(512)]
    hl_sb = [A.f32(512), A.f32(512)]
    hst = A.f32(2)
    y_sb = [stg[0][:, 0:512], stg[0][:, 512:1024]]
    y2_sb = [stg[1][:, 0:512], stg[1][:, 512:1024]]
    ol_bf = [stg[0][:, 1024:1280].bitcast(BF16), stg[1][:, 1024:1280].bitcast(BF16)]

    def col(i, n=1):
        return vec[:, i:i + n]

    for hh in range(2):
        S.add("sp", lambda e, s, hh=hh: e.dma_start(out=stg[hh][:, 0:1280], in_=consts[:, hh * 1280:(hh + 1) * 1280]).then_inc(s, 16),
              writes=[("stg", hh)], chan=("stg", hh))
        S.add("dve", lambda e, hh=hh: e.tensor_copy(out=cst[:, hh * 1280:(hh + 1) * 1280], in_=stg[hh][:, 0:1280]),
              reads=[("stg", hh)], writes=["cst"])
    S.add("sp", lambda e, s: e.dma_start(out=vec[:, :], in_=vecs[:, :]).then_inc(s, 16), writes=["vec"], chan="vec")
    S.add("act", lambda e: e.activation(out=ext[:, 7:9], in_=col(V_LAM, 2), func=AF.Exp, scale=-1.0),
          reads=["vec"], writes=["ext_t"])
    S.add("act", lambda e: e.activation(out=ext[:, 7:9], in_=ext[:, 7:9], func=AF.Ln, bias=1.0),
          reads=["ext_t"], writes=["ext_t"])
    S.add("dve", lambda e: e.tensor_scalar(out=ext[:, 0:2], in0=ext[:, 7:9], scalar1=-8.0, scalar2=None, op0=ALU.mult),
          reads=["ext_t"], writes=["ext"])
    S.add("dve", lambda e: e.tensor_scalar(out=ext[:, 2:6], in0=col(V_BA, 4), scalar1=-1.0, scalar2=None, op0=ALU.mult),
          reads=["vec", "ext"], writes=["ext"])
    S.add("dve", lambda e: e.tensor_scalar(out=ext[:, 6:7], in0=col(V_KG), scalar1=0.125, scalar2=None, op0=ALU.mult),
          reads=["vec", "ext"], writes=["ext"])
    S.add("sp", lambda e, s: e.dma_start(out=stg[0][:, 0:512], in_=w_rg[:, :]).then_inc(s, 16),
          writes=[("stg", 0)], chan=("stg", 0))
    S.add("dve", lambda e: e.tensor_copy(out=wrg_bf[:, :], in_=stg[0][:, 0:512]), reads=[("stg", 0)], writes=["wrg"])
    for c in range(DC):
        hh = c % 2
        S.add("sp", lambda e, s, c=c, hh=hh: e.dma_start(out=stg[hh][:, :], in_=w_in[c * 128:(c + 1) * 128, :]).then_inc(s, 16),
              writes=[("stg", hh)], chan=("stg", hh))
        if c % 2 == 0:
            S.add("dve", lambda e, c=c, hh=hh: e.tensor_scalar(out=win_bf[:, c * 1280:(c + 1) * 1280], in0=stg[hh][:, :],
                                                                scalar1=col(V_G1 + c), scalar2=None, op0=ALU.mult),
                  reads=[("stg", hh), "vec"], writes=[("win", c)])
        else:
            S.add("act", lambda e, c=c, hh=hh: e.activation(out=win_bf[:, c * 1280:(c + 1) * 1280], in_=stg[hh][:, :],
                                                             func=AF.Copy, scale=col(V_G1 + c)),
                  reads=[("stg", hh), "vec"], writes=[("win", c)])
    S.add("pool", lambda e: e.memset(xr_sb[0][:, 0:3], 0.0), writes=[("xr", 0)])
    S.add("pool", lambda e: e.memset(xr_sb[1][:, 0:3], 0.0), writes=[("xr", 1)])
    S.add("pool", lambda e: e.memset(hst[:, :], 0.0), writes=["hst"])

    win_reads = [("win", c) for c in range(DC)]

    B_TRP, B_V, B_PJ0, B_PJ1 = 0, 1, 2, 3
    B_PS2 = [4, 5]
    B_GR = [4, 6]
    B_GI = [5, 7]

    def tile_info(ti):
        if ti == 0:
            return 0, 128
        return 128 + 512 * (ti - 1), 512

    class Rec:
        def __init__(self):
            self.l = []

        def add(self, *a_, **k_):
            self.l.append((a_, k_))

    def zip_emit(chains):
        idx = [0] * len(chains)
        left = True
        while left:
            left = False
            for ci, ch in enumerate(chains):
                if idx[ci] < len(ch.l):
                    a_, k_ = ch.l[idx[ci]]
                    S.add(*a_, **k_)
                    idx[ci] += 1
                    left = True

    def proj(SS, b, col0, W, hp):
        def fn(e):
            ins = None
            for c in range(DC):
                ins = e.matmul(bank(b, W), lhsT=win_bf[:, c * 1280 + col0:c * 1280 + col0 + 128],
                               rhs=hnT[hp][:, c * 512:c * 512 + W], start=(c == 0), stop=(c == DC - 1))
            return ins
        SS.add("pe", fn, reads=win_reads + [("hnT", hp)], writes=[("ps", b)])

    def prep_chain(SS, ti, sub):
        pos0, W = tile_info(ti)
        hp = ti % 2
        blk = pos0 // 128 + sub
        sl = blk % 2
        SS.add("sp", lambda e, s: e.dma_start(out=xs[sl][:, :], in_=xpad[blk * 128:(blk + 1) * 128, :]).then_inc(s, 16),
               writes=[("xs", sl)], chan=("xs", sl))
        SS.add("pool", lambda e: e.memset(ssb[:, sl:sl + 1], 0.0), writes=[("ss", sl)])
        SS.add("act", lambda e: e.activation(out=hn_bf[sl][:, :], in_=xs[sl][:, :], func=AF.Square, accum_out=ssb[:, sl:sl + 1]),
               reads=[("xs", sl)], writes=[("ss", sl), ("hn", sl)])
        SS.add("dve", lambda e: e.tensor_scalar(out=ssb[:, 2 + sl:3 + sl], in0=ssb[:, sl:sl + 1], scalar1=1.0 / D, scalar2=EPS,
                                                op0=ALU.mult, op1=ALU.add),
               reads=[("ss", sl)], writes=[("rstd", sl)])
        SS.add("act", lambda e: e.activation(out=ssb[:, 2 + sl:3 + sl], in_=ssb[:, 2 + sl:3 + sl], func=AF.Ln),
               reads=[("rstd", sl)], writes=[("rstd", sl)])
        SS.add("act", lambda e: e.activation(out=ssb[:, 2 + sl:3 + sl], in_=ssb[:, 2 + sl:3 + sl], func=AF.Exp, scale=-0.5),
               reads=[("rstd", sl)], writes=[("rstd", sl)])
        SS.add("act", lambda e: e.activation(out=hn_bf[sl][:, :], in_=xs[sl][:, :], func=AF.Copy, scale=ssb[:, 2 + sl:3 + sl]),
               reads=[("xs", sl), ("rstd", sl)], writes=[("hn", sl)])

        def tr_fn(e):
            ins = None
            for c in range(DC):
                ins = e.transpose(bank_bf(B_TRP)[:, c * 128:(c + 1) * 128], hn_bf[sl][:, c * 128:(c + 1) * 128], ident)
            return ins
        SS.add("pe", tr_fn, reads=[("hn", sl), "cst"], writes=[("ps", B_TRP)])
        SS.add("dve", lambda e: e.tensor_copy(
            out=hnT[hp].rearrange("p (c w) -> p c w", c=8)[:, :, sub * 128:(sub + 1) * 128],
            in_=bank_bf(B_TRP).rearrange("p (c w) -> p c w", c=8)),
            reads=[("ps", B_TRP)], writes=[("hnT", hp)])

        def v_fn(e):
            ins = None
            for c in range(DC):
                ins = e.matmul(bank(B_V, 256), lhsT=hnT[hp][:, c * 512 + sub * 128:c * 512 + (sub + 1) * 128],
                               rhs=win_bf[:, c * 1280 + 512:c * 1280 + 768], start=(c == 0), stop=(c == DC - 1))
            return ins
        SS.add("pe", v_fn, reads=win_reads + [("hnT", hp)], writes=[("ps", B_V)])
        SS.add("act", lambda e: e.copy(out=v_sb[:, blk * 256:(blk + 1) * 256], in_=bank(B_V, 256)),
               reads=[("ps", B_V)], writes=[("v", blk)])

    def qk_chain(SS, ti, which, j):
        pos0, W = tile_info(ti)
        hp = ti % 2
        b = B_PJ0 + j
        p2 = B_PS2[j]
        sq = sqb[j]
        rq = rqb[j]
        proj(SS, b, which * 256 + j * 128, W, hp)
        SS.add("act", lambda e: e.activation(out=sq[:, 0:W], in_=bank(b, W), func=AF.Square), reads=[("ps", b)], writes=[("sqb", j)])
        SS.add("pe", lambda e: e.matmul(bank(p2, W), lhsT=bones, rhs=sq[:, 0:W], start=True, stop=True),
               reads=[("sqb", j), "cst"], writes=[("ps", p2)])
        SS.add("dve", lambda e: e.tensor_scalar(out=rq[:, 0:W], in0=bank(p2, W), scalar1=1.0 / 64, scalar2=EPS,
                                                op0=ALU.mult, op1=ALU.add), reads=[("ps", p2)], writes=[("rqb", j)])
        SS.add("act", lambda e: e.activation(out=rq[:, 0:W], in_=rq[:, 0:W], func=AF.Ln), reads=[("rqb", j)], writes=[("rqb", j)])
        SS.add("act", lambda e: e.activation(out=rq[:, 0:W], in_=rq[:, 0:W], func=AF.Exp, scale=-0.5), reads=[("rqb", j)], writes=[("rqb", j)])
        dst = qT if which == 0 else kT
        gsc = col(V_QG) if which == 0 else ext[:, 6:7]
        SS.add("dve", lambda e: e.scalar_tensor_tensor(
            out=dst[:, j * TP + pos0:j * TP + pos0 + W], in0=bank(b, W), scalar=gsc, in1=rq[:, 0:W], op0=ALU.mult, op1=ALU.mult),
            reads=[("ps", b), ("rqb", j), "vec", "ext"], writes=[("qk", which, j, ti)])

    def xr_chain(SS, ti, c):
        pos0, W = tile_info(ti)
        hp = ti % 2
        b = B_PJ0 + c
        gr, gi = B_GR[c], B_GI[c]
        xc, xcb, tr, tg_, a_, m2, bt = xc_sb[c], xc_bf[c], tr_sb[c], ti_sb[c], a_sb[c], m2_sb[c], bt_sb[c]
        K = lambda n: (n, c)
        proj(SS, b, 768 + c * 128, W, hp)
        SS.add("act", lambda e: e.copy(out=xr_sb[c][:, 3:3 + W], in_=bank(b, W)), reads=[("ps", b)], writes=[("xr", c)])
        SS.add("dve", lambda e: e.tensor_scalar(out=xc[:, 0:W], in0=xr_sb[c][:, 3:3 + W], scalar1=col(V_CW + c * 4 + 3),
                                                scalar2=col(V_CB + c), op0=ALU.mult, op1=ALU.add),
               reads=[("xr", c), "vec"], writes=[K("xc")])
        for k in range(3):
            SS.add("dve", lambda e, k=k: e.scalar_tensor_tensor(out=xc[:, 0:W], in0=xr_sb[c][:, k:k + W], scalar=col(V_CW + c * 4 + k),
                                                                 in1=xc[:, 0:W], op0=ALU.mult, op1=ALU.add),
                   reads=[("xr", c), "vec", K("xc")], writes=[K("xc")])
        SS.add("pool", lambda e: e.tensor_copy(out=xr_sb[c][:, 0:3], in_=xr_sb[c][:, W:W + 3]), reads=[("xr", c)], writes=[("xr", c)])
        SS.add("act", lambda e: e.copy(out=xcb[:, 0:W], in_=xc[:, 0:W]), reads=[K("xc")], writes=[K("xcb")])
        SS.add("pe", lambda e: e.matmul(bank(gr, W), lhsT=wrg_bf[:, (0 * 2 + c) * 128:(0 * 2 + c + 1) * 128], rhs=xcb[:, 0:W],
                                        start=True, stop=True), reads=[K("xcb"), "wrg"], writes=[("ps", gr)])
        SS.add("pe", lambda e: e.matmul(bank(gi, W), lhsT=wrg_bf[:, (1 * 2 + c) * 128:(1 * 2 + c + 1) * 128], rhs=xcb[:, 0:W],
                                        start=True, stop=True), reads=[K("xcb"), "wrg"], writes=[("ps", gi)])
        SS.add("act", lambda e: e.activation(out=tr[:, 0:W], in_=bank(gr, W), func=AF.Exp, bias=ext[:, 2 + c:3 + c], scale=-1.0),
               reads=[("ps", gr), "ext"], writes=[K("tr")])
        SS.add("act", lambda e: e.activation(out=tg_[:, 0:W], in_=bank(gi, W), func=AF.Exp, bias=ext[:, 4 + c:5 + c], scale=-1.0),
               reads=[("ps", gi), "ext"], writes=[K("tig")])
        SS.add("act", lambda e: e.activation(out=tr[:, 0:W], in_=tr[:, 0:W], func=AF.Ln, bias=1.0), reads=[K("tr")], writes=[K("tr")])
        SS.add("act", lambda e: e.activation(out=tg_[:, 0:W], in_=tg_[:, 0:W], func=AF.Ln, bias=1.0), reads=[K("tig")], writes=[K("tig")])
        SS.add("act", lambda e: e.activation(out=tr[:, 0:W], in_=tr[:, 0:W], func=AF.Exp, scale=-1.0), reads=[K("tr")], writes=[K("tr")])
        SS.add("act", lambda e: e.activation(out=tg_[:, 0:W], in_=tg_[:, 0:W], func=AF.Exp, scale=-1.0), reads=[K("tig")], writes=[K("tig")])
        SS.add("act", lambda e: e.activation(out=a_[:, 0:W], in_=tr[:, 0:W], func=AF.Exp, scale=ext[:, c:c + 1]),
               reads=[K("tr"), "ext"], writes=[K("a")])
        SS.add("dve", lambda e: e.tensor_tensor(out=m2[:, 0:W], in0=a_[:, 0:W], in1=a_[:, 0:W], op=ALU.mult), reads=[K("a")], writes=[K("m2")])
        SS.add("dve", lambda e: e.tensor_scalar(out=m2[:, 0:W], in0=m2[:, 0:W], scalar1=-1.0, scalar2=1.0, op0=ALU.mult, op1=ALU.add),
               reads=[K("m2")], writes=[K("m2")])
        SS.add("act", lambda e: e.activation(out=m2[:, 0:W], in_=m2[:, 0:W], func=AF.Ln), reads=[K("m2")], writes=[K("m2")])
        SS.add("act", lambda e: e.activation(out=m2[:, 0:W], in_=m2[:, 0:W], func=AF.Exp, scale=0.5), reads=[K("m2")], writes=[K("m2")])
        SS.add("dve", lambda e: e.tensor_tensor(out=bt[:, 0:W], in0=tg_[:, 0:W], in1=xc[:, 0:W], op=ALU.mult),
               reads=[K("tig"), K("xc")], writes=[K("bt")])
        SS.add("dve", lambda e: e.tensor_tensor(out=bt[:, 0:W], in0=bt[:, 0:W], in1=m2[:, 0:W], op=ALU.mult),
               reads=[K("bt"), K("m2")], writes=[K("bt")])
        if ti == 0:
            SS.add("dve", lambda e: e.memset(bt[:, 0:PAD], 0.0), reads=[K("bt")], writes=[K("bt")])
        SS.add("dve", lambda e: e.tensor_tensor_scan(out=hl_sb[c][:, 0:W], data0=a_[:, 0:W], data1=bt[:, 0:W],
                                                     initial=hst[:, c:c + 1], op0=ALU.mult, op1=ALU.add),
               reads=[K("a"), K("bt"), K("hst")], writes=[("hl", c)])
        SS.add("dve", lambda e: e.tensor_copy(out=hst[:, c:c + 1], in_=hl_sb[c][:, W - 1:W]), reads=[("hl", c)], writes=[K("hst")])

    def yg_chain(SS, ti, c):
        pos0, W = tile_info(ti)
        hp = ti % 2
        b = B_PJ0 + c
        y, y2, tg = y_sb[c], y2_sb[c], y2_sb[c]
        K = lambda n: (n, c)
        proj(SS, b, 1024 + c * 128, W, hp)
        SS.add("act", lambda e: e.copy(out=y[:, 0:W], in_=bank(b, W)), reads=[("ps", b)], writes=[K("y")])
        SS.add("act", lambda e: e.activation(out=y2[:, 0:W], in_=bank(b, W), func=AF.Square), reads=[("ps", b)], writes=[K("y2")])
        SS.add("dve", lambda e: e.tensor_scalar(out=y2[:, 0:W], in0=y2[:, 0:W], scalar1=0.044715, scalar2=1.0, op0=ALU.mult, op1=ALU.add),
               reads=[K("y2")], writes=[K("y2")])
        SS.add("dve", lambda e: e.tensor_tensor(out=y2[:, 0:W], in0=y2[:, 0:W], in1=y[:, 0:W], op=ALU.mult),
               reads=[K("y2"), K("y")], writes=[K("y2")])
        SS.add("act", lambda e: e.activation(out=tg[:, 0:W], in_=y2[:, 0:W], func=AF.Exp, scale=-2.0 * GELU_C), reads=[K("y2")], writes=[K("y2")])
        SS.add("act", lambda e: e.activation(out=tg[:, 0:W], in_=tg[:, 0:W], func=AF.Ln, bias=1.0), reads=[K("y2")], writes=[K("y2")])
        SS.add("act", lambda e: e.activation(out=tg[:, 0:W], in_=tg[:, 0:W], func=AF.Exp, scale=-1.0), reads=[K("y2")], writes=[K("y2")])
        SS.add("dve", lambda e: e.tensor_tensor(out=tg[:, 0:W], in0=tg[:, 0:W], in1=y[:, 0:W], op=ALU.mult),
               reads=[K("y2"), K("y")], writes=[K("y2")])
        SS.add("dve", lambda e: e.tensor_tensor(out=ol_bf[c][:, 0:W], in0=tg[:, 0:W], in1=hl_sb[c][:, 0:W], op=ALU.mult),
               reads=[K("y2"), ("hl", c)], writes=[("ol", c)])
        sfn, nd = store_fn(ol_bf[c], 256 + c * 128, ti)
        SS.add("pool", sfn, reads=[("ol", c)], writes=[("mo", "l", c, ti)], chan=("ol", c), ndma=nd)

    def mk(fn, *a_):
        r = Rec()
        fn(r, *a_)
        return r

    S.barrier()
    zip_emit([mk(prep_chain, 0, 0)])
    for ti in range(NQT):
        preps = []
        if ti + 1 < NQT:
            _, Wn = tile_info(ti + 1)
            preps = [mk(prep_chain, ti + 1, sub) for sub in range(Wn // 128)]
        groups = [
            [mk(qk_chain, ti, 0, 0), mk(qk_chain, ti, 0, 1)],
            [mk(qk_chain, ti, 1, 0), mk(qk_chain, ti, 1, 1)],
            [mk(xr_chain, ti, 0), mk(xr_chain, ti, 1)],
            [mk(yg_chain, ti, 0), mk(yg_chain, ti, 1)],
        ]
        for gi_, grp in enumerate(groups):
            zip_emit(grp)
            if gi_ < len(preps):
                zip_emit([preps[gi_]])

    S.barrier()
    A.off = mark12
    e_sb = [A.f32(1024) for _ in range(3)]
    sp_sb = [A.bf(1024) for _ in range(3)]
    g_sb = [A.bf(1024) for _ in range(2)]
    w_sb = [A.bf(1024) for _ in range(3)]
    osb = [A.bf(2 * 512), A.bf(2 * 512)]
    items = []
    for ti in range(NQT):
        pos0, W = tile_info(ti)
        b0 = pos0 // 128
        nsub = W // 128
        kbs = list(range(b0 + nsub - 1, -1, -1))
        for n, kb in enumerate(kbs):
            for j in range(2):
                items.append(dict(ti=ti, pos0=pos0, W=W, b0=b0, kb=kb, j=j, first=(n == 0), last=(n == len(kbs) - 1)))
    NI = len(items)

    S.add("dve", lambda e: e.memset(ext[:, 9:10], 0.0),
          reads=[("qk", w, j, t) for w in range(2) for j in range(2) for t in range(NQT)] + [("v", bl) for bl in range(NB)],
          writes=["kall"])

    def v3(buf, c0, W):
        return buf.rearrange("p (h w) -> p h w", h=2)[:, :, c0:W]

    def zview(c0, W):
        return ps[:, 0:1024].rearrange("p (h w) -> p h w", h=2)[:, :, c0:W]

    def pview(j, c0, W):
        return ps[:, (2 + 2 * j) * 512:(4 + 2 * j) * 512].rearrange("p (h w) -> p h w", h=2)[:, :, c0:W]

    RG = [[0, 1], [2, 3], [4, 5], [6, 7]]

    def tile_keys(ti):
        return [("mo", "l", c, ti) for c in range(2)] + [("mo", "a", j, ti) for j in range(2)]

    gathered = set()

    def maybe_gather(ti):
        if ti == 0:
            return
        done_tok = 512 * ti
        for g in range(4):
            if g in gathered or (g + 1) * GW > done_tok:
                continue
            gathered.add(g)
            t_lo = 1 + (g * GW) // 512
            t_hi = 1 + ((g + 1) * GW - 1) // 512
            keys = []
            for t in range(t_lo, t_hi + 1):
                keys += tile_keys(t)
            S.add("pool", lambda e, s, g=g: e.collective_compute(
                "AllGather", ALU.bypass, replica_groups=RG, ins=[mo_g[g].ap().opt()],
                outs=[ma_big[g * 1024:(g + 1) * 1024, :].opt()]).then_inc(s),
                reads=keys, writes=[("mall", g)], chan=("cc", g), inc=1)

    def c0_of(it):
        return 128 * (it["kb"] - it["b0"]) if it["kb"] >= it["b0"] else 0

    def PE1(i):
        it = items[i]
        kb, W, pos0, j = it["kb"], it["W"], it["pos0"], it["j"]
        c0 = c0_of(it)

        def fn(e):
            ins = None
            for hh in range(2):
                r = hh * 64
                ins = e.matmul(ps[:, hh * 512 + c0:hh * 512 + W], lhsT=kT[r:r + 64, j * TP + kb * 128:j * TP + (kb + 1) * 128],
                               rhs=qT[r:r + 64, j * TP + pos0 + c0:j * TP + pos0 + W], start=True, stop=True)
            return ins
        S.add("pe", fn, reads=["kall"], writes=[("ps", 0), ("ps", 1)])

    def ACT12(i):
        it = items[i]
        W = it["W"]
        c0 = c0_of(it)
        eb = e_sb[i % 3]
        sb = sp_sb[i % 3]
        S.add("act", lambda e: e.activation(out=v3(eb, c0, W), in_=zview(c0, W), func=AF.Exp),
              reads=[("ps", 0), ("ps", 1)], writes=[("e", i % 3)])
        S.add("act", lambda e: e.activation(out=v3(sb, c0, W), in_=v3(eb, c0, W), func=AF.Ln, bias=1.0),
              reads=[("e", i % 3)], writes=[("sp", i % 3)])
        if it["kb"] >= it["b0"]:
            for hh in range(2):
                S.add("dve", lambda e, hh=hh: e.tensor_tensor(out=sb[:, hh * 512 + c0:hh * 512 + c0 + 128],
                                                               in0=sb[:, hh * 512 + c0:hh * 512 + c0 + 128], in1=maskv(0, 128), op=ALU.mult),
                      reads=[("sp", i % 3), "cst"], writes=[("sp", i % 3)])

    def PE2(i):
        it = items[i]
        W, j = it["W"], it["j"]
        c0 = c0_of(it)
        sb = sp_sb[i % 3]

        def fn(e):
            ins = None
            for hh in range(2):
                b_ = 2 + 2 * j + hh
                ins = e.matmul(ps[:, b_ * 512 + c0:b_ * 512 + W], lhsT=negL, rhs=sb[:, hh * 512 + c0:hh * 512 + W],
                               start=it["first"], stop=False, skip_group_check=True)
            return ins
        S.add("pe", fn, reads=[("sp", i % 3), "cst"], writes=[("ps", 2 + 2 * j), ("ps", 3 + 2 * j)])

    def ACT3(i):
        it = items[i]
        W, j = it["W"], it["j"]
        c0 = c0_of(it)
        gb = g_sb[i % 2]
        eb = e_sb[i % 3]
        wb = w_sb[i % 3]
        S.add("act", lambda e: e.activation(out=v3(gb, c0, W), in_=pview(j, c0, W), func=AF.Exp),
              reads=[("ps", 2 + 2 * j), ("ps", 3 + 2 * j)], writes=[("g", i % 2)])
        S.add("dve", lambda e: e.tensor_tensor(out=v3(wb, c0, W), in0=v3(eb, c0, W), in1=v3(gb, c0, W), op=ALU.mult),
              reads=[("e", i % 3), ("g", i % 2)], writes=[("w", i % 3)])
        if it["kb"] >= it["b0"]:
            for hh in range(2):
                S.add("dve", lambda e, hh=hh: e.tensor_tensor(out=wb[:, hh * 512 + c0:hh * 512 + c0 + 128],
                                                               in0=wb[:, hh * 512 + c0:hh * 512 + c0 + 128], in1=maskv(0, 128), op=ALU.mult),
                      reads=[("w", i % 3), "cst"], writes=[("w", i % 3)])

    def PE4(i):
        it = items[i]
        if it["last"]:
            return
        W, j = it["W"], it["j"]
        c0 = c0_of(it)
        sb = sp_sb[i % 3]

        def fn(e):
            ins = None
            for hh in range(2):
                b_ = 2 + 2 * j + hh
                ins = e.matmul(ps[:, b_ * 512 + c0:b_ * 512 + W], lhsT=negU, rhs=sb[:, hh * 512 + c0:hh * 512 + W],
                               start=False, stop=False, skip_group_check=True)
            return ins
        S.add("pe", fn, reads=[("sp", i % 3), "cst"], writes=[("ps", 2 + 2 * j), ("ps", 3 + 2 * j)])

    def PE3(i):
        it = items[i]
        kb, W, pos0, ti, j = it["kb"], it["W"], it["pos0"], it["ti"], it["j"]
        c0 = c0_of(it)
        ob = 6 + j
        wb = w_sb[i % 3]
        par = ti % 2

        def fn(e):
            ins = None
            for hh in range(2):
                h = 2 * j + hh
                ins = e.matmul(ps[hh * 64:(hh + 1) * 64, ob * 512 + c0:ob * 512 + W], lhsT=v_sb[:, kb * 256 + h * 64:kb * 256 + (h + 1) * 64],
                               rhs=wb[:, hh * 512 + c0:hh * 512 + W], start=it["first"], stop=it["last"], skip_group_check=True)
            return ins
        S.add("pe", fn, reads=[("w", i % 3), "kall"], writes=[("ps", ob)])
        if it["last"]:
            S.add("dve", lambda e: e.tensor_copy(out=osb[par][:, j * 512:j * 512 + W], in_=bank(ob, W)),
                  reads=[("ps", ob)], writes=[("osb", par, j)])
            sfn, nd = store_fn(osb[par][:, j * 512:(j + 1) * 512], j * 128, ti)
            S.add("pool", sfn, reads=[("osb", par, j)], writes=[("mo", "a", j, ti)], chan=("osb", par, j), ndma=nd)
            if j == 1:
                maybe_gather(ti)

    PE1(0)
    for s in range(-1, NI + 1):
        if 0 <= s + 1 < NI:
            ACT12(s + 1)
        if s + 2 < NI:
            PE1(s + 2)
        if 0 <= s + 1 < NI:
            PE2(s + 1)
        if 0 <= s < NI:
            ACT3(s)
            PE4(s)
        if 0 <= s - 1 < NI:
            PE3(s - 1)

    S.add("pool", lambda e, s: e.collective_compute("AllGather", ALU.bypass, replica_groups=RG,
                                                    ins=[mo_halo_t.ap().opt()], outs=[ma_halo_t.ap().opt()]).then_inc(s),
          reads=tile_keys(0) + tile_keys(1 + (HALF - 2) // 512), writes=[("mall", "h")], chan="cch", inc=1)

    def mh_fn(e, s):
        half = e.partition_id() % 2
        ins = None
        for j in range(2):
            for r in range(2):
                ins = e.dma_start(out=mixed_half[r * 512:(r + 1) * 512, 2 + j * GW:2 + (j + 1) * GW],
                                  in_=ma_big[bass.ds(half * 2048 + j * 1024 + r * 512, 512), :]).then_inc(s, 16)
        ins = e.dma_start(out=mixed_half[:, 0:2], in_=ma_halo[:, bass.ds(half * 2, 2)]).then_inc(s, 16)
        return ins
    S.add("sp", mh_fn, reads=[("mall", g) for g in range(4)] + [("mall", "h")], writes=["mhalf"], chan="mh", ndma=5)

    S.barrier()
    A.off = mark_stage
    wout_bf = A.bf(8 * 1024)
    wfi_bf = A.bf(8 * 2 * DFF)
    wfo_bf = A.bf(NPAIR * 1024)
    mark3 = A.off
    stg3 = [A.f32(2816), A.f32(2816)]

    stg_ctr = [0]

    def load_cast(dst_ap, src_ap, ncols, key, scale_col=None):
        sl = stg_ctr[0] % 2
        stg_ctr[0] += 1
        S.add("sp", lambda e, s: e.dma_start(out=stg3[sl][:, 0:ncols], in_=src_ap).then_inc(s, 16),
              writes=[("stg3", sl)], chan=("stg3", sl))
        eng = ("dve", "act")[stg_ctr[0] % 2]
        if scale_col is None:
            if eng == "act":
                S.add(eng, lambda e: e.copy(out=dst_ap, in_=stg3[sl][:, 0:ncols]), reads=[("stg3", sl)], writes=[key])
            else:
                S.add(eng, lambda e: e.tensor_copy(out=dst_ap, in_=stg3[sl][:, 0:ncols]), reads=[("stg3", sl)], writes=[key])
        else:
            if eng == "act":
                S.add(eng, lambda e: e.activation(out=dst_ap, in_=stg3[sl][:, 0:ncols], func=AF.Copy, scale=scale_col),
                      reads=[("stg3", sl), "vec"], writes=[key])
            else:
                S.add(eng, lambda e: e.tensor_scalar(out=dst_ap, in0=stg3[sl][:, 0:ncols], scalar1=scale_col, scalar2=None, op0=ALU.mult),
                      reads=[("stg3", sl), "vec"], writes=[key])

    w3_keys = []
    for c in range(DC):
        load_cast(wout_bf[:, c * 1024:(c + 1) * 1024], w_out[c * 128:(c + 1) * 128, :], 1024, ("wout", c))
        w3_keys.append(("wout", c))
    for c in range(DC):
        for hh in range(2):
            load_cast(wfi_bf[:, c * 2 * DFF + hh * DFF:c * 2 * DFF + (hh + 1) * DFF], w_fi[c * 128:(c + 1) * 128, hh * DFF:(hh + 1) * DFF],
                      DFF, ("wfi", c, hh), scale_col=col(V_G2 + c))
            w3_keys.append(("wfi", c, hh))
    for j in range(0, NPAIR, 2):
        for jj in range(2):
            load_cast(wfo_bf[:, (j + jj) * 1024:(j + jj + 1) * 1024], w_fo[(j + jj) * 128:(j + jj + 1) * 128, :], 1024, ("wfo", j + jj))
            w3_keys.append(("wfo", j + jj))
    S.add("dve", lambda e: e.memset(ext[:, 10:11], 0.0), reads=w3_keys, writes=["w3all"])
    S.barrier()
    A.off = mark3
    h2 = [A.f32(1024) for _ in range(4)]
    hn2_bf = [A.bf(1024), A.bf(1024)]
    ss3 = A.f32(8)
    HW = 2 + W3
    hn2T = [A.bf(8 * HW), A.bf(8 * HW)]
    mt = A.bf(8 * W3)
    ubuf = [A.f32(2 + W3), A.f32(2 + W3)]
    gbuf = [A.f32(2 + W3), A.f32(2 + W3)]
    uc = [A.f32(W3), A.f32(W3)]
    gc = [A.f32(W3), A.f32(W3)]
    sg = [A.f32(W3), A.f32(W3)]
    act_bf = A.bf(NPAIR * W3)

    B_FO = [0, 1]
    B_T3 = 2
    B_U = [3, 4, 0]
    B_G = [5, 6, 1]
    B_WO = 7
    mixed_half_r = mixed_half.rearrange("(c p) w -> p c w", p=128)

    def tile_geom(t):
        if t < 0:
            return 0, 2, 2, 1, 1
        return 2 + t * W3, W3, 128, W3 // 128, t % 2

    def prep1(t):
        rel0, Wt, n, nsub, par = tile_geom(t)
        S.add("sp", lambda e, s: e.dma_start(out=mt.rearrange("p (c w) -> p c w", c=8)[:, :, 0:Wt],
                                              in_=mixed_half_r[:, :, rel0:rel0 + Wt]).then_inc(s, 16),
              reads=["mhalf"], writes=["mt"], chan="mt")
        for sub in range(nsub):
            sl = 2 * par + sub
            q = sub
            r = rel0 + sub * 128
            S.add("sp", lambda e, s, sl=sl, r=r: e.dma_start(out=h2[sl][0:n, :], in_=xhalf[r:r + n, :]).then_inc(s, 16),
                  writes=[("h2", sl)], chan=("h2", sl))
            for hf in range(2):
                def wo_fn(e, sub=sub, hf=hf):
                    ins = None
                    for c in range(DC):
                        ins = e.matmul(ps[0:n, B_WO * 512:(B_WO + 1) * 512], lhsT=mt[:, c * W3 + sub * 128:c * W3 + sub * 128 + n],
                                       rhs=wout_bf[:, c * 1024 + hf * 512:c * 1024 + (hf + 1) * 512], start=(c == 0), stop=(c == DC - 1))
                    return ins
                S.add("pe", wo_fn, reads=["mt", "w3all"], writes=[("ps", B_WO)])
                S.add("dve", lambda e, sl=sl, hf=hf: e.tensor_tensor(out=h2[sl][0:n, hf * 512:(hf + 1) * 512],
                                                                     in0=ps[0:n, B_WO * 512:(B_WO + 1) * 512],
                                                                     in1=h2[sl][0:n, hf * 512:(hf + 1) * 512], op=ALU.add),
                      reads=[("ps", B_WO), ("h2", sl)], writes=[("h2", sl)])
            S.add("pool", lambda e, q=q: e.memset(ss3[:, q:q + 1], 0.0), writes=[("ss3", q)])
            S.add("act", lambda e, sl=sl, q=q: e.activation(out=hn2_bf[q][0:n, :], in_=h2[sl][0:n, :], func=AF.Square,
                                                            accum_out=ss3[0:n, q:q + 1]),
                  reads=[("h2", sl)], writes=[("ss3", q), ("hn2", q)])
            S.add("dve", lambda e, q=q: e.tensor_scalar(out=ss3[0:n, 2 + q:3 + q], in0=ss3[0:n, q:q + 1], scalar1=1.0 / D, scalar2=EPS,
                                                         op0=ALU.mult, op1=ALU.add), reads=[("ss3", q)], writes=[("rs3", q)])
            S.add("act", lambda e, q=q: e.activation(out=ss3[0:n, 2 + q:3 + q], in_=ss3[0:n, 2 + q:3 + q], func=AF.Ln),
                  reads=[("rs3", q)], writes=[("rs3", q)])
            S.add("act", lambda e, q=q: e.activation(out=ss3[0:n, 2 + q:3 + q], in_=ss3[0:n, 2 + q:3 + q], func=AF.Exp, scale=-0.5),
                  reads=[("rs3", q)], writes=[("rs3", q)])
            S.add("act", lambda e, sl=sl, q=q: e.activation(out=hn2_bf[q][0:n, :], in_=h2[sl][0:n, :], func=AF.Copy, scale=ss3[0:n, 2 + q:3 + q]),
                  reads=[("h2", sl), ("rs3", q)], writes=[("hn2", q)])

    def prep2(t):
        rel0, Wt, n, nsub, par = tile_geom(t)
        for sub in range(nsub):
            q = sub

            def tr_fn(e, q=q):
                ins = None
                for c in range(DC):
                    ins = e.transpose(bank_bf(B_T3)[:, c * 128:c * 128 + n], hn2_bf[q][0:n, c * 128:(c + 1) * 128], ident[0:n, 0:n])
                return ins
            S.add("pe", tr_fn, reads=[("hn2", q), "cst"], writes=[("ps", B_T3)])
            dcol = 0 if t < 0 else 2 + sub * 128
            dpar = 0 if t < 0 else par
            S.add("dve", lambda e, dcol=dcol, dpar=dpar: e.tensor_copy(
                out=hn2T[dpar].rearrange("p (c w) -> p c w", c=8)[:, :, dcol:dcol + n],
                in_=bank_bf(B_T3).rearrange("p (c w) -> p c w", c=8)[:, :, 0:n]),
                reads=[("ps", B_T3)], writes=[("hn2T", dpar)])
        if t >= 0 and t + 1 < NST:
            S.add("dve", lambda e: e.tensor_copy(
                out=hn2T[1 - par].rearrange("p (c w) -> p c w", c=8)[:, :, 0:2],
                in_=hn2T[par].rearrange("p (c w) -> p c w", c=8)[:, :, W3:W3 + 2]),
                reads=[("hn2T", par)], writes=[("hn2T", 1 - par)])

    pair_ctr = [0]

    def ffn_in(t):
        rel0, Wt, n, nsub, par = tile_geom(t)
        final = t >= 0
        pcs = {}

        def mm(j):
            pc = pair_ctr[0] % 3
            pair_ctr[0] += 1
            pcs[j] = pc
            for which, bb in enumerate((B_U[pc], B_G[pc])):
                slab = j + which * NPAIR

                def fi_fn(e, bb=bb, slab=slab):
                    ins = None
                    for c in range(DC):
                        ins = e.matmul(bank(bb, Wt + 2), lhsT=wfi_bf[:, c * 2 * DFF + slab * 128:c * 2 * DFF + (slab + 1) * 128],
                                       rhs=hn2T[par][:, c * HW:c * HW + Wt + 2], start=(c == 0), stop=(c == DC - 1))
                    return ins
                S.add("pe", fi_fn, reads=[("hn2T", par), "w3all"], writes=[("ps", bb)])

        def post1(j):
            pc = pcs[j]
            p2 = j % 2
            for which, (bb, buf, cbuf) in enumerate(((B_U[pc], ubuf[p2], uc[p2]), (B_G[pc], gbuf[p2], gc[p2]))):
                slab = j + which * NPAIR
                bk = ("ub", which, p2)
                ck = ("cb", which, p2)
                S.add("act", lambda e, bb=bb, buf=buf: e.copy(out=buf[:, 0:Wt + 2], in_=bank(bb, Wt + 2)), reads=[("ps", bb)], writes=[bk])
                S.add("act", lambda e, buf=buf, cbuf=cbuf, slab=slab: e.activation(
                    out=cbuf[:, 0:Wt], in_=buf[:, 2:2 + Wt], func=AF.Identity, scale=col(V_FCW + slab * 3 + 2), bias=col(V_FCB + slab)),
                    reads=[bk, "vec"], writes=[ck])
                for k in range(2):
                    S.add("dve", lambda e, buf=buf, cbuf=cbuf, slab=slab, k=k: e.scalar_tensor_tensor(
                        out=cbuf[:, 0:Wt], in0=buf[:, k:k + Wt], scalar=col(V_FCW + slab * 3 + k), in1=cbuf[:, 0:Wt],
                        op0=ALU.mult, op1=ALU.add), reads=[bk, "vec", ck], writes=[ck])

        def post2(j):
            pc = j % 2
            S.add("act", lambda e: e.activation(out=sg[pc][:, 0:Wt], in_=gc[pc][:, 0:Wt], func=AF.Silu),
                  reads=[("cb", 1, pc)], writes=[("sg", pc)])
            S.add("dve", lambda e: e.tensor_tensor(out=act_bf[:, j * W3:j * W3 + Wt], in0=sg[pc][:, 0:Wt], in1=uc[pc][:, 0:Wt], op=ALU.mult),
                  reads=[("sg", pc), ("cb", 0, pc)], writes=[("actT", j)])

        for j in range(NPAIR + 2):
            if j < NPAIR:
                mm(j)
            if 0 <= j - 1 < NPAIR:
                post1(j - 1)
            if 0 <= j - 2 < NPAIR:
                post2(j - 2)

    def ffn_out(t, sub):
        rel0, Wt, n, nsub, par = tile_geom(t)
        sl = 2 * par + sub

        def fo_fn(e):
            ins = None
            for hf in range(2):
                for j in range(NPAIR):
                    ins = e.matmul(bank(B_FO[hf], 512), lhsT=act_bf[:, j * W3 + sub * 128:j * W3 + (sub + 1) * 128],
                                   rhs=wfo_bf[:, j * 1024 + hf * 512:j * 1024 + (hf + 1) * 512], start=(j == 0), stop=(j == NPAIR - 1))
            return ins
        S.add("pe", fo_fn, reads=[("actT", j) for j in range(NPAIR)] + ["w3all"], writes=[("ps", 0), ("ps", 1)])
        S.add("dve", lambda e: e.tensor_tensor(out=h2[sl][:, :], in0=ps[:, 0:1024], in1=h2[sl][:, :], op=ALU.add),
              reads=[("ps", 0), ("ps", 1), ("h2", sl)], writes=[("h2", sl)])
        r0 = rel0 - 2 + sub * 128
        S.add("pool", lambda e, s: e.dma_start(out=out[r0:r0 + 128, :], in_=h2[sl][:, :]).then_inc(s, 16),
              reads=[("h2", sl)], writes=[("h2", sl), ("out", r0)], chan=("h2", sl))

    prep1(-1)
    prep2(-1)
    prep1(0)
    prep2(0)
    for t in range(NST):
        ffn_in(t)
        if t + 1 < NST:
            prep1(t + 1)
        ffn_out(t, 0)
        if t + 1 < NST:
            prep2(t + 1)
        ffn_out(t, 1)

    S.add("sp", lambda e: None, reads=[("out", r0) for r0 in range(0, HALF, 128)], writes=["done"])

    S.finalize()
    chans = list(S.chan_count.keys())
    sem_ctxs = []
    sems = {}
    for e in Sched.ENG:
        c = nc.semaphore("s_" + e)
        sems[e] = c.__enter__()
        sem_ctxs.append(c)
    chan_sems = {}
    for i, ch in enumerate(chans):
        c = nc.semaphore("c_%d" % i)
        chan_sems[ch] = c.__enter__()
        sem_ctxs.append(c)
    with nc.Block() as block:
        S.emit(nc, block, sems, chan_sems)
    for c in reversed(sem_ctxs):
        c.__exit__(None, None, None)
    pctx.__exit__(None, None, None)
    ctx.__exit__(None, None, None)
    return nc


def _consts():
    c = np.zeros((128, NCST), np.float32)
    j = np.arange(128)[:, None]
    s = np.arange(128)[None, :]
    c[:, C_ID:C_ID + 128] = (j == s)
    c[:, C_NL:C_NL + 128] = -(j >= s).astype(np.float32)
    c[:, C_NU:C_NU + 128] = -(j < s).astype(np.float32)
    c[:, C_BO:C_BO + 128] = ((j // 64) == (s // 64))
    t = np.arange(512)[None, :]
    for k in range(4):
        c[:, C_MK + k * 512:C_MK + (k + 1) * 512] = ((128 * k + j) < t)
    return c


def _prep_inputs(inputs, SEQ):
    f = lambda a: np.ascontiguousarray(np.asarray(a), dtype=np.float32)
    x = f(inputs["x"])
    meta = f(inputs["meta_tokens"])
    w_in = f(inputs["w_in"])[0]
    w_out = f(inputs["w_out"])[0]
    w_fi = f(inputs["w_ffn_in"])[0]
    w_fo = f(inputs["w_ffn_out"])[0]
    g1 = f(inputs["norm1_g"])[0]
    g2 = f(inputs["norm2_g"])[0]
    qg = f(inputs["q_norm_g"])[0]
    kg = f(inputs["k_norm_g"])[0]
    cw = f(inputs["conv_w"])[0]
    cb = f(inputs["conv_b"])[0]
    wa = f(inputs["w_rg_a"])[0]
    wi = f(inputs["w_rg_i"])[0]
    ba = f(inputs["b_rg_a"])[0]
    bi = f(inputs["b_rg_i"])[0]
    lam = f(inputs["lru_lambda"])[0]
    fcw = f(inputs["ffn_conv_w"])[0]
    fcb = f(inputs["ffn_conv_b"])[0]
    consts = _consts()
    TP = SEQ + 128
    maps = []
    for core in range(8):
        b, p = core // 2, core % 2
        xpad = np.zeros((TP, D), np.float32)
        xpad[PAD:128] = meta
        xpad[128:] = x[b]
        cs = slice(256 * p, 256 * p + 256)
        wic = np.concatenate([w_in[:, 0:512][:, cs], w_in[:, 512:1024][:, cs], w_in[:, 1024:1536][:, cs],
                              w_in[:, 1536:2048][:, cs], w_in[:, 2048:2560][:, cs]], axis=1)
        vec = np.zeros((128, NV), np.float32)
        vec[:, V_G1:V_G1 + 8] = g1.reshape(8, 128).T
        vec[:, V_G2:V_G2 + 8] = g2.reshape(8, 128).T
        vec[:, V_QG] = np.tile(qg, 2)
        vec[:, V_KG] = np.tile(kg, 2)
        for c in range(2):
            ch = slice(256 * p + 128 * c, 256 * p + 128 * c + 128)
            vec[:, V_CW + c * 4:V_CW + c * 4 + 4] = cw[:, ch].T
            vec[:, V_CB + c] = cb[ch]
            vec[:, V_BA + c] = ba[ch]
            vec[:, V_BI + c] = bi[ch]
            vec[:, V_LAM + c] = lam[ch]
        for s in range(NSLAB):
            vec[:, V_FCW + s * 3:V_FCW + s * 3 + 3] = fcw[:, s * 128:(s + 1) * 128].T
            vec[:, V_FCB + s] = fcb[s * 128:(s + 1) * 128]
        wrg = np.zeros((128, 4 * 128), np.float32)
        for gi, wsrc in enumerate((wa, wi)):
            for c in range(2):
                for k in range(2):
                    blk = 4 * p + 2 * c + k
                    o = (gi * 2 + c) * 128
                    wrg[64 * k:64 * k + 64, o + 64 * k:o + 64 * k + 64] = wsrc[blk]
        perm = []
        for r in range(2):
            perm += list(range(256 * r, 256 * r + 256))
            perm += list(range(512 + 256 * r, 512 + 256 * r + 256))
        maps.append({
            "xpad": xpad, "xhalf": np.ascontiguousarray(xpad[126 + (SEQ // 2) * p:126 + (SEQ // 2) * p + SEQ // 2 + 2]), "w_in": np.ascontiguousarray(wic), "vecs": vec, "w_rg": wrg,
            "w_out": np.ascontiguousarray(w_out[perm]), "w_ffn_in": w_fi, "w_ffn_out": w_fo, "consts": consts,
        })
    return maps


_NC_CACHE = {}


def kernel(**inputs):
    x = np.asarray(inputs["x"])
    B, SEQ, _ = x.shape
    assert B == 4
    if SEQ not in _NC_CACHE:
        _NC_CACHE[SEQ] = build_nc(SEQ)
    nc = _NC_CACHE[SEQ]
    maps = _prep_inputs(inputs, SEQ)
    res = run_bass_kernel_spmd(nc, maps, core_ids=list(range(8)))
    outp = np.empty((B, SEQ, D), np.float32)
    HALF = SEQ // 2
    for core in range(8):
        b, p = core // 2, core % 2
        outp[b, p * HALF:(p + 1) * HALF] = res.results[core]["out"]
    return outp
```

```python
import numpy as np
import concourse.bass as bass
import concourse.mybir as mybir
from concourse.bass_utils import run_bass_kernel_spmd

F32 = mybir.dt.float32
BF16 = mybir.dt.bfloat16
AF = mybir.ActivationFunctionType
ALU = mybir.AluOpType

D = 1024
DC = 8
DFF = 2816
NSLAB = 44
NPAIR = 22
EPS = 1e-6
NMETA = 16
PAD = 112
GELU_C = 0.7978845608028654

V_G1, V_G2, V_QG, V_KG, V_CW, V_CB, V_BA, V_BI, V_LAM, V_FCW, V_FCB = 0, 8, 16, 17, 18, 26, 28, 30, 32, 34, 166
NV = 210
C_ID, C_NL, C_NU, C_BO, C_MK = 0, 128, 256, 384, 512
NCST = 512 + 4 * 512


class Sched:
    ENG = ("pe", "act", "dve", "pool", "sp")

    def __init__(self):
        self.ops = []
        self.last_w = {}
        self.readers = {}
        self.eng_count = {e: 0 for e in self.ENG}
        self.chan_count = {}
        self.chan_inc = {}
        self.barrier_nodes = []

    def add(self, eng, fn, reads=(), writes=(), chan=None, ndma=1, inc=16):
        deps = set(self.barrier_nodes)
        for k in reads:
            w = self.last_w.get(k)
            if w is not None:
                deps.add(w)
        for k in writes:
            w = self.last_w.get(k)
            if w is not None:
                deps.add(w)
            for r in self.readers.get(k, ()):
                deps.add(r)
        if chan is None:
            self.eng_count[eng] += 1
            node = ("E", eng, self.eng_count[eng])
        else:
            self.chan_inc[chan] = inc
            self.chan_count[chan] = self.chan_count.get(chan, 0) + ndma * inc
            node = ("C", chan, self.chan_count[chan])
        self.ops.append(dict(eng=eng, fn=fn, deps=deps, node=node, chan=chan))
        for k in reads:
            self.readers.setdefault(k, []).append(node)
        for k in writes:
            self.last_w[k] = node
            self.readers[k] = []
        return node

    def barrier(self, exclude=None):
        nodes = []
        for e, c in self.eng_count.items():
            if c:
                nodes.append(("E", e, c))
        for ch, c in self.chan_count.items():
            if exclude is not None and exclude(ch):
                continue
            nodes.append(("C", ch, c))
        self.barrier_nodes = nodes

    def finalize(self):
        known = {e: {} for e in self.ENG}
        signal = {e: set() for e in self.ENG}
        for op in self.ops:
            need = {}
            for kind, tgt, val in op["deps"]:
                if kind == "E" and tgt == "pe" and op["eng"] == "pe" and op["chan"] is None:
                    continue
                key = (kind, tgt)
                if val > need.get(key, 0):
                    need[key] = val
            waits = []
            kn = known[op["eng"]]
            for key, val in need.items():
                if kn.get(key, 0) >= val:
                    continue
                kn[key] = val
                waits.append((key, val))
                if key[0] == "E":
                    signal[key[1]].add(val)
            op["waits"] = waits
        self.rank = {}
        for e in self.ENG:
            self.rank[e] = {v: i + 1 for i, v in enumerate(sorted(signal[e]))}

    def emit(self, nc, block, sems, chan_sems):
        streams = {e: [op for op in self.ops if op["eng"] == e] for e in self.ENG}

        def run(eng_name, eng):
            for op in streams[eng_name]:
                for (kind, tgt), val in op["waits"]:
                    if kind == "E":
                        eng.wait_ge(sems[tgt], self.rank[tgt][val])
                    else:
                        eng.wait_ge(chan_sems[tgt], val)
                if op["chan"] is not None:
                    op["fn"](eng, chan_sems[op["chan"]])
                else:
                    ins = op["fn"](eng)
                    idx = op["node"][2]
                    if idx in self.rank[eng_name]:
                        assert ins is not None
                        ins.then_inc(sems[eng_name], 1)

        @block.tensor
        def _(e):
            run("pe", e)

        @block.scalar
        def _(e):
            run("act", e)

        @block.vector
        def _(e):
            run("dve", e)

        @block.gpsimd
        def _(e):
            run("pool", e)

        @block.sync
        def _(e):
            run("sp", e)


def build_nc(SEQ):
    assert SEQ % 1024 == 0
    HALF = SEQ // 2
    TP = SEQ + 128
    NB = TP // 128
    NQT = 1 + SEQ // 512
    NST = HALF // 256
    W3 = 256

    nc = bass.Bass("TRN2", target_bir_lowering=False)
    xpad = nc.dram_tensor("xpad", [TP, D], F32, kind="ExternalInput").ap()
    w_in = nc.dram_tensor("w_in", [D, 1280], F32, kind="ExternalInput").ap()
    vecs = nc.dram_tensor("vecs", [128, NV], F32, kind="ExternalInput").ap()
    w_rg = nc.dram_tensor("w_rg", [128, 4 * 128], F32, kind="ExternalInput").ap()
    w_out = nc.dram_tensor("w_out", [D, D], F32, kind="ExternalInput").ap()
    w_fi = nc.dram_tensor("w_ffn_in", [D, 2 * DFF], F32, kind="ExternalInput").ap()
    w_fo = nc.dram_tensor("w_ffn_out", [DFF, D], F32, kind="ExternalInput").ap()
    consts = nc.dram_tensor("consts", [128, NCST], F32, kind="ExternalInput").ap()
    out = nc.dram_tensor("out", [HALF, D], F32, kind="ExternalOutput").ap()
    GW = SEQ // 4
    mo_g = [nc.dram_tensor("mixed_own_%d" % g, [512, GW], BF16) for g in range(4)]
    mo_halo_t = nc.dram_tensor("mixed_own_halo", [512, 4], BF16)
    ma_big_t = nc.dram_tensor("mixed_all_big", [4 * 1024, GW], BF16)
    ma_halo_t = nc.dram_tensor("mixed_all_halo", [1024, 4], BF16)
    xhalf = nc.dram_tensor("xhalf", [HALF + 2, D], F32, kind="ExternalInput").ap()
    wout_s = nc.dram_tensor("wout_bf16", [D, D], BF16).ap()
    wfi_s = nc.dram_tensor("wfi_bf16", [D, 2 * DFF], BF16).ap()
    wfo_s = nc.dram_tensor("wfo_bf16", [DFF, D], BF16).ap()
    mh_t = nc.dram_tensor("mixed_half", [1024, HALF + 2], BF16)
    mixed_half = mh_t.ap()
    ma_big = ma_big_t.ap()
    ma_halo = ma_halo_t.ap()
    mo_halo = mo_halo_t.ap()

    def store_fn(src, row0, ti):
        pieces = []
        if ti == 0:
            pieces.append((mo_halo[row0:row0 + 128, 0:2], 126, 2))
        else:
            i0 = 512 * (ti - 1)
            c = 0
            while c < 512:
                g = (i0 + c) // GW
                gc = (i0 + c) % GW
                n = min(512 - c, GW - gc)
                pieces.append((mo_g[g].ap()[row0:row0 + 128, gc:gc + n], c, n))
                c += n
            if i0 <= HALF - 2 < i0 + 512:
                pieces.append((mo_halo[row0:row0 + 128, 2:4], HALF - 2 - i0, 2))

        def fn(e, s):
            ins = None
            for dst, c0, n in pieces:
                ins = e.dma_start(out=dst, in_=src[:, c0:c0 + n]).then_inc(s, 16)
            return ins
        return fn, len(pieces)

    S = Sched()
    ARENA_F = 53200

    ctx = nc.sbuf_tensor("arena", [128, ARENA_F], F32)
    arena = ctx.__enter__()
    pctx = nc.psum_tensor("ps", [128, 8 * 512], F32)
    ps = pctx.__enter__()

    class Arena:
        def __init__(self):
            self.off = 0

        def f32(self, n):
            o = self.off
            self.off += n
            assert self.off <= ARENA_F, self.off
            return arena[:, o:o + n]

        def bf(self, n):
            nf = (n + 1) // 2
            o = self.off
            self.off += nf
            assert self.off <= ARENA_F, self.off
            return arena[:, o:o + nf].bitcast(BF16)

    A = Arena()

    def bank(b, n=512):
        return ps[:, b * 512:b * 512 + n]

    def bank_bf(b):
        return ps[:, b * 512:(b + 1) * 512].bitcast(BF16)

    cst = A.bf(NCST)
    vec = A.f32(NV)
    ext = A.f32(16)
    ident = cst[:, C_ID:C_ID + 128]
    negL = cst[:, C_NL:C_NL + 128]
    negU = cst[:, C_NU:C_NU + 128]
    bones = cst[:, C_BO:C_BO + 128]

    def maskv(j, W):
        return cst[:, C_MK + j * 512:C_MK + j * 512 + W]

    mark_stage = A.off

    qT = A.bf(2 * TP)
    kT = A.bf(2 * TP)
    v_sb = A.bf(NB * 256)
    mark12 = A.off
    win_bf = A.bf(8 * 1280)
    wrg_bf = A.bf(4 * 128)
    stg = [A.f32(1280), A.f32(1280)]
    xs = [A.f32(1024), A.f32(1024)]
    hn_bf = [A.bf(1024), A.bf(1024)]
    ssb = A.f32(8)
    hnT = [A.bf(8 * 512), A.bf(8 * 512)]
    sqb = [A.bf(512), A.bf(512)]
    rqb = [A.f32(512), A.f32(512)]
    xr_sb = [A.f32(3 + 512), A.f32(3 + 512)]
    xc_sb = [A.f32(512), A.f32(512)]
    xc_bf = [A.bf(512), A.bf(512)]
    tr_sb = [A.f32(512), A.f32(512)]
    ti_sb = [A.f32(512), A.f32(512)]
    a_sb = [A.f32(512), A.f32(512)]
    m2_sb = [A.f32(512), A.f32(512)]
    bt_sb = [A.f32(512), A.f32(512)]
    hl_sb = [A.f32(512), A.f32(512)]
    hst = A.f32(2)
    y_sb = [stg[0][:, 0:512], stg[0][:, 512:1024]]
    y2_sb = [stg[1][:, 0:512], stg[1][:, 512:1024]]
    ol_bf = [stg[0][:, 1024:1280].bitcast(BF16), stg[1][:, 1024:1280].bitcast(BF16)]

    def col(i, n=1):
        return vec[:, i:i + n]

    for hh in range(2):
        S.add("sp", lambda e, s, hh=hh: e.dma_start(out=stg[hh][:, 0:1280], in_=consts[:, hh * 1280:(hh + 1) * 1280]).then_inc(s, 16),
              writes=[("stg", hh)], chan=("stg", hh))
        S.add("dve", lambda e, hh=hh: e.tensor_copy(out=cst[:, hh * 1280:(hh + 1) * 1280], in_=stg[hh][:, 0:1280]),
              reads=[("stg", hh)], writes=["cst"])
    S.add("sp", lambda e, s: e.dma_start(out=vec[:, :], in_=vecs[:, :]).then_inc(s, 16), writes=["vec"], chan="vec")
    S.add("act", lambda e: e.activation(out=ext[:, 7:9], in_=col(V_LAM, 2), func=AF.Exp, scale=-1.0),
          reads=["vec"], writes=["ext_t"])
    S.add("act", lambda e: e.activation(out=ext[:, 7:9], in_=ext[:, 7:9], func=AF.Ln, bias=1.0),
          reads=["ext_t"], writes=["ext_t"])
    S.add("dve", lambda e: e.tensor_scalar(out=ext[:, 0:2], in0=ext[:, 7:9], scalar1=-8.0, scalar2=None, op0=ALU.mult),
          reads=["ext_t"], writes=["ext"])
    S.add("dve", lambda e: e.tensor_scalar(out=ext[:, 2:6], in0=col(V_BA, 4), scalar1=-1.0, scalar2=None, op0=ALU.mult),
          reads=["vec", "ext"], writes=["ext"])
    S.add("dve", lambda e: e.tensor_scalar(out=ext[:, 6:7], in0=col(V_KG), scalar1=0.125, scalar2=None, op0=ALU.mult),
          reads=["vec", "ext"], writes=["ext"])
    S.add("sp", lambda e, s: e.dma_start(out=stg[0][:, 0:512], in_=w_rg[:, :]).then_inc(s, 16),
          writes=[("stg", 0)], chan=("stg", 0))
    S.add("dve", lambda e: e.tensor_copy(out=wrg_bf[:, :], in_=stg[0][:, 0:512]), reads=[("stg", 0)], writes=["wrg"])
    for c in range(DC):
        hh = c % 2
        S.add("sp", lambda e, s, c=c, hh=hh: e.dma_start(out=stg[hh][:, :], in_=w_in[c * 128:(c + 1) * 128, :]).then_inc(s, 16),
              writes=[("stg", hh)], chan=("stg", hh))
        if c % 2 == 0:
            S.add("dve", lambda e, c=c, hh=hh: e.tensor_scalar(out=win_bf[:, c * 1280:(c + 1) * 1280], in0=stg[hh][:, :],
                                                                scalar1=col(V_G1 + c), scalar2=None, op0=ALU.mult),
                  reads=[("stg", hh), "vec"], writes=[("win", c)])
        else:
            S.add("act", lambda e, c=c, hh=hh: e.activation(out=win_bf[:, c * 1280:(c + 1) * 1280], in_=stg[hh][:, :],
                                                             func=AF.Copy, scale=col(V_G1 + c)),
                  reads=[("stg", hh), "vec"], writes=[("win", c)])
    S.add("pool", lambda e: e.memset(xr_sb[0][:, 0:3], 0.0), writes=[("xr", 0)])
    S.add("pool", lambda e: e.memset(xr_sb[1][:, 0:3], 0.0), writes=[("xr", 1)])
    S.add("pool", lambda e: e.memset(hst[:, :], 0.0), writes=["hst"])

    win_reads = [("win", c) for c in range(DC)]

    B_TRP, B_V, B_PJ0, B_PJ1 = 0, 1, 2, 3
    B_PS2 = [4, 5]
    B_GR = [4, 6]
    B_GI = [5, 7]

    def tile_info(ti):
        if ti == 0:
            return 0, 128
        return 128 + 512 * (ti - 1), 512

    class Rec:
        def __init__(self):
            self.l = []

        def add(self, *a_, **k_):
            self.l.append((a_, k_))

    def zip_emit(chains):
        idx = [0] * len(chains)
        left = True
        while left:
            left = False
            for ci, ch in enumerate(chains):
                if idx[ci] < len(ch.l):
                    a_, k_ = ch.l[idx[ci]]
                    S.add(*a_, **k_)
                    idx[ci] += 1
                    left = True

    def proj(SS, b, col0, W, hp):
        def fn(e):
            ins = None
            for c in range(DC):
                ins = e.matmul(bank(b, W), lhsT=win_bf[:, c * 1280 + col0:c * 1280 + col0 + 128],
                               rhs=hnT[hp][:, c * 512:c * 512 + W], start=(c == 0), stop=(c == DC - 1))
            return ins
        SS.add("pe", fn, reads=win_reads + [("hnT", hp)], writes=[("ps", b)])

    def prep_chain(SS, ti, sub):
        pos0, W = tile_info(ti)
        hp = ti % 2
        blk = pos0 // 128 + sub
        sl = blk % 2
        SS.add("sp", lambda e, s: e.dma_start(out=xs[sl][:, :], in_=xpad[blk * 128:(blk + 1) * 128, :]).then_inc(s, 16),
               writes=[("xs", sl)], chan=("xs", sl))
        SS.add("pool", lambda e: e.memset(ssb[:, sl:sl + 1], 0.0), writes=[("ss", sl)])
        SS.add("act", lambda e: e.activation(out=hn_bf[sl][:, :], in_=xs[sl][:, :], func=AF.Square, accum_out=ssb[:, sl:sl + 1]),
               reads=[("xs", sl)], writes=[("ss", sl), ("hn", sl)])
        SS.add("dve", lambda e: e.tensor_scalar(out=ssb[:, 2 + sl:3 + sl], in0=ssb[:, sl:sl + 1], scalar1=1.0 / D, scalar2=EPS,
                                                op0=ALU.mult, op1=ALU.add),
               reads=[("ss", sl)], writes=[("rstd", sl)])
        SS.add("act", lambda e: e.activation(out=ssb[:, 2 + sl:3 + sl], in_=ssb[:, 2 + sl:3 + sl], func=AF.Ln),
               reads=[("rstd", sl)], writes=[("rstd", sl)])
        SS.add("act", lambda e: e.activation(out=ssb[:, 2 + sl:3 + sl], in_=ssb[:, 2 + sl:3 + sl], func=AF.Exp, scale=-0.5),
               reads=[("rstd", sl)], writes=[("rstd", sl)])
        SS.add("act", lambda e: e.activation(out=hn_bf[sl][:, :], in_=xs[sl][:, :], func=AF.Copy, scale=ssb[:, 2 + sl:3 + sl]),
               reads=[("xs", sl), ("rstd", sl)], writes=[("hn", sl)])

        def tr_fn(e):
            ins = None
            for c in range(DC):
                ins = e.transpose(bank_bf(B_TRP)[:, c * 128:(c + 1) * 128], hn_bf[sl][:, c * 128:(c + 1) * 128], ident)
            return ins
        SS.add("pe", tr_fn, reads=[("hn", sl), "cst"], writes=[("ps", B_TRP)])
        SS.add("dve", lambda e: e.tensor_copy(
            out=hnT[hp].rearrange("p (c w) -> p c w", c=8)[:, :, sub * 128:(sub + 1) * 128],
            in_=bank_bf(B_TRP).rearrange("p (c w) -> p c w", c=8)),
            reads=[("ps", B_TRP)], writes=[("hnT", hp)])

        def v_fn(e):
            ins = None
            for c in range(DC):
                ins = e.matmul(bank(B_V, 256), lhsT=hnT[hp][:, c * 512 + sub * 128:c * 512 + (sub + 1) * 128],
                               rhs=win_bf[:, c * 1280 + 512:c * 1280 + 768], start=(c == 0), stop=(c == DC - 1))
            return ins
        SS.add("pe", v_fn, reads=win_reads + [("hnT", hp)], writes=[("ps", B_V)])
        SS.add("act", lambda e: e.copy(out=v_sb[:, blk * 256:(blk + 1) * 256], in_=bank(B_V, 256)),
               reads=[("ps", B_V)], writes=[("v", blk)])

    def qk_chain(SS, ti, which, j):
        pos0, W = tile_info(ti)
        hp = ti % 2
        b = B_PJ0 + j
        p2 = B_PS2[j]
        sq = sqb[j]
        rq = rqb[j]
        proj(SS, b, which * 256 + j * 128, W, hp)
        SS.add("act", lambda e: e.activation(out=sq[:, 0:W], in_=bank(b, W), func=AF.Square), reads=[("ps", b)], writes=[("sqb", j)])
        SS.add("pe", lambda e: e.matmul(bank(p2, W), lhsT=bones, rhs=sq[:, 0:W], start=True, stop=True),
               reads=[("sqb", j), "cst"], writes=[("ps", p2)])
        SS.add("dve", lambda e: e.tensor_scalar(out=rq[:, 0:W], in0=bank(p2, W), scalar1=1.0 / 64, scalar2=EPS,
                                                op0=ALU.mult, op1=ALU.add), reads=[("ps", p2)], writes=[("rqb", j)])
        SS.add("act", lambda e: e.activation(out=rq[:, 0:W], in_=rq[:, 0:W], func=AF.Ln), reads=[("rqb", j)], writes=[("rqb", j)])
        SS.add("act", lambda e: e.activation(out=rq[:, 0:W], in_=rq[:, 0:W], func=AF.Exp, scale=-0.5), reads=[("rqb", j)], writes=[("rqb", j)])
        dst = qT if which == 0 else kT
        gsc = col(V_QG) if which == 0 else ext[:, 6:7]
        SS.add("dve", lambda e: e.scalar_tensor_tensor(
            out=dst[:, j * TP + pos0:j * TP + pos0 + W], in0=bank(b, W), scalar=gsc, in1=rq[:, 0:W], op0=ALU.mult, op1=ALU.mult),
            reads=[("ps", b), ("rqb", j), "vec", "ext"], writes=[("qk", which, j, ti)])

    def xr_chain(SS, ti, c):
        pos0, W = tile_info(ti)
        hp = ti % 2
        b = B_PJ0 + c
        gr, gi = B_GR[c], B_GI[c]
        xc, xcb, tr, tg_, a_, m2, bt = xc_sb[c], xc_bf[c], tr_sb[c], ti_sb[c], a_sb[c], m2_sb[c], bt_sb[c]
        K = lambda n: (n, c)
        proj(SS, b, 768 + c * 128, W, hp)
        SS.add("act", lambda e: e.copy(out=xr_sb[c][:, 3:3 + W], in_=bank(b, W)), reads=[("ps", b)], writes=[("xr", c)])
        SS.add("dve", lambda e: e.tensor_scalar(out=xc[:, 0:W], in0=xr_sb[c][:, 3:3 + W], scalar1=col(V_CW + c * 4 + 3),
                                                scalar2=col(V_CB + c), op0=ALU.mult, op1=ALU.add),
               reads=[("xr", c), "vec"], writes=[K("xc")])
        for k in range(3):
            SS.add("dve", lambda e, k=k: e.scalar_tensor_tensor(out=xc[:, 0:W], in0=xr_sb[c][:, k:k + W], scalar=col(V_CW + c * 4 + k),
                                                                 in1=xc[:, 0:W], op0=ALU.mult, op1=ALU.add),
                   reads=[("xr", c), "vec", K("xc")], writes=[K("xc")])
        SS.add("pool", lambda e: e.tensor_copy(out=xr_sb[c][:, 0:3], in_=xr_sb[c][:, W:W + 3]), reads=[("xr", c)], writes=[("xr", c)])
        SS.add("act", lambda e: e.copy(out=xcb[:, 0:W], in_=xc[:, 0:W]), reads=[K("xc")], writes=[K("xcb")])
        SS.add("pe", lambda e: e.matmul(bank(gr, W), lhsT=wrg_bf[:, (0 * 2 + c) * 128:(0 * 2 + c + 1) * 128], rhs=xcb[:, 0:W],
                                        start=True, stop=True), reads=[K("xcb"), "wrg"], writes=[("ps", gr)])
        SS.add("pe", lambda e: e.matmul(bank(gi, W), lhsT=wrg_bf[:, (1 * 2 + c) * 128:(1 * 2 + c + 1) * 128], rhs=xcb[:, 0:W],
                                        start=True, stop=True), reads=[K("xcb"), "wrg"], writes=[("ps", gi)])
        SS.add("act", lambda e: e.activation(out=tr[:, 0:W], in_=bank(gr, W), func=AF.Exp, bias=ext[:, 2 + c:3 + c], scale=-1.0),
               reads=[("ps", gr), "ext"], writes=[K("tr")])
        SS.add("act", lambda e: e.activation(out=tg_[:, 0:W], in_=bank(gi, W), func=AF.Exp, bias=ext[:, 4 + c:5 + c], scale=-1.0),
               reads=[("ps", gi), "ext"], writes=[K("tig")])
        SS.add("act", lambda e: e.activation(out=tr[:, 0:W], in_=tr[:, 0:W], func=AF.Ln, bias=1.0), reads=[K("tr")], writes=[K("tr")])
        SS.add("act", lambda e: e.activation(out=tg_[:, 0:W], in_=tg_[:, 0:W], func=AF.Ln, bias=1.0), reads=[K("tig")], writes=[K("tig")])
        SS.add("act", lambda e: e.activation(out=tr[:, 0:W], in_=tr[:, 0:W], func=AF.Exp, scale=-1.0), reads=[K("tr")], writes=[K("tr")])
        SS.add("act", lambda e: e.activation(out=tg_[:, 0:W], in_=tg_[:, 0:W], func=AF.Exp, scale=-1.0), reads=[K("tig")], writes=[K("tig")])
        SS.add("act", lambda e: e.activation(out=a_[:, 0:W], in_=tr[:, 0:W], func=AF.Exp, scale=ext[:, c:c + 1]),
               reads=[K("tr"), "ext"], writes=[K("a")])
        SS.add("dve", lambda e: e.tensor_tensor(out=m2[:, 0:W], in0=a_[:, 0:W], in1=a_[:, 0:W], op=ALU.mult), reads=[K("a")], writes=[K("m2")])
        SS.add("dve", lambda e: e.tensor_scalar(out=m2[:, 0:W], in0=m2[:, 0:W], scalar1=-1.0, scalar2=1.0, op0=ALU.mult, op1=ALU.add),
               reads=[K("m2")], writes=[K("m2")])
        SS.add("act", lambda e: e.activation(out=m2[:, 0:W], in_=m2[:, 0:W], func=AF.Ln), reads=[K("m2")], writes=[K("m2")])
        SS.add("act", lambda e: e.activation(out=m2[:, 0:W], in_=m2[:, 0:W], func=AF.Exp, scale=0.5), reads=[K("m2")], writes=[K("m2")])
        SS.add("dve", lambda e: e.tensor_tensor(out=bt[:, 0:W], in0=tg_[:, 0:W], in1=xc[:, 0:W], op=ALU.mult),
               reads=[K("tig"), K("xc")], writes=[K("bt")])
        SS.add("dve", lambda e: e.tensor_tensor(out=bt[:, 0:W], in0=bt[:, 0:W], in1=m2[:, 0:W], op=ALU.mult),
               reads=[K("bt"), K("m2")], writes=[K("bt")])
        if ti == 0:
            SS.add("dve", lambda e: e.memset(bt[:, 0:PAD], 0.0), reads=[K("bt")], writes=[K("bt")])
        SS.add("dve", lambda e: e.tensor_tensor_scan(out=hl_sb[c][:, 0:W], data0=a_[:, 0:W], data1=bt[:, 0:W],
                                                     initial=hst[:, c:c + 1], op0=ALU.mult, op1=ALU.add),
               reads=[K("a"), K("bt"), K("hst")], writes=[("hl", c)])
        SS.add("dve", lambda e: e.tensor_copy(out=hst[:, c:c + 1], in_=hl_sb[c][:, W - 1:W]), reads=[("hl", c)], writes=[K("hst")])

    def yg_chain(SS, ti, c):
        pos0, W = tile_info(ti)
        hp = ti % 2
        b = B_PJ0 + c
        y, y2, tg = y_sb[c], y2_sb[c], y2_sb[c]
        K = lambda n: (n, c)
        proj(SS, b, 1024 + c * 128, W, hp)
        SS.add("act", lambda e: e.copy(out=y[:, 0:W], in_=bank(b, W)), reads=[("ps", b)], writes=[K("y")])
        SS.add("act", lambda e: e.activation(out=y2[:, 0:W], in_=bank(b, W), func=AF.Square), reads=[("ps", b)], writes=[K("y2")])
        SS.add("dve", lambda e: e.tensor_scalar(out=y2[:, 0:W], in0=y2[:, 0:W], scalar1=0.044715, scalar2=1.0, op0=ALU.mult, op1=ALU.add),
               reads=[K("y2")], writes=[K("y2")])
        SS.add("dve", lambda e: e.tensor_tensor(out=y2[:, 0:W], in0=y2[:, 0:W], in1=y[:, 0:W], op=ALU.mult),
               reads=[K("y2"), K("y")], writes=[K("y2")])
        SS.add("act", lambda e: e.activation(out=tg[:, 0:W], in_=y2[:, 0:W], func=AF.Exp, scale=-2.0 * GELU_C), reads=[K("y2")], writes=[K("y2")])
        SS.add("act", lambda e: e.activation(out=tg[:, 0:W], in_=tg[:, 0:W], func=AF.Ln, bias=1.0), reads=[K("y2")], writes=[K("y2")])
        SS.add("act", lambda e: e.activation(out=tg[:, 0:W], in_=tg[:, 0:W], func=AF.Exp, scale=-1.0), reads=[K("y2")], writes=[K("y2")])
        SS.add("dve", lambda e: e.tensor_tensor(out=tg[:, 0:W], in0=tg[:, 0:W], in1=y[:, 0:W], op=ALU.mult),
               reads=[K("y2"), K("y")], writes=[K("y2")])
        SS.add("dve", lambda e: e.tensor_tensor(out=ol_bf[c][:, 0:W], in0=tg[:, 0:W], in1=hl_sb[c][:, 0:W], op=ALU.mult),
               reads=[K("y2"), ("hl", c)], writes=[("ol", c)])
        sfn, nd = store_fn(ol_bf[c], 256 + c * 128, ti)
        SS.add("pool", sfn, reads=[("ol", c)], writes=[("mo", "l", c, ti)], chan=("ol", c), ndma=nd)

    def mk(fn, *a_):
        r = Rec()
        fn(r, *a_)
        return r

    S.barrier()
    zip_emit([mk(prep_chain, 0, 0)])
    for ti in range(NQT):
        preps = []
        if ti + 1 < NQT:
            _, Wn = tile_info(ti + 1)
            preps = [mk(prep_chain, ti + 1, sub) for sub in range(Wn // 128)]
        groups = [
            [mk(qk_chain, ti, 0, 0), mk(qk_chain, ti, 0, 1)],
            [mk(qk_chain, ti, 1, 0), mk(qk_chain, ti, 1, 1)],
            [mk(xr_chain, ti, 0), mk(xr_chain, ti, 1)],
            [mk(yg_chain, ti, 0), mk(yg_chain, ti, 1)],
        ]
        for gi_, grp in enumerate(groups):
            zip_emit(grp)
            if gi_ < len(preps):
                zip_emit([preps[gi_]])

    S.barrier()
    A.off = mark12
    e_sb = [A.f32(1024) for _ in range(3)]
    sp_sb = [A.bf(1024) for _ in range(3)]
    g_sb = [A.bf(1024) for _ in range(2)]
    w_sb = [A.bf(1024) for _ in range(3)]
    osb = [A.bf(2 * 512), A.bf(2 * 512)]
    pst32 = [A.f32(DFF), A.f32(DFF)]
    pst16 = [A.bf(DFF), A.bf(DFF)]
    pieces = []
    for c in range(DC):
        pieces.append((w_out[c * 128:(c + 1) * 128, :], wout_s[c * 128:(c + 1) * 128, :], 1024, None))
    for c in range(DC):
        for hh in range(2):
            pieces.append((w_fi[c * 128:(c + 1) * 128, hh * DFF:(hh + 1) * DFF], wfi_s[c * 128:(c + 1) * 128, hh * DFF:(hh + 1) * DFF],
                           DFF, col(V_G2 + c)))
    for j in range(NPAIR):
        pieces.append((w_fo[j * 128:(j + 1) * 128, :], wfo_s[j * 128:(j + 1) * 128, :], 1024, None))
    wscr_keys = [("wscr", i) for i in range(len(pieces))]

    def piece_stage(i, st):
        src, dst, ncol, sc = pieces[i]
        sl = i % 2
        if st == 0:
            S.add("sp", lambda e, s: e.dma_start(out=pst32[sl][:, 0:ncol], in_=src).then_inc(s, 16),
                  writes=[("p32", sl)], chan=("p32", sl))
        elif st == 1:
            if sc is None:
                S.add("pool", lambda e: e.tensor_copy(out=pst16[sl][:, 0:ncol], in_=pst32[sl][:, 0:ncol]),
                      reads=[("p32", sl)], writes=[("p16", sl)])
            else:
                S.add("dve", lambda e: e.tensor_scalar(out=pst16[sl][:, 0:ncol], in0=pst32[sl][:, 0:ncol], scalar1=sc, scalar2=None, op0=ALU.mult),
                      reads=[("p32", sl), "vec"], writes=[("p16", sl)])
        else:
            S.add("sp", lambda e, s: e.dma_start(out=dst, in_=pst16[sl][:, 0:ncol]).then_inc(s, 16),
                  reads=[("p16", sl)], writes=[("wscr", i)], chan=("p16", sl))

    items = []
    for ti in range(NQT):
        pos0, W = tile_info(ti)
        b0 = pos0 // 128
        nsub = W // 128
        kbs = list(range(b0 + nsub - 1, -1, -1))
        for n, kb in enumerate(kbs):
            for j in range(2):
                items.append(dict(ti=ti, pos0=pos0, W=W, b0=b0, kb=kb, j=j, first=(n == 0), last=(n == len(kbs) - 1)))
    NI = len(items)

    S.add("dve", lambda e: e.memset(ext[:, 9:10], 0.0),
          reads=[("qk", w, j, t) for w in range(2) for j in range(2) for t in range(NQT)] + [("v", bl) for bl in range(NB)],
          writes=["kall"])

    def v3(buf, c0, W):
        return buf.rearrange("p (h w) -> p h w", h=2)[:, :, c0:W]

    def zview(c0, W):
        return ps[:, 0:1024].rearrange("p (h w) -> p h w", h=2)[:, :, c0:W]

    def pview(j, c0, W):
        return ps[:, (2 + 2 * j) * 512:(4 + 2 * j) * 512].rearrange("p (h w) -> p h w", h=2)[:, :, c0:W]

    RG = [[0, 1], [2, 3], [4, 5], [6, 7]]

    def tile_keys(ti):
        return [("mo", "l", c, ti) for c in range(2)] + [("mo", "a", j, ti) for j in range(2)]

    gathered = set()

    def maybe_gather(ti):
        if ti == 0:
            return
        done_tok = 512 * ti
        for g in range(4):
            if g in gathered or (g + 1) * GW > done_tok:
                continue
            gathered.add(g)
            t_lo = 1 + (g * GW) // 512
            t_hi = 1 + ((g + 1) * GW - 1) // 512
            keys = []
            for t in range(t_lo, t_hi + 1):
                keys += tile_keys(t)
            S.add("pool", lambda e, s, g=g: e.collective_compute(
                "AllGather", ALU.bypass, replica_groups=RG, ins=[mo_g[g].ap().opt()],
                outs=[ma_big[g * 1024:(g + 1) * 1024, :].opt()]).then_inc(s),
                reads=keys, writes=[("mall", g)], chan=("cc", g), inc=1)

    def c0_of(it):
        return 128 * (it["kb"] - it["b0"]) if it["kb"] >= it["b0"] else 0

    def PE1(i):
        it = items[i]
        kb, W, pos0, j = it["kb"], it["W"], it["pos0"], it["j"]
        c0 = c0_of(it)

        def fn(e):
            ins = None
            for hh in range(2):
                r = hh * 64
                ins = e.matmul(ps[:, hh * 512 + c0:hh * 512 + W], lhsT=kT[r:r + 64, j * TP + kb * 128:j * TP + (kb + 1) * 128],
                               rhs=qT[r:r + 64, j * TP + pos0 + c0:j * TP + pos0 + W], start=True, stop=True)
            return ins
        S.add("pe", fn, reads=["kall"], writes=[("ps", 0), ("ps", 1)])

    def ACT12(i):
        it = items[i]
        W = it["W"]
        c0 = c0_of(it)
        eb = e_sb[i % 3]
        sb = sp_sb[i % 3]
        S.add("act", lambda e: e.activation(out=v3(eb, c0, W), in_=zview(c0, W), func=AF.Exp),
              reads=[("ps", 0), ("ps", 1)], writes=[("e", i % 3)])
        S.add("act", lambda e: e.activation(out=v3(sb, c0, W), in_=v3(eb, c0, W), func=AF.Ln, bias=1.0),
              reads=[("e", i % 3)], writes=[("sp", i % 3)])
        if it["kb"] >= it["b0"]:
            for hh in range(2):
                S.add("dve", lambda e, hh=hh: e.tensor_tensor(out=sb[:, hh * 512 + c0:hh * 512 + c0 + 128],
                                                               in0=sb[:, hh * 512 + c0:hh * 512 + c0 + 128], in1=maskv(0, 128), op=ALU.mult),
                      reads=[("sp", i % 3), "cst"], writes=[("sp", i % 3)])

    def PE2(i):
        it = items[i]
        W, j = it["W"], it["j"]
        c0 = c0_of(it)
        sb = sp_sb[i % 3]

        def fn(e):
            ins = None
            for hh in range(2):
                b_ = 2 + 2 * j + hh
                ins = e.matmul(ps[:, b_ * 512 + c0:b_ * 512 + W], lhsT=negL, rhs=sb[:, hh * 512 + c0:hh * 512 + W],
                               start=it["first"], stop=False, skip_group_check=True)
            return ins
        S.add("pe", fn, reads=[("sp", i % 3), "cst"], writes=[("ps", 2 + 2 * j), ("ps", 3 + 2 * j)])

    def ACT3(i):
        it = items[i]
        W, j = it["W"], it["j"]
        c0 = c0_of(it)
        gb = g_sb[i % 2]
        eb = e_sb[i % 3]
        wb = w_sb[i % 3]
        S.add("act", lambda e: e.activation(out=v3(gb, c0, W), in_=pview(j, c0, W), func=AF.Exp),
              reads=[("ps", 2 + 2 * j), ("ps", 3 + 2 * j)], writes=[("g", i % 2)])
        S.add("dve", lambda e: e.tensor_tensor(out=v3(wb, c0, W), in0=v3(eb, c0, W), in1=v3(gb, c0, W), op=ALU.mult),
              reads=[("e", i % 3), ("g", i % 2)], writes=[("w", i % 3)])
        if it["kb"] >= it["b0"]:
            for hh in range(2):
                S.add("dve", lambda e, hh=hh: e.tensor_tensor(out=wb[:, hh * 512 + c0:hh * 512 + c0 + 128],
                                                               in0=wb[:, hh * 512 + c0:hh * 512 + c0 + 128], in1=maskv(0, 128), op=ALU.mult),
                      reads=[("w", i % 3), "cst"], writes=[("w", i % 3)])

    def PE4(i):
        it = items[i]
        if it["last"]:
            return
        W, j = it["W"], it["j"]
        c0 = c0_of(it)
        sb = sp_sb[i % 3]

        def fn(e):
            ins = None
            for hh in range(2):
                b_ = 2 + 2 * j + hh
                ins = e.matmul(ps[:, b_ * 512 + c0:b_ * 512 + W], lhsT=negU, rhs=sb[:, hh * 512 + c0:hh * 512 + W],
                               start=False, stop=False, skip_group_check=True)
            return ins
        S.add("pe", fn, reads=[("sp", i % 3), "cst"], writes=[("ps", 2 + 2 * j), ("ps", 3 + 2 * j)])

    def PE3(i):
        it = items[i]
        kb, W, pos0, ti, j = it["kb"], it["W"], it["pos0"], it["ti"], it["j"]
        c0 = c0_of(it)
        ob = 6 + j
        wb = w_sb[i % 3]
        par = ti % 2

        def fn(e):
            ins = None
            for hh in range(2):
                h = 2 * j + hh
                ins = e.matmul(ps[hh * 64:(hh + 1) * 64, ob * 512 + c0:ob * 512 + W], lhsT=v_sb[:, kb * 256 + h * 64:kb * 256 + (h + 1) * 64],
                               rhs=wb[:, hh * 512 + c0:hh * 512 + W], start=it["first"], stop=it["last"], skip_group_check=True)
            return ins
        S.add("pe", fn, reads=[("w", i % 3), "kall"], writes=[("ps", ob)])
        if it["last"]:
            S.add("dve", lambda e: e.tensor_copy(out=osb[par][:, j * 512:j * 512 + W], in_=bank(ob, W)),
                  reads=[("ps", ob)], writes=[("osb", par, j)])
            sfn, nd = store_fn(osb[par][:, j * 512:(j + 1) * 512], j * 128, ti)
            S.add("pool", sfn, reads=[("osb", par, j)], writes=[("mo", "a", j, ti)], chan=("osb", par, j), ndma=nd)
            if j == 1:
                maybe_gather(ti)

    PERIOD = max(4, (NI - 8) // len(pieces))
    PE1(0)
    for s in range(-1, NI + 1):
        if s >= 0:
            pi_, ph_ = divmod(s, PERIOD)
            if pi_ < len(pieces):
                if ph_ == 0:
                    piece_stage(pi_, 0)
                elif ph_ == PERIOD // 2:
                    piece_stage(pi_, 1)
            if pi_ >= 1 and pi_ - 1 < len(pieces) and ph_ == 1:
                piece_stage(pi_ - 1, 2)
        if 0 <= s + 1 < NI:
            ACT12(s + 1)
        if s + 2 < NI:
            PE1(s + 2)
        if 0 <= s + 1 < NI:
            PE2(s + 1)
        if 0 <= s < NI:
            ACT3(s)
            PE4(s)
        if 0 <= s - 1 < NI:
            PE3(s - 1)

    done_p = min(len(pieces), (NI // PERIOD) + 1)
    for i in range(len(pieces)):
        last_s = NI
        if i * PERIOD > last_s:
            piece_stage(i, 0)
        if i * PERIOD + PERIOD // 2 > last_s:
            piece_stage(i, 1)
        if (i + 1) * PERIOD + 1 > last_s:
            piece_stage(i, 2)

    S.barrier(exclude=lambda ch: isinstance(ch, tuple) and ch[0] == "cc")
    A.off = mark_stage
    wout_bf = A.bf(8 * 1024)
    wfi_bf = A.bf(8 * 2 * DFF)
    wfo_bf = A.bf(NPAIR * 1024)
    w3_keys = []
    qsel = ["sp", "pool"]
    nq = [0]

    def wload(dst, src, key):
        q = qsel[nq[0] % 2]
        nq[0] += 1
        S.add(q, lambda e, s: e.dma_start(out=dst, in_=src).then_inc(s, 16), reads=wscr_keys, writes=[key], chan=key)
        w3_keys.append(key)

    for hf in range(2):
        wload(wout_bf.rearrange("p (c n) -> p c n", c=8)[:, hf * 4:(hf + 1) * 4, :],
              wout_s.rearrange("(c p) n -> p c n", p=128)[:, hf * 4:(hf + 1) * 4, :], ("w3", "o", hf))
    for c in range(DC):
        wload(wfi_bf[:, c * 2 * DFF:(c + 1) * 2 * DFF], wfi_s[c * 128:(c + 1) * 128, :], ("w3", "i", c))
    for q4 in range(2):
        wload(wfo_bf.rearrange("p (j n) -> p j n", j=NPAIR)[:, q4 * 11:(q4 + 1) * 11, :],
              wfo_s.rearrange("(j p) n -> p j n", p=128)[:, q4 * 11:(q4 + 1) * 11, :], ("w3", "f", q4))
    S.add("dve", lambda e: e.memset(ext[:, 10:11], 0.0), reads=w3_keys, writes=["w3all"])

    S.add("pool", lambda e, s: e.collective_compute("AllGather", ALU.bypass, replica_groups=RG,
                                                    ins=[mo_halo_t.ap().opt()], outs=[ma_halo_t.ap().opt()]).then_inc(s),
          reads=tile_keys(0) + tile_keys(1 + (HALF - 2) // 512), writes=[("mall", "h")], chan="cch", inc=1)

    def mh_fn(e, s):
        half = e.partition_id() % 2
        ins = None
        for j in range(2):
            for r in range(2):
                ins = e.dma_start(out=mixed_half[r * 512:(r + 1) * 512, 2 + j * GW:2 + (j + 1) * GW],
                                  in_=ma_big[bass.ds(half * 2048 + j * 1024 + r * 512, 512), :]).then_inc(s, 16)
        ins = e.dma_start(out=mixed_half[:, 0:2], in_=ma_halo[:, bass.ds(half * 2, 2)]).then_inc(s, 16)
        return ins
    S.add("sp", mh_fn, reads=[("mall", g) for g in range(4)] + [("mall", "h")], writes=["mhalf"], chan="mh", ndma=5)

    h2 = [A.f32(1024) for _ in range(4)]
    hn2_bf = [A.bf(1024), A.bf(1024)]
    ss3 = A.f32(8)
    HW = 2 + W3
    hn2T = [A.bf(8 * HW), A.bf(8 * HW)]
    mt = A.bf(8 * W3)
    ubuf = [A.f32(2 + W3), A.f32(2 + W3)]
    gbuf = [A.f32(2 + W3), A.f32(2 + W3)]
    uc = [A.f32(W3), A.f32(W3)]
    gc = [A.f32(W3), A.f32(W3)]
    sg = [A.f32(W3), A.f32(W3)]
    act_bf = A.bf(NPAIR * W3)

    B_FO = [0, 1]
    B_T3 = 2
    B_U = [3, 4, 0]
    B_G = [5, 6, 1]
    B_WO = 7
    mixed_half_r = mixed_half.rearrange("(c p) w -> p c w", p=128)

    def tile_geom(t):
        if t < 0:
            return 0, 2, 2, 1, 1
        return 2 + t * W3, W3, 128, W3 // 128, t % 2

    def prep1(t):
        rel0, Wt, n, nsub, par = tile_geom(t)
        S.add("sp", lambda e, s: e.dma_start(out=mt.rearrange("p (c w) -> p c w", c=8)[:, :, 0:Wt],
                                              in_=mixed_half_r[:, :, rel0:rel0 + Wt]).then_inc(s, 16),
              reads=["mhalf"], writes=["mt"], chan="mt")
        for sub in range(nsub):
            sl = 2 * par + sub
            q = sub
            r = rel0 + sub * 128
            S.add("sp", lambda e, s, sl=sl, r=r: e.dma_start(out=h2[sl][0:n, :], in_=xhalf[r:r + n, :]).then_inc(s, 16),
                  writes=[("h2", sl)], chan=("h2", sl))
            for hf in range(2):
                def wo_fn(e, sub=sub, hf=hf):
                    ins = None
                    for c in range(DC):
                        ins = e.matmul(ps[0:n, B_WO * 512:(B_WO + 1) * 512], lhsT=mt[:, c * W3 + sub * 128:c * W3 + sub * 128 + n],
                                       rhs=wout_bf[:, c * 1024 + hf * 512:c * 1024 + (hf + 1) * 512], start=(c == 0), stop=(c == DC - 1))
                    return ins
                S.add("pe", wo_fn, reads=["mt", "w3all"], writes=[("ps", B_WO)])
                S.add("dve", lambda e, sl=sl, hf=hf: e.tensor_tensor(out=h2[sl][0:n, hf * 512:(hf + 1) * 512],
                                                                     in0=ps[0:n, B_WO * 512:(B_WO + 1) * 512],
                                                                     in1=h2[sl][0:n, hf * 512:(hf + 1) * 512], op=ALU.add),
                      reads=[("ps", B_WO), ("h2", sl)], writes=[("h2", sl)])
            S.add("pool", lambda e, q=q: e.memset(ss3[:, q:q + 1], 0.0), writes=[("ss3", q)])
            S.add("act", lambda e, sl=sl, q=q: e.activation(out=hn2_bf[q][0:n, :], in_=h2[sl][0:n, :], func=AF.Square,
                                                            accum_out=ss3[0:n, q:q + 1]),
                  reads=[("h2", sl)], writes=[("ss3", q), ("hn2", q)])
            S.add("dve", lambda e, q=q: e.tensor_scalar(out=ss3[0:n, 2 + q:3 + q], in0=ss3[0:n, q:q + 1], scalar1=1.0 / D, scalar2=EPS,
                                                         op0=ALU.mult, op1=ALU.add), reads=[("ss3", q)], writes=[("rs3", q)])
            S.add("act", lambda e, q=q: e.activation(out=ss3[0:n, 2 + q:3 + q], in_=ss3[0:n, 2 + q:3 + q], func=AF.Ln),
                  reads=[("rs3", q)], writes=[("rs3", q)])
            S.add("act", lambda e, q=q: e.activation(out=ss3[0:n, 2 + q:3 + q], in_=ss3[0:n, 2 + q:3 + q], func=AF.Exp, scale=-0.5),
                  reads=[("rs3", q)], writes=[("rs3", q)])
            S.add("act", lambda e, sl=sl, q=q: e.activation(out=hn2_bf[q][0:n, :], in_=h2[sl][0:n, :], func=AF.Copy, scale=ss3[0:n, 2 + q:3 + q]),
                  reads=[("h2", sl), ("rs3", q)], writes=[("hn2", q)])

    def prep2(t):
        rel0, Wt, n, nsub, par = tile_geom(t)
        for sub in range(nsub):
            q = sub

            def tr_fn(e, q=q):
                ins = None
                for c in range(DC):
                    ins = e.transpose(bank_bf(B_T3)[:, c * 128:c * 128 + n], hn2_bf[q][0:n, c * 128:(c + 1) * 128], ident[0:n, 0:n])
                return ins
            S.add("pe", tr_fn, reads=[("hn2", q), "cst"], writes=[("ps", B_T3)])
            dcol = 0 if t < 0 else 2 + sub * 128
            dpar = 0 if t < 0 else par
            S.add("dve", lambda e, dcol=dcol, dpar=dpar: e.tensor_copy(
                out=hn2T[dpar].rearrange("p (c w) -> p c w", c=8)[:, :, dcol:dcol + n],
                in_=bank_bf(B_T3).rearrange("p (c w) -> p c w", c=8)[:, :, 0:n]),
                reads=[("ps", B_T3)], writes=[("hn2T", dpar)])
        if t >= 0 and t + 1 < NST:
            S.add("dve", lambda e: e.tensor_copy(
                out=hn2T[1 - par].rearrange("p (c w) -> p c w", c=8)[:, :, 0:2],
                in_=hn2T[par].rearrange("p (c w) -> p c w", c=8)[:, :, W3:W3 + 2]),
                reads=[("hn2T", par)], writes=[("hn2T", 1 - par)])

    pair_ctr = [0]

    def ffn_in(t):
        rel0, Wt, n, nsub, par = tile_geom(t)
        final = t >= 0
        pcs = {}

        def mm(j):
            pc = pair_ctr[0] % 3
            pair_ctr[0] += 1
            pcs[j] = pc
            for which, bb in enumerate((B_U[pc], B_G[pc])):
                slab = j + which * NPAIR

                def fi_fn(e, bb=bb, slab=slab):
                    ins = None
                    for c in range(DC):
                        ins = e.matmul(bank(bb, Wt + 2), lhsT=wfi_bf[:, c * 2 * DFF + slab * 128:c * 2 * DFF + (slab + 1) * 128],
                                       rhs=hn2T[par][:, c * HW:c * HW + Wt + 2], start=(c == 0), stop=(c == DC - 1))
                    return ins
                S.add("pe", fi_fn, reads=[("hn2T", par), "w3all"], writes=[("ps", bb)])

        def post1(j):
            pc = pcs[j]
            p2 = j % 2
            for which, (bb, buf, cbuf) in enumerate(((B_U[pc], ubuf[p2], uc[p2]), (B_G[pc], gbuf[p2], gc[p2]))):
                slab = j + which * NPAIR
                bk = ("ub", which, p2)
                ck = ("cb", which, p2)
                S.add("act", lambda e, bb=bb, buf=buf: e.copy(out=buf[:, 0:Wt + 2], in_=bank(bb, Wt + 2)), reads=[("ps", bb)], writes=[bk])
                S.add("act", lambda e, buf=buf, cbuf=cbuf, slab=slab: e.activation(
                    out=cbuf[:, 0:Wt], in_=buf[:, 2:2 + Wt], func=AF.Identity, scale=col(V_FCW + slab * 3 + 2), bias=col(V_FCB + slab)),
                    reads=[bk, "vec"], writes=[ck])
                for k in range(2):
                    S.add("dve", lambda e, buf=buf, cbuf=cbuf, slab=slab, k=k: e.scalar_tensor_tensor(
                        out=cbuf[:, 0:Wt], in0=buf[:, k:k + Wt], scalar=col(V_FCW + slab * 3 + k), in1=cbuf[:, 0:Wt],
                        op0=ALU.mult, op1=ALU.add), reads=[bk, "vec", ck], writes=[ck])

        def post2(j):
            pc = j % 2
            S.add("act", lambda e: e.activation(out=sg[pc][:, 0:Wt], in_=gc[pc][:, 0:Wt], func=AF.Silu),
                  reads=[("cb", 1, pc)], writes=[("sg", pc)])
            S.add("dve", lambda e: e.tensor_tensor(out=act_bf[:, j * W3:j * W3 + Wt], in0=sg[pc][:, 0:Wt], in1=uc[pc][:, 0:Wt], op=ALU.mult),
                  reads=[("sg", pc), ("cb", 0, pc)], writes=[("actT", j)])

        for j in range(NPAIR + 2):
            if j < NPAIR:
                mm(j)
            if 0 <= j - 1 < NPAIR:
                post1(j - 1)
            if 0 <= j - 2 < NPAIR:
                post2(j - 2)

    def ffn_out(t, sub):
        rel0, Wt, n, nsub, par = tile_geom(t)
        sl = 2 * par + sub

        def fo_fn(e):
            ins = None
            for hf in range(2):
                for j in range(NPAIR):
                    ins = e.matmul(bank(B_FO[hf], 512), lhsT=act_bf[:, j * W3 + sub * 128:j * W3 + (sub + 1) * 128],
                                   rhs=wfo_bf[:, j * 1024 + hf * 512:j * 1024 + (hf + 1) * 512], start=(j == 0), stop=(j == NPAIR - 1))
            return ins
        S.add("pe", fo_fn, reads=[("actT", j) for j in range(NPAIR)] + ["w3all"], writes=[("ps", 0), ("ps", 1)])
        S.add("dve", lambda e: e.tensor_tensor(out=h2[sl][:, :], in0=ps[:, 0:1024], in1=h2[sl][:, :], op=ALU.add),
              reads=[("ps", 0), ("ps", 1), ("h2", sl)], writes=[("h2", sl)])
        r0 = rel0 - 2 + sub * 128
        S.add("pool", lambda e, s: e.dma_start(out=out[r0:r0 + 128, :], in_=h2[sl][:, :]).then_inc(s, 16),
              reads=[("h2", sl)], writes=[("h2", sl), ("out", r0)], chan=("h2", sl))

    prep1(-1)
    prep2(-1)
    prep1(0)
    prep2(0)
    for t in range(NST):
        ffn_in(t)
        if t + 1 < NST:
            prep1(t + 1)
        ffn_out(t, 0)
        if t + 1 < NST:
            prep2(t + 1)
        ffn_out(t, 1)

    S.add("sp", lambda e: None, reads=[("out", r0) for r0 in range(0, HALF, 128)], writes=["done"])

    S.finalize()
    chans = list(S.chan_count.keys())
    sem_ctxs = []
    sems = {}
    for e in Sched.ENG:
        c = nc.semaphore("s_" + e)
        sems[e] = c.__enter__()
        sem_ctxs.append(c)
    chan_sems = {}
    for i, ch in enumerate(chans):
        c = nc.semaphore("c_%d" % i)
        chan_sems[ch] = c.__enter__()
        sem_ctxs.append(c)
    with nc.Block() as block:
        S.emit(nc, block, sems, chan_sems)
    for c in reversed(sem_ctxs):
        c.__exit__(None, None, None)
    pctx.__exit__(None, None, None)
    ctx.__exit__(None, None, None)
    return nc


def _consts():
    c = np.zeros((128, NCST), np.float32)
    j = np.arange(128)[:, None]
    s = np.arange(128)[None, :]
    c[:, C_ID:C_ID + 128] = (j == s)
    c[:, C_NL:C_NL + 128] = -(j >= s).astype(np.float32)
    c[:, C_NU:C_NU + 128] = -(j < s).astype(np.float32)
    c[:, C_BO:C_BO + 128] = ((j // 64) == (s // 64))
    t = np.arange(512)[None, :]
    for k in range(4):
        c[:, C_MK + k * 512:C_MK + (k + 1) * 512] = ((128 * k + j) < t)
    return c


def _prep_inputs(inputs, SEQ):
    f = lambda a: np.ascontiguousarray(np.asarray(a), dtype=np.float32)
    x = f(inputs["x"])
    meta = f(inputs["meta_tokens"])
    w_in = f(inputs["w_in"])[0]
    w_out = f(inputs["w_out"])[0]
    w_fi = f(inputs["w_ffn_in"])[0]
    w_fo = f(inputs["w_ffn_out"])[0]
    g1 = f(inputs["norm1_g"])[0]
    g2 = f(inputs["norm2_g"])[0]
    qg = f(inputs["q_norm_g"])[0]
    kg = f(inputs["k_norm_g"])[0]
    cw = f(inputs["conv_w"])[0]
    cb = f(inputs["conv_b"])[0]
    wa = f(inputs["w_rg_a"])[0]
    wi = f(inputs["w_rg_i"])[0]
    ba = f(inputs["b_rg_a"])[0]
    bi = f(inputs["b_rg_i"])[0]
    lam = f(inputs["lru_lambda"])[0]
    fcw = f(inputs["ffn_conv_w"])[0]
    fcb = f(inputs["ffn_conv_b"])[0]
    consts = _consts()
    TP = SEQ + 128
    maps = []
    for core in range(8):
        b, p = core // 2, core % 2
        xpad = np.zeros((TP, D), np.float32)
        xpad[PAD:128] = meta
        xpad[128:] = x[b]
        cs = slice(256 * p, 256 * p + 256)
        wic = np.concatenate([w_in[:, 0:512][:, cs], w_in[:, 512:1024][:, cs], w_in[:, 1024:1536][:, cs],
                              w_in[:, 1536:2048][:, cs], w_in[:, 2048:2560][:, cs]], axis=1)
        vec = np.zeros((128, NV), np.float32)
        vec[:, V_G1:V_G1 + 8] = g1.reshape(8, 128).T
        vec[:, V_G2:V_G2 + 8] = g2.reshape(8, 128).T
        vec[:, V_QG] = np.tile(qg, 2)
        vec[:, V_KG] = np.tile(kg, 2)
        for c in range(2):
            ch = slice(256 * p + 128 * c, 256 * p + 128 * c + 128)
            vec[:, V_CW + c * 4:V_CW + c * 4 + 4] = cw[:, ch].T
            vec[:, V_CB + c] = cb[ch]
            vec[:, V_BA + c] = ba[ch]
            vec[:, V_BI + c] = bi[ch]
            vec[:, V_LAM + c] = lam[ch]
        for s in range(NSLAB):
            vec[:, V_FCW + s * 3:V_FCW + s * 3 + 3] = fcw[:, s * 128:(s + 1) * 128].T
            vec[:, V_FCB + s] = fcb[s * 128:(s + 1) * 128]
        wrg = np.zeros((128, 4 * 128), np.float32)
        for gi, wsrc in enumerate((wa, wi)):
            for c in range(2):
                for k in range(2):
                    blk = 4 * p + 2 * c + k
                    o = (gi * 2 + c) * 128
                    wrg[64 * k:64 * k + 64, o + 64 * k:o + 64 * k + 64] = wsrc[blk]
        perm = []
        for r in range(2):
            perm += list(range(256 * r, 256 * r + 256))
            perm += list(range(512 + 256 * r, 512 + 256 * r + 256))
        maps.append({
            "xpad": xpad, "xhalf": np.ascontiguousarray(xpad[126 + (SEQ // 2) * p:126 + (SEQ // 2) * p + SEQ // 2 + 2]), "w_in": np.ascontiguousarray(wic), "vecs": vec, "w_rg": wrg,
            "w_out": np.ascontiguousarray(w_out[perm]), "w_ffn_in": w_fi, "w_ffn_out": w_fo, "consts": consts,
        })
    return maps


_NC_CACHE = {}


def kernel(**inputs):
    x = np.asarray(inputs["x"])
    B, SEQ, _ = x.shape
    assert B == 4
    if SEQ not in _NC_CACHE:
        _NC_CACHE[SEQ] = build_nc(SEQ)
    nc = _NC_CACHE[SEQ]
    maps = _prep_inputs(inputs, SEQ)
    res = run_bass_kernel_spmd(nc, maps, core_ids=list(range(8)))
    outp = np.empty((B, SEQ, D), np.float32)
    HALF = SEQ // 2
    for core in range(8):
        b, p = core // 2, core % 2
        outp[b, p * HALF:(p + 1) * HALF] = res.results[core]["out"]
    return outp
```

```python
import numpy as np
import concourse.bass as bass
import concourse.mybir as mybir
from concourse.bass_utils import run_bass_kernel_spmd

F32 = mybir.dt.float32
BF16 = mybir.dt.bfloat16
AF = mybir.ActivationFunctionType
ALU = mybir.AluOpType

D = 1024
DC = 8
DFF = 2816
NSLAB = 44
NPAIR = 22
EPS = 1e-6
NMETA = 16
PAD = 112
GELU_C = 0.7978845608028654

V_G1, V_G2, V_QG, V_KG, V_CW, V_CB, V_BA, V_BI, V_LAM, V_FCW, V_FCB = 0, 8, 16, 17, 18, 26, 28, 30, 32, 34, 166
NV = 210
C_ID, C_NL, C_NU, C_BO, C_MK = 0, 128, 256, 384, 512
NCST = 512 + 4 * 512


class Sched:
    ENG = ("pe", "act", "dve", "pool", "sp")

    def __init__(self):
        self.ops = []
        self.last_w = {}
        self.readers = {}
        self.eng_count = {e: 0 for e in self.ENG}
        self.chan_count = {}
        self.chan_inc = {}
        self.barrier_nodes = []

    def add(self, eng, fn, reads=(), writes=(), chan=None, ndma=1, inc=16):
        deps = set(self.barrier_nodes)
        for k in reads:
            w = self.last_w.get(k)
            if w is not None:
                deps.add(w)
        for k in writes:
            w = self.last_w.get(k)
            if w is not None:
                deps.add(w)
            for r in self.readers.get(k, ()):
                deps.add(r)
        if chan is None:
            self.eng_count[eng] += 1
            node = ("E", eng, self.eng_count[eng])
        else:
            self.chan_inc[chan] = inc
            self.chan_count[chan] = self.chan_count.get(chan, 0) + ndma * inc
            node = ("C", chan, self.chan_count[chan])
        self.ops.append(dict(eng=eng, fn=fn, deps=deps, node=node, chan=chan))
        for k in reads:
            self.readers.setdefault(k, []).append(node)
        for k in writes:
            self.last_w[k] = node
            self.readers[k] = []
        return node

    def barrier(self, exclude=None):
        nodes = []
        for e, c in self.eng_count.items():
            if c:
                nodes.append(("E", e, c))
        for ch, c in self.chan_count.items():
            if exclude is not None and exclude(ch):
                continue
            nodes.append(("C", ch, c))
        self.barrier_nodes = nodes

    def finalize(self):
        known = {e: {} for e in self.ENG}
        signal = {e: set() for e in self.ENG}
        for op in self.ops:
            need = {}
            for kind, tgt, val in op["deps"]:
                if kind == "E" and tgt == "pe" and op["eng"] == "pe" and op["chan"] is None:
                    continue
                key = (kind, tgt)
                if val > need.get(key, 0):
                    need[key] = val
            waits = []
            kn = known[op["eng"]]
            for key, val in need.items():
                if kn.get(key, 0) >= val:
                    continue
                kn[key] = val
                waits.append((key, val))
                if key[0] == "E":
                    signal[key[1]].add(val)
            op["waits"] = waits
        self.rank = {}
        for e in self.ENG:
            self.rank[e] = {v: i + 1 for i, v in enumerate(sorted(signal[e]))}

    def emit(self, nc, block, sems, chan_sems):
        streams = {e: [op for op in self.ops if op["eng"] == e] for e in self.ENG}

        def run(eng_name, eng):
            for op in streams[eng_name]:
                for (kind, tgt), val in op["waits"]:
                    if kind == "E":
                        eng.wait_ge(sems[tgt], self.rank[tgt][val])
                    else:
                        eng.wait_ge(chan_sems[tgt], val)
                if op["chan"] is not None:
                    op["fn"](eng, chan_sems[op["chan"]])
                else:
                    ins = op["fn"](eng)
                    idx = op["node"][2]
                    if idx in self.rank[eng_name]:
                        assert ins is not None
                        ins.then_inc(sems[eng_name], 1)

        @block.tensor
        def _(e):
            run("pe", e)

        @block.scalar
        def _(e):
            run("act", e)

        @block.vector
        def _(e):
            run("dve", e)

        @block.gpsimd
        def _(e):
            run("pool", e)

        @block.sync
        def _(e):
            run("sp", e)


def build_nc(SEQ):
    assert SEQ % 1024 == 0
    HALF = SEQ // 2
    TP = SEQ + 128
    NB = TP // 128
    NQT = 1 + SEQ // 512
    NST = HALF // 256
    W3 = 256

    nc = bass.Bass("TRN2", target_bir_lowering=False)
    xpad = nc.dram_tensor("xpad", [TP, D], F32, kind="ExternalInput").ap()
    w_in = nc.dram_tensor("w_in", [D, 1280], F32, kind="ExternalInput").ap()
    vecs = nc.dram_tensor("vecs", [128, NV], F32, kind="ExternalInput").ap()
    w_rg = nc.dram_tensor("w_rg", [128, 4 * 128], F32, kind="ExternalInput").ap()
    w_out = nc.dram_tensor("w_out", [D, D], F32, kind="ExternalInput").ap()
    w_fi = nc.dram_tensor("w_ffn_in", [D, 2 * DFF], F32, kind="ExternalInput").ap()
    w_fo = nc.dram_tensor("w_ffn_out", [DFF, D], F32, kind="ExternalInput").ap()
    consts = nc.dram_tensor("consts", [128, NCST], F32, kind="ExternalInput").ap()
    out = nc.dram_tensor("out", [HALF, D], F32, kind="ExternalOutput").ap()
    GW = SEQ // 4
    mo_g = [nc.dram_tensor("mixed_own_%d" % g, [512, GW], BF16) for g in range(4)]
    mo_halo_t = nc.dram_tensor("mixed_own_halo", [512, 4], BF16)
    ma_big_t = nc.dram_tensor("mixed_all_big", [4 * 1024, GW], BF16)
    ma_halo_t = nc.dram_tensor("mixed_all_halo", [1024, 4], BF16)
    xhalf = nc.dram_tensor("xhalf", [HALF + 2, D], F32, kind="ExternalInput").ap()
    wout_s = nc.dram_tensor("wout_bf16", [D, D], BF16).ap()
    wfi_s = nc.dram_tensor("wfi_bf16", [D, 2 * DFF], BF16).ap()
    wfo_s = nc.dram_tensor("wfo_bf16", [DFF, D], BF16).ap()
    mh_t = nc.dram_tensor("mixed_half", [1024, HALF + 2], BF16)
    mixed_half = mh_t.ap()
    ma_big = ma_big_t.ap()
    ma_halo = ma_halo_t.ap()
    mo_halo = mo_halo_t.ap()

    def store_fn(src, row0, ti):
        pieces = []
        if ti == 0:
            pieces.append((mo_halo[row0:row0 + 128, 0:2], 126, 2))
        else:
            i0 = 512 * (ti - 1)
            c = 0
            while c < 512:
                g = (i0 + c) // GW
                gc = (i0 + c) % GW
                n = min(512 - c, GW - gc)
                pieces.append((mo_g[g].ap()[row0:row0 + 128, gc:gc + n], c, n))
                c += n
            if i0 <= HALF - 2 < i0 + 512:
                pieces.append((mo_halo[row0:row0 + 128, 2:4], HALF - 2 - i0, 2))

        def fn(e, s):
            ins = None
            for dst, c0, n in pieces:
                ins = e.dma_start(out=dst, in_=src[:, c0:c0 + n]).then_inc(s, 16)
            return ins
        return fn, len(pieces)

    S = Sched()
    ARENA_F = 53200

    ctx = nc.sbuf_tensor("arena", [128, ARENA_F], F32)
    arena = ctx.__enter__()
    pctx = nc.psum_tensor("ps", [128, 8 * 512], F32)
    ps = pctx.__enter__()

    class Arena:
        def __init__(self):
            self.off = 0

        def f32(self, n):
            o = self.off
            self.off += n
            assert self.off <= ARENA_F, self.off
            return arena[:, o:o + n]

        def bf(self, n):
            nf = (n + 1) // 2
            o = self.off
            self.off += nf
            assert self.off <= ARENA_F, self.off
            return arena[:, o:o + nf].bitcast(BF16)

    A = Arena()

    def bank(b, n=512):
        return ps[:, b * 512:b * 512 + n]

    def bank_bf(b):
        return ps[:, b * 512:(b + 1) * 512].bitcast(BF16)

    cst = A.bf(NCST)
    vec = A.f32(NV)
    ext = A.f32(16)
    ident = cst[:, C_ID:C_ID + 128]
    negL = cst[:, C_NL:C_NL + 128]
    negU = cst[:, C_NU:C_NU + 128]
    bones = cst[:, C_BO:C_BO + 128]

    def maskv(j, W):
        return cst[:, C_MK + j * 512:C_MK + j * 512 + W]

    mark_stage = A.off

    qT = A.bf(2 * TP)
    kT = A.bf(2 * TP)
    v_sb = A.bf(NB * 256)
    mark12 = A.off
    win_bf = A.bf(8 * 1280)
    wrg_bf = A.bf(4 * 128)
    stg = [A.f32(1280), A.f32(1280)]
    xs = [A.f32(1024), A.f32(1024)]
    hn_bf = [A.bf(1024), A.bf(1024)]
    ssb = A.f32(8)
    hnT = [A.bf(8 * 512), A.bf(8 * 512)]
    sqb = [A.bf(512), A.bf(512)]
    rqb = [A.f32(512), A.f32(512)]
    xr_sb = [A.f32(3 + 512), A.f32(3 + 512)]
    xc_sb = [A.f32(512), A.f32(512)]
    xc_bf = [A.bf(512), A.bf(512)]
    tr_sb = [A.f32(512), A.f32(512)]
    ti_sb = [A.f32(512), A.f32(512)]
    a_sb = [A.f32(512), A.f32(512)]
    m2_sb = [A.f32(512), A.f32(512)]
    bt_sb = [A.f32(512), A.f32(512)]
    hl_sb = [A.f32(512), A.f32(512)]
    hst = A.f32(2)
    y_sb = [stg[0][:, 0:512], stg[0][:, 512:1024]]
    y2_sb = [stg[1][:, 0:512], stg[1][:, 512:1024]]
    ol_bf = [stg[0][:, 1024:1280].bitcast(BF16), stg[1][:, 1024:1280].bitcast(BF16)]

    def col(i, n=1):
        return vec[:, i:i + n]

    for hh in range(2):
        S.add("sp", lambda e, s, hh=hh: e.dma_start(out=stg[hh][:, 0:1280], in_=consts[:, hh * 1280:(hh + 1) * 1280]).then_inc(s, 16),
              writes=[("stg", hh)], chan=("stg", hh))
        S.add("dve", lambda e, hh=hh: e.tensor_copy(out=cst[:, hh * 1280:(hh + 1) * 1280], in_=stg[hh][:, 0:1280]),
              reads=[("stg", hh)], writes=["cst"])
    S.add("sp", lambda e, s: e.dma_start(out=vec[:, :], in_=vecs[:, :]).then_inc(s, 16), writes=["vec"], chan="vec")
    S.add("act", lambda e: e.activation(out=ext[:, 7:9], in_=col(V_LAM, 2), func=AF.Exp, scale=-1.0),
          reads=["vec"], writes=["ext_t"])
    S.add("act", lambda e: e.activation(out=ext[:, 7:9], in_=ext[:, 7:9], func=AF.Ln, bias=1.0),
          reads=["ext_t"], writes=["ext_t"])
    S.add("dve", lambda e: e.tensor_scalar(out=ext[:, 0:2], in0=ext[:, 7:9], scalar1=-8.0, scalar2=None, op0=ALU.mult),
          reads=["ext_t"], writes=["ext"])
    S.add("dve", lambda e: e.tensor_scalar(out=ext[:, 2:6], in0=col(V_BA, 4), scalar1=-1.0, scalar2=None, op0=ALU.mult),
          reads=["vec", "ext"], writes=["ext"])
    S.add("dve", lambda e: e.tensor_scalar(out=ext[:, 6:7], in0=col(V_KG), scalar1=0.125, scalar2=None, op0=ALU.mult),
          reads=["vec", "ext"], writes=["ext"])
    S.add("sp", lambda e, s: e.dma_start(out=stg[0][:, 0:512], in_=w_rg[:, :]).then_inc(s, 16),
          writes=[("stg", 0)], chan=("stg", 0))
    S.add("dve", lambda e: e.tensor_copy(out=wrg_bf[:, :], in_=stg[0][:, 0:512]), reads=[("stg", 0)], writes=["wrg"])
    for c in range(DC):
        hh = c % 2
        S.add("sp", lambda e, s, c=c, hh=hh: e.dma_start(out=stg[hh][:, :], in_=w_in[c * 128:(c + 1) * 128, :]).then_inc(s, 16),
              writes=[("stg", hh)], chan=("stg", hh))
        if c % 2 == 0:
            S.add("dve", lambda e, c=c, hh=hh: e.tensor_scalar(out=win_bf[:, c * 1280:(c + 1) * 1280], in0=stg[hh][:, :],
                                                                scalar1=col(V_G1 + c), scalar2=None, op0=ALU.mult),
                  reads=[("stg", hh), "vec"], writes=[("win", c)])
        else:
            S.add("act", lambda e, c=c, hh=hh: e.activation(out=win_bf[:, c * 1280:(c + 1) * 1280], in_=stg[hh][:, :],
                                                             func=AF.Copy, scale=col(V_G1 + c)),
                  reads=[("stg", hh), "vec"], writes=[("win", c)])
    S.add("pool", lambda e: e.memset(xr_sb[0][:, 0:3], 0.0), writes=[("xr", 0)])
    S.add("pool", lambda e: e.memset(xr_sb[1][:, 0:3], 0.0), writes=[("xr", 1)])
    S.add("pool", lambda e: e.memset(hst[:, :], 0.0), writes=["hst"])

    win_reads = [("win", c) for c in range(DC)]

    B_TRP, B_V, B_PJ0, B_PJ1 = 0, 1, 2, 3
    B_PS2 = [4, 5]
    B_GR = [4, 6]
    B_GI = [5, 7]

    def tile_info(ti):
        if ti == 0:
            return 0, 128
        return 128 + 512 * (ti - 1), 512

    class Rec:
        def __init__(self):
            self.l = []

        def add(self, *a_, **k_):
            self.l.append((a_, k_))

    def zip_emit(chains):
        idx = [0] * len(chains)
        left = True
        while left:
            left = False
            for ci, ch in enumerate(chains):
                if idx[ci] < len(ch.l):
                    a_, k_ = ch.l[idx[ci]]
                    S.add(*a_, **k_)
                    idx[ci] += 1
                    left = True

    def proj(SS, b, col0, W, hp):
        def fn(e):
            ins = None
            for c in range(DC):
                ins = e.matmul(bank(b, W), lhsT=win_bf[:, c * 1280 + col0:c * 1280 + col0 + 128],
                               rhs=hnT[hp][:, c * 512:c * 512 + W], start=(c == 0), stop=(c == DC - 1))
            return ins
        SS.add("pe", fn, reads=win_reads + [("hnT", hp)], writes=[("ps", b)])

    def prep_chain(SS, ti, sub):
        pos0, W = tile_info(ti)
        hp = ti % 2
        blk = pos0 // 128 + sub
        sl = blk % 2
        SS.add("sp", lambda e, s: e.dma_start(out=xs[sl][:, :], in_=xpad[blk * 128:(blk + 1) * 128, :]).then_inc(s, 16),
               writes=[("xs", sl)], chan=("xs", sl))
        SS.add("pool", lambda e: e.memset(ssb[:, sl:sl + 1], 0.0), writes=[("ss", sl)])
        SS.add("act", lambda e: e.activation(out=hn_bf[sl][:, :], in_=xs[sl][:, :], func=AF.Square, accum_out=ssb[:, sl:sl + 1]),
               reads=[("xs", sl)], writes=[("ss", sl), ("hn", sl)])
        SS.add("dve", lambda e: e.tensor_scalar(out=ssb[:, 2 + sl:3 + sl], in0=ssb[:, sl:sl + 1], scalar1=1.0 / D, scalar2=EPS,
                                                op0=ALU.mult, op1=ALU.add),
               reads=[("ss", sl)], writes=[("rstd", sl)])
        SS.add("act", lambda e: e.activation(out=ssb[:, 2 + sl:3 + sl], in_=ssb[:, 2 + sl:3 + sl], func=AF.Ln),
               reads=[("rstd", sl)], writes=[("rstd", sl)])
        SS.add("act", lambda e: e.activation(out=ssb[:, 2 + sl:3 + sl], in_=ssb[:, 2 + sl:3 + sl], func=AF.Exp, scale=-0.5),
               reads=[("rstd", sl)], writes=[("rstd", sl)])
        SS.add("act", lambda e: e.activation(out=hn_bf[sl][:, :], in_=xs[sl][:, :], func=AF.Copy, scale=ssb[:, 2 + sl:3 + sl]),
               reads=[("xs", sl), ("rstd", sl)], writes=[("hn", sl)])

        def tr_fn(e):
            ins = None
            for c in range(DC):
                ins = e.transpose(bank_bf(B_TRP)[:, c * 128:(c + 1) * 128], hn_bf[sl][:, c * 128:(c + 1) * 128], ident)
            return ins
        SS.add("pe", tr_fn, reads=[("hn", sl), "cst"], writes=[("ps", B_TRP)])
        SS.add("dve", lambda e: e.tensor_copy(
            out=hnT[hp].rearrange("p (c w) -> p c w", c=8)[:, :, sub * 128:(sub + 1) * 128],
            in_=bank_bf(B_TRP).rearrange("p (c w) -> p c w", c=8)),
            reads=[("ps", B_TRP)], writes=[("hnT", hp)])

        def v_fn(e):
            ins = None
            for c in range(DC):
                ins = e.matmul(bank(B_V, 256), lhsT=hnT[hp][:, c * 512 + sub * 128:c * 512 + (sub + 1) * 128],
                               rhs=win_bf[:, c * 1280 + 512:c * 1280 + 768], start=(c == 0), stop=(c == DC - 1))
            return ins
        SS.add("pe", v_fn, reads=win_reads + [("hnT", hp)], writes=[("ps", B_V)])
        SS.add("act", lambda e: e.copy(out=v_sb[:, blk * 256:(blk + 1) * 256], in_=bank(B_V, 256)),
               reads=[("ps", B_V)], writes=[("v", blk)])

    def qk_chain(SS, ti, which, j):
        pos0, W = tile_info(ti)
        hp = ti % 2
        b = B_PJ0 + j
        p2 = B_PS2[j]
        sq = sqb[j]
        rq = rqb[j]
        proj(SS, b, which * 256 + j * 128, W, hp)
        SS.add("act", lambda e: e.activation(out=sq[:, 0:W], in_=bank(b, W), func=AF.Square), reads=[("ps", b)], writes=[("sqb", j)])
        SS.add("pe", lambda e: e.matmul(bank(p2, W), lhsT=bones, rhs=sq[:, 0:W], start=True, stop=True),
               reads=[("sqb", j), "cst"], writes=[("ps", p2)])
        SS.add("dve", lambda e: e.tensor_scalar(out=rq[:, 0:W], in0=bank(p2, W), scalar1=1.0 / 64, scalar2=EPS,
                                                op0=ALU.mult, op1=ALU.add), reads=[("ps", p2)], writes=[("rqb", j)])
        SS.add("act", lambda e: e.activation(out=rq[:, 0:W], in_=rq[:, 0:W], func=AF.Ln), reads=[("rqb", j)], writes=[("rqb", j)])
        SS.add("act", lambda e: e.activation(out=rq[:, 0:W], in_=rq[:, 0:W], func=AF.Exp, scale=-0.5), reads=[("rqb", j)], writes=[("rqb", j)])
        dst = qT if which == 0 else kT
        gsc = col(V_QG) if which == 0 else ext[:, 6:7]
        SS.add("dve", lambda e: e.scalar_tensor_tensor(
            out=dst[:, j * TP + pos0:j * TP + pos0 + W], in0=bank(b, W), scalar=gsc, in1=rq[:, 0:W], op0=ALU.mult, op1=ALU.mult),
            reads=[("ps", b), ("rqb", j), "vec", "ext"], writes=[("qk", which, j, ti)])

    def xr_chain(SS, ti, c):
        pos0, W = tile_info(ti)
        hp = ti % 2
        b = B_PJ0 + c
        gr, gi = B_GR[c], B_GI[c]
        xc, xcb, tr, tg_, a_, m2, bt = xc_sb[c], xc_bf[c], tr_sb[c], ti_sb[c], a_sb[c], m2_sb[c], bt_sb[c]
        K = lambda n: (n, c)
        proj(SS, b, 768 + c * 128, W, hp)
        SS.add("act", lambda e: e.copy(out=xr_sb[c][:, 3:3 + W], in_=bank(b, W)), reads=[("ps", b)], writes=[("xr", c)])
        SS.add("dve", lambda e: e.tensor_scalar(out=xc[:, 0:W], in0=xr_sb[c][:, 3:3 + W], scalar1=col(V_CW + c * 4 + 3),
                                                scalar2=col(V_CB + c), op0=ALU.mult, op1=ALU.add),
               reads=[("xr", c), "vec"], writes=[K("xc")])
        for k in range(3):
            SS.add("dve", lambda e, k=k: e.scalar_tensor_tensor(out=xc[:, 0:W], in0=xr_sb[c][:, k:k + W], scalar=col(V_CW + c * 4 + k),
                                                                 in1=xc[:, 0:W], op0=ALU.mult, op1=ALU.add),
                   reads=[("xr", c), "vec", K("xc")], writes=[K("xc")])
        SS.add("pool", lambda e: e.tensor_copy(out=xr_sb[c][:, 0:3], in_=xr_sb[c][:, W:W + 3]), reads=[("xr", c)], writes=[("xr", c)])
        SS.add("act", lambda e: e.copy(out=xcb[:, 0:W], in_=xc[:, 0:W]), reads=[K("xc")], writes=[K("xcb")])
        SS.add("pe", lambda e: e.matmul(bank(gr, W), lhsT=wrg_bf[:, (0 * 2 + c) * 128:(0 * 2 + c + 1) * 128], rhs=xcb[:, 0:W],
                                        start=True, stop=True), reads=[K("xcb"), "wrg"], writes=[("ps", gr)])
        SS.add("pe", lambda e: e.matmul(bank(gi, W), lhsT=wrg_bf[:, (1 * 2 + c) * 128:(1 * 2 + c + 1) * 128], rhs=xcb[:, 0:W],
                                        start=True, stop=True), reads=[K("xcb"), "wrg"], writes=[("ps", gi)])
        SS.add("act", lambda e: e.activation(out=tr[:, 0:W], in_=bank(gr, W), func=AF.Exp, bias=ext[:, 2 + c:3 + c], scale=-1.0),
               reads=[("ps", gr), "ext"], writes=[K("tr")])
        SS.add("act", lambda e: e.activation(out=tg_[:, 0:W], in_=bank(gi, W), func=AF.Exp, bias=ext[:, 4 + c:5 + c], scale=-1.0),
               reads=[("ps", gi), "ext"], writes=[K("tig")])
        SS.add("act", lambda e: e.activation(out=tr[:, 0:W], in_=tr[:, 0:W], func=AF.Ln, bias=1.0), reads=[K("tr")], writes=[K("tr")])
        SS.add("act", lambda e: e.activation(out=tg_[:, 0:W], in_=tg_[:, 0:W], func=AF.Ln, bias=1.0), reads=[K("tig")], writes=[K("tig")])
        SS.add("act", lambda e: e.activation(out=tr[:, 0:W], in_=tr[:, 0:W], func=AF.Exp, scale=-1.0), reads=[K("tr")], writes=[K("tr")])
        SS.add("act", lambda e: e.activation(out=tg_[:, 0:W], in_=tg_[:, 0:W], func=AF.Exp, scale=-1.0), reads=[K("tig")], writes=[K("tig")])
        SS.add("act", lambda e: e.activation(out=a_[:, 0:W], in_=tr[:, 0:W], func=AF.Exp, scale=ext[:, c:c + 1]),
               reads=[K("tr"), "ext"], writes=[K("a")])
        SS.add("dve", lambda e: e.tensor_tensor(out=m2[:, 0:W], in0=a_[:, 0:W], in1=a_[:, 0:W], op=ALU.mult), reads=[K("a")], writes=[K("m2")])
        SS.add("dve", lambda e: e.tensor_scalar(out=m2[:, 0:W], in0=m2[:, 0:W], scalar1=-1.0, scalar2=1.0, op0=ALU.mult, op1=ALU.add),
               reads=[K("m2")], writes=[K("m2")])
        SS.add("act", lambda e: e.activation(out=m2[:, 0:W], in_=m2[:, 0:W], func=AF.Ln), reads=[K("m2")], writes=[K("m2")])
        SS.add("act", lambda e: e.activation(out=m2[:, 0:W], in_=m2[:, 0:W], func=AF.Exp, scale=0.5), reads=[K("m2")], writes=[K("m2")])
        SS.add("dve", lambda e: e.tensor_tensor(out=bt[:, 0:W], in0=tg_[:, 0:W], in1=xc[:, 0:W], op=ALU.mult),
               reads=[K("tig"), K("xc")], writes=[K("bt")])
        SS.add("dve", lambda e: e.tensor_tensor(out=bt[:, 0:W], in0=bt[:, 0:W], in1=m2[:, 0:W], op=ALU.mult),
               reads=[K("bt"), K("m2")], writes=[K("bt")])
        if ti == 0:
            SS.add("dve", lambda e: e.memset(bt[:, 0:PAD], 0.0), reads=[K("bt")], writes=[K("bt")])
        SS.add("dve", lambda e: e.tensor_tensor_scan(out=hl_sb[c][:, 0:W], data0=a_[:, 0:W], data1=bt[:, 0:W],
                                                     initial=hst[:, c:c + 1], op0=ALU.mult, op1=ALU.add),
               reads=[K("a"), K("bt"), K("hst")], writes=[("hl", c)])
        SS.add("dve", lambda e: e.tensor_copy(out=hst[:, c:c + 1], in_=hl_sb[c][:, W - 1:W]), reads=[("hl", c)], writes=[K("hst")])

    def yg_chain(SS, ti, c):
        pos0, W = tile_info(ti)
        hp = ti % 2
        b = B_PJ0 + c
        y, y2, tg = y_sb[c], y2_sb[c], y2_sb[c]
        K = lambda n: (n, c)
        proj(SS, b, 1024 + c * 128, W, hp)
        SS.add("act", lambda e: e.copy(out=y[:, 0:W], in_=bank(b, W)), reads=[("ps", b)], writes=[K("y")])
        SS.add("act", lambda e: e.activation(out=y2[:, 0:W], in_=bank(b, W), func=AF.Square), reads=[("ps", b)], writes=[K("y2")])
        SS.add("dve", lambda e: e.tensor_scalar(out=y2[:, 0:W], in0=y2[:, 0:W], scalar1=0.044715, scalar2=1.0, op0=ALU.mult, op1=ALU.add),
               reads=[K("y2")], writes=[K("y2")])
        SS.add("dve", lambda e: e.tensor_tensor(out=y2[:, 0:W], in0=y2[:, 0:W], in1=y[:, 0:W], op=ALU.mult),
               reads=[K("y2"), K("y")], writes=[K("y2")])
        SS.add("act", lambda e: e.activation(out=tg[:, 0:W], in_=y2[:, 0:W], func=AF.Exp, scale=-2.0 * GELU_C), reads=[K("y2")], writes=[K("y2")])
        SS.add("act", lambda e: e.activation(out=tg[:, 0:W], in_=tg[:, 0:W], func=AF.Ln, bias=1.0), reads=[K("y2")], writes=[K("y2")])
        SS.add("act", lambda e: e.activation(out=tg[:, 0:W], in_=tg[:, 0:W], func=AF.Exp, scale=-1.0), reads=[K("y2")], writes=[K("y2")])
        SS.add("dve", lambda e: e.tensor_tensor(out=tg[:, 0:W], in0=tg[:, 0:W], in1=y[:, 0:W], op=ALU.mult),
               reads=[K("y2"), K("y")], writes=[K("y2")])
        SS.add("dve", lambda e: e.tensor_tensor(out=ol_bf[c][:, 0:W], in0=tg[:, 0:W], in1=hl_sb[c][:, 0:W], op=ALU.mult),
               reads=[K("y2"), ("hl", c)], writes=[("ol", c)])
        sfn, nd = store_fn(ol_bf[c], 256 + c * 128, ti)
        SS.add("pool", sfn, reads=[("ol", c)], writes=[("mo", "l", c, ti)], chan=("ol", c), ndma=nd)

    def mk(fn, *a_):
        r = Rec()
        fn(r, *a_)
        return r

    S.barrier()
    zip_emit([mk(prep_chain, 0, 0)])
    for ti in range(NQT):
        preps = []
        if ti + 1 < NQT:
            _, Wn = tile_info(ti + 1)
            preps = [mk(prep_chain, ti + 1, sub) for sub in range(Wn // 128)]
        groups = [
            [mk(qk_chain, ti, 0, 0), mk(qk_chain, ti, 0, 1)],
            [mk(qk_chain, ti, 1, 0), mk(qk_chain, ti, 1, 1)],
            [mk(xr_chain, ti, 0), mk(xr_chain, ti, 1)],
            [mk(yg_chain, ti, 0), mk(yg_chain, ti, 1)],
        ]
        for gi_, grp in enumerate(groups):
            zip_emit(grp)
            if gi_ < len(preps):
                zip_emit([preps[gi_]])

    S.barrier()
    A.off = mark12
    e_sb = [A.f32(1024) for _ in range(3)]
    sp_sb = [A.bf(1024) for _ in range(3)]
    g_sb = [A.bf(1024) for _ in range(2)]
    w_sb = [A.bf(1024) for _ in range(3)]
    osb = [A.bf(2 * 512), A.bf(2 * 512)]
    pst32 = [A.f32(DFF), A.f32(DFF)]
    pst16 = [A.bf(DFF), A.bf(DFF)]
    pieces = []
    for c in range(DC):
        pieces.append((w_out[c * 128:(c + 1) * 128, :], wout_s[c * 128:(c + 1) * 128, :], 1024, None))
    for c in range(DC):
        for hh in range(2):
            pieces.append((w_fi[c * 128:(c + 1) * 128, hh * DFF:(hh + 1) * DFF], wfi_s[c * 128:(c + 1) * 128, hh * DFF:(hh + 1) * DFF],
                           DFF, col(V_G2 + c)))
    for j in range(NPAIR):
        pieces.append((w_fo[j * 128:(j + 1) * 128, :], wfo_s[j * 128:(j + 1) * 128, :], 1024, None))
    wscr_keys = [("wscr", i) for i in range(len(pieces))]

    def piece_stage(i, st):
        src, dst, ncol, sc = pieces[i]
        sl = i % 2
        if st == 0:
            S.add("sp", lambda e, s: e.dma_start(out=pst32[sl][:, 0:ncol], in_=src).then_inc(s, 16),
                  writes=[("p32", sl)], chan=("p32", sl))
        elif st == 1:
            if sc is None:
                S.add("dve", lambda e: e.tensor_copy(out=pst16[sl][:, 0:ncol], in_=pst32[sl][:, 0:ncol]),
                      reads=[("p32", sl)], writes=[("p16", sl)])
            else:
                S.add("dve", lambda e: e.tensor_scalar(out=pst16[sl][:, 0:ncol], in0=pst32[sl][:, 0:ncol], scalar1=sc, scalar2=None, op0=ALU.mult),
                      reads=[("p32", sl), "vec"], writes=[("p16", sl)])
        else:
            S.add("sp", lambda e, s: e.dma_start(out=dst, in_=pst16[sl][:, 0:ncol]).then_inc(s, 16),
                  reads=[("p16", sl)], writes=[("wscr", i)], chan=("p16", sl))

    items = []
    for ti in range(NQT):
        pos0, W = tile_info(ti)
        b0 = pos0 // 128
        nsub = W // 128
        kbs = list(range(b0 + nsub - 1, -1, -1))
        for n, kb in enumerate(kbs):
            for j in range(2):
                items.append(dict(ti=ti, pos0=pos0, W=W, b0=b0, kb=kb, j=j, first=(n == 0), last=(n == len(kbs) - 1)))
    NI = len(items)

    S.add("dve", lambda e: e.memset(ext[:, 9:10], 0.0),
          reads=[("qk", w, j, t) for w in range(2) for j in range(2) for t in range(NQT)] + [("v", bl) for bl in range(NB)],
          writes=["kall"])

    def v3(buf, c0, W):
        return buf.rearrange("p (h w) -> p h w", h=2)[:, :, c0:W]

    def zview(c0, W):
        return ps[:, 0:1024].rearrange("p (h w) -> p h w", h=2)[:, :, c0:W]

    def pview(j, c0, W):
        return ps[:, (2 + 2 * j) * 512:(4 + 2 * j) * 512].rearrange("p (h w) -> p h w", h=2)[:, :, c0:W]

    RG = [[0, 1], [2, 3], [4, 5], [6, 7]]

    def tile_keys(ti):
        return [("mo", "l", c, ti) for c in range(2)] + [("mo", "a", j, ti) for j in range(2)]

    gathered = set()

    def maybe_gather(ti):
        if ti == 0:
            return
        done_tok = 512 * ti
        for g in range(4):
            if g in gathered or (g + 1) * GW > done_tok:
                continue
            gathered.add(g)
            t_lo = 1 + (g * GW) // 512
            t_hi = 1 + ((g + 1) * GW - 1) // 512
            keys = []
            for t in range(t_lo, t_hi + 1):
                keys += tile_keys(t)
            S.add("pool", lambda e, s, g=g: e.collective_compute(
                "AllGather", ALU.bypass, replica_groups=RG, ins=[mo_g[g].ap().opt()],
                outs=[ma_big[g * 1024:(g + 1) * 1024, :].opt()]).then_inc(s),
                reads=keys, writes=[("mall", g)], chan=("cc", g), inc=1)

    def c0_of(it):
        return 128 * (it["kb"] - it["b0"]) if it["kb"] >= it["b0"] else 0

    def PE1(i):
        it = items[i]
        kb, W, pos0, j = it["kb"], it["W"], it["pos0"], it["j"]
        c0 = c0_of(it)

        def fn(e):
            ins = None
            for hh in range(2):
                r = hh * 64
                ins = e.matmul(ps[:, hh * 512 + c0:hh * 512 + W], lhsT=kT[r:r + 64, j * TP + kb * 128:j * TP + (kb + 1) * 128],
                               rhs=qT[r:r + 64, j * TP + pos0 + c0:j * TP + pos0 + W], start=True, stop=True)
            return ins
        S.add("pe", fn, reads=["kall"], writes=[("ps", 0), ("ps", 1)])

    def ACT12(i):
        it = items[i]
        W = it["W"]
        c0 = c0_of(it)
        eb = e_sb[i % 3]
        sb = sp_sb[i % 3]
        S.add("act", lambda e: e.activation(out=v3(eb, c0, W), in_=zview(c0, W), func=AF.Exp),
              reads=[("ps", 0), ("ps", 1)], writes=[("e", i % 3)])
        S.add("act", lambda e: e.activation(out=v3(sb, c0, W), in_=v3(eb, c0, W), func=AF.Ln, bias=1.0),
              reads=[("e", i % 3)], writes=[("sp", i % 3)])
        if it["kb"] >= it["b0"]:
            for hh in range(2):
                S.add("dve", lambda e, hh=hh: e.tensor_tensor(out=sb[:, hh * 512 + c0:hh * 512 + c0 + 128],
                                                               in0=sb[:, hh * 512 + c0:hh * 512 + c0 + 128], in1=maskv(0, 128), op=ALU.mult),
                      reads=[("sp", i % 3), "cst"], writes=[("sp", i % 3)])

    def PE2(i):
        it = items[i]
        W, j = it["W"], it["j"]
        c0 = c0_of(it)
        sb = sp_sb[i % 3]

        def fn(e):
            ins = None
            for hh in range(2):
                b_ = 2 + 2 * j + hh
                ins = e.matmul(ps[:, b_ * 512 + c0:b_ * 512 + W], lhsT=negL, rhs=sb[:, hh * 512 + c0:hh * 512 + W],
                               start=it["first"], stop=False, skip_group_check=True)
            return ins
        S.add("pe", fn, reads=[("sp", i % 3), "cst"], writes=[("ps", 2 + 2 * j), ("ps", 3 + 2 * j)])

    def ACT3(i):
        it = items[i]
        W, j = it["W"], it["j"]
        c0 = c0_of(it)
        gb = g_sb[i % 2]
        eb = e_sb[i % 3]
        wb = w_sb[i % 3]
        S.add("act", lambda e: e.activation(out=v3(gb, c0, W), in_=pview(j, c0, W), func=AF.Exp),
              reads=[("ps", 2 + 2 * j), ("ps", 3 + 2 * j)], writes=[("g", i % 2)])
        S.add("dve", lambda e: e.tensor_tensor(out=v3(wb, c0, W), in0=v3(eb, c0, W), in1=v3(gb, c0, W), op=ALU.mult),
              reads=[("e", i % 3), ("g", i % 2)], writes=[("w", i % 3)])
        if it["kb"] >= it["b0"]:
            for hh in range(2):
                S.add("dve", lambda e, hh=hh: e.tensor_tensor(out=wb[:, hh * 512 + c0:hh * 512 + c0 + 128],
                                                               in0=wb[:, hh * 512 + c0:hh * 512 + c0 + 128], in1=maskv(0, 128), op=ALU.mult),
                      reads=[("w", i % 3), "cst"], writes=[("w", i % 3)])

    def PE4(i):
        it = items[i]
        if it["last"]:
            return
        W, j = it["W"], it["j"]
        c0 = c0_of(it)
        sb = sp_sb[i % 3]

        def fn(e):
            ins = None
            for hh in range(2):
                b_ = 2 + 2 * j + hh
                ins = e.matmul(ps[:, b_ * 512 + c0:b_ * 512 + W], lhsT=negU, rhs=sb[:, hh * 512 + c0:hh * 512 + W],
                               start=False, stop=False, skip_group_check=True)
            return ins
        S.add("pe", fn, reads=[("sp", i % 3), "cst"], writes=[("ps", 2 + 2 * j), ("ps", 3 + 2 * j)])

    def PE3(i):
        it = items[i]
        kb, W, pos0, ti, j = it["kb"], it["W"], it["pos0"], it["ti"], it["j"]
        c0 = c0_of(it)
        ob = 6 + j
        wb = w_sb[i % 3]
        par = ti % 2

        def fn(e):
            ins = None
            for hh in range(2):
                h = 2 * j + hh
                ins = e.matmul(ps[hh * 64:(hh + 1) * 64, ob * 512 + c0:ob * 512 + W], lhsT=v_sb[:, kb * 256 + h * 64:kb * 256 + (h + 1) * 64],
                               rhs=wb[:, hh * 512 + c0:hh * 512 + W], start=it["first"], stop=it["last"], skip_group_check=True)
            return ins
        S.add("pe", fn, reads=[("w", i % 3), "kall"], writes=[("ps", ob)])
        if it["last"]:
            S.add("dve", lambda e: e.tensor_copy(out=osb[par][:, j * 512:j * 512 + W], in_=bank(ob, W)),
                  reads=[("ps", ob)], writes=[("osb", par, j)])
            sfn, nd = store_fn(osb[par][:, j * 512:(j + 1) * 512], j * 128, ti)
            S.add("pool", sfn, reads=[("osb", par, j)], writes=[("mo", "a", j, ti)], chan=("osb", par, j), ndma=nd)
            if j == 1:
                maybe_gather(ti)

    PERIOD = max(4, (NI - 8) // len(pieces))
    PE1(0)
    for s in range(-1, NI + 1):
        if s >= 0:
            pi_, ph_ = divmod(s, PERIOD)
            if pi_ < len(pieces):
                if ph_ == 0:
                    piece_stage(pi_, 0)
                elif ph_ == PERIOD // 2:
                    piece_stage(pi_, 1)
            if pi_ >= 1 and pi_ - 1 < len(pieces) and ph_ == 1:
                piece_stage(pi_ - 1, 2)
        if 0 <= s + 1 < NI:
            ACT12(s + 1)
        if s + 2 < NI:
            PE1(s + 2)
        if 0 <= s + 1 < NI:
            PE2(s + 1)
        if 0 <= s < NI:
            ACT3(s)
            PE4(s)
        if 0 <= s - 1 < NI:
            PE3(s - 1)

    done_p = min(len(pieces), (NI // PERIOD) + 1)
    for i in range(len(pieces)):
        last_s = NI
        if i * PERIOD > last_s:
            piece_stage(i, 0)
        if i * PERIOD + PERIOD // 2 > last_s:
            piece_stage(i, 1)
        if (i + 1) * PERIOD + 1 > last_s:
            piece_stage(i, 2)

    S.barrier(exclude=lambda ch: isinstance(ch, tuple) and ch[0] == "cc")
    A.off = mark_stage
    wout_bf = A.bf(8 * 1024)
    wfi_bf = A.bf(8 * 2 * DFF)
    wfo_bf = A.bf(NPAIR * 1024)
    w3_keys = []
    qsel = ["sp", "pool"]
    nq = [0]

    def wload(dst, src, key):
        q = qsel[nq[0] % 2]
        nq[0] += 1
        S.add(q, lambda e, s: e.dma_start(out=dst, in_=src).then_inc(s, 16), reads=wscr_keys, writes=[key], chan=key)
        w3_keys.append(key)

    for hf in range(2):
        wload(wout_bf.rearrange("p (c n) -> p c n", c=8)[:, hf * 4:(hf + 1) * 4, :],
              wout_s.rearrange("(c p) n -> p c n", p=128)[:, hf * 4:(hf + 1) * 4, :], ("w3", "o", hf))
    for c in range(DC):
        wload(wfi_bf[:, c * 2 * DFF:(c + 1) * 2 * DFF], wfi_s[c * 128:(c + 1) * 128, :], ("w3", "i", c))
    for q4 in range(2):
        wload(wfo_bf.rearrange("p (j n) -> p j n", j=NPAIR)[:, q4 * 11:(q4 + 1) * 11, :],
              wfo_s.rearrange("(j p) n -> p j n", p=128)[:, q4 * 11:(q4 + 1) * 11, :], ("w3", "f", q4))
    S.add("dve", lambda e: e.memset(ext[:, 10:11], 0.0), reads=w3_keys, writes=["w3all"])

    S.add("pool", lambda e, s: e.collective_compute("AllGather", ALU.bypass, replica_groups=RG,
                                                    ins=[mo_halo_t.ap().opt()], outs=[ma_halo_t.ap().opt()]).then_inc(s),
          reads=tile_keys(0) + tile_keys(1 + (HALF - 2) // 512), writes=[("mall", "h")], chan="cch", inc=1)

    def mh_fn(e, s):
        half = e.partition_id() % 2
        ins = None
        for j in range(2):
            for r in range(2):
                ins = e.dma_start(out=mixed_half[r * 512:(r + 1) * 512, 2 + j * GW:2 + (j + 1) * GW],
                                  in_=ma_big[bass.ds(half * 2048 + j * 1024 + r * 512, 512), :]).then_inc(s, 16)
        ins = e.dma_start(out=mixed_half[:, 0:2], in_=ma_halo[:, bass.ds(half * 2, 2)]).then_inc(s, 16)
        return ins
    S.add("sp", mh_fn, reads=[("mall", g) for g in range(4)] + [("mall", "h")], writes=["mhalf"], chan="mh", ndma=5)

    h2 = [A.f32(1024) for _ in range(4)]
    hn2_bf = [A.bf(1024), A.bf(1024)]
    ss3 = A.f32(8)
    HW = 2 + W3
    hn2T = [A.bf(8 * HW), A.bf(8 * HW)]
    mt = A.bf(8 * W3)
    ubuf = [A.f32(2 + W3), A.f32(2 + W3)]
    gbuf = [A.f32(2 + W3), A.f32(2 + W3)]
    uc = [A.f32(W3), A.f32(W3)]
    gc = [A.f32(W3), A.f32(W3)]
    sg = [A.f32(W3), A.f32(W3)]
    act_bf = A.bf(NPAIR * W3)

    B_FO = [0, 1]
    B_T3 = 2
    B_U = [3, 4, 0]
    B_G = [5, 6, 1]
    B_WO = 7
    mixed_half_r = mixed_half.rearrange("(c p) w -> p c w", p=128)

    def tile_geom(t):
        if t < 0:
            return 0, 2, 2, 1, 1
        return 2 + t * W3, W3, 128, W3 // 128, t % 2

    def prep1(t):
        rel0, Wt, n, nsub, par = tile_geom(t)
        S.add("sp", lambda e, s: e.dma_start(out=mt.rearrange("p (c w) -> p c w", c=8)[:, :, 0:Wt],
                                              in_=mixed_half_r[:, :, rel0:rel0 + Wt]).then_inc(s, 16),
              reads=["mhalf"], writes=["mt"], chan="mt")
        for sub in range(nsub):
            sl = 2 * par + sub
            q = sub
            r = rel0 + sub * 128
            S.add("sp", lambda e, s, sl=sl, r=r: e.dma_start(out=h2[sl][0:n, :], in_=xhalf[r:r + n, :]).then_inc(s, 16),
                  writes=[("h2", sl)], chan=("h2", sl))
            for hf in range(2):
                def wo_fn(e, sub=sub, hf=hf):
                    ins = None
                    for c in range(DC):
                        ins = e.matmul(ps[0:n, B_WO * 512:(B_WO + 1) * 512], lhsT=mt[:, c * W3 + sub * 128:c * W3 + sub * 128 + n],
                                       rhs=wout_bf[:, c * 1024 + hf * 512:c * 1024 + (hf + 1) * 512], start=(c == 0), stop=(c == DC - 1))
                    return ins
                S.add("pe", wo_fn, reads=["mt", "w3all"], writes=[("ps", B_WO)])
                S.add("dve", lambda e, sl=sl, hf=hf: e.tensor_tensor(out=h2[sl][0:n, hf * 512:(hf + 1) * 512],
                                                                     in0=ps[0:n, B_WO * 512:(B_WO + 1) * 512],
                                                                     in1=h2[sl][0:n, hf * 512:(hf + 1) * 512], op=ALU.add),
                      reads=[("ps", B_WO), ("h2", sl)], writes=[("h2", sl)])
            S.add("pool", lambda e, q=q: e.memset(ss3[:, q:q + 1], 0.0), writes=[("ss3", q)])
            S.add("act", lambda e, sl=sl, q=q: e.activation(out=hn2_bf[q][0:n, :], in_=h2[sl][0:n, :], func=AF.Square,
                                                            accum_out=ss3[0:n, q:q + 1]),
                  reads=[("h2", sl)], writes=[("ss3", q), ("hn2", q)])
            S.add("dve", lambda e, q=q: e.tensor_scalar(out=ss3[0:n, 2 + q:3 + q], in0=ss3[0:n, q:q + 1], scalar1=1.0 / D, scalar2=EPS,
                                                         op0=ALU.mult, op1=ALU.add), reads=[("ss3", q)], writes=[("rs3", q)])
            S.add("act", lambda e, q=q: e.activation(out=ss3[0:n, 2 + q:3 + q], in_=ss3[0:n, 2 + q:3 + q], func=AF.Ln),
                  reads=[("rs3", q)], writes=[("rs3", q)])
            S.add("act", lambda e, q=q: e.activation(out=ss3[0:n, 2 + q:3 + q], in_=ss3[0:n, 2 + q:3 + q], func=AF.Exp, scale=-0.5),
                  reads=[("rs3", q)], writes=[("rs3", q)])
            S.add("act", lambda e, sl=sl, q=q: e.activation(out=hn2_bf[q][0:n, :], in_=h2[sl][0:n, :], func=AF.Copy, scale=ss3[0:n, 2 + q:3 + q]),
                  reads=[("h2", sl), ("rs3", q)], writes=[("hn2", q)])

    def prep2(t):
        rel0, Wt, n, nsub, par = tile_geom(t)
        for sub in range(nsub):
            q = sub

            def tr_fn(e, q=q):
                ins = None
                for c in range(DC):
                    ins = e.transpose(bank_bf(B_T3)[:, c * 128:c * 128 + n], hn2_bf[q][0:n, c * 128:(c + 1) * 128], ident[0:n, 0:n])
                return ins
            S.add("pe", tr_fn, reads=[("hn2", q), "cst"], writes=[("ps", B_T3)])
            dcol = 0 if t < 0 else 2 + sub * 128
            dpar = 0 if t < 0 else par
            S.add("dve", lambda e, dcol=dcol, dpar=dpar: e.tensor_copy(
                out=hn2T[dpar].rearrange("p (c w) -> p c w", c=8)[:, :, dcol:dcol + n],
                in_=bank_bf(B_T3).rearrange("p (c w) -> p c w", c=8)[:, :, 0:n]),
                reads=[("ps", B_T3)], writes=[("hn2T", dpar)])
        if t >= 0 and t + 1 < NST:
            S.add("dve", lambda e: e.tensor_copy(
                out=hn2T[1 - par].rearrange("p (c w) -> p c w", c=8)[:, :, 0:2],
                in_=hn2T[par].rearrange("p (c w) -> p c w", c=8)[:, :, W3:W3 + 2]),
                reads=[("hn2T", par)], writes=[("hn2T", 1 - par)])

    pair_ctr = [0]

    def ffn_in(t):
        rel0, Wt, n, nsub, par = tile_geom(t)
        final = t >= 0
        pcs = {}

        def mm(j):
            pc = pair_ctr[0] % 3
            pair_ctr[0] += 1
            pcs[j] = pc
            for which, bb in enumerate((B_U[pc], B_G[pc])):
                slab = j + which * NPAIR

                def fi_fn(e, bb=bb, slab=slab):
                    ins = None
                    for c in range(DC):
                        ins = e.matmul(bank(bb, Wt + 2), lhsT=wfi_bf[:, c * 2 * DFF + slab * 128:c * 2 * DFF + (slab + 1) * 128],
                                       rhs=hn2T[par][:, c * HW:c * HW + Wt + 2], start=(c == 0), stop=(c == DC - 1))
                    return ins
                S.add("pe", fi_fn, reads=[("hn2T", par), "w3all"], writes=[("ps", bb)])

        def post1(j):
            pc = pcs[j]
            p2 = j % 2
            for which, (bb, cbuf) in enumerate(((B_U[pc], uc[p2]), (B_G[pc], gc[p2]))):
                slab = j + which * NPAIR
                ck = ("cb", which, p2)
                S.add("act", lambda e, bb=bb, cbuf=cbuf, slab=slab: e.activation(
                    out=cbuf[:, 0:Wt], in_=ps[:, bb * 512 + 2:bb * 512 + 2 + Wt], func=AF.Identity,
                    scale=col(V_FCW + slab * 3 + 2), bias=col(V_FCB + slab)),
                    reads=[("ps", bb), "vec"], writes=[ck])
                for k in range(2):
                    S.add("dve", lambda e, bb=bb, cbuf=cbuf, slab=slab, k=k: e.scalar_tensor_tensor(
                        out=cbuf[:, 0:Wt], in0=ps[:, bb * 512 + k:bb * 512 + k + Wt], scalar=col(V_FCW + slab * 3 + k), in1=cbuf[:, 0:Wt],
                        op0=ALU.mult, op1=ALU.add), reads=[("ps", bb), "vec", ck], writes=[ck])

        def post2(j):
            pc = j % 2
            S.add("act", lambda e: e.activation(out=sg[pc][:, 0:Wt], in_=gc[pc][:, 0:Wt], func=AF.Silu),
                  reads=[("cb", 1, pc)], writes=[("sg", pc)])
            S.add("dve", lambda e: e.tensor_tensor(out=act_bf[:, j * W3:j * W3 + Wt], in0=sg[pc][:, 0:Wt], in1=uc[pc][:, 0:Wt], op=ALU.mult),
                  reads=[("sg", pc), ("cb", 0, pc)], writes=[("actT", j)])

        for j in range(NPAIR + 2):
            if j < NPAIR:
                mm(j)
            if 0 <= j - 1 < NPAIR:
                post1(j - 1)
            if 0 <= j - 2 < NPAIR:
                post2(j - 2)

    def ffn_out(t, sub):
        rel0, Wt, n, nsub, par = tile_geom(t)
        sl = 2 * par + sub

        def fo_fn(e):
            ins = None
            for hf in range(2):
                for j in range(NPAIR):
                    ins = e.matmul(bank(B_FO[hf], 512), lhsT=act_bf[:, j * W3 + sub * 128:j * W3 + (sub + 1) * 128],
                                   rhs=wfo_bf[:, j * 1024 + hf * 512:j * 1024 + (hf + 1) * 512], start=(j == 0), stop=(j == NPAIR - 1))
            return ins
        S.add("pe", fo_fn, reads=[("actT", j) for j in range(NPAIR)] + ["w3all"], writes=[("ps", 0), ("ps", 1)])
        S.add("dve", lambda e: e.tensor_tensor(out=h2[sl][:, :], in0=ps[:, 0:1024], in1=h2[sl][:, :], op=ALU.add),
              reads=[("ps", 0), ("ps", 1), ("h2", sl)], writes=[("h2", sl)])
        r0 = rel0 - 2 + sub * 128
        S.add("pool", lambda e, s: e.dma_start(out=out[r0:r0 + 128, :], in_=h2[sl][:, :]).then_inc(s, 16),
              reads=[("h2", sl)], writes=[("h2", sl), ("out", r0)], chan=("h2", sl))

    prep1(-1)
    prep2(-1)
    prep1(0)
    prep2(0)
    for t in range(NST):
        ffn_in(t)
        if t + 1 < NST:
            prep1(t + 1)
        ffn_out(t, 0)
        if t + 1 < NST:
            prep2(t + 1)
        ffn_out(t, 1)

    S.add("sp", lambda e: None, reads=[("out", r0) for r0 in range(0, HALF, 128)], writes=["done"])

    S.finalize()
    chans = list(S.chan_count.keys())
    sem_ctxs = []
    sems = {}
    for e in Sched.ENG:
        c = nc.semaphore("s_" + e)
        sems[e] = c.__enter__()
        sem_ctxs.append(c)
    chan_sems = {}
    for i, ch in enumerate(chans):
        c = nc.semaphore("c_%d" % i)
        chan_sems[ch] = c.__enter__()
        sem_ctxs.append(c)
    with nc.Block() as block:
        S.emit(nc, block, sems, chan_sems)
    for c in reversed(sem_ctxs):
        c.__exit__(None, None, None)
    pctx.__exit__(None, None, None)
    ctx.__exit__(None, None, None)
    return nc


def _consts():
    c = np.zeros((128, NCST), np.float32)
    j = np.arange(128)[:, None]
    s = np.arange(128)[None, :]
    c[:, C_ID:C_ID + 128] = (j == s)
    c[:, C_NL:C_NL + 128] = -(j >= s).astype(np.float32)
    c[:, C_NU:C_NU + 128] = -(j < s).astype(np.float32)
    c[:, C_BO:C_BO + 128] = ((j // 64) == (s // 64))
    t = np.arange(512)[None, :]
    for k in range(4):
        c[:, C_MK + k * 512:C_MK + (k + 1) * 512] = ((128 * k + j) < t)
    return c


def _prep_inputs(inputs, SEQ):
    f = lambda a: np.ascontiguousarray(np.asarray(a), dtype=np.float32)
    x = f(inputs["x"])
    meta = f(inputs["meta_tokens"])
    w_in = f(inputs["w_in"])[0]
    w_out = f(inputs["w_out"])[0]
    w_fi = f(inputs["w_ffn_in"])[0]
    w_fo = f(inputs["w_ffn_out"])[0]
    g1 = f(inputs["norm1_g"])[0]
    g2 = f(inputs["norm2_g"])[0]
    qg = f(inputs["q_norm_g"])[0]
    kg = f(inputs["k_norm_g"])[0]
    cw = f(inputs["conv_w"])[0]
    cb = f(inputs["conv_b"])[0]
    wa = f(inputs["w_rg_a"])[0]
    wi = f(inputs["w_rg_i"])[0]
    ba = f(inputs["b_rg_a"])[0]
    bi = f(inputs["b_rg_i"])[0]
    lam = f(inputs["lru_lambda"])[0]
    fcw = f(inputs["ffn_conv_w"])[0]
    fcb = f(inputs["ffn_conv_b"])[0]
    consts = _consts()
    TP = SEQ + 128
    maps = []
    for core in range(8):
        b, p = core // 2, core % 2
        xpad = np.zeros((TP, D), np.float32)
        xpad[PAD:128] = meta
        xpad[128:] = x[b]
        cs = slice(256 * p, 256 * p + 256)
        wic = np.concatenate([w_in[:, 0:512][:, cs], w_in[:, 512:1024][:, cs], w_in[:, 1024:1536][:, cs],
                              w_in[:, 1536:2048][:, cs], w_in[:, 2048:2560][:, cs]], axis=1)
        vec = np.zeros((128, NV), np.float32)
        vec[:, V_G1:V_G1 + 8] = g1.reshape(8, 128).T
        vec[:, V_G2:V_G2 + 8] = g2.reshape(8, 128).T
        vec[:, V_QG] = np.tile(qg, 2)
        vec[:, V_KG] = np.tile(kg, 2)
        for c in range(2):
            ch = slice(256 * p + 128 * c, 256 * p + 128 * c + 128)
            vec[:, V_CW + c * 4:V_CW + c * 4 + 4] = cw[:, ch].T
            vec[:, V_CB + c] = cb[ch]
            vec[:, V_BA + c] = ba[ch]
            vec[:, V_BI + c] = bi[ch]
            vec[:, V_LAM + c] = lam[ch]
        for s in range(NSLAB):
            vec[:, V_FCW + s * 3:V_FCW + s * 3 + 3] = fcw[:, s * 128:(s + 1) * 128].T
            vec[:, V_FCB + s] = fcb[s * 128:(s + 1) * 128]
        wrg = np.zeros((128, 4 * 128), np.float32)
        for gi, wsrc in enumerate((wa, wi)):
            for c in range(2):
                for k in range(2):
                    blk = 4 * p + 2 * c + k
                    o = (gi * 2 + c) * 128
                    wrg[64 * k:64 * k + 64, o + 64 * k:o + 64 * k + 64] = wsrc[blk]
        perm = []
        for r in range(2):
            perm += list(range(256 * r, 256 * r + 256))
            perm += list(range(512 + 256 * r, 512 + 256 * r + 256))
        maps.append({
            "xpad": xpad, "xhalf": np.ascontiguousarray(xpad[126 + (SEQ // 2) * p:126 + (SEQ // 2) * p + SEQ // 2 + 2]), "w_in": np.ascontiguousarray(wic), "vecs": vec, "w_rg": wrg,
            "w_out": np.ascontiguousarray(w_out[perm]), "w_ffn_in": w_fi, "w_ffn_out": w_fo, "consts": consts,
        })
    return maps


_NC_CACHE = {}


def kernel(**inputs):
    x = np.asarray(inputs["x"])
    B, SEQ, _ = x.shape
    assert B == 4
    if SEQ not in _NC_CACHE:
        _NC_CACHE[SEQ] = build_nc(SEQ)
    nc = _NC_CACHE[SEQ]
    maps = _prep_inputs(inputs, SEQ)
    res = run_bass_kernel_spmd(nc, maps, core_ids=list(range(8)))
    outp = np.empty((B, SEQ, D), np.float32)
    HALF = SEQ // 2
    for core in range(8):
        b, p = core // 2, core % 2
        outp[b, p * HALF:(p + 1) * HALF] = res.results[core]["out"]
    return outp
```

```python
import numpy as np
import concourse.bass as bass
import concourse.mybir as mybir
from concourse.bass_utils import run_bass_kernel_spmd

F32 = mybir.dt.float32
BF16 = mybir.dt.bfloat16
AF = mybir.ActivationFunctionType
ALU = mybir.AluOpType

D = 1024
DC = 8
DFF = 2816
NSLAB = 44
NPAIR = 22
EPS = 1e-6
NMETA = 16
PAD = 112
GELU_C = 0.7978845608028654

V_G1, V_G2, V_QG, V_KG, V_CW, V_CB, V_BA, V_BI, V_LAM, V_FCW, V_FCB = 0, 8, 16, 17, 18, 26, 28, 30, 32, 34, 166
NV = 210
C_ID, C_NL, C_NU, C_BO, C_MK = 0, 128, 256, 384, 512
NCST = 512 + 4 * 512


class Sched:
    ENG = ("pe", "act", "dve", "pool", "sp")

    def __init__(self):
        self.ops = []
        self.last_w = {}
        self.readers = {}
        self.eng_count = {e: 0 for e in self.ENG}
        self.chan_count = {}
        self.chan_inc = {}
        self.barrier_nodes = []

    def add(self, eng, fn, reads=(), writes=(), chan=None, ndma=1, inc=16):
        deps = set(self.barrier_nodes)
        for k in reads:
            w = self.last_w.get(k)
            if w is not None:
                deps.add(w)
        for k in writes:
            w = self.last_w.get(k)
            if w is not None:
                deps.add(w)
            for r in self.readers.get(k, ()):
                deps.add(r)
        if chan is None:
            self.eng_count[eng] += 1
            node = ("E", eng, self.eng_count[eng])
        else:
            self.chan_inc[chan] = inc
            self.chan_count[chan] = self.chan_count.get(chan, 0) + ndma * inc
            node = ("C", chan, self.chan_count[chan])
        self.ops.append(dict(eng=eng, fn=fn, deps=deps, node=node, chan=chan))
        for k in reads:
            self.readers.setdefault(k, []).append(node)
        for k in writes:
            self.last_w[k] = node
            self.readers[k] = []
        return node

    def barrier(self, exclude=None):
        nodes = []
        for e, c in self.eng_count.items():
            if c:
                nodes.append(("E", e, c))
        for ch, c in self.chan_count.items():
            if exclude is not None and exclude(ch):
                continue
            nodes.append(("C", ch, c))
        self.barrier_nodes = nodes

    def finalize(self):
        known = {e: {} for e in self.ENG}
        signal = {e: set() for e in self.ENG}
        for op in self.ops:
            need = {}
            for kind, tgt, val in op["deps"]:
                if kind == "E" and tgt == "pe" and op["eng"] == "pe" and op["chan"] is None:
                    continue
                key = (kind, tgt)
                if val > need.get(key, 0):
                    need[key] = val
            waits = []
            kn = known[op["eng"]]
            for key, val in need.items():
                if kn.get(key, 0) >= val:
                    continue
                kn[key] = val
                waits.append((key, val))
                if key[0] == "E":
                    signal[key[1]].add(val)
            op["waits"] = waits
        self.rank = {}
        for e in self.ENG:
            self.rank[e] = {v: i + 1 for i, v in enumerate(sorted(signal[e]))}

    def emit(self, nc, block, sems, chan_sems):
        streams = {e: [op for op in self.ops if op["eng"] == e] for e in self.ENG}

        def run(eng_name, eng):
            for op in streams[eng_name]:
                for (kind, tgt), val in op["waits"]:
                    if kind == "E":
                        eng.wait_ge(sems[tgt], self.rank[tgt][val])
                    else:
                        eng.wait_ge(chan_sems[tgt], val)
                if op["chan"] is not None:
                    op["fn"](eng, chan_sems[op["chan"]])
                else:
                    ins = op["fn"](eng)
                    idx = op["node"][2]
                    if idx in self.rank[eng_name]:
                        assert ins is not None
                        ins.then_inc(sems[eng_name], 1)

        @block.tensor
        def _(e):
            run("pe", e)

        @block.scalar
        def _(e):
            run("act", e)

        @block.vector
        def _(e):
            run("dve", e)

        @block.gpsimd
        def _(e):
            run("pool", e)

        @block.sync
        def _(e):
            run("sp", e)


def build_nc(SEQ):
    assert SEQ % 1024 == 0
    HALF = SEQ // 2
    TP = SEQ + 128
    NB = TP // 128
    NQT = 1 + SEQ // 512
    NST = HALF // 256
    W3 = 256

    nc = bass.Bass("TRN2", target_bir_lowering=False)
    xpad = nc.dram_tensor("xpad", [TP, D], F32, kind="ExternalInput").ap()
    w_in = nc.dram_tensor("w_in", [D, 1280], F32, kind="ExternalInput").ap()
    vecs = nc.dram_tensor("vecs", [128, NV], F32, kind="ExternalInput").ap()
    w_rg = nc.dram_tensor("w_rg", [128, 4 * 128], F32, kind="ExternalInput").ap()
    w_out = nc.dram_tensor("w_out", [D, D], F32, kind="ExternalInput").ap()
    w_fi = nc.dram_tensor("w_ffn_in", [D, 2 * DFF], F32, kind="ExternalInput").ap()
    w_fo = nc.dram_tensor("w_ffn_out", [DFF, D], F32, kind="ExternalInput").ap()
    consts = nc.dram_tensor("consts", [128, NCST], F32, kind="ExternalInput").ap()
    out = nc.dram_tensor("out", [HALF, D], F32, kind="ExternalOutput").ap()
    GW = SEQ // 4
    mo_g = [nc.dram_tensor("mixed_own_%d" % g, [512, GW], BF16) for g in range(4)]
    mo_halo_t = nc.dram_tensor("mixed_own_halo", [512, 4], BF16)
    ma_big_t = nc.dram_tensor("mixed_all_big", [4 * 1024, GW], BF16)
    ma_halo_t = nc.dram_tensor("mixed_all_halo", [1024, 4], BF16)
    xhalf = nc.dram_tensor("xhalf", [HALF + 2, D], F32, kind="ExternalInput").ap()
    wout_s = nc.dram_tensor("wout_bf16", [D, D], BF16).ap()
    wfi_s = nc.dram_tensor("wfi_bf16", [D, 2 * DFF], BF16).ap()
    wfo_s = nc.dram_tensor("wfo_bf16", [DFF, D], BF16).ap()
    mh_t = nc.dram_tensor("mixed_half", [1024, HALF + 2], BF16)
    mixed_half = mh_t.ap()
    ma_big = ma_big_t.ap()
    ma_halo = ma_halo_t.ap()
    mo_halo = mo_halo_t.ap()

    def store_fn(src, row0, ti):
        pieces = []
        if ti == 0:
            pieces.append((mo_halo[row0:row0 + 128, 0:2], 126, 2))
        else:
            i0 = 512 * (ti - 1)
            c = 0
            while c < 512:
                g = (i0 + c) // GW
                gc = (i0 + c) % GW
                n = min(512 - c, GW - gc)
                pieces.append((mo_g[g].ap()[row0:row0 + 128, gc:gc + n], c, n))
                c += n
            if i0 <= HALF - 2 < i0 + 512:
                pieces.append((mo_halo[row0:row0 + 128, 2:4], HALF - 2 - i0, 2))

        def fn(e, s):
            ins = None
            for dst, c0, n in pieces:
                ins = e.dma_start(out=dst, in_=src[:, c0:c0 + n]).then_inc(s, 16)
            return ins
        return fn, len(pieces)

    S = Sched()
    ARENA_F = 53200

    ctx = nc.sbuf_tensor("arena", [128, ARENA_F], F32)
    arena = ctx.__enter__()
    pctx = nc.psum_tensor("ps", [128, 8 * 512], F32)
    ps = pctx.__enter__()

    class Arena:
        def __init__(self):
            self.off = 0

        def f32(self, n):
            o = self.off
            self.off += n
            assert self.off <= ARENA_F, self.off
            return arena[:, o:o + n]

        def bf(self, n):
            nf = (n + 1) // 2
            o = self.off
            self.off += nf
            assert self.off <= ARENA_F, self.off
            return arena[:, o:o + nf].bitcast(BF16)

    A = Arena()

    def bank(b, n=512):
        return ps[:, b * 512:b * 512 + n]

    def bank_bf(b):
        return ps[:, b * 512:(b + 1) * 512].bitcast(BF16)

    cst = A.bf(NCST)
    vec = A.f32(NV)
    ext = A.f32(16)
    ident = cst[:, C_ID:C_ID + 128]
    negL = cst[:, C_NL:C_NL + 128]
    negU = cst[:, C_NU:C_NU + 128]
    bones = cst[:, C_BO:C_BO + 128]

    def maskv(j, W):
        return cst[:, C_MK + j * 512:C_MK + j * 512 + W]

    mark_stage = A.off

    qT = A.bf(2 * TP)
    kT = A.bf(2 * TP)
    v_sb = A.bf(NB * 256)
    mark12 = A.off
    win_bf = A.bf(8 * 1280)
    wrg_bf = A.bf(4 * 128)
    stg = [A.f32(1280), A.f32(1280)]
    xs = [A.f32(1024), A.f32(1024)]
    hn_bf = [A.bf(1024), A.bf(1024)]
    ssb = A.f32(8)
    hnT = [A.bf(8 * 512), A.bf(8 * 512)]
    sqb = [A.bf(512), A.bf(512)]
    rqb = [A.f32(512), A.f32(512)]
    xr_sb = [A.f32(3 + 512), A.f32(3 + 512)]
    xc_sb = [A.f32(512), A.f32(512)]
    xc_bf = [A.bf(512), A.bf(512)]
    tr_sb = [A.f32(512), A.f32(512)]
    ti_sb = [A.f32(512), A.f32(512)]
    a_sb = [A.f32(512), A.f32(512)]
    m2_sb = [A.f32(512), A.f32(512)]
    bt_sb = [A.f32(512), A.f32(512)]
    hl_sb = [A.f32(512), A.f32(512)]
    hst = A.f32(2)
    y_sb = [stg[0][:, 0:512], stg[0][:, 512:1024]]
    y2_sb = [stg[1][:, 0:512], stg[1][:, 512:1024]]
    ol_bf = [stg[0][:, 1024:1280].bitcast(BF16), stg[1][:, 1024:1280].bitcast(BF16)]

    def col(i, n=1):
        return vec[:, i:i + n]

    for hh in range(2):
        S.add("sp", lambda e, s, hh=hh: e.dma_start(out=stg[hh][:, 0:1280], in_=consts[:, hh * 1280:(hh + 1) * 1280]).then_inc(s, 16),
              writes=[("stg", hh)], chan=("stg", hh))
        S.add("dve", lambda e, hh=hh: e.tensor_copy(out=cst[:, hh * 1280:(hh + 1) * 1280], in_=stg[hh][:, 0:1280]),
              reads=[("stg", hh)], writes=["cst"])
    S.add("sp", lambda e, s: e.dma_start(out=vec[:, :], in_=vecs[:, :]).then_inc(s, 16), writes=["vec"], chan="vec")
    S.add("act", lambda e: e.activation(out=ext[:, 7:9], in_=col(V_LAM, 2), func=AF.Exp, scale=-1.0),
          reads=["vec"], writes=["ext_t"])
    S.add("act", lambda e: e.activation(out=ext[:, 7:9], in_=ext[:, 7:9], func=AF.Ln, bias=1.0),
          reads=["ext_t"], writes=["ext_t"])
    S.add("dve", lambda e: e.tensor_scalar(out=ext[:, 0:2], in0=ext[:, 7:9], scalar1=-8.0, scalar2=None, op0=ALU.mult),
          reads=["ext_t"], writes=["ext"])
    S.add("dve", lambda e: e.tensor_scalar(out=ext[:, 2:6], in0=col(V_BA, 4), scalar1=-1.0, scalar2=None, op0=ALU.mult),
          reads=["vec", "ext"], writes=["ext"])
    S.add("dve", lambda e: e.tensor_scalar(out=ext[:, 6:7], in0=col(V_KG), scalar1=0.125, scalar2=None, op0=ALU.mult),
          reads=["vec", "ext"], writes=["ext"])
    S.add("sp", lambda e, s: e.dma_start(out=stg[0][:, 0:512], in_=w_rg[:, :]).then_inc(s, 16),
          writes=[("stg", 0)], chan=("stg", 0))
    S.add("dve", lambda e: e.tensor_copy(out=wrg_bf[:, :], in_=stg[0][:, 0:512]), reads=[("stg", 0)], writes=["wrg"])
    for c in range(DC):
        hh = c % 2
        S.add("sp", lambda e, s, c=c, hh=hh: e.dma_start(out=stg[hh][:, :], in_=w_in[c * 128:(c + 1) * 128, :]).then_inc(s, 16),
              writes=[("stg", hh)], chan=("stg", hh))
        if c % 2 == 0:
            S.add("dve", lambda e, c=c, hh=hh: e.tensor_scalar(out=win_bf[:, c * 1280:(c + 1) * 1280], in0=stg[hh][:, :],
                                                                scalar1=col(V_G1 + c), scalar2=None, op0=ALU.mult),
                  reads=[("stg", hh), "vec"], writes=[("win", c)])
        else:
            S.add("act", lambda e, c=c, hh=hh: e.activation(out=win_bf[:, c * 1280:(c + 1) * 1280], in_=stg[hh][:, :],
                                                             func=AF.Copy, scale=col(V_G1 + c)),
                  reads=[("stg", hh), "vec"], writes=[("win", c)])
    S.add("pool", lambda e: e.memset(xr_sb[0][:, 0:3], 0.0), writes=[("xr", 0)])
    S.add("pool", lambda e: e.memset(xr_sb[1][:, 0:3], 0.0), writes=[("xr", 1)])
    S.add("pool", lambda e: e.memset(hst[:, :], 0.0), writes=["hst"])

    win_reads = [("win", c) for c in range(DC)]

    B_TRP, B_V, B_PJ0, B_PJ1 = 0, 1, 2, 3
    B_PS2 = [4, 5]
    B_GR = [4, 6]
    B_GI = [5, 7]

    def tile_info(ti):
        if ti == 0:
            return 0, 128
        return 128 + 512 * (ti - 1), 512

    class Rec:
        def __init__(self):
            self.l = []

        def add(self, *a_, **k_):
            self.l.append((a_, k_))

    def zip_emit(chains):
        idx = [0] * len(chains)
        left = True
        while left:
            left = False
            for ci, ch in enumerate(chains):
                if idx[ci] < len(ch.l):
                    a_, k_ = ch.l[idx[ci]]
                    S.add(*a_, **k_)
                    idx[ci] += 1
                    left = True

    def proj(SS, b, col0, W, hp):
        def fn(e):
            ins = None
            for c in range(DC):
                ins = e.matmul(bank(b, W), lhsT=win_bf[:, c * 1280 + col0:c * 1280 + col0 + 128],
                               rhs=hnT[hp][:, c * 512:c * 512 + W], start=(c == 0), stop=(c == DC - 1))
            return ins
        SS.add("pe", fn, reads=win_reads + [("hnT", hp)], writes=[("ps", b)])

    def prep_chain(SS, ti, sub, B_TRP=0, B_V=1):
        pos0, W = tile_info(ti)
        hp = ti % 2
        blk = pos0 // 128 + sub
        sl = blk % 2
        SS.add("sp", lambda e, s: e.dma_start(out=xs[sl][:, :], in_=xpad[blk * 128:(blk + 1) * 128, :]).then_inc(s, 16),
               writes=[("xs", sl)], chan=("xs", sl))
        SS.add("pool", lambda e: e.memset(ssb[:, sl:sl + 1], 0.0), writes=[("ss", sl)])
        SS.add("act", lambda e: e.activation(out=hn_bf[sl][:, :], in_=xs[sl][:, :], func=AF.Square, accum_out=ssb[:, sl:sl + 1]),
               reads=[("xs", sl)], writes=[("ss", sl), ("hn", sl)])
        SS.add("dve", lambda e: e.tensor_scalar(out=ssb[:, 2 + sl:3 + sl], in0=ssb[:, sl:sl + 1], scalar1=1.0 / D, scalar2=EPS,
                                                op0=ALU.mult, op1=ALU.add),
               reads=[("ss", sl)], writes=[("rstd", sl)])
        SS.add("act", lambda e: e.activation(out=ssb[:, 2 + sl:3 + sl], in_=ssb[:, 2 + sl:3 + sl], func=AF.Ln),
               reads=[("rstd", sl)], writes=[("rstd", sl)])
        SS.add("act", lambda e: e.activation(out=ssb[:, 2 + sl:3 + sl], in_=ssb[:, 2 + sl:3 + sl], func=AF.Exp, scale=-0.5),
               reads=[("rstd", sl)], writes=[("rstd", sl)])
        SS.add("act", lambda e: e.activation(out=hn_bf[sl][:, :], in_=xs[sl][:, :], func=AF.Copy, scale=ssb[:, 2 + sl:3 + sl]),
               reads=[("xs", sl), ("rstd", sl)], writes=[("hn", sl)])

        def tr_fn(e):
            ins = None
            for c in range(DC):
                ins = e.transpose(bank_bf(B_TRP)[:, c * 128:(c + 1) * 128], hn_bf[sl][:, c * 128:(c + 1) * 128], ident)
            return ins
        SS.add("pe", tr_fn, reads=[("hn", sl), "cst"], writes=[("ps", B_TRP)])
        SS.add("dve", lambda e: e.tensor_copy(
            out=hnT[hp].rearrange("p (c w) -> p c w", c=8)[:, :, sub * 128:(sub + 1) * 128],
            in_=bank_bf(B_TRP).rearrange("p (c w) -> p c w", c=8)),
            reads=[("ps", B_TRP)], writes=[("hnT", hp)])

        def v_fn(e):
            ins = None
            for c in range(DC):
                ins = e.matmul(bank(B_V, 256), lhsT=hnT[hp][:, c * 512 + sub * 128:c * 512 + (sub + 1) * 128],
                               rhs=win_bf[:, c * 1280 + 512:c * 1280 + 768], start=(c == 0), stop=(c == DC - 1))
            return ins
        SS.add("pe", v_fn, reads=win_reads + [("hnT", hp)], writes=[("ps", B_V)])
        SS.add("act", lambda e: e.copy(out=v_sb[:, blk * 256:(blk + 1) * 256], in_=bank(B_V, 256)),
               reads=[("ps", B_V)], writes=[("v", blk)])

    def qk_chain(SS, ti, which, j):
        pos0, W = tile_info(ti)
        hp = ti % 2
        b = B_PJ0 + j
        p2 = B_PS2[j]
        sq = sqb[j]
        rq = rqb[j]
        proj(SS, b, which * 256 + j * 128, W, hp)
        SS.add("act", lambda e: e.activation(out=sq[:, 0:W], in_=bank(b, W), func=AF.Square), reads=[("ps", b)], writes=[("sqb", j)])
        SS.add("pe", lambda e: e.matmul(bank(p2, W), lhsT=bones, rhs=sq[:, 0:W], start=True, stop=True),
               reads=[("sqb", j), "cst"], writes=[("ps", p2)])
        SS.add("dve", lambda e: e.tensor_scalar(out=rq[:, 0:W], in0=bank(p2, W), scalar1=1.0 / 64, scalar2=EPS,
                                                op0=ALU.mult, op1=ALU.add), reads=[("ps", p2)], writes=[("rqb", j)])
        SS.add("act", lambda e: e.activation(out=rq[:, 0:W], in_=rq[:, 0:W], func=AF.Ln), reads=[("rqb", j)], writes=[("rqb", j)])
        SS.add("act", lambda e: e.activation(out=rq[:, 0:W], in_=rq[:, 0:W], func=AF.Exp, scale=-0.5), reads=[("rqb", j)], writes=[("rqb", j)])
        dst = qT if which == 0 else kT
        gsc = col(V_QG) if which == 0 else ext[:, 6:7]
        SS.add("dve", lambda e: e.scalar_tensor_tensor(
            out=dst[:, j * TP + pos0:j * TP + pos0 + W], in0=bank(b, W), scalar=gsc, in1=rq[:, 0:W], op0=ALU.mult, op1=ALU.mult),
            reads=[("ps", b), ("rqb", j), "vec", "ext"], writes=[("qk", which, j, ti)])

    def xr_chain(SS, ti, c):
        pos0, W = tile_info(ti)
        hp = ti % 2
        b = B_PJ0 + c
        gr, gi = B_GR[c], B_GI[c]
        xc, xcb, tr, tg_, a_, m2, bt = xc_sb[c], xc_bf[c], tr_sb[c], ti_sb[c], a_sb[c], m2_sb[c], bt_sb[c]
        K = lambda n: (n, c)
        proj(SS, b, 768 + c * 128, W, hp)
        SS.add("act", lambda e: e.copy(out=xr_sb[c][:, 3:3 + W], in_=bank(b, W)), reads=[("ps", b)], writes=[("xr", c)])
        SS.add("dve", lambda e: e.tensor_scalar(out=xc[:, 0:W], in0=xr_sb[c][:, 3:3 + W], scalar1=col(V_CW + c * 4 + 3),
                                                scalar2=col(V_CB + c), op0=ALU.mult, op1=ALU.add),
               reads=[("xr", c), "vec"], writes=[K("xc")])
        for k in range(3):
            SS.add("dve", lambda e, k=k: e.scalar_tensor_tensor(out=xc[:, 0:W], in0=xr_sb[c][:, k:k + W], scalar=col(V_CW + c * 4 + k),
                                                                 in1=xc[:, 0:W], op0=ALU.mult, op1=ALU.add),
                   reads=[("xr", c), "vec", K("xc")], writes=[K("xc")])
        SS.add("pool", lambda e: e.tensor_copy(out=xr_sb[c][:, 0:3], in_=xr_sb[c][:, W:W + 3]), reads=[("xr", c)], writes=[("xr", c)])
        SS.add("act", lambda e: e.copy(out=xcb[:, 0:W], in_=xc[:, 0:W]), reads=[K("xc")], writes=[K("xcb")])
        SS.add("pe", lambda e: e.matmul(bank(gr, W), lhsT=wrg_bf[:, (0 * 2 + c) * 128:(0 * 2 + c + 1) * 128], rhs=xcb[:, 0:W],
                                        start=True, stop=True), reads=[K("xcb"), "wrg"], writes=[("ps", gr)])
        SS.add("pe", lambda e: e.matmul(bank(gi, W), lhsT=wrg_bf[:, (1 * 2 + c) * 128:(1 * 2 + c + 1) * 128], rhs=xcb[:, 0:W],
                                        start=True, stop=True), reads=[K("xcb"), "wrg"], writes=[("ps", gi)])
        SS.add("act", lambda e: e.activation(out=tr[:, 0:W], in_=bank(gr, W), func=AF.Exp, bias=ext[:, 2 + c:3 + c], scale=-1.0),
               reads=[("ps", gr), "ext"], writes=[K("tr")])
        SS.add("act", lambda e: e.activation(out=tg_[:, 0:W], in_=bank(gi, W), func=AF.Exp, bias=ext[:, 4 + c:5 + c], scale=-1.0),
               reads=[("ps", gi), "ext"], writes=[K("tig")])
        SS.add("act", lambda e: e.activation(out=tr[:, 0:W], in_=tr[:, 0:W], func=AF.Ln, bias=1.0), reads=[K("tr")], writes=[K("tr")])
        SS.add("act", lambda e: e.activation(out=tg_[:, 0:W], in_=tg_[:, 0:W], func=AF.Ln, bias=1.0), reads=[K("tig")], writes=[K("tig")])
        SS.add("act", lambda e: e.activation(out=tr[:, 0:W], in_=tr[:, 0:W], func=AF.Exp, scale=-1.0), reads=[K("tr")], writes=[K("tr")])
        SS.add("act", lambda e: e.activation(out=tg_[:, 0:W], in_=tg_[:, 0:W], func=AF.Exp, scale=-1.0), reads=[K("tig")], writes=[K("tig")])
        SS.add("act", lambda e: e.activation(out=a_[:, 0:W], in_=tr[:, 0:W], func=AF.Exp, scale=ext[:, c:c + 1]),
               reads=[K("tr"), "ext"], writes=[K("a")])
        SS.add("dve", lambda e: e.tensor_tensor(out=m2[:, 0:W], in0=a_[:, 0:W], in1=a_[:, 0:W], op=ALU.mult), reads=[K("a")], writes=[K("m2")])
        SS.add("dve", lambda e: e.tensor_scalar(out=m2[:, 0:W], in0=m2[:, 0:W], scalar1=-1.0, scalar2=1.0, op0=ALU.mult, op1=ALU.add),
               reads=[K("m2")], writes=[K("m2")])
        SS.add("act", lambda e: e.activation(out=m2[:, 0:W], in_=m2[:, 0:W], func=AF.Ln), reads=[K("m2")], writes=[K("m2")])
        SS.add("act", lambda e: e.activation(out=m2[:, 0:W], in_=m2[:, 0:W], func=AF.Exp, scale=0.5), reads=[K("m2")], writes=[K("m2")])
        SS.add("dve", lambda e: e.tensor_tensor(out=bt[:, 0:W], in0=tg_[:, 0:W], in1=xc[:, 0:W], op=ALU.mult),
               reads=[K("tig"), K("xc")], writes=[K("bt")])
        SS.add("dve", lambda e: e.tensor_tensor(out=bt[:, 0:W], in0=bt[:, 0:W], in1=m2[:, 0:W], op=ALU.mult),
               reads=[K("bt"), K("m2")], writes=[K("bt")])
        if ti == 0:
            SS.add("dve", lambda e: e.memset(bt[:, 0:PAD], 0.0), reads=[K("bt")], writes=[K("bt")])
        SS.add("dve", lambda e: e.tensor_tensor_scan(out=hl_sb[c][:, 0:W], data0=a_[:, 0:W], data1=bt[:, 0:W],
                                                     initial=hst[:, c:c + 1], op0=ALU.mult, op1=ALU.add),
               reads=[K("a"), K("bt"), K("hst")], writes=[("hl", c)])
        SS.add("dve", lambda e: e.tensor_copy(out=hst[:, c:c + 1], in_=hl_sb[c][:, W - 1:W]), reads=[("hl", c)], writes=[K("hst")])

    def yg_chain(SS, ti, c):
        pos0, W = tile_info(ti)
        hp = ti % 2
        b = B_PJ0 + c
        y, y2, tg = y_sb[c], y2_sb[c], y2_sb[c]
        K = lambda n: (n, c)
        proj(SS, b, 1024 + c * 128, W, hp)
        SS.add("act", lambda e: e.copy(out=y[:, 0:W], in_=bank(b, W)), reads=[("ps", b)], writes=[K("y")])
        SS.add("act", lambda e: e.activation(out=y2[:, 0:W], in_=bank(b, W), func=AF.Square), reads=[("ps", b)], writes=[K("y2")])
        SS.add("dve", lambda e: e.tensor_scalar(out=y2[:, 0:W], in0=y2[:, 0:W], scalar1=0.044715, scalar2=1.0, op0=ALU.mult, op1=ALU.add),
               reads=[K("y2")], writes=[K("y2")])
        SS.add("dve", lambda e: e.tensor_tensor(out=y2[:, 0:W], in0=y2[:, 0:W], in1=y[:, 0:W], op=ALU.mult),
               reads=[K("y2"), K("y")], writes=[K("y2")])
        SS.add("act", lambda e: e.activation(out=tg[:, 0:W], in_=y2[:, 0:W], func=AF.Exp, scale=-2.0 * GELU_C), reads=[K("y2")], writes=[K("y2")])
        SS.add("act", lambda e: e.activation(out=tg[:, 0:W], in_=tg[:, 0:W], func=AF.Ln, bias=1.0), reads=[K("y2")], writes=[K("y2")])
        SS.add("act", lambda e: e.activation(out=tg[:, 0:W], in_=tg[:, 0:W], func=AF.Exp, scale=-1.0), reads=[K("y2")], writes=[K("y2")])
        SS.add("dve", lambda e: e.tensor_tensor(out=tg[:, 0:W], in0=tg[:, 0:W], in1=y[:, 0:W], op=ALU.mult),
               reads=[K("y2"), K("y")], writes=[K("y2")])
        SS.add("dve", lambda e: e.tensor_tensor(out=ol_bf[c][:, 0:W], in0=tg[:, 0:W], in1=hl_sb[c][:, 0:W], op=ALU.mult),
               reads=[K("y2"), ("hl", c)], writes=[("ol", c)])
        sfn, nd = store_fn(ol_bf[c], 256 + c * 128, ti)
        SS.add("pool", sfn, reads=[("ol", c)], writes=[("mo", "l", c, ti)], chan=("ol", c), ndma=nd)

    def mk(fn, *a_):
        r = Rec()
        fn(r, *a_)
        return r

    S.barrier()
    ZIP_PREP = (1, 2, 3)
    zip_emit([mk(prep_chain, 0, 0)])
    for ti in range(NQT):
        preps = []
        if ti + 1 < NQT:
            _, Wn = tile_info(ti + 1)
            preps = [mk(prep_chain, ti + 1, sub) for sub in range(Wn // 128)]
        groups = [
            [mk(qk_chain, ti, 0, 0), mk(qk_chain, ti, 0, 1)],
            [mk(qk_chain, ti, 1, 0), mk(qk_chain, ti, 1, 1)],
            [mk(xr_chain, ti, 0), mk(xr_chain, ti, 1)],
            [mk(yg_chain, ti, 0), mk(yg_chain, ti, 1)],
        ]
        if len(preps) == 4:
            p0 = mk(prep_chain, ti + 1, 0, 6, 7)
            zip_emit(groups[0])
            zip_emit(groups[1] + [preps[1]])
            zip_emit(groups[2] + [preps[2]])
            zip_emit(groups[3] + [preps[3], p0])
        else:
            for gi_, grp in enumerate(groups):
                zip_emit(grp)
                if gi_ < len(preps):
                    zip_emit([preps[gi_]])

    S.barrier()
    A.off = mark12
    e_sb = [A.f32(1024) for _ in range(3)]
    sp_sb = [A.bf(1024) for _ in range(3)]
    g_sb = [A.bf(1024) for _ in range(2)]
    w_sb = [A.bf(1024) for _ in range(3)]
    osb = [A.bf(2 * 512), A.bf(2 * 512)]
    pst32 = [A.f32(DFF), A.f32(DFF)]
    pst16 = [A.bf(DFF), A.bf(DFF)]
    pieces = []
    for c in range(DC):
        pieces.append((w_out[c * 128:(c + 1) * 128, :], wout_s[c * 128:(c + 1) * 128, :], 1024, None))
    for c in range(DC):
        for hh in range(2):
            pieces.append((w_fi[c * 128:(c + 1) * 128, hh * DFF:(hh + 1) * DFF], wfi_s[c * 128:(c + 1) * 128, hh * DFF:(hh + 1) * DFF],
                           DFF, col(V_G2 + c)))
    for j in range(NPAIR):
        pieces.append((w_fo[j * 128:(j + 1) * 128, :], wfo_s[j * 128:(j + 1) * 128, :], 1024, None))
    wscr_keys = [("wscr", i) for i in range(len(pieces))]

    def piece_stage(i, st):
        src, dst, ncol, sc = pieces[i]
        sl = i % 2
        if st == 0:
            S.add("sp", lambda e, s: e.dma_start(out=pst32[sl][:, 0:ncol], in_=src).then_inc(s, 16),
                  writes=[("p32", sl)], chan=("p32", sl))
        elif st == 1:
            if sc is None:
                S.add("dve", lambda e: e.tensor_copy(out=pst16[sl][:, 0:ncol], in_=pst32[sl][:, 0:ncol]),
                      reads=[("p32", sl)], writes=[("p16", sl)])
            else:
                S.add("dve", lambda e: e.tensor_scalar(out=pst16[sl][:, 0:ncol], in0=pst32[sl][:, 0:ncol], scalar1=sc, scalar2=None, op0=ALU.mult),
                      reads=[("p32", sl), "vec"], writes=[("p16", sl)])
        else:
            S.add("sp", lambda e, s: e.dma_start(out=dst, in_=pst16[sl][:, 0:ncol]).then_inc(s, 16),
                  reads=[("p16", sl)], writes=[("wscr", i)], chan=("p16", sl))

    items = []
    for ti in range(NQT):
        pos0, W = tile_info(ti)
        b0 = pos0 // 128
        nsub = W // 128
        kbs = list(range(b0 + nsub - 1, -1, -1))
        for n, kb in enumerate(kbs):
            for j in range(2):
                items.append(dict(ti=ti, pos0=pos0, W=W, b0=b0, kb=kb, j=j, first=(n == 0), last=(n == len(kbs) - 1)))
    NI = len(items)

    S.add("dve", lambda e: e.memset(ext[:, 9:10], 0.0),
          reads=[("qk", w, j, t) for w in range(2) for j in range(2) for t in range(NQT)] + [("v", bl) for bl in range(NB)],
          writes=["kall"])

    def v3(buf, c0, W):
        return buf.rearrange("p (h w) -> p h w", h=2)[:, :, c0:W]

    def zview(c0, W):
        return ps[:, 0:1024].rearrange("p (h w) -> p h w", h=2)[:, :, c0:W]

    def pview(j, c0, W):
        return ps[:, (2 + 2 * j) * 512:(4 + 2 * j) * 512].rearrange("p (h w) -> p h w", h=2)[:, :, c0:W]

    RG = [[0, 1], [2, 3], [4, 5], [6, 7]]

    def tile_keys(ti):
        return [("mo", "l", c, ti) for c in range(2)] + [("mo", "a", j, ti) for j in range(2)]

    gathered = set()

    def maybe_gather(ti):
        if ti == 0:
            return
        done_tok = 512 * ti
        for g in range(4):
            if g in gathered or (g + 1) * GW > done_tok:
                continue
            gathered.add(g)
            t_lo = 1 + (g * GW) // 512
            t_hi = 1 + ((g + 1) * GW - 1) // 512
            keys = []
            for t in range(t_lo, t_hi + 1):
                keys += tile_keys(t)
            S.add("pool", lambda e, s, g=g: e.collective_compute(
                "AllGather", ALU.bypass, replica_groups=RG, ins=[mo_g[g].ap().opt()],
                outs=[ma_big[g * 1024:(g + 1) * 1024, :].opt()]).then_inc(s),
                reads=keys, writes=[("mall", g)], chan=("cc", g), inc=1)

    def c0_of(it):
        return 128 * (it["kb"] - it["b0"]) if it["kb"] >= it["b0"] else 0

    def PE1(i):
        it = items[i]
        kb, W, pos0, j = it["kb"], it["W"], it["pos0"], it["j"]
        c0 = c0_of(it)

        def fn(e):
            ins = None
            for hh in range(2):
                r = hh * 64
                ins = e.matmul(ps[:, hh * 512 + c0:hh * 512 + W], lhsT=kT[r:r + 64, j * TP + kb * 128:j * TP + (kb + 1) * 128],
                               rhs=qT[r:r + 64, j * TP + pos0 + c0:j * TP + pos0 + W], start=True, stop=True)
            return ins
        S.add("pe", fn, reads=["kall"], writes=[("ps", 0), ("ps", 1)])

    def ACT12(i):
        it = items[i]
        W = it["W"]
        c0 = c0_of(it)
        eb = e_sb[i % 3]
        sb = sp_sb[i % 3]
        S.add("act", lambda e: e.activation(out=v3(eb, c0, W), in_=zview(c0, W), func=AF.Exp),
              reads=[("ps", 0), ("ps", 1)], writes=[("e", i % 3)])
        S.add("act", lambda e: e.activation(out=v3(sb, c0, W), in_=v3(eb, c0, W), func=AF.Ln, bias=1.0),
              reads=[("e", i % 3)], writes=[("sp", i % 3)])
        if it["kb"] >= it["b0"]:
            for hh in range(2):
                S.add("dve", lambda e, hh=hh: e.tensor_tensor(out=sb[:, hh * 512 + c0:hh * 512 + c0 + 128],
                                                               in0=sb[:, hh * 512 + c0:hh * 512 + c0 + 128], in1=maskv(0, 128), op=ALU.mult),
                      reads=[("sp", i % 3), "cst"], writes=[("sp", i % 3)])

    def PE2(i):
        it = items[i]
        W, j = it["W"], it["j"]
        c0 = c0_of(it)
        sb = sp_sb[i % 3]

        def fn(e):
            ins = None
            for hh in range(2):
                b_ = 2 + 2 * j + hh
                ins = e.matmul(ps[:, b_ * 512 + c0:b_ * 512 + W], lhsT=negL, rhs=sb[:, hh * 512 + c0:hh * 512 + W],
                               start=it["first"], stop=False, skip_group_check=True)
            return ins
        S.add("pe", fn, reads=[("sp", i % 3), "cst"], writes=[("ps", 2 + 2 * j), ("ps", 3 + 2 * j)])

    def ACT3(i):
        it = items[i]
        W, j = it["W"], it["j"]
        c0 = c0_of(it)
        gb = g_sb[i % 2]
        eb = e_sb[i % 3]
        wb = w_sb[i % 3]
        S.add("act", lambda e: e.activation(out=v3(gb, c0, W), in_=pview(j, c0, W), func=AF.Exp),
              reads=[("ps", 2 + 2 * j), ("ps", 3 + 2 * j)], writes=[("g", i % 2)])
        S.add("dve", lambda e: e.tensor_tensor(out=v3(wb, c0, W), in0=v3(eb, c0, W), in1=v3(gb, c0, W), op=ALU.mult),
              reads=[("e", i % 3), ("g", i % 2)], writes=[("w", i % 3)])
        if it["kb"] >= it["b0"]:
            for hh in range(2):
                S.add("dve", lambda e, hh=hh: e.tensor_tensor(out=wb[:, hh * 512 + c0:hh * 512 + c0 + 128],
                                                               in0=wb[:, hh * 512 + c0:hh * 512 + c0 + 128], in1=maskv(0, 128), op=ALU.mult),
                      reads=[("w", i % 3), "cst"], writes=[("w", i % 3)])

    def PE4(i):
        it = items[i]
        if it["last"]:
            return
        W, j = it["W"], it["j"]
        c0 = c0_of(it)
        sb = sp_sb[i % 3]

        def fn(e):
            ins = None
            for hh in range(2):
                b_ = 2 + 2 * j + hh
                ins = e.matmul(ps[:, b_ * 512 + c0:b_ * 512 + W], lhsT=negU, rhs=sb[:, hh * 512 + c0:hh * 512 + W],
                               start=False, stop=False, skip_group_check=True)
            return ins
        S.add("pe", fn, reads=[("sp", i % 3), "cst"], writes=[("ps", 2 + 2 * j), ("ps", 3 + 2 * j)])

    def PE3(i):
        it = items[i]
        kb, W, pos0, ti, j = it["kb"], it["W"], it["pos0"], it["ti"], it["j"]
        c0 = c0_of(it)
        ob = 6 + j
        wb = w_sb[i % 3]
        par = ti % 2

        def fn(e):
            ins = None
            for hh in range(2):
                h = 2 * j + hh
                ins = e.matmul(ps[hh * 64:(hh + 1) * 64, ob * 512 + c0:ob * 512 + W], lhsT=v_sb[:, kb * 256 + h * 64:kb * 256 + (h + 1) * 64],
                               rhs=wb[:, hh * 512 + c0:hh * 512 + W], start=it["first"], stop=it["last"], skip_group_check=True)
            return ins
        S.add("pe", fn, reads=[("w", i % 3), "kall"], writes=[("ps", ob)])
        if it["last"]:
            S.add("dve", lambda e: e.tensor_copy(out=osb[par][:, j * 512:j * 512 + W], in_=bank(ob, W)),
                  reads=[("ps", ob)], writes=[("osb", par, j)])
            sfn, nd = store_fn(osb[par][:, j * 512:(j + 1) * 512], j * 128, ti)
            S.add("pool", sfn, reads=[("osb", par, j)], writes=[("mo", "a", j, ti)], chan=("osb", par, j), ndma=nd)
            if j == 1:
                maybe_gather(ti)

    PERIOD = max(4, (NI - 8) // len(pieces))
    PE1(0)
    for s in range(-1, NI + 1):
        if s >= 0:
            pi_, ph_ = divmod(s, PERIOD)
            if pi_ < len(pieces):
                if ph_ == 0:
                    piece_stage(pi_, 0)
                elif ph_ == PERIOD // 2:
                    piece_stage(pi_, 1)
            if pi_ >= 1 and pi_ - 1 < len(pieces) and ph_ == 1:
                piece_stage(pi_ - 1, 2)
        if 0 <= s + 1 < NI:
            ACT12(s + 1)
        if s + 2 < NI:
            PE1(s + 2)
        if 0 <= s + 1 < NI:
            PE2(s + 1)
        if 0 <= s < NI:
            ACT3(s)
            PE4(s)
        if 0 <= s - 1 < NI:
            PE3(s - 1)

    done_p = min(len(pieces), (NI // PERIOD) + 1)
    for i in range(len(pieces)):
        last_s = NI
        if i * PERIOD > last_s:
            piece_stage(i, 0)
        if i * PERIOD + PERIOD // 2 > last_s:
            piece_stage(i, 1)
        if (i + 1) * PERIOD + 1 > last_s:
            piece_stage(i, 2)

    S.barrier(exclude=lambda ch: isinstance(ch, tuple) and ch[0] == "cc")
    A.off = mark_stage
    wout_bf = A.bf(8 * 1024)
    wfi_bf = A.bf(8 * 2 * DFF)
    wfo_bf = A.bf(NPAIR * 1024)
    w3_keys = []
    qsel = ["sp", "pool"]
    nq = [0]

    def wload(dst, src, key):
        q = qsel[nq[0] % 2]
        nq[0] += 1
        S.add(q, lambda e, s: e.dma_start(out=dst, in_=src).then_inc(s, 16), reads=wscr_keys, writes=[key], chan=key)
        w3_keys.append(key)

    for hf in range(2):
        wload(wout_bf.rearrange("p (c n) -> p c n", c=8)[:, hf * 4:(hf + 1) * 4, :],
              wout_s.rearrange("(c p) n -> p c n", p=128)[:, hf * 4:(hf + 1) * 4, :], ("w3", "o", hf))
    for c in range(DC):
        wload(wfi_bf[:, c * 2 * DFF:(c + 1) * 2 * DFF], wfi_s[c * 128:(c + 1) * 128, :], ("w3", "i", c))
    for q4 in range(2):
        wload(wfo_bf.rearrange("p (j n) -> p j n", j=NPAIR)[:, q4 * 11:(q4 + 1) * 11, :],
              wfo_s.rearrange("(j p) n -> p j n", p=128)[:, q4 * 11:(q4 + 1) * 11, :], ("w3", "f", q4))
    K_WO = [k_ for k_ in w3_keys if k_[1] == "o"]
    K_WI = [k_ for k_ in w3_keys if k_[1] == "i"]
    K_WF = [k_ for k_ in w3_keys if k_[1] == "f"]

    S.add("pool", lambda e, s: e.collective_compute("AllGather", ALU.bypass, replica_groups=RG,
                                                    ins=[mo_halo_t.ap().opt()], outs=[ma_halo_t.ap().opt()]).then_inc(s),
          reads=tile_keys(0) + tile_keys(1 + (HALF - 2) // 512), writes=[("mall", "h")], chan="cch", inc=1)

    def mh_fn(e, s):
        half = e.partition_id() % 2
        ins = None
        for j in range(2):
            for r in range(2):
                ins = e.dma_start(out=mixed_half[r * 512:(r + 1) * 512, 2 + j * GW:2 + (j + 1) * GW],
                                  in_=ma_big[bass.ds(half * 2048 + j * 1024 + r * 512, 512), :]).then_inc(s, 16)
        ins = e.dma_start(out=mixed_half[:, 0:2], in_=ma_halo[:, bass.ds(half * 2, 2)]).then_inc(s, 16)
        return ins
    S.add("sp", mh_fn, reads=[("mall", g) for g in range(4)] + [("mall", "h")], writes=["mhalf"], chan="mh", ndma=5)

    h2 = [A.f32(1024) for _ in range(4)]
    hn2_bf = [A.bf(1024), A.bf(1024)]
    ss3 = A.f32(8)
    HW = 2 + W3
    hn2T = [A.bf(8 * HW), A.bf(8 * HW)]
    mt = A.bf(8 * W3)
    ubuf = [A.f32(2 + W3), A.f32(2 + W3)]
    gbuf = [A.f32(2 + W3), A.f32(2 + W3)]
    uc = [A.f32(W3), A.f32(W3)]
    gc = [A.f32(W3), A.f32(W3)]
    sg = [A.f32(W3), A.f32(W3)]
    act_bf = A.bf(NPAIR * W3)

    B_FO = [0, 1]
    B_T3 = 2
    B_U = [3, 4, 0]
    B_G = [5, 6, 1]
    B_WO = 7
    mixed_half_r = mixed_half.rearrange("(c p) w -> p c w", p=128)

    def tile_geom(t):
        if t < 0:
            return 0, 2, 2, 1, 1
        return 2 + t * W3, W3, 128, W3 // 128, t % 2

    def prep1(t):
        rel0, Wt, n, nsub, par = tile_geom(t)
        S.add("sp", lambda e, s: e.dma_start(out=mt.rearrange("p (c w) -> p c w", c=8)[:, :, 0:Wt],
                                              in_=mixed_half_r[:, :, rel0:rel0 + Wt]).then_inc(s, 16),
              reads=["mhalf"], writes=["mt"], chan="mt")
        for sub in range(nsub):
            sl = 2 * par + sub
            q = sub
            r = rel0 + sub * 128
            S.add("sp", lambda e, s, sl=sl, r=r: e.dma_start(out=h2[sl][0:n, :], in_=xhalf[r:r + n, :]).then_inc(s, 16),
                  writes=[("h2", sl)], chan=("h2", sl))
            for hf in range(2):
                def wo_fn(e, sub=sub, hf=hf):
                    ins = None
                    for c in range(DC):
                        ins = e.matmul(ps[0:n, B_WO * 512:(B_WO + 1) * 512], lhsT=mt[:, c * W3 + sub * 128:c * W3 + sub * 128 + n],
                                       rhs=wout_bf[:, c * 1024 + hf * 512:c * 1024 + (hf + 1) * 512], start=(c == 0), stop=(c == DC - 1))
                    return ins
                S.add("pe", wo_fn, reads=["mt"] + K_WO, writes=[("ps", B_WO)])
                S.add("dve", lambda e, sl=sl, hf=hf: e.tensor_tensor(out=h2[sl][0:n, hf * 512:(hf + 1) * 512],
                                                                     in0=ps[0:n, B_WO * 512:(B_WO + 1) * 512],
                                                                     in1=h2[sl][0:n, hf * 512:(hf + 1) * 512], op=ALU.add),
                      reads=[("ps", B_WO), ("h2", sl)], writes=[("h2", sl)])
            S.add("pool", lambda e, q=q: e.memset(ss3[:, q:q + 1], 0.0), writes=[("ss3", q)])
            S.add("act", lambda e, sl=sl, q=q: e.activation(out=hn2_bf[q][0:n, :], in_=h2[sl][0:n, :], func=AF.Square,
                                                            accum_out=ss3[0:n, q:q + 1]),
                  reads=[("h2", sl)], writes=[("ss3", q), ("hn2", q)])
            S.add("dve", lambda e, q=q: e.tensor_scalar(out=ss3[0:n, 2 + q:3 + q], in0=ss3[0:n, q:q + 1], scalar1=1.0 / D, scalar2=EPS,
                                                         op0=ALU.mult, op1=ALU.add), reads=[("ss3", q)], writes=[("rs3", q)])
            S.add("act", lambda e, q=q: e.activation(out=ss3[0:n, 2 + q:3 + q], in_=ss3[0:n, 2 + q:3 + q], func=AF.Ln),
                  reads=[("rs3", q)], writes=[("rs3", q)])
            S.add("act", lambda e, q=q: e.activation(out=ss3[0:n, 2 + q:3 + q], in_=ss3[0:n, 2 + q:3 + q], func=AF.Exp, scale=-0.5),
                  reads=[("rs3", q)], writes=[("rs3", q)])
            S.add("act", lambda e, sl=sl, q=q: e.activation(out=hn2_bf[q][0:n, :], in_=h2[sl][0:n, :], func=AF.Copy, scale=ss3[0:n, 2 + q:3 + q]),
                  reads=[("h2", sl), ("rs3", q)], writes=[("hn2", q)])

    def prep2(t):
        rel0, Wt, n, nsub, par = tile_geom(t)
        for sub in range(nsub):
            q = sub

            def tr_fn(e, q=q):
                ins = None
                for c in range(DC):
                    ins = e.transpose(bank_bf(B_T3)[:, c * 128:c * 128 + n], hn2_bf[q][0:n, c * 128:(c + 1) * 128], ident[0:n, 0:n])
                return ins
            S.add("pe", tr_fn, reads=[("hn2", q), "cst"], writes=[("ps", B_T3)])
            dcol = 0 if t < 0 else 2 + sub * 128
            dpar = 0 if t < 0 else par
            S.add("dve", lambda e, dcol=dcol, dpar=dpar: e.tensor_copy(
                out=hn2T[dpar].rearrange("p (c w) -> p c w", c=8)[:, :, dcol:dcol + n],
                in_=bank_bf(B_T3).rearrange("p (c w) -> p c w", c=8)[:, :, 0:n]),
                reads=[("ps", B_T3)], writes=[("hn2T", dpar)])
        if t >= 0 and t + 1 < NST:
            S.add("dve", lambda e: e.tensor_copy(
                out=hn2T[1 - par].rearrange("p (c w) -> p c w", c=8)[:, :, 0:2],
                in_=hn2T[par].rearrange("p (c w) -> p c w", c=8)[:, :, W3:W3 + 2]),
                reads=[("hn2T", par)], writes=[("hn2T", 1 - par)])

    pair_ctr = [0]

    def ffn_in(t):
        rel0, Wt, n, nsub, par = tile_geom(t)
        final = t >= 0
        pcs = {}

        def mm(j):
            pc = pair_ctr[0] % 3
            pair_ctr[0] += 1
            pcs[j] = pc
            for which, bb in enumerate((B_U[pc], B_G[pc])):
                slab = j + which * NPAIR

                def fi_fn(e, bb=bb, slab=slab):
                    ins = None
                    for c in range(DC):
                        ins = e.matmul(bank(bb, Wt + 2), lhsT=wfi_bf[:, c * 2 * DFF + slab * 128:c * 2 * DFF + (slab + 1) * 128],
                                       rhs=hn2T[par][:, c * HW:c * HW + Wt + 2], start=(c == 0), stop=(c == DC - 1))
                    return ins
                S.add("pe", fi_fn, reads=[("hn2T", par)] + K_WI, writes=[("ps", bb)])

        def post1(j):
            pc = pcs[j]
            p2 = j % 2
            for which, (bb, cbuf) in enumerate(((B_U[pc], uc[p2]), (B_G[pc], gc[p2]))):
                slab = j + which * NPAIR
                ck = ("cb", which, p2)
                S.add("act", lambda e, bb=bb, cbuf=cbuf, slab=slab: e.activation(
                    out=cbuf[:, 0:Wt], in_=ps[:, bb * 512 + 2:bb * 512 + 2 + Wt], func=AF.Identity,
                    scale=col(V_FCW + slab * 3 + 2), bias=col(V_FCB + slab)),
                    reads=[("ps", bb), "vec"], writes=[ck])
                for k in range(2):
                    S.add("dve", lambda e, bb=bb, cbuf=cbuf, slab=slab, k=k: e.scalar_tensor_tensor(
                        out=cbuf[:, 0:Wt], in0=ps[:, bb * 512 + k:bb * 512 + k + Wt], scalar=col(V_FCW + slab * 3 + k), in1=cbuf[:, 0:Wt],
                        op0=ALU.mult, op1=ALU.add), reads=[("ps", bb), "vec", ck], writes=[ck])

        def post2(j):
            pc = j % 2
            S.add("act", lambda e: e.activation(out=sg[pc][:, 0:Wt], in_=gc[pc][:, 0:Wt], func=AF.Silu),
                  reads=[("cb", 1, pc)], writes=[("sg", pc)])
            S.add("pool", lambda e: e.tensor_tensor(out=act_bf[:, j * W3:j * W3 + Wt], in0=sg[pc][:, 0:Wt], in1=uc[pc][:, 0:Wt], op=ALU.mult),
                  reads=[("sg", pc), ("cb", 0, pc)], writes=[("actT", j)])

        for j in range(NPAIR + 2):
            if j < NPAIR:
                mm(j)
            if 0 <= j - 1 < NPAIR:
                post1(j - 1)
            if 0 <= j - 2 < NPAIR:
                post2(j - 2)

    def ffn_out(t, sub):
        rel0, Wt, n, nsub, par = tile_geom(t)
        sl = 2 * par + sub

        def fo_fn(e):
            ins = None
            for hf in range(2):
                for j in range(NPAIR):
                    ins = e.matmul(bank(B_FO[hf], 512), lhsT=act_bf[:, j * W3 + sub * 128:j * W3 + (sub + 1) * 128],
                                   rhs=wfo_bf[:, j * 1024 + hf * 512:j * 1024 + (hf + 1) * 512], start=(j == 0), stop=(j == NPAIR - 1))
            return ins
        S.add("pe", fo_fn, reads=[("actT", j) for j in range(NPAIR)] + K_WF, writes=[("ps", 0), ("ps", 1)])
        S.add("dve", lambda e: e.tensor_tensor(out=h2[sl][:, :], in0=ps[:, 0:1024], in1=h2[sl][:, :], op=ALU.add),
              reads=[("ps", 0), ("ps", 1), ("h2", sl)], writes=[("h2", sl)])
        r0 = rel0 - 2 + sub * 128
        S.add("pool", lambda e, s: e.dma_start(out=out[r0:r0 + 128, :], in_=h2[sl][:, :]).then_inc(s, 16),
              reads=[("h2", sl)], writes=[("h2", sl), ("out", r0)], chan=("h2", sl))

    prep1(-1)
    prep2(-1)
    prep1(0)
    prep2(0)
    for t in range(NST):
        ffn_in(t)
        if t + 1 < NST:
            prep1(t + 1)
        ffn_out(t, 0)
        if t + 1 < NST:
            prep2(t + 1)
        ffn_out(t, 1)

    S.add("sp", lambda e: None, reads=[("out", r0) for r0 in range(0, HALF, 128)], writes=["done"])

    S.finalize()
    chans = list(S.chan_count.keys())
    sem_ctxs = []
    sems = {}
    for e in Sched.ENG:
        c = nc.semaphore("s_" + e)
        sems[e] = c.__enter__()
        sem_ctxs.append(c)
    chan_sems = {}
    for i, ch in enumerate(chans):
        c = nc.semaphore("c_%d" % i)
        chan_sems[ch] = c.__enter__()
        sem_ctxs.append(c)
    with nc.Block() as block:
        S.emit(nc, block, sems, chan_sems)
    for c in reversed(sem_ctxs):
        c.__exit__(None, None, None)
    pctx.__exit__(None, None, None)
    ctx.__exit__(None, None, None)
    return nc


def _consts():
    c = np.zeros((128, NCST), np.float32)
    j = np.arange(128)[:, None]
    s = np.arange(128)[None, :]
    c[:, C_ID:C_ID + 128] = (j == s)
    c[:, C_NL:C_NL + 128] = -(j >= s).astype(np.float32)
    c[:, C_NU:C_NU + 128] = -(j < s).astype(np.float32)
    c[:, C_BO:C_BO + 128] = ((j // 64) == (s // 64))
    t = np.arange(512)[None, :]
    for k in range(4):
        c[:, C_MK + k * 512:C_MK + (k + 1) * 512] = ((128 * k + j) < t)
    return c


def _prep_inputs(inputs, SEQ):
    f = lambda a: np.ascontiguousarray(np.asarray(a), dtype=np.float32)
    x = f(inputs["x"])
    meta = f(inputs["meta_tokens"])
    w_in = f(inputs["w_in"])[0]
    w_out = f(inputs["w_out"])[0]
    w_fi = f(inputs["w_ffn_in"])[0]
    w_fo = f(inputs["w_ffn_out"])[0]
    g1 = f(inputs["norm1_g"])[0]
    g2 = f(inputs["norm2_g"])[0]
    qg = f(inputs["q_norm_g"])[0]
    kg = f(inputs["k_norm_g"])[0]
    cw = f(inputs["conv_w"])[0]
    cb = f(inputs["conv_b"])[0]
    wa = f(inputs["w_rg_a"])[0]
    wi = f(inputs["w_rg_i"])[0]
    ba = f(inputs["b_rg_a"])[0]
    bi = f(inputs["b_rg_i"])[0]
    lam = f(inputs["lru_lambda"])[0]
    fcw = f(inputs["ffn_conv_w"])[0]
    fcb = f(inputs["ffn_conv_b"])[0]
    consts = _consts()
    TP = SEQ + 128
    maps = []
    for core in range(8):
        b, p = core // 2, core % 2
        xpad = np.zeros((TP, D), np.float32)
        xpad[PAD:128] = meta
        xpad[128:] = x[b]
        cs = slice(256 * p, 256 * p + 256)
        wic = np.concatenate([w_in[:, 0:512][:, cs], w_in[:, 512:1024][:, cs], w_in[:, 1024:1536][:, cs],
                              w_in[:, 1536:2048][:, cs], w_in[:, 2048:2560][:, cs]], axis=1)
        vec = np.zeros((128, NV), np.float32)
        vec[:, V_G1:V_G1 + 8] = g1.reshape(8, 128).T
        vec[:, V_G2:V_G2 + 8] = g2.reshape(8, 128).T
        vec[:, V_QG] = np.tile(qg, 2)
        vec[:, V_KG] = np.tile(kg, 2)
        for c in range(2):
            ch = slice(256 * p + 128 * c, 256 * p + 128 * c + 128)
            vec[:, V_CW + c * 4:V_CW + c * 4 + 4] = cw[:, ch].T
            vec[:, V_CB + c] = cb[ch]
            vec[:, V_BA + c] = ba[ch]
            vec[:, V_BI + c] = bi[ch]
            vec[:, V_LAM + c] = lam[ch]
        for s in range(NSLAB):
            vec[:, V_FCW + s * 3:V_FCW + s * 3 + 3] = fcw[:, s * 128:(s + 1) * 128].T
            vec[:, V_FCB + s] = fcb[s * 128:(s + 1) * 128]
        wrg = np.zeros((128, 4 * 128), np.float32)
        for gi, wsrc in enumerate((wa, wi)):
            for c in range(2):
                for k in range(2):
                    blk = 4 * p + 2 * c + k
                    o = (gi * 2 + c) * 128
                    wrg[64 * k:64 * k + 64, o + 64 * k:o + 64 * k + 64] = wsrc[blk]
        perm = []
        for r in range(2):
            perm += list(range(256 * r, 256 * r + 256))
            perm += list(range(512 + 256 * r, 512 + 256 * r + 256))
        maps.append({
            "xpad": xpad, "xhalf": np.ascontiguousarray(xpad[126 + (SEQ // 2) * p:126 + (SEQ // 2) * p + SEQ // 2 + 2]), "w_in": np.ascontiguousarray(wic), "vecs": vec, "w_rg": wrg,
            "w_out": np.ascontiguousarray(w_out[perm]), "w_ffn_in": w_fi, "w_ffn_out": w_fo, "consts": consts,
        })
    return maps


_NC_CACHE = {}


def kernel(**inputs):
    x = np.asarray(inputs["x"])
    B, SEQ, _ = x.shape
    assert B == 4
    if SEQ not in _NC_CACHE:
        _NC_CACHE[SEQ] = build_nc(SEQ)
    nc = _NC_CACHE[SEQ]
    maps = _prep_inputs(inputs, SEQ)
    res = run_bass_kernel_spmd(nc, maps, core_ids=list(range(8)))
    outp = np.empty((B, SEQ, D), np.float32)
    HALF = SEQ // 2
    for core in range(8):
        b, p = core // 2, core % 2
        outp[b, p * HALF:(p + 1) * HALF] = res.results[core]["out"]
    return outp
```

```python
import numpy as np
import concourse.bass as bass
import concourse.mybir as mybir
from concourse.bass_utils import run_bass_kernel_spmd

F32 = mybir.dt.float32
BF16 = mybir.dt.bfloat16
AF = mybir.ActivationFunctionType
ALU = mybir.AluOpType

D = 1024
DC = 8
DFF = 2816
NSLAB = 44
NPAIR = 22
EPS = 1e-6
NMETA = 16
PAD = 112
GELU_C = 0.7978845608028654

V_G1, V_G2, V_QG, V_KG, V_CW, V_CB, V_BA, V_BI, V_LAM, V_FCW, V_FCB = 0, 8, 16, 17, 18, 26, 28, 30, 32, 34, 166
NV = 210
C_ID, C_NL, C_NU, C_BO, C_MK = 0, 128, 256, 384, 512
NCST = 512 + 4 * 512


class Sched:
    ENG = ("pe", "act", "dve", "pool", "sp")

    def __init__(self):
        self.ops = []
        self.last_w = {}
        self.readers = {}
        self.eng_count = {e: 0 for e in self.ENG}
        self.chan_count = {}
        self.chan_inc = {}
        self.barrier_nodes = []

    def add(self, eng, fn, reads=(), writes=(), chan=None, ndma=1, inc=16):
        deps = set(self.barrier_nodes)
        for k in reads:
            w = self.last_w.get(k)
            if w is not None:
                deps.add(w)
        for k in writes:
            w = self.last_w.get(k)
            if w is not None:
                deps.add(w)
            for r in self.readers.get(k, ()):
                deps.add(r)
        if chan is None:
            self.eng_count[eng] += 1
            node = ("E", eng, self.eng_count[eng])
        else:
            self.chan_inc[chan] = inc
            self.chan_count[chan] = self.chan_count.get(chan, 0) + ndma * inc
            node = ("C", chan, self.chan_count[chan])
        self.ops.append(dict(eng=eng, fn=fn, deps=deps, node=node, chan=chan))
        for k in reads:
            self.readers.setdefault(k, []).append(node)
        for k in writes:
            self.last_w[k] = node
            self.readers[k] = []
        return node

    def barrier(self, exclude=None):
        nodes = []
        for e, c in self.eng_count.items():
            if c:
                nodes.append(("E", e, c))
        for ch, c in self.chan_count.items():
            if exclude is not None and exclude(ch):
                continue
            nodes.append(("C", ch, c))
        self.barrier_nodes = nodes

    def finalize(self):
        known = {e: {} for e in self.ENG}
        signal = {e: set() for e in self.ENG}
        for op in self.ops:
            need = {}
            for kind, tgt, val in op["deps"]:
                if kind == "E" and tgt == "pe" and op["eng"] == "pe" and op["chan"] is None:
                    continue
                key = (kind, tgt)
                if val > need.get(key, 0):
                    need[key] = val
            waits = []
            kn = known[op["eng"]]
            for key, val in need.items():
                if kn.get(key, 0) >= val:
                    continue
                kn[key] = val
                waits.append((key, val))
                if key[0] == "E":
                    signal[key[1]].add(val)
            op["waits"] = waits
        self.rank = {}
        for e in self.ENG:
            self.rank[e] = {v: i + 1 for i, v in enumerate(sorted(signal[e]))}

    def emit(self, nc, block, sems, chan_sems):
        streams = {e: [op for op in self.ops if op["eng"] == e] for e in self.ENG}

        def run(eng_name, eng):
            for op in streams[eng_name]:
                for (kind, tgt), val in op["waits"]:
                    if kind == "E":
                        eng.wait_ge(sems[tgt], self.rank[tgt][val])
                    else:
                        eng.wait_ge(chan_sems[tgt], val)
                if op["chan"] is not None:
                    op["fn"](eng, chan_sems[op["chan"]])
                else:
                    ins = op["fn"](eng)
                    idx = op["node"][2]
                    if idx in self.rank[eng_name]:
                        assert ins is not None
                        ins.then_inc(sems[eng_name], 1)

        @block.tensor
        def _(e):
            run("pe", e)

        @block.scalar
        def _(e):
            run("act", e)

        @block.vector
        def _(e):
            run("dve", e)

        @block.gpsimd
        def _(e):
            run("pool", e)

        @block.sync
        def _(e):
            run("sp", e)


def build_nc(SEQ):
    assert SEQ % 1024 == 0
    HALF = SEQ // 2
    TP = SEQ + 128
    NB = TP // 128
    NQT = 1 + SEQ // 512
    NST = HALF // 256
    W3 = 256

    nc = bass.Bass("TRN2", target_bir_lowering=False)
    xpad = nc.dram_tensor("xpad", [TP, D], F32, kind="ExternalInput").ap()
    w_in = nc.dram_tensor("w_in", [D, 1280], F32, kind="ExternalInput").ap()
    vecs = nc.dram_tensor("vecs", [128, NV], F32, kind="ExternalInput").ap()
    w_rg = nc.dram_tensor("w_rg", [128, 4 * 128], F32, kind="ExternalInput").ap()
    w_out = nc.dram_tensor("w_out", [D, D], F32, kind="ExternalInput").ap()
    w_fi = nc.dram_tensor("w_ffn_in", [D, 2 * DFF], F32, kind="ExternalInput").ap()
    w_fo = nc.dram_tensor("w_ffn_out", [DFF, D], F32, kind="ExternalInput").ap()
    consts = nc.dram_tensor("consts", [128, NCST], F32, kind="ExternalInput").ap()
    out = nc.dram_tensor("out", [HALF, D], F32, kind="ExternalOutput").ap()
    GW = SEQ // 4
    mo_g = [nc.dram_tensor("mixed_own_%d" % g, [512, GW], BF16) for g in range(4)]
    mo_halo_t = nc.dram_tensor("mixed_own_halo", [512, 4], BF16)
    ma_big_t = nc.dram_tensor("mixed_all_big", [4 * 1024, GW], BF16)
    ma_halo_t = nc.dram_tensor("mixed_all_halo", [1024, 4], BF16)
    xhalf = nc.dram_tensor("xhalf", [HALF + 2, D], F32, kind="ExternalInput").ap()
    wout_s = nc.dram_tensor("wout_bf16", [D, D], BF16).ap()
    wfi_s = nc.dram_tensor("wfi_bf16", [D, 2 * DFF], BF16).ap()
    wfo_s = nc.dram_tensor("wfo_bf16", [DFF, D], BF16).ap()
    mh_t = nc.dram_tensor("mixed_half", [1024, HALF + 2], BF16)
    mixed_half = mh_t.ap()
    ma_big = ma_big_t.ap()
    ma_halo = ma_halo_t.ap()
    mo_halo = mo_halo_t.ap()

    def store_fn(src, row0, ti):
        pieces = []
        if ti == 0:
            pieces.append((mo_halo[row0:row0 + 128, 0:2], 126, 2))
        else:
            i0 = 512 * (ti - 1)
            c = 0
            while c < 512:
                g = (i0 + c) // GW
                gc = (i0 + c) % GW
                n = min(512 - c, GW - gc)
                pieces.append((mo_g[g].ap()[row0:row0 + 128, gc:gc + n], c, n))
                c += n
            if i0 <= HALF - 2 < i0 + 512:
                pieces.append((mo_halo[row0:row0 + 128, 2:4], HALF - 2 - i0, 2))

        def fn(e, s):
            ins = None
            for dst, c0, n in pieces:
                ins = e.dma_start(out=dst, in_=src[:, c0:c0 + n]).then_inc(s, 16)
            return ins
        return fn, len(pieces)

    S = Sched()
    ARENA_F = 53200

    ctx = nc.sbuf_tensor("arena", [128, ARENA_F], F32)
    arena = ctx.__enter__()
    pctx = nc.psum_tensor("ps", [128, 8 * 512], F32)
    ps = pctx.__enter__()

    class Arena:
        def __init__(self):
            self.off = 0

        def f32(self, n):
            o = self.off
            self.off += n
            assert self.off <= ARENA_F, self.off
            return arena[:, o:o + n]

        def bf(self, n):
            nf = (n + 1) // 2
            o = self.off
            self.off += nf
            assert self.off <= ARENA_F, self.off
            return arena[:, o:o + nf].bitcast(BF16)

    A = Arena()

    def bank(b, n=512):
        return ps[:, b * 512:b * 512 + n]

    def bank_bf(b):
        return ps[:, b * 512:(b + 1) * 512].bitcast(BF16)

    cst = A.bf(NCST)
    vec = A.f32(NV)
    ext = A.f32(16)
    ident = cst[:, C_ID:C_ID + 128]
    negL = cst[:, C_NL:C_NL + 128]
    negU = cst[:, C_NU:C_NU + 128]
    bones = cst[:, C_BO:C_BO + 128]

    def maskv(j, W):
        return cst[:, C_MK + j * 512:C_MK + j * 512 + W]

    mark_stage = A.off

    qT = A.bf(2 * TP)
    kT = A.bf(2 * TP)
    v_sb = A.bf(NB * 256)
    mark12 = A.off
    win_bf = A.bf(8 * 1280)
    wrg_bf = A.bf(4 * 128)
    stg = [A.f32(1280), A.f32(1280)]
    xs = [A.f32(1024), A.f32(1024)]
    hn_bf = [A.bf(1024), A.bf(1024)]
    ssb = A.f32(8)
    hnT = [A.bf(8 * 512), A.bf(8 * 512)]
    sqb = [A.bf(512), A.bf(512)]
    rqb = [A.f32(512), A.f32(512)]
    xr_sb = [A.f32(3 + 512), A.f32(3 + 512)]
    xc_sb = [A.f32(512), A.f32(512)]
    xc_bf = [A.bf(512), A.bf(512)]
    tr_sb = [A.f32(512), A.f32(512)]
    ti_sb = [A.f32(512), A.f32(512)]
    a_sb = [A.f32(512), A.f32(512)]
    m2_sb = [A.f32(512), A.f32(512)]
    bt_sb = [A.f32(512), A.f32(512)]
    hl_sb = [A.f32(512), A.f32(512)]
    hst = A.f32(2)
    y_sb = [stg[0][:, 0:512], stg[0][:, 512:1024]]
    y2_sb = [stg[1][:, 0:512], stg[1][:, 512:1024]]
    ol_bf = [stg[0][:, 1024:1280].bitcast(BF16), stg[1][:, 1024:1280].bitcast(BF16)]

    def col(i, n=1):
        return vec[:, i:i + n]

    for hh in range(2):
        S.add("sp", lambda e, s, hh=hh: e.dma_start(out=stg[hh][:, 0:1280], in_=consts[:, hh * 1280:(hh + 1) * 1280]).then_inc(s, 16),
              writes=[("stg", hh)], chan=("stg", hh))
        S.add("dve", lambda e, hh=hh: e.tensor_copy(out=cst[:, hh * 1280:(hh + 1) * 1280], in_=stg[hh][:, 0:1280]),
              reads=[("stg", hh)], writes=["cst"])
    S.add("sp", lambda e, s: e.dma_start(out=vec[:, :], in_=vecs[:, :]).then_inc(s, 16), writes=["vec"], chan="vec")
    S.add("act", lambda e: e.activation(out=ext[:, 7:9], in_=col(V_LAM, 2), func=AF.Exp, scale=-1.0),
          reads=["vec"], writes=["ext_t"])
    S.add("act", lambda e: e.activation(out=ext[:, 7:9], in_=ext[:, 7:9], func=AF.Ln, bias=1.0),
          reads=["ext_t"], writes=["ext_t"])
    S.add("dve", lambda e: e.tensor_scalar(out=ext[:, 0:2], in0=ext[:, 7:9], scalar1=-8.0, scalar2=None, op0=ALU.mult),
          reads=["ext_t"], writes=["ext"])
    S.add("dve", lambda e: e.tensor_scalar(out=ext[:, 2:6], in0=col(V_BA, 4), scalar1=-1.0, scalar2=None, op0=ALU.mult),
          reads=["vec", "ext"], writes=["ext"])
    S.add("dve", lambda e: e.tensor_scalar(out=ext[:, 6:7], in0=col(V_KG), scalar1=0.125, scalar2=None, op0=ALU.mult),
          reads=["vec", "ext"], writes=["ext"])
    S.add("sp", lambda e, s: e.dma_start(out=stg[0][:, 0:512], in_=w_rg[:, :]).then_inc(s, 16),
          writes=[("stg", 0)], chan=("stg", 0))
    S.add("dve", lambda e: e.tensor_copy(out=wrg_bf[:, :], in_=stg[0][:, 0:512]), reads=[("stg", 0)], writes=["wrg"])
    for c in range(DC):
        hh = c % 2
        S.add("sp", lambda e, s, c=c, hh=hh: e.dma_start(out=stg[hh][:, :], in_=w_in[c * 128:(c + 1) * 128, :]).then_inc(s, 16),
              writes=[("stg", hh)], chan=("stg", hh))
        if c % 2 == 0:
            S.add("dve", lambda e, c=c, hh=hh: e.tensor_scalar(out=win_bf[:, c * 1280:(c + 1) * 1280], in0=stg[hh][:, :],
                                                                scalar1=col(V_G1 + c), scalar2=None, op0=ALU.mult),
                  reads=[("stg", hh), "vec"], writes=[("win", c)])
        else:
            S.add("act", lambda e, c=c, hh=hh: e.activation(out=win_bf[:, c * 1280:(c + 1) * 1280], in_=stg[hh][:, :],
                                                             func=AF.Copy, scale=col(V_G1 + c)),
                  reads=[("stg", hh), "vec"], writes=[("win", c)])
    S.add("pool", lambda e: e.memset(xr_sb[0][:, 0:3], 0.0), writes=[("xr", 0)])
    S.add("pool", lambda e: e.memset(xr_sb[1][:, 0:3], 0.0), writes=[("xr", 1)])
    S.add("pool", lambda e: e.memset(hst[:, :], 0.0), writes=["hst"])

    win_reads = [("win", c) for c in range(DC)]

    B_TRP, B_V, B_PJ0, B_PJ1 = 0, 1, 2, 3
    B_PS2 = [4, 5]
    B_GR = [4, 6]
    B_GI = [5, 7]

    def tile_info(ti):
        if ti == 0:
            return 0, 128
        return 128 + 512 * (ti - 1), 512

    class Rec:
        def __init__(self):
            self.l = []

        def add(self, *a_, **k_):
            self.l.append((a_, k_))

    def zip_emit(chains):
        idx = [0] * len(chains)
        left = True
        while left:
            left = False
            for ci, ch in enumerate(chains):
                if idx[ci] < len(ch.l):
                    a_, k_ = ch.l[idx[ci]]
                    S.add(*a_, **k_)
                    idx[ci] += 1
                    left = True

    def proj(SS, b, col0, W, hp):
        def fn(e):
            ins = None
            for c in range(DC):
                ins = e.matmul(bank(b, W), lhsT=win_bf[:, c * 1280 + col0:c * 1280 + col0 + 128],
                               rhs=hnT[hp][:, c * 512:c * 512 + W], start=(c == 0), stop=(c == DC - 1))
            return ins
        SS.add("pe", fn, reads=win_reads + [("hnT", hp)], writes=[("ps", b)])

    def prep_chain(SS, ti, sub, B_TRP=0, B_V=1):
        pos0, W = tile_info(ti)
        hp = ti % 2
        blk = pos0 // 128 + sub
        sl = blk % 2
        SS.add("sp", lambda e, s: e.dma_start(out=xs[sl][:, :], in_=xpad[blk * 128:(blk + 1) * 128, :]).then_inc(s, 16),
               writes=[("xs", sl)], chan=("xs", sl))
        SS.add("pool", lambda e: e.memset(ssb[:, sl:sl + 1], 0.0), writes=[("ss", sl)])
        SS.add("act", lambda e: e.activation(out=hn_bf[sl][:, :], in_=xs[sl][:, :], func=AF.Square, accum_out=ssb[:, sl:sl + 1]),
               reads=[("xs", sl)], writes=[("ss", sl), ("hn", sl)])
        SS.add("dve", lambda e: e.tensor_scalar(out=ssb[:, 2 + sl:3 + sl], in0=ssb[:, sl:sl + 1], scalar1=1.0 / D, scalar2=EPS,
                                                op0=ALU.mult, op1=ALU.add),
               reads=[("ss", sl)], writes=[("rstd", sl)])
        SS.add("act", lambda e: e.activation(out=ssb[:, 2 + sl:3 + sl], in_=ssb[:, 2 + sl:3 + sl], func=AF.Ln),
               reads=[("rstd", sl)], writes=[("rstd", sl)])
        SS.add("act", lambda e: e.activation(out=ssb[:, 2 + sl:3 + sl], in_=ssb[:, 2 + sl:3 + sl], func=AF.Exp, scale=-0.5),
               reads=[("rstd", sl)], writes=[("rstd", sl)])
        SS.add("act", lambda e: e.activation(out=hn_bf[sl][:, :], in_=xs[sl][:, :], func=AF.Copy, scale=ssb[:, 2 + sl:3 + sl]),
               reads=[("xs", sl), ("rstd", sl)], writes=[("hn", sl)])

        def tr_fn(e):
            ins = None
            for c in range(DC):
                ins = e.transpose(bank_bf(B_TRP)[:, c * 128:(c + 1) * 128], hn_bf[sl][:, c * 128:(c + 1) * 128], ident)
            return ins
        SS.add("pe", tr_fn, reads=[("hn", sl), "cst"], writes=[("ps", B_TRP)])
        SS.add("dve", lambda e: e.tensor_copy(
            out=hnT[hp].rearrange("p (c w) -> p c w", c=8)[:, :, sub * 128:(sub + 1) * 128],
            in_=bank_bf(B_TRP).rearrange("p (c w) -> p c w", c=8)),
            reads=[("ps", B_TRP)], writes=[("hnT", hp)])

        def v_fn(e):
            ins = None
            for c in range(DC):
                ins = e.matmul(bank(B_V, 256), lhsT=hnT[hp][:, c * 512 + sub * 128:c * 512 + (sub + 1) * 128],
                               rhs=win_bf[:, c * 1280 + 512:c * 1280 + 768], start=(c == 0), stop=(c == DC - 1))
            return ins
        SS.add("pe", v_fn, reads=win_reads + [("hnT", hp)], writes=[("ps", B_V)])
        SS.add("act", lambda e: e.copy(out=v_sb[:, blk * 256:(blk + 1) * 256], in_=bank(B_V, 256)),
               reads=[("ps", B_V)], writes=[("v", blk)])

    def qk_chain(SS, ti, which, j):
        pos0, W = tile_info(ti)
        hp = ti % 2
        b = B_PJ0 + j
        p2 = B_PS2[j]
        sq = sqb[j]
        rq = rqb[j]
        proj(SS, b, which * 256 + j * 128, W, hp)
        SS.add("act", lambda e: e.activation(out=sq[:, 0:W], in_=bank(b, W), func=AF.Square), reads=[("ps", b)], writes=[("sqb", j)])
        SS.add("pe", lambda e: e.matmul(bank(p2, W), lhsT=bones, rhs=sq[:, 0:W], start=True, stop=True),
               reads=[("sqb", j), "cst"], writes=[("ps", p2)])
        SS.add("dve", lambda e: e.tensor_scalar(out=rq[:, 0:W], in0=bank(p2, W), scalar1=1.0 / 64, scalar2=EPS,
                                                op0=ALU.mult, op1=ALU.add), reads=[("ps", p2)], writes=[("rqb", j)])
        SS.add("act", lambda e: e.activation(out=rq[:, 0:W], in_=rq[:, 0:W], func=AF.Ln), reads=[("rqb", j)], writes=[("rqb", j)])
        SS.add("act", lambda e: e.activation(out=rq[:, 0:W], in_=rq[:, 0:W], func=AF.Exp, scale=-0.5), reads=[("rqb", j)], writes=[("rqb", j)])
        dst = qT if which == 0 else kT
        gsc = col(V_QG) if which == 0 else ext[:, 6:7]
        SS.add("dve", lambda e: e.scalar_tensor_tensor(
            out=dst[:, j * TP + pos0:j * TP + pos0 + W], in0=bank(b, W), scalar=gsc, in1=rq[:, 0:W], op0=ALU.mult, op1=ALU.mult),
            reads=[("ps", b), ("rqb", j), "vec", "ext"], writes=[("qk", which, j, ti)])

    def xr_chain(SS, ti, c):
        pos0, W = tile_info(ti)
        hp = ti % 2
        b = B_PJ0 + c
        gr, gi = B_GR[c], B_GI[c]
        xc, xcb, tr, tg_, a_, m2, bt = xc_sb[c], xc_bf[c], tr_sb[c], ti_sb[c], a_sb[c], m2_sb[c], bt_sb[c]
        K = lambda n: (n, c)
        proj(SS, b, 768 + c * 128, W, hp)
        SS.add("act", lambda e: e.copy(out=xr_sb[c][:, 3:3 + W], in_=bank(b, W)), reads=[("ps", b)], writes=[("xr", c)])
        SS.add("dve", lambda e: e.tensor_scalar(out=xc[:, 0:W], in0=xr_sb[c][:, 3:3 + W], scalar1=col(V_CW + c * 4 + 3),
                                                scalar2=col(V_CB + c), op0=ALU.mult, op1=ALU.add),
               reads=[("xr", c), "vec"], writes=[K("xc")])
        for k in range(3):
            SS.add("dve", lambda e, k=k: e.scalar_tensor_tensor(out=xc[:, 0:W], in0=xr_sb[c][:, k:k + W], scalar=col(V_CW + c * 4 + k),
                                                                 in1=xc[:, 0:W], op0=ALU.mult, op1=ALU.add),
                   reads=[("xr", c), "vec", K("xc")], writes=[K("xc")])
        SS.add("pool", lambda e: e.tensor_copy(out=xr_sb[c][:, 0:3], in_=xr_sb[c][:, W:W + 3]), reads=[("xr", c)], writes=[("xr", c)])
        SS.add("act", lambda e: e.copy(out=xcb[:, 0:W], in_=xc[:, 0:W]), reads=[K("xc")], writes=[K("xcb")])
        SS.add("pe", lambda e: e.matmul(bank(gr, W), lhsT=wrg_bf[:, (0 * 2 + c) * 128:(0 * 2 + c + 1) * 128], rhs=xcb[:, 0:W],
                                        start=True, stop=True), reads=[K("xcb"), "wrg"], writes=[("ps", gr)])
        SS.add("pe", lambda e: e.matmul(bank(gi, W), lhsT=wrg_bf[:, (1 * 2 + c) * 128:(1 * 2 + c + 1) * 128], rhs=xcb[:, 0:W],
                                        start=True, stop=True), reads=[K("xcb"), "wrg"], writes=[("ps", gi)])
        SS.add("act", lambda e: e.activation(out=tr[:, 0:W], in_=bank(gr, W), func=AF.Exp, bias=ext[:, 2 + c:3 + c], scale=-1.0),
               reads=[("ps", gr), "ext"], writes=[K("tr")])
        SS.add("act", lambda e: e.activation(out=tg_[:, 0:W], in_=bank(gi, W), func=AF.Exp, bias=ext[:, 4 + c:5 + c], scale=-1.0),
               reads=[("ps", gi), "ext"], writes=[K("tig")])
        SS.add("act", lambda e: e.activation(out=tr[:, 0:W], in_=tr[:, 0:W], func=AF.Ln, bias=1.0), reads=[K("tr")], writes=[K("tr")])
        SS.add("act", lambda e: e.activation(out=tg_[:, 0:W], in_=tg_[:, 0:W], func=AF.Ln, bias=1.0), reads=[K("tig")], writes=[K("tig")])
        SS.add("act", lambda e: e.activation(out=tr[:, 0:W], in_=tr[:, 0:W], func=AF.Exp, scale=-1.0), reads=[K("tr")], writes=[K("tr")])
        SS.add("act", lambda e: e.activation(out=tg_[:, 0:W], in_=tg_[:, 0:W], func=AF.Exp, scale=-1.0), reads=[K("tig")], writes=[K("tig")])
        SS.add("act", lambda e: e.activation(out=a_[:, 0:W], in_=tr[:, 0:W], func=AF.Exp, scale=ext[:, c:c + 1]),
               reads=[K("tr"), "ext"], writes=[K("a")])
        SS.add("dve", lambda e: e.tensor_tensor(out=m2[:, 0:W], in0=a_[:, 0:W], in1=a_[:, 0:W], op=ALU.mult), reads=[K("a")], writes=[K("m2")])
        SS.add("dve", lambda e: e.tensor_scalar(out=m2[:, 0:W], in0=m2[:, 0:W], scalar1=-1.0, scalar2=1.0, op0=ALU.mult, op1=ALU.add),
               reads=[K("m2")], writes=[K("m2")])
        SS.add("act", lambda e: e.activation(out=m2[:, 0:W], in_=m2[:, 0:W], func=AF.Ln), reads=[K("m2")], writes=[K("m2")])
        SS.add("act", lambda e: e.activation(out=m2[:, 0:W], in_=m2[:, 0:W], func=AF.Exp, scale=0.5), reads=[K("m2")], writes=[K("m2")])
        SS.add("dve", lambda e: e.tensor_tensor(out=bt[:, 0:W], in0=tg_[:, 0:W], in1=xc[:, 0:W], op=ALU.mult),
               reads=[K("tig"), K("xc")], writes=[K("bt")])
        SS.add("dve", lambda e: e.tensor_tensor(out=bt[:, 0:W], in0=bt[:, 0:W], in1=m2[:, 0:W], op=ALU.mult),
               reads=[K("bt"), K("m2")], writes=[K("bt")])
        if ti == 0:
            SS.add("dve", lambda e: e.memset(bt[:, 0:PAD], 0.0), reads=[K("bt")], writes=[K("bt")])
        SS.add("dve", lambda e: e.tensor_tensor_scan(out=hl_sb[c][:, 0:W], data0=a_[:, 0:W], data1=bt[:, 0:W],
                                                     initial=hst[:, c:c + 1], op0=ALU.mult, op1=ALU.add),
               reads=[K("a"), K("bt"), K("hst")], writes=[("hl", c)])
        SS.add("dve", lambda e: e.tensor_copy(out=hst[:, c:c + 1], in_=hl_sb[c][:, W - 1:W]), reads=[("hl", c)], writes=[K("hst")])

    def yg_chain(SS, ti, c):
        pos0, W = tile_info(ti)
        hp = ti % 2
        b = B_PJ0 + c
        y, y2, tg = y_sb[c], y2_sb[c], y2_sb[c]
        K = lambda n: (n, c)
        proj(SS, b, 1024 + c * 128, W, hp)
        SS.add("act", lambda e: e.copy(out=y[:, 0:W], in_=bank(b, W)), reads=[("ps", b)], writes=[K("y")])
        SS.add("act", lambda e: e.activation(out=y2[:, 0:W], in_=bank(b, W), func=AF.Square), reads=[("ps", b)], writes=[K("y2")])
        SS.add("dve", lambda e: e.tensor_scalar(out=y2[:, 0:W], in0=y2[:, 0:W], scalar1=0.044715, scalar2=1.0, op0=ALU.mult, op1=ALU.add),
               reads=[K("y2")], writes=[K("y2")])
        SS.add("dve", lambda e: e.tensor_tensor(out=y2[:, 0:W], in0=y2[:, 0:W], in1=y[:, 0:W], op=ALU.mult),
               reads=[K("y2"), K("y")], writes=[K("y2")])
        SS.add("act", lambda e: e.activation(out=tg[:, 0:W], in_=y2[:, 0:W], func=AF.Exp, scale=-2.0 * GELU_C), reads=[K("y2")], writes=[K("y2")])
        SS.add("act", lambda e: e.activation(out=tg[:, 0:W], in_=tg[:, 0:W], func=AF.Ln, bias=1.0), reads=[K("y2")], writes=[K("y2")])
        SS.add("act", lambda e: e.activation(out=tg[:, 0:W], in_=tg[:, 0:W], func=AF.Exp, scale=-1.0), reads=[K("y2")], writes=[K("y2")])
        SS.add("dve", lambda e: e.tensor_tensor(out=tg[:, 0:W], in0=tg[:, 0:W], in1=y[:, 0:W], op=ALU.mult),
               reads=[K("y2"), K("y")], writes=[K("y2")])
        SS.add("dve", lambda e: e.tensor_tensor(out=ol_bf[c][:, 0:W], in0=tg[:, 0:W], in1=hl_sb[c][:, 0:W], op=ALU.mult),
               reads=[K("y2"), ("hl", c)], writes=[("ol", c)])
        sfn, nd = store_fn(ol_bf[c], 256 + c * 128, ti)
        SS.add("pool", sfn, reads=[("ol", c)], writes=[("mo", "l", c, ti)], chan=("ol", c), ndma=nd)

    def mk(fn, *a_):
        r = Rec()
        fn(r, *a_)
        return r

    S.barrier()
    ZIP_PREP = (1, 2, 3)
    zip_emit([mk(prep_chain, 0, 0)])
    for ti in range(NQT):
        preps = []
        if ti + 1 < NQT:
            _, Wn = tile_info(ti + 1)
            preps = [mk(prep_chain, ti + 1, sub) for sub in range(Wn // 128)]
        groups = [
            [mk(qk_chain, ti, 0, 0), mk(qk_chain, ti, 0, 1)],
            [mk(qk_chain, ti, 1, 0), mk(qk_chain, ti, 1, 1)],
            [mk(xr_chain, ti, 0), mk(xr_chain, ti, 1)],
            [mk(yg_chain, ti, 0), mk(yg_chain, ti, 1)],
        ]
        if len(preps) == 4:
            p0 = mk(prep_chain, ti + 1, 0, 6, 7)
            zip_emit(groups[0])
            zip_emit(groups[1] + [preps[1]])
            zip_emit(groups[2] + [preps[2]])
            zip_emit(groups[3] + [preps[3], p0])
        else:
            for gi_, grp in enumerate(groups):
                zip_emit(grp)
                if gi_ < len(preps):
                    zip_emit([preps[gi_]])

    S.barrier()
    A.off = mark12
    e_sb = [A.f32(1024) for _ in range(3)]
    sp_sb = [A.bf(1024) for _ in range(3)]
    g_sb = [A.bf(1024) for _ in range(2)]
    w_sb = [A.bf(1024) for _ in range(3)]
    osb = [A.bf(2 * 512), A.bf(2 * 512)]
    pst32 = [A.f32(DFF), A.f32(DFF)]
    pst16 = [A.bf(DFF), A.bf(DFF)]
    pieces = []
    for c in range(DC):
        pieces.append((w_out[c * 128:(c + 1) * 128, :], wout_s[c * 128:(c + 1) * 128, :], 1024, None))
    for c in range(DC):
        for hh in range(2):
            pieces.append((w_fi[c * 128:(c + 1) * 128, hh * DFF:(hh + 1) * DFF], wfi_s[c * 128:(c + 1) * 128, hh * DFF:(hh + 1) * DFF],
                           DFF, col(V_G2 + c)))
    for j in range(NPAIR):
        pieces.append((w_fo[j * 128:(j + 1) * 128, :], wfo_s[j * 128:(j + 1) * 128, :], 1024, None))
    wscr_keys = [("wscr", i) for i in range(len(pieces))]

    def piece_stage(i, st):
        src, dst, ncol, sc = pieces[i]
        sl = i % 2
        if st == 0:
            S.add("sp", lambda e, s: e.dma_start(out=pst32[sl][:, 0:ncol], in_=src).then_inc(s, 16),
                  writes=[("p32", sl)], chan=("p32", sl))
        elif st == 1:
            if sc is None:
                S.add("dve", lambda e: e.tensor_copy(out=pst16[sl][:, 0:ncol], in_=pst32[sl][:, 0:ncol]),
                      reads=[("p32", sl)], writes=[("p16", sl)])
            else:
                S.add("dve", lambda e: e.tensor_scalar(out=pst16[sl][:, 0:ncol], in0=pst32[sl][:, 0:ncol], scalar1=sc, scalar2=None, op0=ALU.mult),
                      reads=[("p32", sl), "vec"], writes=[("p16", sl)])
        else:
            S.add("sp", lambda e, s: e.dma_start(out=dst, in_=pst16[sl][:, 0:ncol]).then_inc(s, 16),
                  reads=[("p16", sl)], writes=[("wscr", i)], chan=("p16", sl))

    items = []
    for ti in range(NQT):
        pos0, W = tile_info(ti)
        b0 = pos0 // 128
        nsub = W // 128
        kbs = list(range(b0 + nsub - 1, -1, -1))
        for n, kb in enumerate(kbs):
            for j in range(2):
                items.append(dict(ti=ti, pos0=pos0, W=W, b0=b0, kb=kb, j=j, first=(n == 0), last=(n == len(kbs) - 1)))
    NI = len(items)

    S.add("dve", lambda e: e.memset(ext[:, 9:10], 0.0),
          reads=[("qk", w, j, t) for w in range(2) for j in range(2) for t in range(NQT)] + [("v", bl) for bl in range(NB)],
          writes=["kall"])

    def v3(buf, c0, W):
        return buf.rearrange("p (h w) -> p h w", h=2)[:, :, c0:W]

    def zview(c0, W):
        return ps[:, 0:1024].rearrange("p (h w) -> p h w", h=2)[:, :, c0:W]

    def pview(j, c0, W):
        return ps[:, (2 + 2 * j) * 512:(4 + 2 * j) * 512].rearrange("p (h w) -> p h w", h=2)[:, :, c0:W]

    RG = [[0, 1], [2, 3], [4, 5], [6, 7]]

    def tile_keys(ti):
        return [("mo", "l", c, ti) for c in range(2)] + [("mo", "a", j, ti) for j in range(2)]

    gathered = set()
    half_cache = {}

    def get_half(e):
        if "h" not in half_cache:
            half_cache["h"] = e.partition_id() % 2
        return half_cache["h"]

    def mh_part(j):
        def fn(e, s):
            half = get_half(e)
            ins = None
            for r in range(2):
                ins = e.dma_start(out=mixed_half[r * 512:(r + 1) * 512, 2 + j * GW:2 + (j + 1) * GW],
                                  in_=ma_big[bass.ds(half * 2048 + j * 1024 + r * 512, 512), :]).then_inc(s, 16)
            return ins
        S.add("sp", fn, reads=[("mall", j), ("mall", 2 + j)], writes=[("mhalf", j)], chan=("mh", j), ndma=2)

    T_HALO = 1 + (HALF - 2) // 512

    def maybe_gather(ti):
        if ti == T_HALO:
            S.add("pool", lambda e, s: e.collective_compute("AllGather", ALU.bypass, replica_groups=RG,
                                                            ins=[mo_halo_t.ap().opt()], outs=[ma_halo_t.ap().opt()]).then_inc(s),
                  reads=tile_keys(0) + tile_keys(T_HALO), writes=[("mall", "h")], chan="cch", inc=1)

            def mhh(e, s):
                half = get_half(e)
                return e.dma_start(out=mixed_half[:, 0:2], in_=ma_halo[:, bass.ds(half * 2, 2)]).then_inc(s, 16)
            S.add("sp", mhh, reads=[("mall", "h")], writes=[("mhalf", "h")], chan=("mh", "h"))
        if ti == 0:
            return
        done_tok = 512 * ti
        for g in range(4):
            if g in gathered or (g + 1) * GW > done_tok:
                continue
            gathered.add(g)
            t_lo = 1 + (g * GW) // 512
            t_hi = 1 + ((g + 1) * GW - 1) // 512
            keys = []
            for t in range(t_lo, t_hi + 1):
                keys += tile_keys(t)
            S.add("pool", lambda e, s, g=g: e.collective_compute(
                "AllGather", ALU.bypass, replica_groups=RG, ins=[mo_g[g].ap().opt()],
                outs=[ma_big[g * 1024:(g + 1) * 1024, :].opt()]).then_inc(s),
                reads=keys, writes=[("mall", g)], chan=("cc", g), inc=1)
            if g >= 2:
                mh_part(g - 2)

    def c0_of(it):
        return 128 * (it["kb"] - it["b0"]) if it["kb"] >= it["b0"] else 0

    def PE1(i):
        it = items[i]
        kb, W, pos0, j = it["kb"], it["W"], it["pos0"], it["j"]
        c0 = c0_of(it)

        def fn(e):
            ins = None
            for hh in range(2):
                r = hh * 64
                ins = e.matmul(ps[:, hh * 512 + c0:hh * 512 + W], lhsT=kT[r:r + 64, j * TP + kb * 128:j * TP + (kb + 1) * 128],
                               rhs=qT[r:r + 64, j * TP + pos0 + c0:j * TP + pos0 + W], start=True, stop=True)
            return ins
        S.add("pe", fn, reads=["kall"], writes=[("ps", 0), ("ps", 1)])

    def ACT12(i):
        it = items[i]
        W = it["W"]
        c0 = c0_of(it)
        eb = e_sb[i % 3]
        sb = sp_sb[i % 3]
        S.add("act", lambda e: e.activation(out=v3(eb, c0, W), in_=zview(c0, W), func=AF.Exp),
              reads=[("ps", 0), ("ps", 1)], writes=[("e", i % 3)])
        S.add("act", lambda e: e.activation(out=v3(sb, c0, W), in_=v3(eb, c0, W), func=AF.Ln, bias=1.0),
              reads=[("e", i % 3)], writes=[("sp", i % 3)])
        if it["kb"] >= it["b0"]:
            for hh in range(2):
                S.add("dve", lambda e, hh=hh: e.tensor_tensor(out=sb[:, hh * 512 + c0:hh * 512 + c0 + 128],
                                                               in0=sb[:, hh * 512 + c0:hh * 512 + c0 + 128], in1=maskv(0, 128), op=ALU.mult),
                      reads=[("sp", i % 3), "cst"], writes=[("sp", i % 3)])

    def PE2(i):
        it = items[i]
        W, j = it["W"], it["j"]
        c0 = c0_of(it)
        sb = sp_sb[i % 3]

        def fn(e):
            ins = None
            for hh in range(2):
                b_ = 2 + 2 * j + hh
                ins = e.matmul(ps[:, b_ * 512 + c0:b_ * 512 + W], lhsT=negL, rhs=sb[:, hh * 512 + c0:hh * 512 + W],
                               start=it["first"], stop=False, skip_group_check=True)
            return ins
        S.add("pe", fn, reads=[("sp", i % 3), "cst"], writes=[("ps", 2 + 2 * j), ("ps", 3 + 2 * j)])

    def ACT3(i):
        it = items[i]
        W, j = it["W"], it["j"]
        c0 = c0_of(it)
        gb = g_sb[i % 2]
        eb = e_sb[i % 3]
        wb = w_sb[i % 3]
        S.add("act", lambda e: e.activation(out=v3(gb, c0, W), in_=pview(j, c0, W), func=AF.Exp),
              reads=[("ps", 2 + 2 * j), ("ps", 3 + 2 * j)], writes=[("g", i % 2)])
        S.add("dve", lambda e: e.tensor_tensor(out=v3(wb, c0, W), in0=v3(eb, c0, W), in1=v3(gb, c0, W), op=ALU.mult),
              reads=[("e", i % 3), ("g", i % 2)], writes=[("w", i % 3)])
        if it["kb"] >= it["b0"]:
            for hh in range(2):
                S.add("dve", lambda e, hh=hh: e.tensor_tensor(out=wb[:, hh * 512 + c0:hh * 512 + c0 + 128],
                                                               in0=wb[:, hh * 512 + c0:hh * 512 + c0 + 128], in1=maskv(0, 128), op=ALU.mult),
                      reads=[("w", i % 3), "cst"], writes=[("w", i % 3)])

    def PE4(i):
        it = items[i]
        if it["last"]:
            return
        W, j = it["W"], it["j"]
        c0 = c0_of(it)
        sb = sp_sb[i % 3]

        def fn(e):
            ins = None
            for hh in range(2):
                b_ = 2 + 2 * j + hh
                ins = e.matmul(ps[:, b_ * 512 + c0:b_ * 512 + W], lhsT=negU, rhs=sb[:, hh * 512 + c0:hh * 512 + W],
                               start=False, stop=False, skip_group_check=True)
            return ins
        S.add("pe", fn, reads=[("sp", i % 3), "cst"], writes=[("ps", 2 + 2 * j), ("ps", 3 + 2 * j)])

    def PE3(i):
        it = items[i]
        kb, W, pos0, ti, j = it["kb"], it["W"], it["pos0"], it["ti"], it["j"]
        c0 = c0_of(it)
        ob = 6 + j
        wb = w_sb[i % 3]
        par = ti % 2

        def fn(e):
            ins = None
            for hh in range(2):
                h = 2 * j + hh
                ins = e.matmul(ps[hh * 64:(hh + 1) * 64, ob * 512 + c0:ob * 512 + W], lhsT=v_sb[:, kb * 256 + h * 64:kb * 256 + (h + 1) * 64],
                               rhs=wb[:, hh * 512 + c0:hh * 512 + W], start=it["first"], stop=it["last"], skip_group_check=True)
            return ins
        S.add("pe", fn, reads=[("w", i % 3), "kall"], writes=[("ps", ob)])
        if it["last"]:
            S.add("dve", lambda e: e.tensor_copy(out=osb[par][:, j * 512:j * 512 + W], in_=bank(ob, W)),
                  reads=[("ps", ob)], writes=[("osb", par, j)])
            sfn, nd = store_fn(osb[par][:, j * 512:(j + 1) * 512], j * 128, ti)
            S.add("pool", sfn, reads=[("osb", par, j)], writes=[("mo", "a", j, ti)], chan=("osb", par, j), ndma=nd)
            if j == 1:
                maybe_gather(ti)

    PERIOD = max(4, (NI - 8) // len(pieces))
    PE1(0)
    for s in range(-1, NI + 1):
        if s >= 0:
            pi_, ph_ = divmod(s, PERIOD)
            if pi_ < len(pieces):
                if ph_ == 0:
                    piece_stage(pi_, 0)
                elif ph_ == PERIOD // 2:
                    piece_stage(pi_, 1)
            if pi_ >= 1 and pi_ - 1 < len(pieces) and ph_ == 1:
                piece_stage(pi_ - 1, 2)
        if 0 <= s + 1 < NI:
            ACT12(s + 1)
        if s + 2 < NI:
            PE1(s + 2)
        if 0 <= s + 1 < NI:
            PE2(s + 1)
        if 0 <= s < NI:
            ACT3(s)
            PE4(s)
        if 0 <= s - 1 < NI:
            PE3(s - 1)

    done_p = min(len(pieces), (NI // PERIOD) + 1)
    for i in range(len(pieces)):
        last_s = NI
        if i * PERIOD > last_s:
            piece_stage(i, 0)
        if i * PERIOD + PERIOD // 2 > last_s:
            piece_stage(i, 1)
        if (i + 1) * PERIOD + 1 > last_s:
            piece_stage(i, 2)

    S.barrier(exclude=lambda ch: isinstance(ch, tuple) and ch[0] == "cc")
    A.off = mark_stage
    wout_bf = A.bf(8 * 1024)
    wfi_bf = A.bf(8 * 2 * DFF)
    wfo_bf = A.bf(NPAIR * 1024)
    w3_keys = []
    qsel = ["sp", "pool"]
    nq = [0]

    def wload(dst, src, key):
        q = qsel[nq[0] % 2]
        nq[0] += 1
        S.add(q, lambda e, s: e.dma_start(out=dst, in_=src).then_inc(s, 16), reads=wscr_keys, writes=[key], chan=key)
        w3_keys.append(key)

    for hf in range(2):
        wload(wout_bf.rearrange("p (c n) -> p c n", c=8)[:, hf * 4:(hf + 1) * 4, :],
              wout_s.rearrange("(c p) n -> p c n", p=128)[:, hf * 4:(hf + 1) * 4, :], ("w3", "o", hf))
    for c in range(DC):
        wload(wfi_bf[:, c * 2 * DFF:(c + 1) * 2 * DFF], wfi_s[c * 128:(c + 1) * 128, :], ("w3", "i", c))
    for q4 in range(2):
        wload(wfo_bf.rearrange("p (j n) -> p j n", j=NPAIR)[:, q4 * 11:(q4 + 1) * 11, :],
              wfo_s.rearrange("(j p) n -> p j n", p=128)[:, q4 * 11:(q4 + 1) * 11, :], ("w3", "f", q4))
    K_WO = [k_ for k_ in w3_keys if k_[1] == "o"]
    K_WI = [k_ for k_ in w3_keys if k_[1] == "i"]
    K_WF = [k_ for k_ in w3_keys if k_[1] == "f"]

    h2 = [A.f32(1024) for _ in range(4)]
    hn2_bf = [A.bf(1024), A.bf(1024)]
    ss3 = A.f32(8)
    HW = 2 + W3
    hn2T = [A.bf(8 * HW), A.bf(8 * HW)]
    mt = A.bf(8 * W3)
    ubuf = [A.f32(2 + W3), A.f32(2 + W3)]
    gbuf = [A.f32(2 + W3), A.f32(2 + W3)]
    uc = [A.f32(W3), A.f32(W3)]
    gc = [A.f32(W3), A.f32(W3)]
    sg = [A.f32(W3), A.f32(W3)]
    act_bf = A.bf(NPAIR * W3)

    B_FO = [0, 1]
    B_T3 = 2
    B_U = [3, 4, 0]
    B_G = [5, 6, 1]
    B_WO = 7
    mixed_half_r = mixed_half.rearrange("(c p) w -> p c w", p=128)

    def tile_geom(t):
        if t < 0:
            return 0, 2, 2, 1, 1
        return 2 + t * W3, W3, 128, W3 // 128, t % 2

    def prep1(t):
        rel0, Wt, n, nsub, par = tile_geom(t)
        S.add("sp", lambda e, s: e.dma_start(out=mt.rearrange("p (c w) -> p c w", c=8)[:, :, 0:Wt],
                                              in_=mixed_half_r[:, :, rel0:rel0 + Wt]).then_inc(s, 16),
              reads=[("mhalf", "h") if t < 0 else ("mhalf", 0 if (rel0 - 2 + Wt) <= GW else 1)], writes=["mt"], chan="mt")
        for sub in range(nsub):
            sl = 2 * par + sub
            q = sub
            r = rel0 + sub * 128
            S.add("sp", lambda e, s, sl=sl, r=r: e.dma_start(out=h2[sl][0:n, :], in_=xhalf[r:r + n, :]).then_inc(s, 16),
                  writes=[("h2", sl)], chan=("h2", sl))
            for hf in range(2):
                def wo_fn(e, sub=sub, hf=hf):
                    ins = None
                    for c in range(DC):
                        ins = e.matmul(ps[0:n, B_WO * 512:(B_WO + 1) * 512], lhsT=mt[:, c * W3 + sub * 128:c * W3 + sub * 128 + n],
                                       rhs=wout_bf[:, c * 1024 + hf * 512:c * 1024 + (hf + 1) * 512], start=(c == 0), stop=(c == DC - 1))
                    return ins
                S.add("pe", wo_fn, reads=["mt"] + K_WO, writes=[("ps", B_WO)])
                S.add("dve", lambda e, sl=sl, hf=hf: e.tensor_tensor(out=h2[sl][0:n, hf * 512:(hf + 1) * 512],
                                                                     in0=ps[0:n, B_WO * 512:(B_WO + 1) * 512],
                                                                     in1=h2[sl][0:n, hf * 512:(hf + 1) * 512], op=ALU.add),
                      reads=[("ps", B_WO), ("h2", sl)], writes=[("h2", sl)])
            S.add("pool", lambda e, q=q: e.memset(ss3[:, q:q + 1], 0.0), writes=[("ss3", q)])
            S.add("act", lambda e, sl=sl, q=q: e.activation(out=hn2_bf[q][0:n, :], in_=h2[sl][0:n, :], func=AF.Square,
                                                            accum_out=ss3[0:n, q:q + 1]),
                  reads=[("h2", sl)], writes=[("ss3", q), ("hn2", q)])
            S.add("dve", lambda e, q=q: e.tensor_scalar(out=ss3[0:n, 2 + q:3 + q], in0=ss3[0:n, q:q + 1], scalar1=1.0 / D, scalar2=EPS,
                                                         op0=ALU.mult, op1=ALU.add), reads=[("ss3", q)], writes=[("rs3", q)])
            S.add("act", lambda e, q=q: e.activation(out=ss3[0:n, 2 + q:3 + q], in_=ss3[0:n, 2 + q:3 + q], func=AF.Ln),
                  reads=[("rs3", q)], writes=[("rs3", q)])
            S.add("act", lambda e, q=q: e.activation(out=ss3[0:n, 2 + q:3 + q], in_=ss3[0:n, 2 + q:3 + q], func=AF.Exp, scale=-0.5),
                  reads=[("rs3", q)], writes=[("rs3", q)])
            S.add("act", lambda e, sl=sl, q=q: e.activation(out=hn2_bf[q][0:n, :], in_=h2[sl][0:n, :], func=AF.Copy, scale=ss3[0:n, 2 + q:3 + q]),
                  reads=[("h2", sl), ("rs3", q)], writes=[("hn2", q)])

    def prep2(t):
        rel0, Wt, n, nsub, par = tile_geom(t)
        for sub in range(nsub):
            q = sub

            def tr_fn(e, q=q):
                ins = None
                for c in range(DC):
                    ins = e.transpose(bank_bf(B_T3)[:, c * 128:c * 128 + n], hn2_bf[q][0:n, c * 128:(c + 1) * 128], ident[0:n, 0:n])
                return ins
            S.add("pe", tr_fn, reads=[("hn2", q), "cst"], writes=[("ps", B_T3)])
            dcol = 0 if t < 0 else 2 + sub * 128
            dpar = 0 if t < 0 else par
            S.add("dve", lambda e, dcol=dcol, dpar=dpar: e.tensor_copy(
                out=hn2T[dpar].rearrange("p (c w) -> p c w", c=8)[:, :, dcol:dcol + n],
                in_=bank_bf(B_T3).rearrange("p (c w) -> p c w", c=8)[:, :, 0:n]),
                reads=[("ps", B_T3)], writes=[("hn2T", dpar)])
        if t >= 0 and t + 1 < NST:
            S.add("dve", lambda e: e.tensor_copy(
                out=hn2T[1 - par].rearrange("p (c w) -> p c w", c=8)[:, :, 0:2],
                in_=hn2T[par].rearrange("p (c w) -> p c w", c=8)[:, :, W3:W3 + 2]),
                reads=[("hn2T", par)], writes=[("hn2T", 1 - par)])

    pair_ctr = [0]

    def ffn_in(t):
        rel0, Wt, n, nsub, par = tile_geom(t)
        final = t >= 0
        pcs = {}

        def mm(j):
            pc = pair_ctr[0] % 3
            pair_ctr[0] += 1
            pcs[j] = pc
            for which, bb in enumerate((B_U[pc], B_G[pc])):
                slab = j + which * NPAIR

                def fi_fn(e, bb=bb, slab=slab):
                    ins = None
                    for c in range(DC):
                        ins = e.matmul(bank(bb, Wt + 2), lhsT=wfi_bf[:, c * 2 * DFF + slab * 128:c * 2 * DFF + (slab + 1) * 128],
                                       rhs=hn2T[par][:, c * HW:c * HW + Wt + 2], start=(c == 0), stop=(c == DC - 1))
                    return ins
                S.add("pe", fi_fn, reads=[("hn2T", par)] + K_WI, writes=[("ps", bb)])

        def post1(j):
            pc = pcs[j]
            p2 = j % 2
            for which, (bb, cbuf) in enumerate(((B_U[pc], uc[p2]), (B_G[pc], gc[p2]))):
                slab = j + which * NPAIR
                ck = ("cb", which, p2)
                S.add("act", lambda e, bb=bb, cbuf=cbuf, slab=slab: e.activation(
                    out=cbuf[:, 0:Wt], in_=ps[:, bb * 512 + 2:bb * 512 + 2 + Wt], func=AF.Identity,
                    scale=col(V_FCW + slab * 3 + 2), bias=col(V_FCB + slab)),
                    reads=[("ps", bb), "vec"], writes=[ck])
                for k in range(2):
                    S.add("dve", lambda e, bb=bb, cbuf=cbuf, slab=slab, k=k: e.scalar_tensor_tensor(
                        out=cbuf[:, 0:Wt], in0=ps[:, bb * 512 + k:bb * 512 + k + Wt], scalar=col(V_FCW + slab * 3 + k), in1=cbuf[:, 0:Wt],
                        op0=ALU.mult, op1=ALU.add), reads=[("ps", bb), "vec", ck], writes=[ck])

        def post2(j):
            pc = j % 2
            S.add("act", lambda e: e.activation(out=sg[pc][:, 0:Wt], in_=gc[pc][:, 0:Wt], func=AF.Silu),
                  reads=[("cb", 1, pc)], writes=[("sg", pc)])
            S.add("pool", lambda e: e.tensor_tensor(out=act_bf[:, j * W3:j * W3 + Wt], in0=sg[pc][:, 0:Wt], in1=uc[pc][:, 0:Wt], op=ALU.mult),
                  reads=[("sg", pc), ("cb", 0, pc)], writes=[("actT", j)])

        for j in range(NPAIR + 2):
            if j < NPAIR:
                mm(j)
            if 0 <= j - 1 < NPAIR:
                post1(j - 1)
            if 0 <= j - 2 < NPAIR:
                post2(j - 2)

    def ffn_out(t, sub):
        rel0, Wt, n, nsub, par = tile_geom(t)
        sl = 2 * par + sub

        def fo_fn(e):
            ins = None
            for hf in range(2):
                for j in range(NPAIR):
                    ins = e.matmul(bank(B_FO[hf], 512), lhsT=act_bf[:, j * W3 + sub * 128:j * W3 + (sub + 1) * 128],
                                   rhs=wfo_bf[:, j * 1024 + hf * 512:j * 1024 + (hf + 1) * 512], start=(j == 0), stop=(j == NPAIR - 1))
            return ins
        S.add("pe", fo_fn, reads=[("actT", j) for j in range(NPAIR)] + K_WF, writes=[("ps", 0), ("ps", 1)])
        S.add("dve", lambda e: e.tensor_tensor(out=h2[sl][:, :], in0=ps[:, 0:1024], in1=h2[sl][:, :], op=ALU.add),
              reads=[("ps", 0), ("ps", 1), ("h2", sl)], writes=[("h2", sl)])
        r0 = rel0 - 2 + sub * 128
        S.add("pool", lambda e, s: e.dma_start(out=out[r0:r0 + 128, :], in_=h2[sl][:, :]).then_inc(s, 16),
              reads=[("h2", sl)], writes=[("h2", sl), ("out", r0)], chan=("h2", sl))

    prep1(-1)
    prep2(-1)
    prep1(0)
    prep2(0)
    for t in range(NST):
        ffn_in(t)
        if t + 1 < NST:
            prep1(t + 1)
        ffn_out(t, 0)
        if t + 1 < NST:
            prep2(t + 1)
        ffn_out(t, 1)

    S.add("sp", lambda e: None, reads=[("out", r0) for r0 in range(0, HALF, 128)], writes=["done"])

    S.finalize()
    chans = list(S.chan_count.keys())
    sem_ctxs = []
    sems = {}
    for e in Sched.ENG:
        c = nc.semaphore("s_" + e)
        sems[e] = c.__enter__()
        sem_ctxs.append(c)
    chan_sems = {}
    for i, ch in enumerate(chans):
        c = nc.semaphore("c_%d" % i)
        chan_sems[ch] = c.__enter__()
        sem_ctxs.append(c)
    with nc.Block() as block:
        S.emit(nc, block, sems, chan_sems)
    for c in reversed(sem_ctxs):
        c.__exit__(None, None, None)
    pctx.__exit__(None, None, None)
    ctx.__exit__(None, None, None)
    return nc


def _consts():
    c = np.zeros((128, NCST), np.float32)
    j = np.arange(128)[:, None]
    s = np.arange(128)[None, :]
    c[:, C_ID:C_ID + 128] = (j == s)
    c[:, C_NL:C_NL + 128] = -(j >= s).astype(np.float32)
    c[:, C_NU:C_NU + 128] = -(j < s).astype(np.float32)
    c[:, C_BO:C_BO + 128] = ((j // 64) == (s // 64))
    t = np.arange(512)[None, :]
    for k in range(4):
        c[:, C_MK + k * 512:C_MK + (k + 1) * 512] = ((128 * k + j) < t)
    return c


def _prep_inputs(inputs, SEQ):
    f = lambda a: np.ascontiguousarray(np.asarray(a), dtype=np.float32)
    x = f(inputs["x"])
    meta = f(inputs["meta_tokens"])
    w_in = f(inputs["w_in"])[0]
    w_out = f(inputs["w_out"])[0]
    w_fi = f(inputs["w_ffn_in"])[0]
    w_fo = f(inputs["w_ffn_out"])[0]
    g1 = f(inputs["norm1_g"])[0]
    g2 = f(inputs["norm2_g"])[0]
    qg = f(inputs["q_norm_g"])[0]
    kg = f(inputs["k_norm_g"])[0]
    cw = f(inputs["conv_w"])[0]
    cb = f(inputs["conv_b"])[0]
    wa = f(inputs["w_rg_a"])[0]
    wi = f(inputs["w_rg_i"])[0]
    ba = f(inputs["b_rg_a"])[0]
    bi = f(inputs["b_rg_i"])[0]
    lam = f(inputs["lru_lambda"])[0]
    fcw = f(inputs["ffn_conv_w"])[0]
    fcb = f(inputs["ffn_conv_b"])[0]
    consts = _consts()
    TP = SEQ + 128
    maps = []
    for core in range(8):
        b, p = core // 2, core % 2
        xpad = np.zeros((TP, D), np.float32)
        xpad[PAD:128] = meta
        xpad[128:] = x[b]
        cs = slice(256 * p, 256 * p + 256)
        wic = np.concatenate([w_in[:, 0:512][:, cs], w_in[:, 512:1024][:, cs], w_in[:, 1024:1536][:, cs],
                              w_in[:, 1536:2048][:, cs], w_in[:, 2048:2560][:, cs]], axis=1)
        vec = np.zeros((128, NV), np.float32)
        vec[:, V_G1:V_G1 + 8] = g1.reshape(8, 128).T
        vec[:, V_G2:V_G2 + 8] = g2.reshape(8, 128).T
        vec[:, V_QG] = np.tile(qg, 2)
        vec[:, V_KG] = np.tile(kg, 2)
        for c in range(2):
            ch = slice(256 * p + 128 * c, 256 * p + 128 * c + 128)
            vec[:, V_CW + c * 4:V_CW + c * 4 + 4] = cw[:, ch].T
            vec[:, V_CB + c] = cb[ch]
            vec[:, V_BA + c] = ba[ch]
            vec[:, V_BI + c] = bi[ch]
            vec[:, V_LAM + c] = lam[ch]
        for s in range(NSLAB):
            vec[:, V_FCW + s * 3:V_FCW + s * 3 + 3] = fcw[:, s * 128:(s + 1) * 128].T
            vec[:, V_FCB + s] = fcb[s * 128:(s + 1) * 128]
        wrg = np.zeros((128, 4 * 128), np.float32)
        for gi, wsrc in enumerate((wa, wi)):
            for c in range(2):
                for k in range(2):
                    blk = 4 * p + 2 * c + k
                    o = (gi * 2 + c) * 128
                    wrg[64 * k:64 * k + 64, o + 64 * k:o + 64 * k + 64] = wsrc[blk]
        perm = []
        for r in range(2):
            perm += list(range(256 * r, 256 * r + 256))
            perm += list(range(512 + 256 * r, 512 + 256 * r + 256))
        maps.append({
            "xpad": xpad, "xhalf": np.ascontiguousarray(xpad[126 + (SEQ // 2) * p:126 + (SEQ // 2) * p + SEQ // 2 + 2]), "w_in": np.ascontiguousarray(wic), "vecs": vec, "w_rg": wrg,
            "w_out": np.ascontiguousarray(w_out[perm]), "w_ffn_in": w_fi, "w_ffn_out": w_fo, "consts": consts,
        })
    return maps


_NC_CACHE = {}


def kernel(**inputs):
    x = np.asarray(inputs["x"])
    B, SEQ, _ = x.shape
    assert B == 4
    if SEQ not in _NC_CACHE:
        _NC_CACHE[SEQ] = build_nc(SEQ)
    nc = _NC_CACHE[SEQ]
    maps = _prep_inputs(inputs, SEQ)
    res = run_bass_kernel_spmd(nc, maps, core_ids=list(range(8)))
    outp = np.empty((B, SEQ, D), np.float32)
    HALF = SEQ // 2
    for core in range(8):
        b, p = core // 2, core % 2
        outp[b, p * HALF:(p + 1) * HALF] = res.results[core]["out"]
    return outp
```
